# Optimizing a Trainium2 kernel written in Bass

```python
import jax
import jax.numpy as jnp
from jax import lax
import numpy as np

D_MODEL = 1024
BATCH = 16
SEQ = 2048
DEPTH = 2

CHUNK = 64
Q_BLOCK = 128
RMS_EPS = 1e-6
D_FF = 2816

FOX_HEADS = 8
FOX_HEAD_DIM = 64

MLA_HEADS = 8
MLA_Q_RANK = 256
MLA_KV_RANK = 128
MLA_NOPE_DIM = 64
MLA_ROPE_DIM = 32
MLA_V_DIM = 64
ROPE_THETA = 10000.0

GDN_HEADS = 8
GDN_HEAD_DIM = 128
GDN_CONV = 4

N_EVEN = (DEPTH + 1) // 2
N_ODD = DEPTH // 2

FOX_WIDTH = FOX_HEADS * FOX_HEAD_DIM
ATTN_SPLITS = (FOX_WIDTH, FOX_WIDTH, FOX_WIDTH, FOX_HEADS, MLA_Q_RANK, MLA_KV_RANK, MLA_ROPE_DIM)
ATTN_IN = sum(ATTN_SPLITS)
ATTN_OUT = FOX_WIDTH + MLA_HEADS * MLA_V_DIM
GDN_WIDTH = GDN_HEADS * GDN_HEAD_DIM
GDN_SPLITS = (3 * GDN_WIDTH, GDN_HEADS, GDN_HEADS, GDN_WIDTH)
GDN_IN = sum(GDN_SPLITS)

kernel_name = "hybrid_fox_mla_gdn_macaron"


def _split(t, sizes):
    idx = np.cumsum(sizes)[:-1].tolist()
    return jnp.split(t, idx, axis=-1)


def _rms(x, g):
    xf = x.astype(jnp.float32)
    y = xf * lax.rsqrt(jnp.mean(xf * xf, axis=-1, keepdims=True) + RMS_EPS)
    return (y * g.astype(jnp.float32)).astype(x.dtype)


def _l2norm(x):
    return x * lax.rsqrt(jnp.sum(x * x, axis=-1, keepdims=True) + 1e-6)


def _swiglu(h, w13, w2):
    a, b = jnp.split(h @ w13, 2, axis=-1)
    return (jax.nn.silu(a) * b) @ w2


def _rope(x, pos):
    half = x.shape[-1] // 2
    inv = ROPE_THETA ** (-jnp.arange(half, dtype=jnp.float32) / half)
    ang = pos.astype(jnp.float32)[:, None] * inv[None, :]
    cos, sin = jnp.cos(ang), jnp.sin(ang)
    xf = x.astype(jnp.float32)
    x1, x2 = xf[..., :half], xf[..., half:]
    return jnp.concatenate([x1 * cos - x2 * sin, x2 * cos + x1 * sin], axis=-1).astype(x.dtype)


def _blocked_attention(q, k, v, scale, chunk_causal, log_forget_cum=None):
    s_len = q.shape[2]
    neg = jnp.finfo(jnp.float32).min
    outs = []
    for start in range(0, s_len, Q_BLOCK):
        end = start + Q_BLOCK
        scores = jnp.einsum("bhqd,bhkd->bhqk", q[:, :, start:end], k[:, :, :end],
                            preferred_element_type=jnp.float32) * scale
        t_pos = jnp.arange(start, end)[:, None]
        s_pos = jnp.arange(end)[None, :]
        if chunk_causal:
            mask = (s_pos // CHUNK) <= (t_pos // CHUNK)
        else:
            mask = s_pos <= t_pos
        if log_forget_cum is not None:
            scores = scores + log_forget_cum[:, :, start:end, None] - log_forget_cum[:, :, None, :end]
        probs = jax.nn.softmax(jnp.where(mask, scores, neg), axis=-1)
        outs.append(jnp.einsum("bhqk,bhkd->bhqd", probs.astype(v.dtype), v[:, :, :end]))
    return jnp.concatenate(outs, axis=2)


def _fox_mla_mixer(h, w_in, f_bias, q_norm, w_uq, kv_norm, w_ukv, w_out):
    b, s, _ = h.shape
    pos = jnp.arange(s)
    q_a, k_a, v_a, f_logit, c_q, c_kv, k_pe = _split(h @ w_in, ATTN_SPLITS)

    def heads(t, n):
        return t.reshape(b, s, n, -1).transpose(0, 2, 1, 3)

    log_f = jax.nn.log_sigmoid(f_logit.astype(jnp.float32) + f_bias.astype(jnp.float32))
    f_cum = jnp.cumsum(log_f, axis=1).transpose(0, 2, 1)
    o_a = _blocked_attention(heads(q_a, FOX_HEADS), heads(k_a, FOX_HEADS), heads(v_a, FOX_HEADS),
                             FOX_HEAD_DIM ** -0.5, False, f_cum)

    q_b = heads(_rms(c_q, q_norm) @ w_uq, MLA_HEADS)
    kv_b = heads(_rms(c_kv, kv_norm) @ w_ukv, MLA_HEADS)
    q_nope, q_pe = q_b[..., :MLA_NOPE_DIM], q_b[..., MLA_NOPE_DIM:]
    k_nope, v_b = kv_b[..., :MLA_NOPE_DIM], kv_b[..., MLA_NOPE_DIM:]
    k_pe = _rope(k_pe[:, None], pos)
    q_full = jnp.concatenate([q_nope, _rope(q_pe, pos)], axis=-1)
    k_full = jnp.concatenate([k_nope, jnp.broadcast_to(k_pe, (b, MLA_HEADS, s, MLA_ROPE_DIM))], axis=-1)
    o_b = _blocked_attention(q_full, k_full, v_b, (MLA_NOPE_DIM + MLA_ROPE_DIM) ** -0.5, True)

    def merge(o):
        return o.transpose(0, 2, 1, 3).reshape(b, s, -1)

    return jnp.concatenate([merge(o_a), merge(o_b)], axis=-1) @ w_out


def _causal_conv(x, w):
    k_len, c = w.shape
    return lax.conv_general_dilated(x, w[:, None, :].astype(x.dtype), window_strides=(1,),
                                    padding=[(k_len - 1, 0)],
                                    dimension_numbers=("NWC", "WIO", "NWC"),
                                    feature_group_count=c)


def _chunk_gated_delta(q, k, v, g, beta):
    b, h, s, dk = q.shape
    dv = v.shape[-1]
    n = s // CHUNK
    q = q.reshape(b, h, n, CHUNK, dk)
    k = k.reshape(b, h, n, CHUNK, dk)
    v = v.reshape(b, h, n, CHUNK, dv)
    g = jnp.cumsum(g.reshape(b, h, n, CHUNK), axis=-1)
    beta = beta.reshape(b, h, n, CHUNK)
    idx = jnp.arange(CHUNK)
    incl = idx[:, None] >= idx[None, :]
    strict = idx[:, None] > idx[None, :]
    decay = jnp.exp(jnp.where(incl, g[..., :, None] - g[..., None, :], -jnp.inf))
    k_beta = k * beta[..., None]
    m = jnp.where(strict, jnp.einsum("bhncd,bhnjd->bhncj", k_beta, k) * decay, 0.0)
    eye = jnp.eye(CHUNK, dtype=jnp.float32)
    t_mat = lax.linalg.triangular_solve(eye + m, jnp.broadcast_to(eye, m.shape),
                                        left_side=True, lower=True, unit_diagonal=True)
    u = jnp.einsum("bhncj,bhnjd->bhncd", t_mat, v * beta[..., None])
    w_dec = jnp.einsum("bhncj,bhnjd->bhncd", t_mat, k_beta * jnp.exp(g)[..., None])
    intra = jnp.einsum("bhncd,bhnjd->bhncj", q, k) * decay
    q_dec = q * jnp.exp(g)[..., None]
    k_tail = k * jnp.exp(g[..., -1:] - g)[..., None]
    g_last = jnp.exp(g[..., -1])

    def step(state, xs):
        u_c, w_c, intra_c, q_c, kt_c, gl_c = xs
        v_new = u_c - jnp.einsum("bhcd,bhde->bhce", w_c, state)
        out = jnp.einsum("bhcd,bhde->bhce", q_c, state) + jnp.einsum("bhcj,bhje->bhce", intra_c, v_new)
        state = state * gl_c[..., None, None] + jnp.einsum("bhcd,bhce->bhde", kt_c, v_new)
        return state, out

    xs = (jnp.moveaxis(u, 2, 0), jnp.moveaxis(w_dec, 2, 0), jnp.moveaxis(intra, 2, 0),
          jnp.moveaxis(q_dec, 2, 0), jnp.moveaxis(k_tail, 2, 0), jnp.moveaxis(g_last, 2, 0))
    state0 = jnp.zeros((b, h, dk, dv), jnp.float32)
    _, out = lax.scan(step, state0, xs)
    return jnp.moveaxis(out, 0, 2).reshape(b, h, s, dv)


def _gdn_mixer(h, w_in, conv_w, a_log, dt_bias, out_norm, w_out):
    b, s, _ = h.shape
    qkv, b_logit, a_logit, gate = _split(h @ w_in, GDN_SPLITS)
    qkv = jax.nn.silu(_causal_conv(qkv, conv_w)).astype(jnp.float32)
    q, k, v = jnp.split(qkv, 3, axis=-1)

    def heads(t):
        return t.reshape(b, s, GDN_HEADS, GDN_HEAD_DIM).transpose(0, 2, 1, 3)

    q = _l2norm(heads(q)) * GDN_HEAD_DIM ** -0.5
    k = _l2norm(heads(k))
    v = heads(v)
    beta = jax.nn.sigmoid(b_logit.astype(jnp.float32)).transpose(0, 2, 1)
    g = -jnp.exp(a_log.astype(jnp.float32)) * jax.nn.softplus(
        a_logit.astype(jnp.float32) + dt_bias.astype(jnp.float32))
    g = g.transpose(0, 2, 1)
    o = _chunk_gated_delta(q, k, v, g, beta).transpose(0, 2, 1, 3)
    gate = gate.astype(jnp.float32).reshape(b, s, GDN_HEADS, GDN_HEAD_DIM)
    o = _rms(o, out_norm) * jax.nn.silu(gate)
    return o.reshape(b, s, GDN_WIDTH).astype(h.dtype) @ w_out


def setup_inputs(seed: int = 0) -> dict:
    key = jax.random.key(seed)
    ks = jax.random.split(key, 24)
    f32 = jnp.float32

    def w(k, shape, fan_in):
        return jax.random.normal(k, shape, f32) * fan_in ** -0.5

    def gain(k, shape):
        return 1.0 + 0.02 * jax.random.normal(k, shape, f32)

    dt = jnp.exp(float(np.log(1e-3)) + jax.random.uniform(ks[15], (N_ODD, GDN_HEADS), f32)
                 * float(np.log(1e-1) - np.log(1e-3)))
    return {
        "x": jax.random.normal(ks[0], (BATCH, SEQ, D_MODEL), f32),
        "ffn1_norm": gain(ks[1], (DEPTH, D_MODEL)),
        "ffn1_w13": w(ks[2], (DEPTH, D_MODEL, 2 * D_FF), D_MODEL),
        "ffn1_w2": w(ks[3], (DEPTH, D_FF, D_MODEL), D_FF),
        "mix_norm": gain(ks[4], (DEPTH, D_MODEL)),
        "attn_w_in": w(ks[5], (N_EVEN, D_MODEL, ATTN_IN), D_MODEL),
        "fox_f_bias": 2.0 + 0.1 * jax.random.normal(ks[6], (N_EVEN, FOX_HEADS), f32),
        "mla_q_norm": gain(ks[7], (N_EVEN, MLA_Q_RANK)),
        "mla_w_uq": w(ks[8], (N_EVEN, MLA_Q_RANK, MLA_HEADS * (MLA_NOPE_DIM + MLA_ROPE_DIM)), MLA_Q_RANK),
        "mla_kv_norm": gain(ks[9], (N_EVEN, MLA_KV_RANK)),
        "mla_w_ukv": w(ks[10], (N_EVEN, MLA_KV_RANK, MLA_HEADS * (MLA_NOPE_DIM + MLA_V_DIM)), MLA_KV_RANK),
        "attn_w_out": w(ks[11], (N_EVEN, ATTN_OUT, D_MODEL), ATTN_OUT),
        "gdn_w_in": w(ks[12], (N_ODD, D_MODEL, GDN_IN), D_MODEL),
        "gdn_conv_w": w(ks[13], (N_ODD, GDN_CONV, 3 * GDN_WIDTH), GDN_CONV),
        "gdn_a_log": jnp.log(jax.random.uniform(ks[14], (N_ODD, GDN_HEADS), f32, minval=1.0, maxval=16.0)),
        "gdn_dt_bias": dt + jnp.log(-jnp.expm1(-dt)),
        "gdn_out_norm": gain(ks[16], (N_ODD, GDN_HEAD_DIM)),
        "gdn_w_out": w(ks[17], (N_ODD, GDN_WIDTH, D_MODEL), GDN_WIDTH),
        "ffn2_norm": gain(ks[18], (DEPTH, D_MODEL)),
        "ffn2_w13": w(ks[19], (DEPTH, D_MODEL, 2 * D_FF), D_MODEL),
        "ffn2_w2": w(ks[20], (DEPTH, D_FF, D_MODEL), D_FF),
        "final_norm": gain(ks[21], (D_MODEL,)),
    }


def reference(x, ffn1_norm, ffn1_w13, ffn1_w2, mix_norm, attn_w_in, fox_f_bias, mla_q_norm,
              mla_w_uq, mla_kv_norm, mla_w_ukv, attn_w_out, gdn_w_in, gdn_conv_w, gdn_a_log,
              gdn_dt_bias, gdn_out_norm, gdn_w_out, ffn2_norm, ffn2_w13, ffn2_w2, final_norm):
    for layer in range(DEPTH):
        x = x + 0.5 * _swiglu(_rms(x, ffn1_norm[layer]), ffn1_w13[layer], ffn1_w2[layer])
        h = _rms(x, mix_norm[layer])
        i = layer // 2
        if layer % 2 == 0:
            x = x + _fox_mla_mixer(h, attn_w_in[i], fox_f_bias[i], mla_q_norm[i], mla_w_uq[i],
                                   mla_kv_norm[i], mla_w_ukv[i], attn_w_out[i])
        else:
            x = x + _gdn_mixer(h, gdn_w_in[i], gdn_conv_w[i], gdn_a_log[i], gdn_dt_bias[i],
                               gdn_out_norm[i], gdn_w_out[i])
        x = x + 0.5 * _swiglu(_rms(x, ffn2_norm[layer]), ffn2_w13[layer], ffn2_w2[layer])
    return _rms(x, final_norm)
```

```python
import numpy as np
from contextlib import ExitStack
import concourse.bass as bass
import concourse.mybir as mybir
from concourse.bass_utils import run_bass_kernel_spmd

F32 = mybir.dt.float32
BF16 = mybir.dt.bfloat16
AF = mybir.ActivationFunctionType
ALU = mybir.AluOpType

N_CORES = 8
D = 1024
NT = 4096
SEQ = 2048
DFF = 2816
EPS = 1e-6


class Res:
    __slots__ = ("name", "w", "r", "excl")

    def __init__(self, name):
        self.name = name
        self.excl = False
        self.w = None
        self.r = {}


class Chan:
    __slots__ = ("sem", "cnt", "key")

    def __init__(self, sem, key):
        self.sem = sem
        self.cnt = 0
        self.key = key


class FW:
    SELF_SYNC = {"pe": False, "act": True, "dve": True, "pool": True, "sp": False}

    def __init__(self, nc, es):
        self.nc = nc
        self.es = es
        self.engs = {"pe": nc.tensor, "act": nc.scalar, "dve": nc.vector, "pool": nc.gpsimd, "sp": nc.sync}
        self.sem = {k: es.enter_context(nc.semaphore("s_" + k)) for k in self.engs}
        self.cnt = {k: 0 for k in self.engs}
        self.seen = {k: {} for k in self.engs}
        self.nchan = 0
        self.nres = 0

    def res(self, name=None):
        self.nres += 1
        return Res(name or ("r%d" % self.nres))

    def chan(self):
        self.nchan += 1
        key = "c%d" % self.nchan
        return Chan(self.es.enter_context(self.nc.semaphore(key)), key)

    def _wait(self, e, reads, writes):
        need = {}

        def add(ev):
            if ev is None:
                return
            key, h, v = ev
            if key == e and not self.SELF_SYNC[e]:
                return
            if self.seen[e].get(key, 0) >= v:
                return
            if key not in need or need[key][1] < v:
                need[key] = (h, v)

        for r in reads:
            add(r.w)
        for r in writes:
            add(r.w)
            for ev in r.r.values():
                add(ev)
        for key, (h, v) in need.items():
            self.engs[e].wait_ge(h, v)
            self.seen[e][key] = v

    def _mark(self, ev, reads, writes):
        key = ev[0]
        for r in reads:
            old = r.r.get(key)
            if old is None or old[2] < ev[2]:
                r.r[key] = ev
        for r in writes:
            r.w = ev
            r.r = {}

    def op(self, e, fn, reads=(), writes=()):
        ex = [r for r in reads if r.excl]
        if ex:
            reads = [r for r in reads if not r.excl]
            writes = list(writes) + ex
        self._wait(e, reads, writes)
        ins = fn(self.engs[e])
        self.cnt[e] += 1
        ins.then_inc(self.sem[e], 1)
        self._mark((e, self.sem[e], self.cnt[e]), reads, writes)
        return ins

    def dma(self, e, chan, out, in_, reads=(), writes=()):
        self._wait(e, reads, writes)
        ins = self.engs[e].dma_start(out=out, in_=in_)
        chan.cnt += 16
        ins.then_inc(chan.sem, 16)
        self._mark((chan.key, chan.sem, chan.cnt), reads, writes)
        return ins

    def finish(self, e, resources):
        self._wait(e, resources, ())

    def barrier(self, chans=()):
        for e in self.engs:
            for k in self.engs:
                if k == e or self.cnt[k] == 0:
                    continue
                if self.seen[e].get(k, 0) < self.cnt[k]:
                    self.engs[e].wait_ge(self.sem[k], self.cnt[k])
                    self.seen[e][k] = self.cnt[k]
            for c in chans:
                if c.cnt and self.seen[e].get(c.key, 0) < c.cnt:
                    self.engs[e].wait_ge(c.sem, c.cnt)
                    self.seen[e][c.key] = c.cnt


class Ctx:
    pass


def _tile(K, es, name, shape, dt):
    K.uid = getattr(K, "uid", 0) + 1
    return es.enter_context(K.nc.sbuf_tensor("%s_%d" % (name, K.uid), shape, dt))


def mm(K, out, lhsT, rhs, start, stop, reads, writes):
    return K.fw.op("pe", lambda e: e.matmul(out, lhsT=lhsT, rhs=rhs, start=start, stop=stop), reads=reads, writes=writes)


def phase_input(K, x_dram, XT):
    fw, nc = K.fw, K.nc
    with ExitStack() as es:
        xin = [_tile(K, es, "xin%d" % i, [128, D], F32) for i in range(3)]
        xo = [_tile(K, es, "xo%d" % i, [128, 8, 512], F32) for i in range(2)]
        r_in = [fw.res() for _ in range(3)]
        r_o = [fw.res() for _ in range(2)]
        c_in = [fw.chan() for _ in range(3)]
        c_o = [fw.chan() for _ in range(2)]
        XTv = XT.rearrange("c p n -> p c n")
        ti = 0
        for g in range(NT // 512):
            ob = g % 2
            for j in range(4):
                t = g * 4 + j
                ib = ti % 3
                ti += 1
                fw.dma("sp", c_in[ib], xin[ib][:], x_dram[t * 128:(t + 1) * 128, :], writes=[r_in[ib]])
                for half in range(2):
                    bank = K.bank()
                    for q in range(4):
                        c = half * 4 + q
                        fw.op("pe", lambda e: e.transpose(out=K.ps[:, bank, q * 128:(q + 1) * 128],
                                                          in_=xin[ib][:, c * 128:(c + 1) * 128], identity=K.identf[:]),
                              reads=[r_in[ib], K.r_const], writes=[K.r_ps[bank]])
                    src = K.ps[:, bank, :].rearrange("p (c n) -> p c n", c=4)
                    dst = xo[ob][:, half * 4:half * 4 + 4, j * 128:(j + 1) * 128]
                    if half == 0:
                        fw.op("act", lambda e: e.activation(out=dst, in_=src, func=AF.Copy), reads=[K.r_ps[bank]], writes=[r_o[ob]])
                    else:
                        fw.op("dve", lambda e: e.tensor_copy(out=dst, in_=src), reads=[K.r_ps[bank]], writes=[r_o[ob]])
            fw.dma("sp", c_o[ob], XTv[:, :, g * 512:(g + 1) * 512], xo[ob][:], reads=[r_o[ob]], writes=[K.r_XT[g]])
        fw.barrier(c_in + c_o)


def rms_rstd(K, x_tile, nchunk, n0, n, sq_tile, rstd_out, r_x, r_sq, r_rstd, tmp, r_tmp, dim):
    fw = K.fw
    fw.op("act", lambda e: e.activation(out=sq_tile[:, 0:nchunk, 0:n], in_=x_tile[:, 0:nchunk, n0:n0 + n], func=AF.Square),
          reads=r_x, writes=r_sq)
    bank = K.bank()
    for c in range(nchunk):
        mm(K, K.ps[:, bank, 0:n], K.onesb[:, :], sq_tile[:, c, 0:n], c == 0, c == nchunk - 1, [r_sq[c], K.r_const], [K.r_ps[bank]])
    fw.op("act", lambda e: e.activation(out=tmp[:, 0:n], in_=K.ps[:, bank, 0:n], func=AF.Sqrt, bias=K.epsb[:, 0:1], scale=1.0 / dim),
          reads=[K.r_ps[bank], K.r_const], writes=[r_tmp])
    fw.op("dve", lambda e: e.reciprocal(out=rstd_out, in_=tmp[:, 0:n]), reads=[r_tmp], writes=[r_rstd])


def phase_ffn(K, XT, gain_col, w13, w2):
    fw, nc = K.fw, K.nc
    TG = 1024
    NKC = 8
    NFC = DFF // 128
    with ExitStack() as es:
        xg = _tile(K, es, "xg", [128, 8, TG], F32)
        hT = _tile(K, es, "hT", [128, 8, TG], BF16)
        gT = _tile(K, es, "gT", [128, NFC, TG], BF16)
        rstd = _tile(K, es, "rstd", [128, TG], F32)
        tmp = _tile(K, es, "tmp", [128, 512], F32)
        wa = [_tile(K, es, "wa%d" % i, [128, 8, 512], BF16) for i in range(2)]
        wb = [_tile(K, es, "wb%d" % i, [128, 8, 512], BF16) for i in range(2)]
        w2t = [_tile(K, es, "w2t%d" % i, [128, NFC, 256], BF16) for i in range(2)]
        sa = [_tile(K, es, "sa%d" % i, [128, 512], F32) for i in range(3)]
        r_rstd, r_tmp = [fw.res() for _ in range(2)]
        r_xgc = [fw.res() for _ in range(16)]
        r_hTc = [fw.res() for _ in range(16)]
        r_gTc = [fw.res() for _ in range(NFC * 2)]
        r_w13 = [fw.res() for _ in range(2)]
        r_w2 = [fw.res() for _ in range(2)]
        r_sa = [fw.res() for _ in range(3)]
        c_x, c_xs = fw.chan(), fw.chan()
        c_w13 = [fw.chan() for _ in range(2)]
        c_w2 = [fw.chan() for _ in range(2)]
        XTv = XT.rearrange("c p n -> p c n")
        fblocks = [(b * 512, 512) for b in range(5)] + [(2560, 256)]
        w13v = w13.rearrange("(c p) n -> p c n", p=128)
        w2v = w2.rearrange("(c p) n -> p c n", p=128)
        nw13 = 0
        nw2 = 0
        sai = 0

        def load_w13(bi):
            nonlocal nw13
            f0, fwid = fblocks[bi]
            b = nw13 % 2
            nw13 += 1
            fw.dma("pool", c_w13[b], wa[b][:, :, 0:fwid], w13v[:, :, f0:f0 + fwid], writes=[r_w13[b]])
            fw.dma("pool", c_w13[b], wb[b][:, :, 0:fwid], w13v[:, :, DFF + f0:DFF + f0 + fwid], writes=[r_w13[b]])
            return b

        def load_w2(di):
            nonlocal nw2
            b = nw2 % 2
            nw2 += 1
            fw.dma("pool", c_w2[b], w2t[b][:, :, :], w2v[:, :, di * 256:(di + 1) * 256], writes=[r_w2[b]])
            return b

        for g in range(NT // TG):
            t0 = g * TG
            fw.dma("sp", c_x, xg[:], XTv[:, :, t0:t0 + TG], reads=K.r_XT[2 * g:2 * g + 2], writes=r_xgc)
            wbuf = {0: load_w13(0), 1: load_w13(1)}
            for nt in range(TG // 512):
                rms_rstd(K, xg, 8, nt * 512, 512, gT, rstd[:, nt * 512:(nt + 1) * 512], [r_xgc[c * 2 + nt] for c in range(8)],
                         [r_gTc[c * 2] for c in range(8)], r_rstd, tmp, r_tmp, D)
                for c in range(8):
                    fw.op("dve", lambda e: e.scalar_tensor_tensor(out=hT[:, c, nt * 512:(nt + 1) * 512], in0=xg[:, c, nt * 512:(nt + 1) * 512],
                                                                   scalar=gain_col[:, c:c + 1], in1=rstd[:, nt * 512:(nt + 1) * 512],
                                                                   op0=ALU.mult, op1=ALU.mult),
                          reads=[r_xgc[c * 2 + nt], r_rstd, K.r_const], writes=[r_hTc[c * 2 + nt]])
            w2buf = {}
            for bi, (f0, fwid) in enumerate(fblocks):
                b = wbuf[bi]
                for fcl in range(fwid // 128):
                    fc = f0 // 128 + fcl
                    for nt in range(TG // 512):
                        ba, bb = K.bank(), K.bank()
                        for kc in range(NKC):
                            mm(K, K.ps[:, ba, :], wa[b][:, kc, fcl * 128:(fcl + 1) * 128], hT[:, kc, nt * 512:(nt + 1) * 512],
                               kc == 0, kc == NKC - 1, [r_w13[b], r_hTc[kc * 2 + nt]], [K.r_ps[ba]])
                        for kc in range(NKC):
                            mm(K, K.ps[:, bb, :], wb[b][:, kc, fcl * 128:(fcl + 1) * 128], hT[:, kc, nt * 512:(nt + 1) * 512],
                               kc == 0, kc == NKC - 1, [r_w13[b], r_hTc[kc * 2 + nt]], [K.r_ps[bb]])
                        s = sai % 3
                        sai += 1
                        fw.op("act", lambda e: e.activation(out=sa[s][:], in_=K.ps[:, ba, :], func=AF.Silu), reads=[K.r_ps[ba]], writes=[r_sa[s]])
                        rg = r_gTc[fc * 2 + nt]
                        fw.op("dve", lambda e: e.tensor_tensor(out=gT[:, fc, nt * 512:(nt + 1) * 512], in0=K.ps[:, bb, :], in1=sa[s][:], op=ALU.mult),
                              reads=[K.r_ps[bb], r_sa[s]], writes=[rg])
                if bi + 2 < len(fblocks):
                    wbuf[bi + 2] = load_w13(bi + 2)
                if bi == 3:
                    w2buf[0] = load_w2(0)
                if bi == 4:
                    w2buf[1] = load_w2(1)
            for di in range(4):
                b = w2buf[di]
                for dcl in range(2):
                    dmc = di * 2 + dcl
                    for nt in range(TG // 512):
                        by = K.bank()
                        for fc in range(NFC):
                            mm(K, K.ps[:, by, :], w2t[b][:, fc, dcl * 128:(dcl + 1) * 128], gT[:, fc, nt * 512:(nt + 1) * 512],
                               fc == 0, fc == NFC - 1, [r_w2[b], r_gTc[fc * 2 + nt]], [K.r_ps[by]])
                        fw.op("dve", lambda e: e.scalar_tensor_tensor(out=xg[:, dmc, nt * 512:(nt + 1) * 512], in0=K.ps[:, by, :], scalar=0.5,
                                                                       in1=xg[:, dmc, nt * 512:(nt + 1) * 512], op0=ALU.mult, op1=ALU.add),
                              reads=[K.r_ps[by], r_xgc[dmc * 2 + nt]], writes=[r_xgc[dmc * 2 + nt]])
                if di + 2 < 4:
                    w2buf[di + 2] = load_w2(di + 2)
            fw.dma("sp", c_xs, XTv[:, :, t0:t0 + TG], xg[:], reads=r_xgc, writes=K.r_XT[2 * g:2 * g + 2])
        fw.barrier([c_x, c_xs] + c_w13 + c_w2)


def phase_final(K, XT, gain_col, out_dram):
    fw, nc = K.fw, K.nc
    with ExitStack() as es:
        xg = [_tile(K, es, "fxg%d" % i, [128, 8, 512], F32) for i in range(2)]
        sq = _tile(K, es, "fsq", [128, 8, 512], BF16)
        yn = _tile(K, es, "fyn", [128, 8, 512], F32)
        rstd = _tile(K, es, "frstd", [128, 512], F32)
        tmp = _tile(K, es, "ftmp", [128, 512], F32)
        yo = [_tile(K, es, "fyo%d" % i, [128, D], F32) for i in range(2)]
        r_xg = [fw.res() for _ in range(2)]
        r_yo = [fw.res() for _ in range(2)]
        r_sq, r_yn, r_rstd, r_tmp = [fw.res() for _ in range(4)]
        c_x = [fw.chan() for _ in range(2)]
        c_o = [fw.chan() for _ in range(2)]
        XTv = XT.rearrange("c p n -> p c n")
        oi = 0
        for g in range(NT // 512):
            b = g % 2
            fw.dma("sp", c_x[b], xg[b][:], XTv[:, :, g * 512:(g + 1) * 512], reads=[K.r_XT[g]], writes=[r_xg[b]])
            rms_rstd(K, xg[b], 8, 0, 512, sq, rstd[:, :], [r_xg[b]], [r_sq] * 8, r_rstd, tmp, r_tmp, D)
            for c in range(8):
                fw.op("dve", lambda e: e.scalar_tensor_tensor(out=yn[:, c, :], in0=xg[b][:, c, :], scalar=gain_col[:, c:c + 1], in1=rstd[:, :],
                                                               op0=ALU.mult, op1=ALU.mult),
                      reads=[r_xg[b], r_rstd, K.r_const], writes=[r_yn])
            for j in range(4):
                ob = oi % 2
                oi += 1
                for half in range(2):
                    bank = K.bank()
                    for q in range(4):
                        c = half * 4 + q
                        fw.op("pe", lambda e: e.transpose(out=K.ps[:, bank, q * 128:(q + 1) * 128], in_=yn[:, c, j * 128:(j + 1) * 128],
                                                          identity=K.identf[:]),
                              reads=[r_yn, K.r_const], writes=[K.r_ps[bank]])
                    if half == 0:
                        fw.op("act", lambda e: e.activation(out=yo[ob][:, 0:512], in_=K.ps[:, bank, :], func=AF.Copy),
                              reads=[K.r_ps[bank]], writes=[r_yo[ob]])
                    else:
                        fw.op("dve", lambda e: e.tensor_copy(out=yo[ob][:, 512:1024], in_=K.ps[:, bank, :]),
                              reads=[K.r_ps[bank]], writes=[r_yo[ob]])
                t = g * 4 + j
                fw.dma("sp", c_o[ob], out_dram[t * 128:(t + 1) * 128, :], yo[ob][:], reads=[r_yo[ob]], writes=[K.r_out])
        fw.barrier(c_x + c_o)


VEC_COLS = {}


def _vec_layout():
    off = 0
    lay = {}
    for name, n in [("ffn1_norm", 16), ("mix_norm", 16), ("ffn2_norm", 16), ("final_norm", 8), ("fbias3", 1), ("mla_q_norm", 2),
                    ("mla_kv_norm", 1), ("gdn_conv", 96), ("gdn_dtb", 8), ("gdn_alog", 8), ("gdn_onorm", 128)]:
        lay[name] = (off, n)
        off += n
    return lay, off


def pack_vecs(inp):
    lay, nv = _vec_layout()
    v = np.zeros((128, nv), np.float32)

    def fm(a):
        a = np.asarray(a, np.float32).reshape(-1, 8, 128)
        return a.transpose(2, 0, 1).reshape(128, -1)
    for name in ["ffn1_norm", "mix_norm", "ffn2_norm", "final_norm"]:
        o, n = lay[name]
        v[:, o:o + n] = fm(inp[name])
    fb = np.asarray(inp["fox_f_bias"], np.float32)[0]
    for rep in range(3):
        v[rep * 32:rep * 32 + 8, lay["fbias3"][0]] = fb
    v[:, lay["mla_q_norm"][0]:lay["mla_q_norm"][0] + 2] = np.asarray(inp["mla_q_norm"], np.float32)[0].reshape(2, 128).T
    v[:, lay["mla_kv_norm"][0]] = np.asarray(inp["mla_kv_norm"], np.float32)[0]
    cw = np.asarray(inp["gdn_conv_w"], np.float32)[0].reshape(4, 24, 128)
    v[:, lay["gdn_conv"][0]:lay["gdn_conv"][0] + 96] = cw.transpose(2, 0, 1).reshape(128, 96)
    v[:, lay["gdn_dtb"][0]:lay["gdn_dtb"][0] + 8] = np.asarray(inp["gdn_dt_bias"], np.float32)[0][None, :]
    v[:, lay["gdn_alog"][0]:lay["gdn_alog"][0] + 8] = np.asarray(inp["gdn_a_log"], np.float32)[0][None, :]
    v[:, lay["gdn_onorm"][0]:lay["gdn_onorm"][0] + 128] = np.asarray(inp["gdn_out_norm"], np.float32)[0][None, :]
    return v


def host_consts(inp):
    c = {}
    selq = np.zeros((97, 8, 70), np.float32)
    selk = np.zeros((97, 8, 70), np.float32)
    for h in range(8):
        for p in range(3):
            selq[p * 32 + h, h, 64 + p] = 1.0
            selk[p * 32 + h, h, 67 + p] = -1.0
        selq[96, h, 67:70] = 1.0
        selk[96, h, 64:67] = 1.0
    c["selq"], c["selk"] = selq, selk
    s_idx = np.arange(128)[:, None]
    t_idx = np.arange(128)[None, :]
    c["mask_causal"] = np.where(s_idx <= t_idx, 0.0, NEG).astype(np.float32)
    c["mask_chunk"] = np.where((s_idx // 64) <= (t_idx // 64), 0.0, NEG).astype(np.float32)
    inv = 10000.0 ** (-np.arange(16, dtype=np.float32) / 16)
    ang = np.arange(SEQ, dtype=np.float32)[None, :] * inv[:, None]
    cos, sin = np.cos(ang).astype(np.float32), np.sin(ang).astype(np.float32)
    c["cos2"] = np.concatenate([cos, cos], 0)
    c["sin2s"] = np.concatenate([-sin, sin], 0)
    c["U"] = (s_idx <= t_idx).astype(np.float32)
    c["gmask1"] = np.where(t_idx >= s_idx, 0.0, NEG).astype(np.float32)
    c["gmask2"] = np.where(s_idx > t_idx, 0.0, BIG).astype(np.float32)
    wuq = np.asarray(inp["mla_w_uq"], np.float32)[0].reshape(256, 8, 96)
    c["mla_w_uqs"] = np.ascontiguousarray(np.concatenate([wuq[:, :, 0:64], wuq[:, :, 80:96], wuq[:, :, 64:80]], axis=2).reshape(256, 768))
    return c


CONST_SHAPES = {"selq": [97, 8, 70], "selk": [97, 8, 70], "mask_causal": [128, 128], "mask_chunk": [128, 128], "cos2": [32, SEQ],
                "sin2s": [32, SEQ], "mla_w_uqs": [256, 768], "U": [128, 128], "gmask1": [128, 128], "gmask2": [128, 128]}


W_SHAPES = {
    "ffn1_w13": [2, D, 2 * DFF], "ffn1_w2": [2, DFF, D], "ffn2_w13": [2, D, 2 * DFF], "ffn2_w2": [2, DFF, D],
    "attn_w_in": [1, D, 1960], "mla_w_uq": [1, 256, 768], "mla_w_ukv": [1, 128, 1024], "attn_w_out": [1, D, D],
    "gdn_w_in": [1, D, 4112], "gdn_w_out": [1, D, D],
}


def build(phases=("input", "ffn1_0", "final"), dbg=False):
    nc = bass.Bass("TRN2", target_bir_lowering=False)
    K = Ctx()
    K.nc = nc
    import os
    K.dbg_pairs = int(os.environ.get("DBG_PAIRS", "4"))
    K.dbg_heads = int(os.environ.get("DBG_HEADS", "2"))
    K.dbg_skip = os.environ.get("DBG_SKIP", "").split(",")
    lay, nv = _vec_layout()
    x = nc.dram_tensor("x", [NT, D], F32, kind="ExternalInput").ap()
    vecs_d = nc.dram_tensor("vecs", [128, nv], F32, kind="ExternalInput").ap()
    ident_d = nc.dram_tensor("identf", [128, 128], F32, kind="ExternalInput").ap()
    W = {k: nc.dram_tensor(k, shp, F32, kind="ExternalInput").ap() for k, shp in W_SHAPES.items()}
    C = {k: nc.dram_tensor(k, shp, F32, kind="ExternalInput").ap() for k, shp in CONST_SHAPES.items()}
    out = nc.dram_tensor("out", [NT, D], F32, kind="ExternalOutput").ap()
    XT = nc.dram_tensor("XT", [8, 128, NT], F32, kind="Internal").ap()
    HT = nc.dram_tensor("HT", [8, 128, NT], BF16, kind="Internal").ap()
    AT = nc.dram_tensor("AT", [8, 128, NT], BF16, kind="Internal").ap()
    S = {"QT": nc.dram_tensor("gQT", [8, 128, NT], BF16, kind="Internal").ap(),
         "KT": nc.dram_tensor("gKT", [8, 128, NT], BF16, kind="Internal").ap(),
         "Ktok": nc.dram_tensor("gKtok", [NT, D], BF16, kind="Internal").ap(),
         "Vtok": nc.dram_tensor("gVtok", [NT, D], BF16, kind="Internal").ap(),
         "Gtok": nc.dram_tensor("gGtok", [NT, D], BF16, kind="Internal").ap(),
         "BG": nc.dram_tensor("gBG", [NT, 16], F32, kind="Internal").ap()}
    with ExitStack() as es:
        fw = FW(nc, es)
        K.fw = fw
        K.ps = es.enter_context(nc.psum_tensor("ps", [128, 8, 512], F32))
        K.r_ps = [fw.res("ps%d" % i) for i in range(8)]
        for r_ in K.r_ps:
            r_.excl = True
        K._bank = 0

        def bank():
            b = K._bank
            K._bank = (b + 1) % 8
            return b
        K.bank = bank
        K.r_XT = [fw.res("XT%d" % i) for i in range(NT // 512)]
        K.r_out = fw.res("out")
        K.r_HT = [fw.res("HT%d" % i) for i in range(NT // 512)]
        K.r_AT = [fw.res("AT%d" % i) for i in range(2)]
        S["r_QK"] = [fw.res() for _ in range(2)]
        S["r_tok"] = [fw.res() for _ in range(2)]
        K.r_const = fw.res("const")
        K.identf = _tile(K, es, "identf_sb", [128, 128], F32)
        K.onesb = _tile(K, es, "onesb", [128, 128], BF16)
        K.epsb = _tile(K, es, "epsb", [128, 1], F32)
        K.oneb = _tile(K, es, "oneb", [128, 1], F32)
        K.onesf = _tile(K, es, "onesf", [128, 128], F32)
        K.identb = _tile(K, es, "identb", [128, 128], BF16)
        K.vecs = _tile(K, es, "vecs_sb", [128, nv], F32)
        c0 = fw.chan()
        fw.dma("sp", c0, K.identf[:], ident_d, writes=[K.r_const])
        fw.dma("sp", c0, K.vecs[:], vecs_d, writes=[K.r_const])
        fw.op("dve", lambda e: e.memset(K.onesb[:], 1.0), writes=[K.r_const])
        fw.op("dve", lambda e: e.memset(K.epsb[:], EPS), writes=[K.r_const])
        fw.op("dve", lambda e: e.memset(K.oneb[:], 1.0), writes=[K.r_const])
        fw.op("dve", lambda e: e.memset(K.onesf[:], 1.0), writes=[K.r_const])
        fw.op("dve", lambda e: e.tensor_copy(out=K.identb[:], in_=K.identf[:]), reads=[K.r_const], writes=[K.r_const])
        fw.barrier([c0])

        def vcol(name, layer=0, n=8):
            o, _ = lay[name]
            return K.vecs[:, o + layer * n:o + (layer + 1) * n]

        for ph in phases:
            if ph == "input":
                phase_input(K, x, XT)
            elif ph.startswith("ffn"):
                which, layer = ph[:4], int(ph[5:])
                phase_ffn(K, XT, vcol(which + "_norm", layer), W[which + "_w13"][layer], W[which + "_w2"][layer])
            elif ph.startswith("norm"):
                layer = int(ph[4:])
                phase_norm_to_ht(K, XT, vcol("mix_norm", layer), HT)
            elif ph == "fox":
                o = lay["fbias3"][0]
                phase_fox(K, HT, AT, W["attn_w_in"][0], K.vecs[:, o:o + 1], C)
            elif ph == "mla":
                oq, okv = lay["mla_q_norm"][0], lay["mla_kv_norm"][0]
                phase_mla(K, HT, AT, W["attn_w_in"][0], W["mla_w_uq"][0], C["mla_w_uqs"], W["mla_w_ukv"][0],
                          K.vecs[:, oq:oq + 2], K.vecs[:, okv:okv + 1], C)
            elif ph == "oproj0":
                phase_outproj(K, XT, AT, W["attn_w_out"][0])
            elif ph == "gdnp":
                oc, od, oa = lay["gdn_conv"][0], lay["gdn_dtb"][0], lay["gdn_alog"][0]
                phase_gdn_proj(K, HT, W["gdn_w_in"][0], K.vecs[:, oc:oc + 96], K.vecs[:, od:od + 8], K.vecs[:, oa:oa + 8], S)
            elif ph == "gdnc":
                oo = lay["gdn_onorm"][0]
                phase_gdn_core(K, S, AT, K.vecs[:, oo:oo + 128], C)
            elif ph == "oproj1":
                phase_outproj(K, XT, AT, W["gdn_w_out"][0])
            elif ph == "final":
                phase_final(K, XT, vcol("final_norm"), out)
        fw.finish("sp", [K.r_out])
    return nc


ALL_PHASES = ("input", "ffn1_0", "norm0", "fox", "mla", "oproj0", "ffn2_0", "ffn1_1", "norm1", "gdnp", "gdnc", "oproj1", "ffn2_1", "final")


def make_in_maps(inputs, n_cores=N_CORES):
    x = np.ascontiguousarray(np.asarray(inputs["x"], np.float32)).reshape(n_cores, NT, D)
    shared = {"vecs": pack_vecs(inputs), "identf": np.eye(128, dtype=np.float32)}
    shared.update(host_consts(inputs))
    for k in W_SHAPES:
        shared[k] = np.ascontiguousarray(np.asarray(inputs[k], np.float32))
    return [dict(shared, x=x[i]) for i in range(n_cores)]


def kernel(**inputs):
    nc = build(ALL_PHASES)
    in_maps = make_in_maps(inputs)
    res = run_bass_kernel_spmd(nc, in_maps, core_ids=list(range(N_CORES)))
    out = np.stack([np.asarray(r["out"]) for r in res.results], axis=0)
    return out.reshape(16, SEQ, D).astype(np.float32)


def phase_norm_to_ht(K, XT, gain_col, HT):
    fw = K.fw
    with ExitStack() as es:
        xg = [_tile(K, es, "nxg%d" % i, [128, 8, 512], F32) for i in range(2)]
        hb = [_tile(K, es, "nhb%d" % i, [128, 8, 512], BF16) for i in range(2)]
        sq = _tile(K, es, "nsq", [128, 8, 512], BF16)
        rstd = _tile(K, es, "nrstd", [128, 512], F32)
        tmp = _tile(K, es, "ntmp", [128, 512], F32)
        r_xg = [fw.res() for _ in range(2)]
        r_hb = [fw.res() for _ in range(2)]
        r_sq, r_rstd, r_tmp = [fw.res() for _ in range(3)]
        c_x = [fw.chan() for _ in range(2)]
        c_h = [fw.chan() for _ in range(2)]
        XTv = XT.rearrange("c p n -> p c n")
        HTv = HT.rearrange("c p n -> p c n")
        for g in range(NT // 512):
            b = g % 2
            fw.dma("sp", c_x[b], xg[b][:], XTv[:, :, g * 512:(g + 1) * 512], reads=[K.r_XT[g]], writes=[r_xg[b]])
            rms_rstd(K, xg[b], 8, 0, 512, sq, rstd[:, :], [r_xg[b]], [r_sq] * 8, r_rstd, tmp, r_tmp, D)
            for c in range(8):
                fw.op("dve", lambda e: e.scalar_tensor_tensor(out=hb[b][:, c, :], in0=xg[b][:, c, :], scalar=gain_col[:, c:c + 1], in1=rstd[:, :],
                                                               op0=ALU.mult, op1=ALU.mult),
                      reads=[r_xg[b], r_rstd, K.r_const], writes=[r_hb[b]])
            fw.dma("sp", c_h[b], HTv[:, :, g * 512:(g + 1) * 512], hb[b][:], reads=[r_hb[b]], writes=[K.r_HT[g]])
        fw.barrier(c_x + c_h)


def attention_head(K, A, qa, ka, KR, vlhs, M, obase, drow, maskT, reads_qkv, out_ap, out_res):
    fw = K.fw
    for G in range(SEQ // 512):
        ob = A.obanks[A.oi % len(A.obanks)]
        A.oi += 1
        nkt = 4 * G + 4
        for i in range(nkt):
            r = i - 4 * G
            q0 = max(r, 0) * 128
            N = 512 - q0
            sb = A.sbanks[A.si % len(A.sbanks)]
            A.si += 1
            mm(K, K.ps[:, sb, 0:N], ka[0:KR, i * 128:(i + 1) * 128], qa[0:KR, G * 512 + q0:(G + 1) * 512], True, r < 0,
               reads_qkv, [K.r_ps[sb]])
            if r >= 0:
                mm(K, K.ps[:, sb, 0:128], K.identb[:, :], maskT, False, True, [K.r_const], [K.r_ps[sb]])
            pb = A.pi % len(A.pt)
            A.pi += 1
            fw.op("act", lambda e: e.activation(out=A.pt[pb][:, 0:N], in_=K.ps[:, sb, 0:N], func=AF.Exp),
                  reads=[K.r_ps[sb]], writes=[A.r_pt[pb]])
            mm(K, K.ps[0:M, ob, q0:512], vlhs(i), A.pt[pb][:, 0:N], i == 0, i == nkt - 1, reads_qkv + [A.r_pt[pb]], [K.r_ps[ob]])
        fw.op("dve", lambda e: e.reciprocal(out=A.rd[drow:drow + 1, :], in_=K.ps[drow:drow + 1, ob, :]), reads=[K.r_ps[ob]], writes=[A.r_rd])
        bb = A.bbanks[A.bi % len(A.bbanks)]
        A.bi += 1
        mm(K, K.ps[obase:obase + 64, bb, :], K.onesf[drow:drow + 1, 0:64], A.rd[drow:drow + 1, :], True, True, [A.r_rd, K.r_const], [K.r_ps[bb]])
        cb = A.ci % len(A.bcs)
        A.ci += 1
        fw.op("act", lambda e: e.activation(out=A.bcs[cb][obase:obase + 64, :], in_=K.ps[obase:obase + 64, bb, :], func=AF.Copy),
              reads=[K.r_ps[bb]], writes=[A.r_bcs[cb]])
        fw.op("dve", lambda e: e.tensor_tensor(out=out_ap(G), in0=K.ps[obase:obase + 64, ob, :], in1=A.bcs[cb][obase:obase + 64, :], op=ALU.mult),
              reads=[K.r_ps[ob], A.r_bcs[cb]], writes=[out_res(G)])


class AttnCtx:
    def __init__(self, K, es, tag):
        fw = K.fw
        self.pt = [_tile(K, es, "%spt%d" % (tag, i), [128, 512], BF16) for i in range(4)]
        self.r_pt = [fw.res() for _ in range(4)]
        self.rd = _tile(K, es, tag + "rd", [128, 512], F32)
        self.r_rd = fw.res()
        self.bcs = [_tile(K, es, "%sbcs%d" % (tag, i), [128, 512], F32) for i in range(2)]
        self.r_bcs = [fw.res() for _ in range(2)]
        self.sbanks, self.obanks, self.bbanks = [0, 1, 2, 3], [4, 5], [6, 7]
        self.si = self.oi = self.bi = self.pi = self.ci = 0


NEG = -30000.0
FOX_SCALE = 0.125
MLA_SCALE = 96 ** -0.5


def phase_fox(K, HT, AT, w_in, nfb_col, C):
    fw = K.fw
    with ExitStack() as es:
        A = AttnCtx(K, es, "fx")
        wp = [_tile(K, es, "fxwp%d" % i, [128, 8, 384], BF16) for i in range(2)]
        ht = [_tile(K, es, "fxht%d" % i, [128, 8, 512], BF16) for i in range(2)]
        qa = [_tile(K, es, "fxqa%d" % i, [128, 2, SEQ], BF16) for i in range(2)]
        ka = [_tile(K, es, "fxka%d" % i, [128, 2, SEQ], BF16) for i in range(2)]
        VE = [_tile(K, es, "fxVE%d" % i, [128, 16, 65], BF16) for i in range(2)]
        VO = [_tile(K, es, "fxVO%d" % i, [128, 16, 128], BF16) for i in range(2)]
        ao = [_tile(K, es, "fxao%d" % i, [128, SEQ], BF16) for i in range(2)]
        wf = _tile(K, es, "fxwf", [128, 8, 72], BF16)
        selq = _tile(K, es, "fxselq", [128, 8, 70], BF16)
        selk = _tile(K, es, "fxselk", [128, 8, 70], BF16)
        maskT = _tile(K, es, "fxmask", [128, 128], BF16)
        Fp = _tile(K, es, "fxFp", [128, SEQ], BF16)
        Ff = [_tile(K, es, "fxFf%d" % i, [128, 512], F32) for i in range(2)]
        sp = _tile(K, es, "fxsp", [128, 512], F32)
        ee = _tile(K, es, "fxee", [128, 512], F32)
        HI = _tile(K, es, "fxHI", [128, 512], BF16)
        MID = _tile(K, es, "fxMID", [128, 512], BF16)
        nfb = _tile(K, es, "fxnfb", [128, 1], F32)
        r_wp = [fw.res() for _ in range(2)]
        r_ht = [fw.res() for _ in range(2)]
        r_q = [[fw.res() for _ in range(2)] for _ in range(2)]
        r_k = [[fw.res() for _ in range(2)] for _ in range(2)]
        r_VE = [fw.res() for _ in range(2)]
        r_VO = [fw.res() for _ in range(2)]
        r_ao = [fw.res() for _ in range(2)]
        r_c, r_Fp, r_sp, r_ee, r_HI, r_MID = [fw.res() for _ in range(6)]
        r_Ff = [fw.res() for _ in range(2)]
        c_wp = [fw.chan() for _ in range(2)]
        c_ht = [fw.chan() for _ in range(2)]
        c_ao = [fw.chan() for _ in range(2)]
        c_c = fw.chan()
        w_inv = w_in.rearrange("(c p) n -> p c n", p=128)
        HTv = HT.rearrange("c p n -> p c n")
        fw.op("dve", lambda e: e.memset(wf[:], 0.0), writes=[r_c])
        for rep in range(3):
            fw.dma("pool", c_c, wf[:, :, rep * 32:rep * 32 + 8], w_inv[:, :, 1536:1544], writes=[r_c])
        fw.dma("pool", c_c, selq[0:97, :, :], C["selq"], writes=[r_c])
        fw.dma("pool", c_c, selk[0:97, :, :], C["selk"], writes=[r_c])
        fw.dma("pool", c_c, maskT[:], C["mask_causal"], writes=[r_c])
        fw.op("dve", lambda e: e.tensor_scalar(out=nfb[:], in0=nfb_col, scalar1=-1.0, scalar2=None, op0=ALU.mult), reads=[K.r_const], writes=[r_c])
        fw.op("pool", lambda e: e.memset(Fp[:], 0.0), writes=[r_Fp])
        fw.op("pool", lambda e: e.memset(Fp[96:97, :], 1.0), writes=[r_Fp])
        for b in range(2):
            fw.op("pool", lambda e: e.memset(VE[b][:, :, 64:65], 1.0), writes=[r_VE[b]])
            fw.op("pool", lambda e: e.memset(VO[b][:, :, 0:64], 0.0), writes=[r_VO[b]])
            fw.op("pool", lambda e: e.memset(VO[b][:, :, 0:1], 1.0), writes=[r_VO[b]])
        nht = 0
        npair = 0

        def load_ht(g):
            nonlocal nht
            b = nht % 2
            nht += 1
            fw.dma("sp", c_ht[b], ht[b][:], HTv[:, :, g * 512:(g + 1) * 512], reads=[K.r_HT[g]], writes=[r_ht[b]])
            return b

        for s in range(2):
            for nt in range(4):
                hb = load_ht(s * 4 + nt)
                bank = K.bank()
                for kc in range(8):
                    mm(K, K.ps[0:72, bank, :], wf[:, kc, :], ht[hb][:, kc, :], kc == 0, kc == 7, [r_c, r_ht[hb]], [K.r_ps[bank]])
                fw.op("act", lambda e: e.activation(out=ee[0:72, :], in_=K.ps[0:72, bank, :], func=AF.Exp, scale=-1.0, bias=nfb[0:72, 0:1]),
                      reads=[K.r_ps[bank], r_c], writes=[r_ee])
                fw.op("act", lambda e: e.activation(out=sp[0:72, :], in_=ee[0:72, :], func=AF.Ln, bias=K.oneb[0:72, 0:1], scale=1.0),
                      reads=[r_ee, K.r_const], writes=[r_sp])
                fb = nt % 2
                init = 0.0 if nt == 0 else Ff[1 - fb][0:72, 511:512]
                fw.op("dve", lambda e: e.tensor_tensor_scan(out=Ff[fb][0:72, :], data0=K.onesf[0:72, 0:1].broadcast_to([72, 512]), data1=sp[0:72, :],
                                                            initial=init, op0=ALU.mult, op1=ALU.subtract),
                      reads=[r_sp, K.r_const, r_Ff[1 - fb]], writes=[r_Ff[fb]])
                sl = slice(nt * 512, (nt + 1) * 512)
                fw.op("dve", lambda e: e.tensor_copy(out=HI[0:72, :], in_=Ff[fb][0:72, :]), reads=[r_Ff[fb]], writes=[r_HI])
                fw.op("dve", lambda e: e.tensor_tensor(out=sp[0:72, :], in0=Ff[fb][0:72, :], in1=HI[0:72, :], op=ALU.subtract),
                      reads=[r_Ff[fb], r_HI], writes=[r_sp])
                fw.op("dve", lambda e: e.tensor_copy(out=MID[0:72, :], in_=sp[0:72, :]), reads=[r_sp], writes=[r_MID])
                fw.op("dve", lambda e: e.tensor_tensor(out=sp[0:72, :], in0=sp[0:72, :], in1=MID[0:72, :], op=ALU.subtract),
                      reads=[r_sp, r_MID], writes=[r_sp])
                fw.op("pool", lambda e: e.tensor_copy(out=Fp[0:8, sl], in_=HI[0:8, :]), reads=[r_HI], writes=[r_Fp])
                fw.op("pool", lambda e: e.tensor_copy(out=Fp[32:40, sl], in_=MID[32:40, :]), reads=[r_MID], writes=[r_Fp])
                fw.op("pool", lambda e: e.tensor_copy(out=Fp[64:72, sl], in_=sp[64:72, :]), reads=[r_sp], writes=[r_Fp])
            for j in range(K.dbg_pairs):
                pb = npair % 2
                npair += 1
                fw.dma("pool", c_wp[pb], wp[pb][:, :, 0:128], w_inv[:, :, j * 128:(j + 1) * 128], writes=[r_wp[pb]])
                fw.dma("pool", c_wp[pb], wp[pb][:, :, 128:256], w_inv[:, :, 512 + j * 128:512 + (j + 1) * 128], writes=[r_wp[pb]])
                fw.dma("pool", c_wp[pb], wp[pb][:, :, 256:384], w_inv[:, :, 1024 + j * 128:1024 + (j + 1) * 128], writes=[r_wp[pb]])
                for nt in range(4):
                    hb = load_ht(s * 4 + nt)
                    sl = slice(nt * 512, (nt + 1) * 512)
                    for hh in range(2):
                        h = 2 * j + hh
                        bq = K.bank()
                        for kc in range(8):
                            mm(K, K.ps[0:64, bq, :], wp[pb][:, kc, hh * 64:(hh + 1) * 64], ht[hb][:, kc, :], kc == 0, kc == 7,
                               [r_wp[pb], r_ht[hb]], [K.r_ps[bq]])
                        fw.op("act", lambda e: e.activation(out=qa[pb][0:64, hh, sl], in_=K.ps[0:64, bq, :], func=AF.Copy, scale=FOX_SCALE),
                              reads=[K.r_ps[bq]], writes=[r_q[pb][hh]])
                        bk = K.bank()
                        for kc in range(8):
                            mm(K, K.ps[0:64, bk, :], wp[pb][:, kc, 128 + hh * 64:128 + (hh + 1) * 64], ht[hb][:, kc, :], kc == 0, kc == 7,
                               [r_wp[pb], r_ht[hb]], [K.r_ps[bk]])
                        fw.op("dve", lambda e: e.tensor_copy(out=ka[pb][0:64, hh, sl], in_=K.ps[0:64, bk, :]),
                              reads=[K.r_ps[bk]], writes=[r_k[pb][hh]])
                        if "sel" in K.dbg_skip:
                            continue
                        ba = K.bank()
                        mm(K, K.ps[0:70, ba, :], selq[0:97, h, :], Fp[0:97, sl], True, True, [r_c, r_Fp], [K.r_ps[ba]])
                        fw.op("dve", lambda e: e.tensor_copy(out=qa[pb][64:70, hh, sl], in_=K.ps[64:70, ba, :]),
                              reads=[K.r_ps[ba]], writes=[r_q[pb][hh]])
                        ba = K.bank()
                        mm(K, K.ps[0:70, ba, :], selk[0:97, h, :], Fp[0:97, sl], True, True, [r_c, r_Fp], [K.r_ps[ba]])
                        fw.op("dve", lambda e: e.tensor_copy(out=ka[pb][64:70, hh, sl], in_=K.ps[64:70, ba, :]),
                              reads=[K.r_ps[ba]], writes=[r_k[pb][hh]])
                    if "v" in K.dbg_skip:
                        continue
                    bv = K.bank()
                    for it in range(4):
                        for kc in range(8):
                            mm(K, K.ps[:, bv, it * 128:(it + 1) * 128], ht[hb][:, kc, it * 128:(it + 1) * 128], wp[pb][:, kc, 256:384],
                               kc == 0, kc == 7, [r_wp[pb], r_ht[hb]], [K.r_ps[bv]])
                    src = K.ps[:, bv, :].rearrange("p (t c) -> p t c", t=4)
                    if "vevac" in K.dbg_skip:
                        continue
                    fw.op("act", lambda e: e.activation(out=VE[pb][:, nt * 4:nt * 4 + 4, 0:64], in_=src[:, :, 0:64], func=AF.Copy),
                          reads=[K.r_ps[bv]], writes=[r_VE[pb]])
                    if "vevac2" in K.dbg_skip:
                        continue
                    fw.op("act", lambda e: e.activation(out=VO[pb][:, nt * 4:nt * 4 + 4, 64:128], in_=src[:, :, 64:128], func=AF.Copy),
                          reads=[K.r_ps[bv]], writes=[r_VO[pb]])
                for hh in range(K.dbg_heads):
                    rq = [r_q[pb][hh], r_k[pb][hh], r_VE[pb] if hh == 0 else r_VO[pb]]
                    if hh == 0:
                        attention_head(K, A, qa[pb][:, 0, :], ka[pb][:, 0, :], 70, lambda i: VE[pb][:, i, 0:65], 65, 0, 64, maskT[:, :], rq + [r_c],
                                       lambda G: ao[pb][0:64, G * 512:(G + 1) * 512], lambda G: r_ao[pb])
                    else:
                        attention_head(K, A, qa[pb][:, 1, :], ka[pb][:, 1, :], 70, lambda i: VO[pb][:, i, 0:128], 128, 64, 0, maskT[:, :], rq + [r_c],
                                       lambda G: ao[pb][64:128, G * 512:(G + 1) * 512], lambda G: r_ao[pb])
                if "at" not in K.dbg_skip:
                    fw.dma("sp", c_ao[pb], AT[j, :, s * SEQ:(s + 1) * SEQ], ao[pb][:], reads=[r_ao[pb]], writes=[K.r_AT[s]])
        fw.barrier(c_wp + c_ht + c_ao + [c_c])


def phase_outproj(K, XT, AT, w_out):
    fw = K.fw
    with ExitStack() as es:
        wo = _tile(K, es, "opw", [128, 8, D], BF16)
        at = [_tile(K, es, "opat%d" % i, [128, 8, 512], BF16) for i in range(2)]
        xg = [_tile(K, es, "opxg%d" % i, [128, 8, 512], F32) for i in range(2)]
        r_wo = fw.res()
        r_at = [fw.res() for _ in range(2)]
        r_xg = [[fw.res() for _ in range(8)] for _ in range(2)]
        c_wo = fw.chan()
        c_at = [fw.chan() for _ in range(2)]
        c_x = [fw.chan() for _ in range(2)]
        c_xs = [fw.chan() for _ in range(2)]
        XTv = XT.rearrange("c p n -> p c n")
        ATv = AT.rearrange("c p n -> p c n")
        fw.dma("pool", c_wo, wo[:], w_out.rearrange("(c p) n -> p c n", p=128), writes=[r_wo])
        for g in range(NT // 512):
            b = g % 2
            fw.dma("sp", c_at[b], at[b][:], ATv[:, :, g * 512:(g + 1) * 512], reads=[K.r_AT[g // 4]], writes=[r_at[b]])
            fw.dma("sp", c_x[b], xg[b][:], XTv[:, :, g * 512:(g + 1) * 512], reads=[K.r_XT[g]], writes=r_xg[b])
            for dmc in range(8):
                by = K.bank()
                for c in range(8):
                    mm(K, K.ps[:, by, :], wo[:, c, dmc * 128:(dmc + 1) * 128], at[b][:, c, :], c == 0, c == 7, [r_wo, r_at[b]], [K.r_ps[by]])
                fw.op("dve", lambda e: e.tensor_tensor(out=xg[b][:, dmc, :], in0=K.ps[:, by, :], in1=xg[b][:, dmc, :], op=ALU.add),
                      reads=[K.r_ps[by], r_xg[b][dmc]], writes=[r_xg[b][dmc]])
            fw.dma("sp", c_xs[b], XTv[:, :, g * 512:(g + 1) * 512], xg[b][:], reads=r_xg[b], writes=[K.r_XT[g]])
        fw.barrier([c_wo] + c_at + c_x + c_xs)


def phase_mla(K, HT, AT, w_in, w_uq, w_uqs, w_ukv, qn_col, kvn_col, C):
    fw = K.fw
    with ExitStack() as es:
        A = AttnCtx(K, es, "ml")
        ht = [_tile(K, es, "mlht%d" % i, [128, 8, 512], BF16) for i in range(2)]
        qa = [_tile(K, es, "mlqa%d" % i, [128, 2, SEQ], BF16) for i in range(2)]
        ka = [_tile(K, es, "mlka%d" % i, [128, 2, SEQ], BF16) for i in range(2)]
        VE = [_tile(K, es, "mlVE%d" % i, [128, 16, 65], BF16) for i in range(2)]
        VO = [_tile(K, es, "mlVO%d" % i, [128, 16, 128], BF16) for i in range(2)]
        ao = [_tile(K, es, "mlao%d" % i, [128, SEQ], BF16) for i in range(2)]
        wlat = _tile(K, es, "mlwlat", [128, 8, 384], BF16)
        wkpe = _tile(K, es, "mlwkpe", [128, 8, 96], BF16)
        wkpes = _tile(K, es, "mlwkpes", [128, 8, 96], BF16)
        wuq = _tile(K, es, "mlwuq", [128, 2, 768], BF16)
        wuqs = _tile(K, es, "mlwuqs", [128, 2, 768], BF16)
        wukv = _tile(K, es, "mlwukv", [128, 1024], BF16)
        maskT = _tile(K, es, "mlmask", [128, 128], BF16)
        cos2 = _tile(K, es, "mlcos", [128, SEQ], F32)
        sin2 = _tile(K, es, "mlsin", [128, SEQ], F32)
        cqn = _tile(K, es, "mlcqn", [128, 2, SEQ], BF16)
        ckvn = _tile(K, es, "mlckvn", [128, SEQ], BF16)
        kpe = _tile(K, es, "mlkpe", [128, SEQ], BF16)
        cqf = _tile(K, es, "mlcqf", [128, 3, 512], F32)
        sq = _tile(K, es, "mlsq", [128, 3, 512], BF16)
        rstd = [_tile(K, es, "mlrstd%d" % i, [128, 512], F32) for i in range(2)]
        tmp = _tile(K, es, "mltmp", [128, 512], F32)
        t1 = [_tile(K, es, "mlt1%d" % i, [128, 512], F32) for i in range(2)]
        t2 = [_tile(K, es, "mlt2%d" % i, [128, 512], F32) for i in range(2)]
        r_ht = [fw.res() for _ in range(2)]
        r_q = [[fw.res() for _ in range(2)] for _ in range(2)]
        r_k = [[fw.res() for _ in range(2)] for _ in range(2)]
        r_VE = [fw.res() for _ in range(2)]
        r_VO = [fw.res() for _ in range(2)]
        r_ao = [fw.res() for _ in range(2)]
        r_c, r_cqn, r_ckvn, r_kpe, r_cqf, r_sq, r_tmp = [fw.res() for _ in range(7)]
        r_rstd = [fw.res() for _ in range(2)]
        r_t1 = [fw.res() for _ in range(2)]
        r_t2 = [fw.res() for _ in range(2)]
        c_ht = [fw.chan() for _ in range(2)]
        c_ao = [fw.chan() for _ in range(2)]
        c_c = fw.chan()
        w_inv = w_in.rearrange("(c p) n -> p c n", p=128)
        HTv = HT.rearrange("c p n -> p c n")
        fw.dma("pool", c_c, wlat[:], w_inv[:, :, 1544:1928], writes=[r_c])
        fw.op("dve", lambda e: e.memset(wkpe[:], 0.0), writes=[r_c])
        fw.op("dve", lambda e: e.memset(wkpes[:], 0.0), writes=[r_c])
        fw.dma("pool", c_c, wkpe[:, :, 64:96], w_inv[:, :, 1928:1960], writes=[r_c])
        fw.dma("pool", c_c, wkpes[:, :, 64:80], w_inv[:, :, 1944:1960], writes=[r_c])
        fw.dma("pool", c_c, wkpes[:, :, 80:96], w_inv[:, :, 1928:1944], writes=[r_c])
        fw.dma("pool", c_c, wuq[:], w_uq.rearrange("(c p) n -> p c n", p=128), writes=[r_c])
        fw.dma("pool", c_c, wuqs[:], w_uqs.rearrange("(c p) n -> p c n", p=128), writes=[r_c])
        fw.dma("pool", c_c, wukv[:], w_ukv, writes=[r_c])
        fw.dma("pool", c_c, maskT[:], C["mask_chunk"], writes=[r_c])
        fw.dma("sp", c_c, cos2[64:96, :], C["cos2"], writes=[r_c])
        fw.dma("sp", c_c, sin2[64:96, :], C["sin2s"], writes=[r_c])
        for b in range(2):
            fw.op("pool", lambda e: e.memset(VE[b][:, :, 64:65], 1.0), writes=[r_VE[b]])
            fw.op("pool", lambda e: e.memset(VO[b][:, :, 0:64], 0.0), writes=[r_VO[b]])
            fw.op("pool", lambda e: e.memset(VO[b][:, :, 0:1], 1.0), writes=[r_VO[b]])
        nht = 0
        npair = 0
        ti = 0
        for s in range(2):
            for nt in range(4):
                hb = nht % 2
                nht += 1
                g = s * 4 + nt
                sl = slice(nt * 512, (nt + 1) * 512)
                fw.dma("sp", c_ht[hb], ht[hb][:], HTv[:, :, g * 512:(g + 1) * 512], reads=[K.r_HT[g]], writes=[r_ht[hb]])
                for c in range(3):
                    bank = K.bank()
                    for kc in range(8):
                        mm(K, K.ps[:, bank, :], wlat[:, kc, c * 128:(c + 1) * 128], ht[hb][:, kc, :], kc == 0, kc == 7, [r_c, r_ht[hb]], [K.r_ps[bank]])
                    fw.op("act", lambda e: e.activation(out=cqf[:, c, :], in_=K.ps[:, bank, :], func=AF.Copy), reads=[K.r_ps[bank]], writes=[r_cqf])
                rms_rstd(K, cqf[:, 0:2, :], 2, 0, 512, sq[:, 0:2, :], rstd[0][:, :], [r_cqf], [r_sq] * 2, r_rstd[0], tmp, r_tmp, 256)
                rms_rstd(K, cqf[:, 2:3, :], 1, 0, 512, sq[:, 2:3, :], rstd[1][:, :], [r_cqf], [r_sq], r_rstd[1], tmp, r_tmp, 128)
                for c in range(2):
                    fw.op("dve", lambda e: e.scalar_tensor_tensor(out=cqn[:, c, sl], in0=cqf[:, c, :], scalar=qn_col[:, c:c + 1], in1=rstd[0][:, :],
                                                                   op0=ALU.mult, op1=ALU.mult), reads=[r_cqf, r_rstd[0], K.r_const], writes=[r_cqn])
                fw.op("dve", lambda e: e.scalar_tensor_tensor(out=ckvn[:, sl], in0=cqf[:, 2, :], scalar=kvn_col[:, 0:1], in1=rstd[1][:, :],
                                                               op0=ALU.mult, op1=ALU.mult), reads=[r_cqf, r_rstd[1], K.r_const], writes=[r_ckvn])
                bA, bB = K.bank(), K.bank()
                for kc in range(8):
                    mm(K, K.ps[0:96, bA, :], wkpe[:, kc, :], ht[hb][:, kc, :], kc == 0, kc == 7, [r_c, r_ht[hb]], [K.r_ps[bA]])
                for kc in range(8):
                    mm(K, K.ps[0:96, bB, :], wkpes[:, kc, :], ht[hb][:, kc, :], kc == 0, kc == 7, [r_c, r_ht[hb]], [K.r_ps[bB]])
                tb = ti % 2
                ti += 1
                fw.op("dve", lambda e: e.tensor_tensor(out=t1[tb][64:96, :], in0=K.ps[64:96, bA, :], in1=cos2[64:96, sl], op=ALU.mult),
                      reads=[K.r_ps[bA], r_c], writes=[r_t1[tb]])
                fw.op("dve", lambda e: e.tensor_tensor(out=t2[tb][64:96, :], in0=K.ps[64:96, bB, :], in1=sin2[64:96, sl], op=ALU.mult),
                      reads=[K.r_ps[bB], r_c], writes=[r_t2[tb]])
                fw.op("pool", lambda e: e.tensor_tensor(out=kpe[64:96, sl], in0=t1[tb][64:96, :], in1=t2[tb][64:96, :], op=ALU.add),
                      reads=[r_t1[tb], r_t2[tb]], writes=[r_kpe])
            for j in range(4):
                pb = npair % 2
                npair += 1
                for nt in range(4):
                    sl = slice(nt * 512, (nt + 1) * 512)
                    for hh in range(2):
                        h = 2 * j + hh
                        bA, bB = K.bank(), K.bank()
                        for c in range(2):
                            mm(K, K.ps[0:96, bA, :], wuq[:, c, h * 96:(h + 1) * 96], cqn[:, c, sl], c == 0, c == 1, [r_c, r_cqn], [K.r_ps[bA]])
                        for c in range(2):
                            mm(K, K.ps[0:96, bB, :], wuqs[:, c, h * 96:(h + 1) * 96], cqn[:, c, sl], c == 0, c == 1, [r_c, r_cqn], [K.r_ps[bB]])
                        fw.op("act", lambda e: e.activation(out=qa[pb][0:64, hh, sl], in_=K.ps[0:64, bA, :], func=AF.Copy, scale=MLA_SCALE),
                              reads=[K.r_ps[bA]], writes=[r_q[pb][hh]])
                        tb = ti % 2
                        ti += 1
                        fw.op("dve", lambda e: e.scalar_tensor_tensor(out=t1[tb][64:96, :], in0=K.ps[64:96, bA, :], scalar=MLA_SCALE, in1=cos2[64:96, sl],
                                                                       op0=ALU.mult, op1=ALU.mult), reads=[K.r_ps[bA], r_c], writes=[r_t1[tb]])
                        fw.op("dve", lambda e: e.scalar_tensor_tensor(out=t2[tb][64:96, :], in0=K.ps[64:96, bB, :], scalar=MLA_SCALE, in1=sin2[64:96, sl],
                                                                       op0=ALU.mult, op1=ALU.mult), reads=[K.r_ps[bB], r_c], writes=[r_t2[tb]])
                        fw.op("pool", lambda e: e.tensor_tensor(out=qa[pb][64:96, hh, sl], in0=t1[tb][64:96, :], in1=t2[tb][64:96, :], op=ALU.add),
                              reads=[r_t1[tb], r_t2[tb]], writes=[r_q[pb][hh]])
                        bk = K.bank()
                        mm(K, K.ps[0:64, bk, :], wukv[:, h * 128:h * 128 + 64], ckvn[:, sl], True, True, [r_c, r_ckvn], [K.r_ps[bk]])
                        fw.op("act", lambda e: e.activation(out=ka[pb][0:64, hh, sl], in_=K.ps[0:64, bk, :], func=AF.Copy),
                              reads=[K.r_ps[bk]], writes=[r_k[pb][hh]])
                        fw.op("pool", lambda e: e.tensor_copy(out=ka[pb][64:96, hh, sl], in_=kpe[64:96, sl]), reads=[r_kpe], writes=[r_k[pb][hh]])
                    bv = K.bank()
                    vcols = wukv[:, 2 * j * 128:(2 * j + 2) * 128].rearrange("p (h c) -> p h c", h=2)[:, :, 64:128]
                    for it in range(4):
                        i = nt * 4 + it
                        mm(K, K.ps[:, bv, it * 128:(it + 1) * 128], ckvn[:, i * 128:(i + 1) * 128], vcols, True, True, [r_c, r_ckvn], [K.r_ps[bv]])
                    src = K.ps[:, bv, :].rearrange("p (t c) -> p t c", t=4)
                    fw.op("act", lambda e: e.activation(out=VE[pb][:, nt * 4:nt * 4 + 4, 0:64], in_=src[:, :, 0:64], func=AF.Copy),
                          reads=[K.r_ps[bv]], writes=[r_VE[pb]])
                    fw.op("act", lambda e: e.activation(out=VO[pb][:, nt * 4:nt * 4 + 4, 64:128], in_=src[:, :, 64:128], func=AF.Copy),
                          reads=[K.r_ps[bv]], writes=[r_VO[pb]])
                for hh in range(2):
                    rq = [r_q[pb][hh], r_k[pb][hh], r_VE[pb] if hh == 0 else r_VO[pb], r_c]
                    if hh == 0:
                        attention_head(K, A, qa[pb][:, 0, :], ka[pb][:, 0, :], 96, lambda i: VE[pb][:, i, 0:65], 65, 0, 64, maskT[:, :], rq,
                                       lambda G: ao[pb][0:64, G * 512:(G + 1) * 512], lambda G: r_ao[pb])
                    else:
                        attention_head(K, A, qa[pb][:, 1, :], ka[pb][:, 1, :], 96, lambda i: VO[pb][:, i, 0:128], 128, 64, 0, maskT[:, :], rq,
                                       lambda G: ao[pb][64:128, G * 512:(G + 1) * 512], lambda G: r_ao[pb])
                fw.dma("sp", c_ao[pb], AT[4 + j, :, s * SEQ:(s + 1) * SEQ], ao[pb][:], reads=[r_ao[pb]], writes=[K.r_AT[s]])
        fw.barrier(c_ht + c_ao + [c_c])


GDN_QSCALE = 128 ** -0.5


def phase_gdn_proj(K, HT, w_in, convw_col, dtb_bc, alog_bc, S):
    fw = K.fw
    with ExitStack() as es:
        hts = _tile(K, es, "gpht", [128, 8, NT], BF16)
        wblk = [_tile(K, es, "gpw%d" % i, [128, 8, 512], BF16) for i in range(2)]
        wba = _tile(K, es, "gpwba", [128, 8, 16], BF16)
        dg = [_tile(K, es, "gpdg%d" % i, [128, 4, 4, 128], BF16) for i in range(2)]
        xc = [_tile(K, es, "gpxc%d" % i, [128, 515], BF16) for i in range(4)]
        qs = [_tile(K, es, "gpqs%d" % i, [128, 512], F32) for i in range(2)]
        sq = [_tile(K, es, "gpsq%d" % i, [128, 512], BF16) for i in range(2)]
        tmp = [_tile(K, es, "gptmp%d" % i, [128, 512], F32) for i in range(2)]
        rinv = [_tile(K, es, "gprinv%d" % i, [128, 512], F32) for i in range(2)]
        qn = [_tile(K, es, "gpqn%d" % i, [128, 512], BF16) for i in range(3)]
        tt = [_tile(K, es, "gptt%d" % i, [128, 4, 128], BF16) for i in range(2)]
        gt = [_tile(K, es, "gpgt%d" % i, [128, D], BF16) for i in range(2)]
        bgt = [_tile(K, es, "gpbg%d" % i, [128, 16], F32) for i in range(2)]
        e1 = _tile(K, es, "gpe1", [128, 8], F32)
        nexpa = _tile(K, es, "gpnexpa", [128, 8], F32)
        r_ht = [fw.res() for _ in range(NT // 512)]
        r_w = [fw.res() for _ in range(2)]
        r_dg = [fw.res() for _ in range(2)]
        r_xc = [fw.res() for _ in range(4)]
        r_qs = [fw.res() for _ in range(2)]
        r_sq = [fw.res() for _ in range(2)]
        r_tmp = [fw.res() for _ in range(2)]
        r_rinv = [fw.res() for _ in range(2)]
        r_qn = [fw.res() for _ in range(3)]
        r_tt = [fw.res() for _ in range(2)]
        r_gt = [fw.res() for _ in range(2)]
        r_bg = [fw.res() for _ in range(2)]
        r_c, r_e1 = fw.res(), fw.res()
        c_ht = fw.chan()
        c_w = [fw.chan() for _ in range(2)]
        c_c = fw.chan()
        c_qn = [fw.chan() for _ in range(3)]
        c_tt = [fw.chan() for _ in range(2)]
        c_gt = [fw.chan() for _ in range(2)]
        c_bg = [fw.chan() for _ in range(2)]
        HTv = HT.rearrange("c p n -> p c n")
        w_inv = w_in.rearrange("(c p) n -> p c n", p=128)
        for g in range(NT // 512):
            fw.dma("sp", c_ht, hts[:, :, g * 512:(g + 1) * 512], HTv[:, :, g * 512:(g + 1) * 512], reads=[K.r_HT[g]], writes=[r_ht[g]])
        fw.dma("pool", c_c, wba[:], w_inv[:, :, 3072:3088], writes=[r_c])
        fw.op("act", lambda e: e.activation(out=nexpa[:], in_=alog_bc, func=AF.Exp), reads=[K.r_const], writes=[r_c])
        fw.op("dve", lambda e: e.tensor_scalar(out=nexpa[:], in0=nexpa[:], scalar1=-1.0, scalar2=None, op0=ALU.mult), reads=[r_c], writes=[r_c])
        nw = 0
        ci = 0
        qi = 0
        ti = 0

        def load_w(col0):
            nonlocal nw
            b = nw % 2
            nw += 1
            fw.dma("pool", c_w[b], wblk[b][:], w_inv[:, :, col0:col0 + 512], writes=[r_w[b]])
            return b

        wq = {0: load_w(0)}
        for blk in range(6):
            wb = wq[blk]
            if blk + 1 < 6:
                wq[blk + 1] = load_w((blk + 1) * 512)
            db = blk % 2
            for tap in range(4):
                for fcl in range(4):
                    fc = blk * 4 + fcl
                    col = tap * 24 + fc
                    fw.op("pool", lambda e: e.tensor_scalar(out=dg[db][:, tap, fcl, :], in0=K.identf[:, :], scalar1=convw_col[:, col:col + 1], scalar2=None,
                                                            op0=ALU.mult), reads=[K.r_const], writes=[r_dg[db]])
            kind = "q" if blk < 2 else ("k" if blk < 4 else "v")
            for s in range(2):
                for nt in range(4):
                    g = s * 4 + nt
                    for fcl in range(4):
                        fc = blk * 4 + fcl
                        hd = fc % 8
                        bank = K.bank()
                        for kc in range(8):
                            mm(K, K.ps[:, bank, :], wblk[wb][:, kc, fcl * 128:(fcl + 1) * 128], hts[:, kc, g * 512:(g + 1) * 512], kc == 0, kc == 7,
                               [r_w[wb], r_ht[g]], [K.r_ps[bank]])
                        x = xc[fcl]
                        if nt == 0:
                            fw.op("pool", lambda e: e.memset(x[:, 0:3], 0.0), writes=[r_xc[fcl]])
                        else:
                            fw.op("pool", lambda e: e.tensor_copy(out=x[:, 0:3], in_=x[:, 512:515]), reads=[r_xc[fcl]], writes=[r_xc[fcl]])
                        fw.op("act", lambda e: e.activation(out=x[:, 3:515], in_=K.ps[:, bank, :], func=AF.Copy), reads=[K.r_ps[bank]], writes=[r_xc[fcl]])
                        cb = K.bank()
                        for tap in range(4):
                            mm(K, K.ps[:, cb, :], dg[db][:, tap, fcl, :], x[:, tap:tap + 512], tap == 0, tap == 3, [r_dg[db], r_xc[fcl]], [K.r_ps[cb]])
                        if kind == "v":
                            n_ = qi % 3
                            qi += 1
                            fw.op("act", lambda e: e.activation(out=qn[n_][:], in_=K.ps[:, cb, :], func=AF.Silu), reads=[K.r_ps[cb]], writes=[r_qn[n_]])
                        else:
                            a_ = ci % 2
                            ci += 1
                            fw.op("act", lambda e: e.activation(out=qs[a_][:], in_=K.ps[:, cb, :], func=AF.Silu), reads=[K.r_ps[cb]], writes=[r_qs[a_]])
                            fw.op("pool", lambda e: e.tensor_tensor(out=sq[a_][:], in0=qs[a_][:], in1=qs[a_][:], op=ALU.mult), reads=[r_qs[a_]], writes=[r_sq[a_]])
                            sb = K.bank()
                            mm(K, K.ps[:, sb, :], K.onesb[:, :], sq[a_][:], True, True, [r_sq[a_], K.r_const], [K.r_ps[sb]])
                            fw.op("act", lambda e: e.activation(out=tmp[a_][:], in_=K.ps[:, sb, :], func=AF.Sqrt, bias=K.epsb[:, 0:1], scale=1.0),
                                  reads=[K.r_ps[sb], K.r_const], writes=[r_tmp[a_]])
                            fw.op("dve", lambda e: e.reciprocal(out=rinv[a_][:], in_=tmp[a_][:]), reads=[r_tmp[a_]], writes=[r_rinv[a_]])
                            n_ = qi % 3
                            qi += 1
                            sc = GDN_QSCALE if kind == "q" else 1.0
                            fw.op("dve", lambda e: e.scalar_tensor_tensor(out=qn[n_][:], in0=qs[a_][:], scalar=sc, in1=rinv[a_][:], op0=ALU.mult, op1=ALU.mult),
                                  reads=[r_qs[a_], r_rinv[a_]], writes=[r_qn[n_]])
                        if kind in ("q", "k"):
                            dst = S["QT"] if kind == "q" else S["KT"]
                            fw.dma("sp", c_qn[n_], dst[hd, :, g * 512:(g + 1) * 512], qn[n_][:], reads=[r_qn[n_]], writes=[S["r_QK"][s]])
                        if kind in ("k", "v"):
                            t_ = ti % 2
                            ti += 1
                            tb = K.bank()
                            pbf = K.ps[:, tb, :].bitcast(BF16)
                            for it in range(4):
                                fw.op("pe", lambda e: e.transpose(out=pbf[:, it * 128:(it + 1) * 128], in_=qn[n_][:, it * 128:(it + 1) * 128], identity=K.identb[:]),
                                      reads=[r_qn[n_], K.r_const], writes=[K.r_ps[tb]])
                            fw.op("act", lambda e: e.activation(out=tt[t_][:], in_=pbf[:, 0:512].rearrange("p (t c) -> p t c", t=4), func=AF.Copy),
                                  reads=[K.r_ps[tb]], writes=[r_tt[t_]])
                            dst = S["Ktok"] if kind == "k" else S["Vtok"]
                            fw.dma("sp", c_tt[t_], dst[g * 512:(g + 1) * 512, hd * 128:(hd + 1) * 128].rearrange("(t p) c -> p t c", p=128), tt[t_][:],
                                   reads=[r_tt[t_]], writes=[S["r_tok"][s]])
        wg = [load_w(3088), load_w(3088 + 512)]
        gi = 0
        for g in range(NT // 512):
            s = g // 4
            for it in range(4):
                tok0 = g * 512 + it * 128
                b_ = gi % 2
                gi += 1
                for half in range(2):
                    bank = K.bank()
                    for kc in range(8):
                        mm(K, K.ps[:, bank, :], hts[:, kc, tok0:tok0 + 128], wblk[wg[half]][:, kc, :], kc == 0, kc == 7, [r_w[wg[half]], r_ht[g]], [K.r_ps[bank]])
                    fw.op("act", lambda e: e.activation(out=gt[b_][:, half * 512:(half + 1) * 512], in_=K.ps[:, bank, :], func=AF.Silu),
                          reads=[K.r_ps[bank]], writes=[r_gt[b_]])
                fw.dma("sp", c_gt[b_], S["Gtok"][tok0:tok0 + 128, :], gt[b_][:], reads=[r_gt[b_]], writes=[S["r_tok"][s]])
                bank = K.bank()
                for kc in range(8):
                    mm(K, K.ps[:, bank, 0:16], hts[:, kc, tok0:tok0 + 128], wba[:, kc, :], kc == 0, kc == 7, [r_c, r_ht[g]], [K.r_ps[bank]])
                fw.op("act", lambda e: e.activation(out=bgt[b_][:, 0:8], in_=K.ps[:, bank, 0:8], func=AF.Sigmoid), reads=[K.r_ps[bank]], writes=[r_bg[b_]])
                fw.op("dve", lambda e: e.tensor_tensor(out=e1[:], in0=K.ps[:, bank, 8:16], in1=dtb_bc, op=ALU.add), reads=[K.r_ps[bank], K.r_const], writes=[r_e1])
                fw.op("act", lambda e: e.activation(out=e1[:], in_=e1[:], func=AF.Exp), reads=[r_e1], writes=[r_e1])
                fw.op("act", lambda e: e.activation(out=e1[:], in_=e1[:], func=AF.Ln, bias=K.oneb[:, 0:1], scale=1.0), reads=[r_e1, K.r_const], writes=[r_e1])
                fw.op("dve", lambda e: e.tensor_tensor(out=bgt[b_][:, 8:16], in0=e1[:], in1=nexpa[:], op=ALU.mult), reads=[r_e1, r_c], writes=[r_bg[b_]])
                fw.dma("sp", c_bg[b_], S["BG"][tok0:tok0 + 128, :], bgt[b_][:], reads=[r_bg[b_]], writes=[S["r_tok"][s]])
        fw.barrier([c_ht, c_c] + c_w + c_qn + c_tt + c_gt + c_bg)


BIG = 30000.0


def phase_gdn_core(K, S, AT, onorm_bc, C):
    fw = K.fw
    with ExitStack() as es:
        qT = [_tile(K, es, "gcq%d" % i, [128, 8, 128], BF16) for i in range(2)]
        kT = [_tile(K, es, "gck%d" % i, [128, 8, 128], BF16) for i in range(2)]
        ktok = [_tile(K, es, "gckt%d" % i, [128, D], BF16) for i in range(2)]
        vtok = [_tile(K, es, "gcvt%d" % i, [128, D], BF16) for i in range(2)]
        gtok = [_tile(K, es, "gcgt%d" % i, [128, D], BF16) for i in range(2)]
        bg = [_tile(K, es, "gcbg%d" % i, [128, 16], F32) for i in range(2)]
        U = _tile(K, es, "gcU", [128, 128], F32)
        M1 = _tile(K, es, "gcM1", [128, 128], F32)
        M2 = _tile(K, es, "gcM2", [128, 128], F32)
        St = _tile(K, es, "gcS", [128, 8, 128], F32)
        Sb = _tile(K, es, "gcSb", [128, 8, 128], BF16)
        sc = {n: [_tile(K, es, "gc%s%d" % (n, i), [128, 8], F32) for i in range(2)] for n in ("gam", "ngam", "eg", "gl", "dl", "et", "beg", "nbeta")}
        NB = 3
        ET = [_tile(K, es, "gcET%d" % i, [128, 128], F32) for i in range(NB)]
        Es = [_tile(K, es, "gcEs%d" % i, [128, 128], F32) for i in range(NB)]
        Aa = [_tile(K, es, "gcA%d" % i, [128, 128], F32) for i in range(4)]
        Ab = [_tile(K, es, "gcAT%d" % i, [128, 128], F32) for i in range(4)]
        Pp = [_tile(K, es, "gcP%d" % i, [128, 128], F32) for i in range(4)]
        vb = [_tile(K, es, "gcvb%d" % i, [128, 128], F32) for i in range(NB)]
        kbd = [_tile(K, es, "gckbd%d" % i, [128, 128], F32) for i in range(NB)]
        ktl = [_tile(K, es, "gcktl%d" % i, [128, 128], BF16) for i in range(NB)]
        wT = [_tile(K, es, "gcwT%d" % i, [128, 128], BF16) for i in range(NB)]
        inT = [_tile(K, es, "gcinT%d" % i, [128, 128], BF16) for i in range(NB)]
        usb = [_tile(K, es, "gcusb%d" % i, [128, 128], F32) for i in range(NB)]
        vnw = [_tile(K, es, "gcvnw%d" % i, [128, 128], BF16) for i in range(NB)]
        ivs = [_tile(K, es, "gcivs%d" % i, [128, 128], F32) for i in range(NB)]
        ot = [_tile(K, es, "gcot%d" % i, [128, 8, 128], F32) for i in range(2)]
        ss = [_tile(K, es, "gcss%d" % i, [128, 8], F32) for i in range(2)]
        rstd = [_tile(K, es, "gcrstd%d" % i, [128, 8], F32) for i in range(2)]
        junk = _tile(K, es, "gcjunk", [128, 128], F32)
        og = [_tile(K, es, "gcog%d" % i, [128, 128], F32) for i in range(NB)]
        ogb = [_tile(K, es, "gcogb%d" % i, [128, 128], BF16) for i in range(NB)]
        att = [_tile(K, es, "gcatt%d" % i, [128, 8, 128], BF16) for i in range(2)]

        def RL(n):
            return [fw.res() for _ in range(n)]
        r_in = RL(2)
        r_c = fw.res()
        r_S = RL(8)
        r_Sb = RL(8)
        r_sc = RL(2)
        r_ET, r_Es, r_vb, r_kbd, r_ktl, r_wT, r_inT, r_usb, r_vnw, r_ivs, r_og, r_ogb = [RL(NB) for _ in range(12)]
        r_A, r_B, r_P = RL(4), RL(4), RL(4)
        r_ot, r_ss, r_att = RL(2), RL(2), RL(2)
        r_junk = fw.res()
        c_in = [fw.chan() for _ in range(2)]
        c_att = [fw.chan() for _ in range(2)]
        c_c = fw.chan()
        fw.dma("sp", c_c, U[:], C["U"], writes=[r_c])
        fw.dma("sp", c_c, M1[:], C["gmask1"], writes=[r_c])
        fw.dma("sp", c_c, M2[:], C["gmask2"], writes=[r_c])
        QTv = S["QT"].rearrange("h p n -> p h n")
        KTv = S["KT"].rearrange("h p n -> p h n")
        ATv = AT.rearrange("h p n -> p h n")
        SB_BANKS = [0, 1, 2, 3, 4, 5, 6, 7]
        ia = ib = ip = 0
        step = 0
        for s in range(2):
            for h in range(8):
                fw.op("pool", lambda e: e.memset(St[:, h, :], 0.0), writes=[r_S[h]])
                fw.op("pool", lambda e: e.memset(Sb[:, h, :], 0.0), writes=[r_Sb[h]])
            for b in range(SEQ // 128):
                tok0 = s * SEQ + b * 128
                ib_ = step % 2
                step += 1
                rin = r_in[ib_]
                fw.dma("sp", c_in[ib_], qT[ib_][:], QTv[:, :, tok0:tok0 + 128], reads=[S["r_QK"][s]], writes=[rin])
                fw.dma("sp", c_in[ib_], kT[ib_][:], KTv[:, :, tok0:tok0 + 128], reads=[S["r_QK"][s]], writes=[rin])
                fw.dma("sp", c_in[ib_], ktok[ib_][:], S["Ktok"][tok0:tok0 + 128, :], reads=[S["r_tok"][s]], writes=[rin])
                fw.dma("sp", c_in[ib_], vtok[ib_][:], S["Vtok"][tok0:tok0 + 128, :], reads=[S["r_tok"][s]], writes=[rin])
                fw.dma("sp", c_in[ib_], gtok[ib_][:], S["Gtok"][tok0:tok0 + 128, :], reads=[S["r_tok"][s]], writes=[rin])
                fw.dma("sp", c_in[ib_], bg[ib_][:], S["BG"][tok0:tok0 + 128, :], reads=[S["r_tok"][s]], writes=[rin])
                T_ = {n: sc[n][ib_] for n in sc}
                rs = r_sc[ib_]
                b1, b2 = K.bank(), K.bank()
                mm(K, K.ps[:, b1, 0:8], U[:, :], bg[ib_][:, 8:16], True, True, [r_c, rin], [K.r_ps[b1]])
                mm(K, K.ps[:, b2, 0:8], K.onesf[:, :], bg[ib_][:, 8:16], True, True, [K.r_const, rin], [K.r_ps[b2]])
                fw.op("dve", lambda e: e.tensor_copy(out=T_["gam"][:], in_=K.ps[:, b1, 0:8]), reads=[K.r_ps[b1]], writes=[rs])
                fw.op("dve", lambda e: e.tensor_scalar(out=T_["ngam"][:], in0=K.ps[:, b1, 0:8], scalar1=-1.0, scalar2=None, op0=ALU.mult),
                      reads=[K.r_ps[b1]], writes=[rs])
                fw.op("act", lambda e: e.activation(out=T_["eg"][:], in_=K.ps[:, b1, 0:8], func=AF.Exp), reads=[K.r_ps[b1]], writes=[rs])
                fw.op("act", lambda e: e.activation(out=T_["gl"][:], in_=K.ps[:, b2, 0:8], func=AF.Exp), reads=[K.r_ps[b2]], writes=[rs])
                fw.op("dve", lambda e: e.tensor_tensor(out=T_["dl"][:], in0=K.ps[:, b2, 0:8], in1=T_["gam"][:], op=ALU.subtract),
                      reads=[K.r_ps[b2], rs], writes=[rs])
                fw.op("act", lambda e: e.activation(out=T_["et"][:], in_=T_["dl"][:], func=AF.Exp), reads=[rs], writes=[rs])
                fw.op("dve", lambda e: e.tensor_tensor(out=T_["beg"][:], in0=bg[ib_][:, 0:8], in1=T_["eg"][:], op=ALU.mult), reads=[rin, rs], writes=[rs])
                fw.op("dve", lambda e: e.tensor_scalar(out=T_["nbeta"][:], in0=bg[ib_][:, 0:8], scalar1=-1.0, scalar2=None, op0=ALU.mult),
                      reads=[rin], writes=[rs])
                ob_ = ib_
                for h in range(8):
                    n = (step * 8 + h) % NB
                    hs = slice(h * 128, (h + 1) * 128)
                    gb = bg[ib_][:, 8 + h:9 + h].broadcast_to([128, 128])
                    p1, p2 = K.bank(), K.bank()
                    mm(K, K.ps[:, p1, 0:128], gb, U[:, :], True, False, [rin, r_c], [K.r_ps[p1]])
                    mm(K, K.ps[:, p1, 0:128], K.identf[:, :], M1[:, :], False, True, [K.r_const, r_c], [K.r_ps[p1]])
                    mm(K, K.ps[:, p2, 0:128], gb, U[:, :], True, False, [rin, r_c], [K.r_ps[p2]])
                    mm(K, K.ps[:, p2, 0:128], K.identf[:, :], M2[:, :], False, True, [K.r_const, r_c], [K.r_ps[p2]])
                    fw.op("act", lambda e: e.activation(out=ET[n][:], in_=K.ps[:, p1, 0:128], func=AF.Exp, bias=T_["ngam"][:, h:h + 1], scale=1.0),
                          reads=[K.r_ps[p1], rs], writes=[r_ET[n]])
                    fw.op("act", lambda e: e.activation(out=Es[n][:], in_=K.ps[:, p2, 0:128], func=AF.Exp, bias=T_["gam"][:, h:h + 1], scale=-1.0),
                          reads=[K.r_ps[p2], rs], writes=[r_Es[n]])
                    pk = K.bank()
                    mm(K, K.ps[:, pk, 0:128], kT[ib_][:, h, :], kT[ib_][:, h, :], True, True, [rin], [K.r_ps[pk]])
                    a = ia % 4
                    ia += 1
                    fw.op("dve", lambda e: e.scalar_tensor_tensor(out=Aa[a][:], in0=K.ps[:, pk, 0:128], scalar=T_["nbeta"][:, h:h + 1], in1=Es[n][:],
                                                                   op0=ALU.mult, op1=ALU.mult), reads=[K.r_ps[pk], rs, r_Es[n]], writes=[r_A[a]])
                    pt_ = K.bank()
                    fw.op("pe", lambda e: e.transpose(out=K.ps[:, pt_, 0:128], in_=Aa[a][:], identity=K.identf[:]), reads=[r_A[a], K.r_const], writes=[K.r_ps[pt_]])
                    bt = ib % 4
                    ib += 1
                    fw.op("act", lambda e: e.activation(out=Ab[bt][:], in_=K.ps[:, pt_, 0:128], func=AF.Copy), reads=[K.r_ps[pt_]], writes=[r_B[bt]])
                    p = ip % 4
                    ip += 1
                    fw.op("dve", lambda e: e.tensor_tensor(out=Pp[p][:], in0=K.ps[:, pt_, 0:128], in1=K.identf[:, :], op=ALU.add),
                          reads=[K.r_ps[pt_], K.r_const], writes=[r_P[p]])
                    for lev in range(6):
                        pa = K.bank()
                        mm(K, K.ps[:, pa, 0:128], Ab[bt][:], Aa[a][:], True, True, [r_B[bt], r_A[a]], [K.r_ps[pa]])
                        a2 = ia % 4
                        ia += 1
                        fw.op("act", lambda e: e.activation(out=Aa[a2][:], in_=K.ps[:, pa, 0:128], func=AF.Copy), reads=[K.r_ps[pa]], writes=[r_A[a2]])
                        if lev < 5:
                            pb_ = K.bank()
                            mm(K, K.ps[:, pb_, 0:128], Aa[a][:], Ab[bt][:], True, True, [r_B[bt], r_A[a]], [K.r_ps[pb_]])
                            bt2 = ib % 4
                            ib += 1
                            fw.op("dve", lambda e: e.tensor_copy(out=Ab[bt2][:], in_=K.ps[:, pb_, 0:128]), reads=[K.r_ps[pb_]], writes=[r_B[bt2]])
                        pp = K.bank()
                        mm(K, K.ps[:, pp, 0:128], Aa[a2][:], Pp[p][:], True, True, [r_A[a2], r_P[p]], [K.r_ps[pp]])
                        p2_ = ip % 4
                        ip += 1
                        fw.op("dve", lambda e: e.tensor_tensor(out=Pp[p2_][:], in0=K.ps[:, pp, 0:128], in1=Pp[p][:], op=ALU.add),
                              reads=[K.r_ps[pp], r_P[p]], writes=[r_P[p2_]])
                        a, p = a2, p2_
                        if lev < 5:
                            bt = bt2
                    fw.op("pool", lambda e: e.tensor_scalar(out=vb[n][:], in0=vtok[ib_][:, hs], scalar1=bg[ib_][:, h:h + 1], scalar2=None, op0=ALU.mult),
                          reads=[rin], writes=[r_vb[n]])
                    fw.op("pool", lambda e: e.tensor_scalar(out=kbd[n][:], in0=ktok[ib_][:, hs], scalar1=T_["beg"][:, h:h + 1], scalar2=None, op0=ALU.mult),
                          reads=[rin, rs], writes=[r_kbd[n]])
                    fw.op("pool", lambda e: e.tensor_scalar(out=ktl[n][:], in0=ktok[ib_][:, hs], scalar1=T_["et"][:, h:h + 1], scalar2=None, op0=ALU.mult),
                          reads=[rin, rs], writes=[r_ktl[n]])
                    pu, pw = K.bank(), K.bank()
                    mm(K, K.ps[:, pu, 0:128], Pp[p][:], vb[n][:], True, True, [r_P[p], r_vb[n]], [K.r_ps[pu]])
                    mm(K, K.ps[:, pw, 0:128], kbd[n][:], Pp[p][:], True, True, [r_P[p], r_kbd[n]], [K.r_ps[pw]])
                    fw.op("act", lambda e: e.activation(out=usb[n][:], in_=K.ps[:, pu, 0:128], func=AF.Copy), reads=[K.r_ps[pu]], writes=[r_usb[n]])
                    fw.op("act", lambda e: e.activation(out=wT[n][:], in_=K.ps[:, pw, 0:128], func=AF.Copy), reads=[K.r_ps[pw]], writes=[r_wT[n]])
                    pq = K.bank()
                    mm(K, K.ps[:, pq, 0:128], kT[ib_][:, h, :], qT[ib_][:, h, :], True, True, [rin], [K.r_ps[pq]])
                    fw.op("dve", lambda e: e.tensor_tensor(out=inT[n][:], in0=K.ps[:, pq, 0:128], in1=ET[n][:], op=ALU.mult),
                          reads=[K.r_ps[pq], r_ET[n]], writes=[r_inT[n]])
                    pws = K.bank()
                    mm(K, K.ps[:, pws, 0:128], wT[n][:], Sb[:, h, :], True, True, [r_wT[n], r_Sb[h]], [K.r_ps[pws]])
                    fw.op("dve", lambda e: e.tensor_tensor(out=vnw[n][:], in0=usb[n][:], in1=K.ps[:, pws, 0:128], op=ALU.subtract),
                          reads=[K.r_ps[pws], r_usb[n]], writes=[r_vnw[n]])
                    pqs, piv, pkv = K.bank(), K.bank(), K.bank()
                    mm(K, K.ps[:, pqs, 0:128], qT[ib_][:, h, :], Sb[:, h, :], True, True, [rin, r_Sb[h]], [K.r_ps[pqs]])
                    mm(K, K.ps[:, piv, 0:128], inT[n][:], vnw[n][:], True, True, [r_inT[n], r_vnw[n]], [K.r_ps[piv]])
                    mm(K, K.ps[:, pkv, 0:128], ktl[n][:], vnw[n][:], True, True, [r_ktl[n], r_vnw[n]], [K.r_ps[pkv]])
                    fw.op("act", lambda e: e.activation(out=ivs[n][:], in_=K.ps[:, piv, 0:128], func=AF.Copy), reads=[K.r_ps[piv]], writes=[r_ivs[n]])
                    fw.op("dve", lambda e: e.scalar_tensor_tensor(out=ot[ob_][:, h, :], in0=K.ps[:, pqs, 0:128], scalar=T_["eg"][:, h:h + 1], in1=ivs[n][:],
                                                                   op0=ALU.mult, op1=ALU.add), reads=[K.r_ps[pqs], rs, r_ivs[n]], writes=[r_ot[ob_]])
                    fw.op("dve", lambda e: e.scalar_tensor_tensor(out=St[:, h, :], in0=St[:, h, :], scalar=T_["gl"][:, h:h + 1], in1=K.ps[:, pkv, 0:128],
                                                                   op0=ALU.mult, op1=ALU.add), reads=[K.r_ps[pkv], rs, r_S[h]], writes=[r_S[h]])
                    fw.op("pool", lambda e: e.tensor_copy(out=Sb[:, h, :], in_=St[:, h, :]), reads=[r_S[h]], writes=[r_Sb[h]])
                    fw.op("act", lambda e: e.activation(out=junk[:], in_=ot[ob_][:, h, :], func=AF.Square, accum_out=ss[ob_][:, h:h + 1]),
                          reads=[r_ot[ob_]], writes=[r_junk, r_ss[ob_]])
                fw.op("act", lambda e: e.activation(out=rstd[ob_][:], in_=ss[ob_][:], func=AF.Sqrt, bias=K.epsb[:, 0:1], scale=1.0 / 128),
                      reads=[r_ss[ob_], K.r_const], writes=[r_ss[ob_]])
                fw.op("dve", lambda e: e.reciprocal(out=rstd[ob_][:], in_=rstd[ob_][:]), reads=[r_ss[ob_]], writes=[r_ss[ob_]])
                for h in range(8):
                    n = (step * 8 + h) % NB
                    hs = slice(h * 128, (h + 1) * 128)
                    fw.op("dve", lambda e: e.scalar_tensor_tensor(out=og[n][:], in0=ot[ob_][:, h, :], scalar=rstd[ob_][:, h:h + 1], in1=onorm_bc,
                                                                   op0=ALU.mult, op1=ALU.mult), reads=[r_ot[ob_], r_ss[ob_], K.r_const], writes=[r_og[n]])
                    fw.op("pool", lambda e: e.tensor_tensor(out=ogb[n][:], in0=og[n][:], in1=gtok[ib_][:, hs], op=ALU.mult),
                          reads=[r_og[n], rin], writes=[r_ogb[n]])
                    pt_ = K.bank()
                    ptb = K.ps[:, pt_, :].bitcast(BF16)
                    fw.op("pe", lambda e: e.transpose(out=ptb[:, 0:128], in_=ogb[n][:], identity=K.identb[:]), reads=[r_ogb[n], K.r_const], writes=[K.r_ps[pt_]])
                    fw.op("act", lambda e: e.activation(out=att[ob_][:, h, :], in_=ptb[:, 0:128], func=AF.Copy), reads=[K.r_ps[pt_]], writes=[r_att[ob_]])
                fw.dma("sp", c_att[ob_], ATv[:, :, tok0:tok0 + 128], att[ob_][:], reads=[r_att[ob_]], writes=[K.r_AT[s]])
        fw.barrier(c_in + c_att + [c_c])
```

```python
import numpy as np
from contextlib import ExitStack
import concourse.bass as bass
import concourse.mybir as mybir
from concourse.bass_utils import run_bass_kernel_spmd

F32 = mybir.dt.float32
BF16 = mybir.dt.bfloat16
AF = mybir.ActivationFunctionType
ALU = mybir.AluOpType

N_CORES = 8
D = 1024
NT = 4096
SEQ = 2048
DFF = 2816
EPS = 1e-6


class Res:
    __slots__ = ("name", "w", "r", "excl")

    def __init__(self, name):
        self.name = name
        self.excl = False
        self.w = None
        self.r = {}


class Chan:
    __slots__ = ("sem", "cnt", "key")

    def __init__(self, sem, key):
        self.sem = sem
        self.cnt = 0
        self.key = key


class FW:
    SELF_SYNC = {"pe": False, "act": True, "dve": True, "pool": True, "sp": False}

    def __init__(self, nc, es):
        self.nc = nc
        self.es = es
        self.engs = {"pe": nc.tensor, "act": nc.scalar, "dve": nc.vector, "pool": nc.gpsimd, "sp": nc.sync}
        self.sem = {k: es.enter_context(nc.semaphore("s_" + k)) for k in self.engs}
        self.cnt = {k: 0 for k in self.engs}
        self.seen = {k: {} for k in self.engs}
        self.nchan = 0
        self.nres = 0

    def res(self, name=None):
        self.nres += 1
        return Res(name or ("r%d" % self.nres))

    def chan(self):
        self.nchan += 1
        key = "c%d" % self.nchan
        return Chan(self.es.enter_context(self.nc.semaphore(key)), key)

    def _wait(self, e, reads, writes):
        need = {}

        def add(ev):
            if ev is None:
                return
            key, h, v = ev
            if key == e and not self.SELF_SYNC[e]:
                return
            if self.seen[e].get(key, 0) >= v:
                return
            if key not in need or need[key][1] < v:
                need[key] = (h, v)

        for r in reads:
            add(r.w)
        for r in writes:
            add(r.w)
            for ev in r.r.values():
                add(ev)
        for key, (h, v) in need.items():
            self.engs[e].wait_ge(h, v)
            self.seen[e][key] = v

    def _mark(self, ev, reads, writes):
        key = ev[0]
        for r in reads:
            old = r.r.get(key)
            if old is None or old[2] < ev[2]:
                r.r[key] = ev
        for r in writes:
            r.w = ev
            r.r = {}

    def op(self, e, fn, reads=(), writes=()):
        ex = [r for r in reads if r.excl]
        if ex:
            reads = [r for r in reads if not r.excl]
            writes = list(writes) + ex
        self._wait(e, reads, writes)
        ins = fn(self.engs[e])
        self.cnt[e] += 1
        ins.then_inc(self.sem[e], 1)
        self._mark((e, self.sem[e], self.cnt[e]), reads, writes)
        return ins

    def dma(self, e, chan, out, in_, reads=(), writes=()):
        self._wait(e, reads, writes)
        ins = self.engs[e].dma_start(out=out, in_=in_)
        chan.cnt += 16
        ins.then_inc(chan.sem, 16)
        self._mark((chan.key, chan.sem, chan.cnt), reads, writes)
        return ins

    def finish(self, e, resources):
        self._wait(e, resources, ())

    def barrier(self, chans=()):
        for e in self.engs:
            for k in self.engs:
                if k == e or self.cnt[k] == 0:
                    continue
                if self.seen[e].get(k, 0) < self.cnt[k]:
                    self.engs[e].wait_ge(self.sem[k], self.cnt[k])
                    self.seen[e][k] = self.cnt[k]
            for c in chans:
                if c.cnt and self.seen[e].get(c.key, 0) < c.cnt:
                    self.engs[e].wait_ge(c.sem, c.cnt)
                    self.seen[e][c.key] = c.cnt


class Ctx:
    pass


def _tile(K, es, name, shape, dt):
    K.uid = getattr(K, "uid", 0) + 1
    return es.enter_context(K.nc.sbuf_tensor("%s_%d" % (name, K.uid), shape, dt))


def mm(K, out, lhsT, rhs, start, stop, reads, writes):
    return K.fw.op("pe", lambda e: e.matmul(out, lhsT=lhsT, rhs=rhs, start=start, stop=stop), reads=reads, writes=writes)


def phase_input(K, x_dram, XT):
    fw, nc = K.fw, K.nc
    with ExitStack() as es:
        xin = [_tile(K, es, "xin%d" % i, [128, D], F32) for i in range(3)]
        xo = [_tile(K, es, "xo%d" % i, [128, 8, 512], F32) for i in range(2)]
        r_in = [fw.res() for _ in range(3)]
        r_o = [fw.res() for _ in range(2)]
        c_in = [fw.chan() for _ in range(3)]
        c_o = [fw.chan() for _ in range(2)]
        XTv = XT.rearrange("c p n -> p c n")
        ti = 0
        for g in range(NT // 512):
            ob = g % 2
            for j in range(4):
                t = g * 4 + j
                ib = ti % 3
                ti += 1
                fw.dma("sp", c_in[ib], xin[ib][:], x_dram[t * 128:(t + 1) * 128, :], writes=[r_in[ib]])
                for half in range(2):
                    bank = K.bank()
                    for q in range(4):
                        c = half * 4 + q
                        fw.op("pe", lambda e: e.transpose(out=K.ps[:, bank, q * 128:(q + 1) * 128],
                                                          in_=xin[ib][:, c * 128:(c + 1) * 128], identity=K.identf[:]),
                              reads=[r_in[ib], K.r_const], writes=[K.r_ps[bank]])
                    src = K.ps[:, bank, :].rearrange("p (c n) -> p c n", c=4)
                    dst = xo[ob][:, half * 4:half * 4 + 4, j * 128:(j + 1) * 128]
                    if half == 0:
                        fw.op("act", lambda e: e.activation(out=dst, in_=src, func=AF.Copy), reads=[K.r_ps[bank]], writes=[r_o[ob]])
                    else:
                        fw.op("dve", lambda e: e.tensor_copy(out=dst, in_=src), reads=[K.r_ps[bank]], writes=[r_o[ob]])
            fw.dma("sp", c_o[ob], XTv[:, :, g * 512:(g + 1) * 512], xo[ob][:], reads=[r_o[ob]], writes=[K.r_XT[g]])
        fw.barrier(c_in + c_o)


def rms_rstd(K, x_tile, nchunk, n0, n, sq_tile, rstd_out, r_x, r_sq, r_rstd, tmp, r_tmp, dim):
    fw = K.fw
    fw.op("act", lambda e: e.activation(out=sq_tile[:, 0:nchunk, 0:n], in_=x_tile[:, 0:nchunk, n0:n0 + n], func=AF.Square),
          reads=r_x, writes=r_sq)
    bank = K.bank()
    for c in range(nchunk):
        mm(K, K.ps[:, bank, 0:n], K.onesb[:, :], sq_tile[:, c, 0:n], c == 0, c == nchunk - 1, [r_sq[c], K.r_const], [K.r_ps[bank]])
    fw.op("act", lambda e: e.activation(out=tmp[:, 0:n], in_=K.ps[:, bank, 0:n], func=AF.Sqrt, bias=K.epsb[:, 0:1], scale=1.0 / dim),
          reads=[K.r_ps[bank], K.r_const], writes=[r_tmp])
    fw.op("dve", lambda e: e.reciprocal(out=rstd_out, in_=tmp[:, 0:n]), reads=[r_tmp], writes=[r_rstd])


def phase_ffn(K, XT, gain_col, w13, w2):
    fw, nc = K.fw, K.nc
    TG = 1024
    NKC = 8
    NFC = DFF // 128
    with ExitStack() as es:
        xg = _tile(K, es, "xg", [128, 8, TG], F32)
        hT = _tile(K, es, "hT", [128, 8, TG], BF16)
        gT = _tile(K, es, "gT", [128, NFC, TG], BF16)
        rstd = _tile(K, es, "rstd", [128, TG], F32)
        tmp = _tile(K, es, "tmp", [128, 512], F32)
        wa = [_tile(K, es, "wa%d" % i, [128, 8, 512], BF16) for i in range(2)]
        wb = [_tile(K, es, "wb%d" % i, [128, 8, 512], BF16) for i in range(2)]
        w2t = [_tile(K, es, "w2t%d" % i, [128, NFC, 256], BF16) for i in range(2)]
        sa = [_tile(K, es, "sa%d" % i, [128, 512], F32) for i in range(3)]
        r_rstd, r_tmp = [fw.res() for _ in range(2)]
        r_xgc = [fw.res() for _ in range(16)]
        r_hTc = [fw.res() for _ in range(16)]
        r_gTc = [fw.res() for _ in range(NFC * 2)]
        r_w13 = [fw.res() for _ in range(2)]
        r_w2 = [fw.res() for _ in range(2)]
        r_sa = [fw.res() for _ in range(3)]
        c_x, c_xs = fw.chan(), fw.chan()
        c_w13 = [fw.chan() for _ in range(2)]
        c_w2 = [fw.chan() for _ in range(2)]
        XTv = XT.rearrange("c p n -> p c n")
        fblocks = [(b * 512, 512) for b in range(5)] + [(2560, 256)]
        w13v = w13.rearrange("(c p) n -> p c n", p=128)
        w2v = w2.rearrange("(c p) n -> p c n", p=128)
        nw13 = 0
        nw2 = 0
        sai = 0

        def load_w13(bi):
            nonlocal nw13
            f0, fwid = fblocks[bi]
            b = nw13 % 2
            nw13 += 1
            fw.dma("pool", c_w13[b], wa[b][:, :, 0:fwid], w13v[:, :, f0:f0 + fwid], writes=[r_w13[b]])
            fw.dma("pool", c_w13[b], wb[b][:, :, 0:fwid], w13v[:, :, DFF + f0:DFF + f0 + fwid], writes=[r_w13[b]])
            return b

        def load_w2(di):
            nonlocal nw2
            b = nw2 % 2
            nw2 += 1
            fw.dma("pool", c_w2[b], w2t[b][:, :, :], w2v[:, :, di * 256:(di + 1) * 256], writes=[r_w2[b]])
            return b

        for g in range(NT // TG):
            t0 = g * TG
            fw.dma("sp", c_x, xg[:], XTv[:, :, t0:t0 + TG], reads=K.r_XT[2 * g:2 * g + 2], writes=r_xgc)
            wbuf = {0: load_w13(0), 1: load_w13(1)}
            for nt in range(TG // 512):
                rms_rstd(K, xg, 8, nt * 512, 512, gT, rstd[:, nt * 512:(nt + 1) * 512], [r_xgc[c * 2 + nt] for c in range(8)],
                         [r_gTc[c * 2] for c in range(8)], r_rstd, tmp, r_tmp, D)
                for c in range(8):
                    fw.op("dve", lambda e: e.scalar_tensor_tensor(out=hT[:, c, nt * 512:(nt + 1) * 512], in0=xg[:, c, nt * 512:(nt + 1) * 512],
                                                                   scalar=gain_col[:, c:c + 1], in1=rstd[:, nt * 512:(nt + 1) * 512],
                                                                   op0=ALU.mult, op1=ALU.mult),
                          reads=[r_xgc[c * 2 + nt], r_rstd, K.r_const], writes=[r_hTc[c * 2 + nt]])
            w2buf = {}
            for bi, (f0, fwid) in enumerate(fblocks):
                b = wbuf[bi]
                for fcl in range(fwid // 128):
                    fc = f0 // 128 + fcl
                    for nt in range(TG // 512):
                        ba, bb = K.bank(), K.bank()
                        for kc in range(NKC):
                            mm(K, K.ps[:, ba, :], wa[b][:, kc, fcl * 128:(fcl + 1) * 128], hT[:, kc, nt * 512:(nt + 1) * 512],
                               kc == 0, kc == NKC - 1, [r_w13[b], r_hTc[kc * 2 + nt]], [K.r_ps[ba]])
                        for kc in range(NKC):
                            mm(K, K.ps[:, bb, :], wb[b][:, kc, fcl * 128:(fcl + 1) * 128], hT[:, kc, nt * 512:(nt + 1) * 512],
                               kc == 0, kc == NKC - 1, [r_w13[b], r_hTc[kc * 2 + nt]], [K.r_ps[bb]])
                        s = sai % 3
                        sai += 1
                        fw.op("act", lambda e: e.activation(out=sa[s][:], in_=K.ps[:, ba, :], func=AF.Silu), reads=[K.r_ps[ba]], writes=[r_sa[s]])
                        rg = r_gTc[fc * 2 + nt]
                        fw.op("dve", lambda e: e.tensor_tensor(out=gT[:, fc, nt * 512:(nt + 1) * 512], in0=K.ps[:, bb, :], in1=sa[s][:], op=ALU.mult),
                              reads=[K.r_ps[bb], r_sa[s]], writes=[rg])
                if bi + 2 < len(fblocks):
                    wbuf[bi + 2] = load_w13(bi + 2)
                if bi == 3:
                    w2buf[0] = load_w2(0)
                if bi == 4:
                    w2buf[1] = load_w2(1)
            for di in range(4):
                b = w2buf[di]
                for dcl in range(2):
                    dmc = di * 2 + dcl
                    for nt in range(TG // 512):
                        by = K.bank()
                        for fc in range(NFC):
                            mm(K, K.ps[:, by, :], w2t[b][:, fc, dcl * 128:(dcl + 1) * 128], gT[:, fc, nt * 512:(nt + 1) * 512],
                               fc == 0, fc == NFC - 1, [r_w2[b], r_gTc[fc * 2 + nt]], [K.r_ps[by]])
                        fw.op("dve", lambda e: e.scalar_tensor_tensor(out=xg[:, dmc, nt * 512:(nt + 1) * 512], in0=K.ps[:, by, :], scalar=0.5,
                                                                       in1=xg[:, dmc, nt * 512:(nt + 1) * 512], op0=ALU.mult, op1=ALU.add),
                              reads=[K.r_ps[by], r_xgc[dmc * 2 + nt]], writes=[r_xgc[dmc * 2 + nt]])
                if di + 2 < 4:
                    w2buf[di + 2] = load_w2(di + 2)
            fw.dma("sp", c_xs, XTv[:, :, t0:t0 + TG], xg[:], reads=r_xgc, writes=K.r_XT[2 * g:2 * g + 2])
        fw.barrier([c_x, c_xs] + c_w13 + c_w2)


def phase_final(K, XT, gain_col, out_dram):
    fw, nc = K.fw, K.nc
    with ExitStack() as es:
        xg = [_tile(K, es, "fxg%d" % i, [128, 8, 512], F32) for i in range(2)]
        sq = _tile(K, es, "fsq", [128, 8, 512], BF16)
        yn = _tile(K, es, "fyn", [128, 8, 512], F32)
        rstd = _tile(K, es, "frstd", [128, 512], F32)
        tmp = _tile(K, es, "ftmp", [128, 512], F32)
        yo = [_tile(K, es, "fyo%d" % i, [128, D], F32) for i in range(2)]
        r_xg = [fw.res() for _ in range(2)]
        r_yo = [fw.res() for _ in range(2)]
        r_sq, r_yn, r_rstd, r_tmp = [fw.res() for _ in range(4)]
        c_x = [fw.chan() for _ in range(2)]
        c_o = [fw.chan() for _ in range(2)]
        XTv = XT.rearrange("c p n -> p c n")
        oi = 0
        for g in range(NT // 512):
            b = g % 2
            fw.dma("sp", c_x[b], xg[b][:], XTv[:, :, g * 512:(g + 1) * 512], reads=[K.r_XT[g]], writes=[r_xg[b]])
            rms_rstd(K, xg[b], 8, 0, 512, sq, rstd[:, :], [r_xg[b]], [r_sq] * 8, r_rstd, tmp, r_tmp, D)
            for c in range(8):
                fw.op("dve", lambda e: e.scalar_tensor_tensor(out=yn[:, c, :], in0=xg[b][:, c, :], scalar=gain_col[:, c:c + 1], in1=rstd[:, :],
                                                               op0=ALU.mult, op1=ALU.mult),
                      reads=[r_xg[b], r_rstd, K.r_const], writes=[r_yn])
            for j in range(4):
                ob = oi % 2
                oi += 1
                for half in range(2):
                    bank = K.bank()
                    for q in range(4):
                        c = half * 4 + q
                        fw.op("pe", lambda e: e.transpose(out=K.ps[:, bank, q * 128:(q + 1) * 128], in_=yn[:, c, j * 128:(j + 1) * 128],
                                                          identity=K.identf[:]),
                              reads=[r_yn, K.r_const], writes=[K.r_ps[bank]])
                    if half == 0:
                        fw.op("act", lambda e: e.activation(out=yo[ob][:, 0:512], in_=K.ps[:, bank, :], func=AF.Copy),
                              reads=[K.r_ps[bank]], writes=[r_yo[ob]])
                    else:
                        fw.op("dve", lambda e: e.tensor_copy(out=yo[ob][:, 512:1024], in_=K.ps[:, bank, :]),
                              reads=[K.r_ps[bank]], writes=[r_yo[ob]])
                t = g * 4 + j
                fw.dma("sp", c_o[ob], out_dram[t * 128:(t + 1) * 128, :], yo[ob][:], reads=[r_yo[ob]], writes=[K.r_out])
        fw.barrier(c_x + c_o)


VEC_COLS = {}


def _vec_layout():
    off = 0
    lay = {}
    for name, n in [("ffn1_norm", 16), ("mix_norm", 16), ("ffn2_norm", 16), ("final_norm", 8), ("fbias3", 1), ("mla_q_norm", 2),
                    ("mla_kv_norm", 1), ("gdn_conv", 96), ("gdn_dtb", 8), ("gdn_alog", 8), ("gdn_onorm", 128)]:
        lay[name] = (off, n)
        off += n
    return lay, off


def pack_vecs(inp):
    lay, nv = _vec_layout()
    v = np.zeros((128, nv), np.float32)

    def fm(a):
        a = np.asarray(a, np.float32).reshape(-1, 8, 128)
        return a.transpose(2, 0, 1).reshape(128, -1)
    for name in ["ffn1_norm", "mix_norm", "ffn2_norm", "final_norm"]:
        o, n = lay[name]
        v[:, o:o + n] = fm(inp[name])
    fb = np.asarray(inp["fox_f_bias"], np.float32)[0]
    for rep in range(3):
        v[rep * 32:rep * 32 + 8, lay["fbias3"][0]] = fb
    v[:, lay["mla_q_norm"][0]:lay["mla_q_norm"][0] + 2] = np.asarray(inp["mla_q_norm"], np.float32)[0].reshape(2, 128).T
    v[:, lay["mla_kv_norm"][0]] = np.asarray(inp["mla_kv_norm"], np.float32)[0]
    cw = np.asarray(inp["gdn_conv_w"], np.float32)[0].reshape(4, 24, 128)
    v[:, lay["gdn_conv"][0]:lay["gdn_conv"][0] + 96] = cw.transpose(2, 0, 1).reshape(128, 96)
    v[:, lay["gdn_dtb"][0]:lay["gdn_dtb"][0] + 8] = np.asarray(inp["gdn_dt_bias"], np.float32)[0][None, :]
    v[:, lay["gdn_alog"][0]:lay["gdn_alog"][0] + 8] = np.asarray(inp["gdn_a_log"], np.float32)[0][None, :]
    v[:, lay["gdn_onorm"][0]:lay["gdn_onorm"][0] + 128] = np.asarray(inp["gdn_out_norm"], np.float32)[0][None, :]
    return v


def host_consts(inp):
    c = {}
    selq = np.zeros((97, 8, 70), np.float32)
    selk = np.zeros((97, 8, 70), np.float32)
    for h in range(8):
        for p in range(3):
            selq[p * 32 + h, h, 64 + p] = 1.0
            selk[p * 32 + h, h, 67 + p] = -1.0
        selq[96, h, 67:70] = 1.0
        selk[96, h, 64:67] = 1.0
    c["selq"], c["selk"] = selq, selk
    s_idx = np.arange(128)[:, None]
    t_idx = np.arange(128)[None, :]
    c["mask_causal"] = np.where(s_idx <= t_idx, 0.0, NEG).astype(np.float32)
    c["mask_chunk"] = np.where((s_idx // 64) <= (t_idx // 64), 0.0, NEG).astype(np.float32)
    inv = 10000.0 ** (-np.arange(16, dtype=np.float32) / 16)
    ang = np.arange(SEQ, dtype=np.float32)[None, :] * inv[:, None]
    cos, sin = np.cos(ang).astype(np.float32), np.sin(ang).astype(np.float32)
    c["cos2"] = np.concatenate([cos, cos], 0)
    c["sin2s"] = np.concatenate([-sin, sin], 0)
    c["U"] = (s_idx <= t_idx).astype(np.float32)
    c["gmask1"] = np.where(t_idx >= s_idx, 0.0, NEG).astype(np.float32)
    c["gmask2"] = np.where(s_idx > t_idx, 0.0, BIG).astype(np.float32)
    c["gstrict"] = (s_idx > t_idx).astype(np.float32)
    wuq = np.asarray(inp["mla_w_uq"], np.float32)[0].reshape(256, 8, 96)
    c["mla_w_uqs"] = np.ascontiguousarray(np.concatenate([wuq[:, :, 0:64], wuq[:, :, 80:96], wuq[:, :, 64:80]], axis=2).reshape(256, 768))
    return c


CONST_SHAPES = {"selq": [97, 8, 70], "selk": [97, 8, 70], "mask_causal": [128, 128], "mask_chunk": [128, 128], "cos2": [32, SEQ],
                "sin2s": [32, SEQ], "mla_w_uqs": [256, 768], "U": [128, 128], "gmask1": [128, 128], "gmask2": [128, 128],
                "gstrict": [128, 128]}


W_SHAPES = {
    "ffn1_w13": [2, D, 2 * DFF], "ffn1_w2": [2, DFF, D], "ffn2_w13": [2, D, 2 * DFF], "ffn2_w2": [2, DFF, D],
    "attn_w_in": [1, D, 1960], "mla_w_uq": [1, 256, 768], "mla_w_ukv": [1, 128, 1024], "attn_w_out": [1, D, D],
    "gdn_w_in": [1, D, 4112], "gdn_w_out": [1, D, D],
}


def build(phases=("input", "ffn1_0", "final"), dbg=False):
    nc = bass.Bass("TRN2", target_bir_lowering=False)
    K = Ctx()
    K.nc = nc
    import os
    K.dbg_pairs = int(os.environ.get("DBG_PAIRS", "4"))
    K.dbg_heads = int(os.environ.get("DBG_HEADS", "2"))
    K.dbg_skip = os.environ.get("DBG_SKIP", "").split(",")
    lay, nv = _vec_layout()
    x = nc.dram_tensor("x", [NT, D], F32, kind="ExternalInput").ap()
    vecs_d = nc.dram_tensor("vecs", [128, nv], F32, kind="ExternalInput").ap()
    ident_d = nc.dram_tensor("identf", [128, 128], F32, kind="ExternalInput").ap()
    W = {k: nc.dram_tensor(k, shp, F32, kind="ExternalInput").ap() for k, shp in W_SHAPES.items()}
    C = {k: nc.dram_tensor(k, shp, F32, kind="ExternalInput").ap() for k, shp in CONST_SHAPES.items()}
    out = nc.dram_tensor("out", [NT, D], F32, kind="ExternalOutput").ap()
    XT = nc.dram_tensor("XT", [8, 128, NT], F32, kind="Internal").ap()
    HT = nc.dram_tensor("HT", [8, 128, NT], BF16, kind="Internal").ap()
    AT = nc.dram_tensor("AT", [8, 128, NT], BF16, kind="Internal").ap()
    S = {"QT": nc.dram_tensor("gQT", [8, 128, NT], BF16, kind="Internal").ap(),
         "KT": nc.dram_tensor("gKT", [8, 128, NT], BF16, kind="Internal").ap(),
         "Ktok": nc.dram_tensor("gKtok", [NT, D], BF16, kind="Internal").ap(),
         "Vtok": nc.dram_tensor("gVtok", [NT, D], BF16, kind="Internal").ap(),
         "Gtok": nc.dram_tensor("gGtok", [NT, D], BF16, kind="Internal").ap(),
         "BG": nc.dram_tensor("gBG", [NT, 16], F32, kind="Internal").ap()}
    with ExitStack() as es:
        fw = FW(nc, es)
        K.fw = fw
        K.ps = es.enter_context(nc.psum_tensor("ps", [128, 8, 512], F32))
        K.r_ps = [fw.res("ps%d" % i) for i in range(8)]
        for r_ in K.r_ps:
            r_.excl = True
        K._bank = 0

        def bank():
            b = K._bank
            K._bank = (b + 1) % 8
            return b
        K.bank = bank
        K.r_XT = [fw.res("XT%d" % i) for i in range(NT // 512)]
        K.r_out = fw.res("out")
        K.r_HT = [fw.res("HT%d" % i) for i in range(NT // 512)]
        K.r_AT = [fw.res("AT%d" % i) for i in range(2)]
        S["r_QK"] = [fw.res() for _ in range(2)]
        S["r_tok"] = [fw.res() for _ in range(2)]
        K.r_const = fw.res("const")
        K.identf = _tile(K, es, "identf_sb", [128, 128], F32)
        K.onesb = _tile(K, es, "onesb", [128, 128], BF16)
        K.epsb = _tile(K, es, "epsb", [128, 1], F32)
        K.oneb = _tile(K, es, "oneb", [128, 1], F32)
        K.onesf = _tile(K, es, "onesf", [128, 128], F32)
        K.identb = _tile(K, es, "identb", [128, 128], BF16)
        K.vecs = _tile(K, es, "vecs_sb", [128, nv], F32)
        c0 = fw.chan()
        fw.dma("sp", c0, K.identf[:], ident_d, writes=[K.r_const])
        fw.dma("sp", c0, K.vecs[:], vecs_d, writes=[K.r_const])
        fw.op("dve", lambda e: e.memset(K.onesb[:], 1.0), writes=[K.r_const])
        fw.op("dve", lambda e: e.memset(K.epsb[:], EPS), writes=[K.r_const])
        fw.op("dve", lambda e: e.memset(K.oneb[:], 1.0), writes=[K.r_const])
        fw.op("dve", lambda e: e.memset(K.onesf[:], 1.0), writes=[K.r_const])
        fw.op("dve", lambda e: e.tensor_copy(out=K.identb[:], in_=K.identf[:]), reads=[K.r_const], writes=[K.r_const])
        fw.barrier([c0])

        def vcol(name, layer=0, n=8):
            o, _ = lay[name]
            return K.vecs[:, o + layer * n:o + (layer + 1) * n]

        for ph in phases:
            if ph == "input":
                phase_input(K, x, XT)
            elif ph.startswith("ffn"):
                which, layer = ph[:4], int(ph[5:])
                phase_ffn(K, XT, vcol(which + "_norm", layer), W[which + "_w13"][layer], W[which + "_w2"][layer])
            elif ph.startswith("norm"):
                layer = int(ph[4:])
                phase_norm_to_ht(K, XT, vcol("mix_norm", layer), HT)
            elif ph == "fox":
                o = lay["fbias3"][0]
                phase_fox(K, HT, AT, W["attn_w_in"][0], K.vecs[:, o:o + 1], C)
            elif ph == "mla":
                oq, okv = lay["mla_q_norm"][0], lay["mla_kv_norm"][0]
                phase_mla(K, HT, AT, W["attn_w_in"][0], W["mla_w_uq"][0], C["mla_w_uqs"], W["mla_w_ukv"][0],
                          K.vecs[:, oq:oq + 2], K.vecs[:, okv:okv + 1], C)
            elif ph == "oproj0":
                phase_outproj(K, XT, AT, W["attn_w_out"][0])
            elif ph == "gdnp":
                oc, od, oa = lay["gdn_conv"][0], lay["gdn_dtb"][0], lay["gdn_alog"][0]
                phase_gdn_proj(K, HT, W["gdn_w_in"][0], K.vecs[:, oc:oc + 96], K.vecs[:, od:od + 8], K.vecs[:, oa:oa + 8], S)
            elif ph == "gdnc":
                oo = lay["gdn_onorm"][0]
                phase_gdn_core(K, S, AT, K.vecs[:, oo:oo + 128], C)
            elif ph == "oproj1":
                phase_outproj(K, XT, AT, W["gdn_w_out"][0])
            elif ph == "final":
                phase_final(K, XT, vcol("final_norm"), out)
        fw.finish("sp", [K.r_out])
    return nc


ALL_PHASES = ("input", "ffn1_0", "norm0", "fox", "mla", "oproj0", "ffn2_0", "ffn1_1", "norm1", "gdnp", "gdnc", "oproj1", "ffn2_1", "final")


def make_in_maps(inputs, n_cores=N_CORES):
    x = np.ascontiguousarray(np.asarray(inputs["x"], np.float32)).reshape(n_cores, NT, D)
    shared = {"vecs": pack_vecs(inputs), "identf": np.eye(128, dtype=np.float32)}
    shared.update(host_consts(inputs))
    for k in W_SHAPES:
        shared[k] = np.ascontiguousarray(np.asarray(inputs[k], np.float32))
    return [dict(shared, x=x[i]) for i in range(n_cores)]


def kernel(**inputs):
    nc = build(ALL_PHASES)
    in_maps = make_in_maps(inputs)
    res = run_bass_kernel_spmd(nc, in_maps, core_ids=list(range(N_CORES)))
    out = np.stack([np.asarray(r["out"]) for r in res.results], axis=0)
    return out.reshape(16, SEQ, D).astype(np.float32)


def phase_norm_to_ht(K, XT, gain_col, HT):
    fw = K.fw
    with ExitStack() as es:
        xg = [_tile(K, es, "nxg%d" % i, [128, 8, 512], F32) for i in range(2)]
        hb = [_tile(K, es, "nhb%d" % i, [128, 8, 512], BF16) for i in range(2)]
        sq = _tile(K, es, "nsq", [128, 8, 512], BF16)
        rstd = _tile(K, es, "nrstd", [128, 512], F32)
        tmp = _tile(K, es, "ntmp", [128, 512], F32)
        r_xg = [fw.res() for _ in range(2)]
        r_hb = [fw.res() for _ in range(2)]
        r_sq, r_rstd, r_tmp = [fw.res() for _ in range(3)]
        c_x = [fw.chan() for _ in range(2)]
        c_h = [fw.chan() for _ in range(2)]
        XTv = XT.rearrange("c p n -> p c n")
        HTv = HT.rearrange("c p n -> p c n")
        for g in range(NT // 512):
            b = g % 2
            fw.dma("sp", c_x[b], xg[b][:], XTv[:, :, g * 512:(g + 1) * 512], reads=[K.r_XT[g]], writes=[r_xg[b]])
            rms_rstd(K, xg[b], 8, 0, 512, sq, rstd[:, :], [r_xg[b]], [r_sq] * 8, r_rstd, tmp, r_tmp, D)
            for c in range(8):
                fw.op("dve", lambda e: e.scalar_tensor_tensor(out=hb[b][:, c, :], in0=xg[b][:, c, :], scalar=gain_col[:, c:c + 1], in1=rstd[:, :],
                                                               op0=ALU.mult, op1=ALU.mult),
                      reads=[r_xg[b], r_rstd, K.r_const], writes=[r_hb[b]])
            fw.dma("sp", c_h[b], HTv[:, :, g * 512:(g + 1) * 512], hb[b][:], reads=[r_hb[b]], writes=[K.r_HT[g]])
        fw.barrier(c_x + c_h)


def attention_head(K, A, qa, ka, KR, vlhs, M, obase, drow, maskT, reads_qkv, out_ap, out_res):
    fw = K.fw
    LOOK = 2
    for G in range(SEQ // 512):
        ob = A.obanks[A.oi % len(A.obanks)]
        A.oi += 1
        nkt = 4 * G + 4
        pend = []

        def score(i):
            r = i - 4 * G
            q0 = max(r, 0) * 128
            N = 512 - q0
            sb = A.sbanks[A.si % len(A.sbanks)]
            A.si += 1
            mm(K, K.ps[:, sb, 0:N], ka[0:KR, i * 128:(i + 1) * 128], qa[0:KR, G * 512 + q0:(G + 1) * 512], True, r < 0,
               reads_qkv, [K.r_ps[sb]])
            if r >= 0:
                mm(K, K.ps[:, sb, 0:128], K.identb[:, :], maskT, False, True, [K.r_const], [K.r_ps[sb]])
            pb = A.pi % len(A.pt)
            A.pi += 1
            fw.op("act", lambda e: e.activation(out=A.pt[pb][:, 0:N], in_=K.ps[:, sb, 0:N], func=AF.Exp),
                  reads=[K.r_ps[sb]], writes=[A.r_pt[pb]])
            pend.append((i, q0, N, pb))

        def pv():
            i, q0, N, pb = pend.pop(0)
            mm(K, K.ps[0:M, ob, q0:512], vlhs(i), A.pt[pb][:, 0:N], i == 0, i == nkt - 1, reads_qkv + [A.r_pt[pb]], [K.r_ps[ob]])

        for i in range(nkt):
            score(i)
            if len(pend) > LOOK:
                pv()
        while pend:
            pv()
        fw.op("dve", lambda e: e.reciprocal(out=A.rd[drow:drow + 1, :], in_=K.ps[drow:drow + 1, ob, :]), reads=[K.r_ps[ob]], writes=[A.r_rd])
        bb = A.bbanks[A.bi % len(A.bbanks)]
        A.bi += 1
        mm(K, K.ps[obase:obase + 64, bb, :], K.onesf[drow:drow + 1, 0:64], A.rd[drow:drow + 1, :], True, True, [A.r_rd, K.r_const], [K.r_ps[bb]])
        cb = A.ci % len(A.bcs)
        A.ci += 1
        fw.op("act", lambda e: e.activation(out=A.bcs[cb][obase:obase + 64, :], in_=K.ps[obase:obase + 64, bb, :], func=AF.Copy),
              reads=[K.r_ps[bb]], writes=[A.r_bcs[cb]])
        fw.op("dve", lambda e: e.tensor_tensor(out=out_ap(G), in0=K.ps[obase:obase + 64, ob, :], in1=A.bcs[cb][obase:obase + 64, :], op=ALU.mult),
              reads=[K.r_ps[ob], A.r_bcs[cb]], writes=[out_res(G)])


class AttnCtx:
    def __init__(self, K, es, tag):
        fw = K.fw
        self.pt = [_tile(K, es, "%spt%d" % (tag, i), [128, 512], BF16) for i in range(6)]
        self.r_pt = [fw.res() for _ in range(6)]
        self.rd = _tile(K, es, tag + "rd", [128, 512], F32)
        self.r_rd = fw.res()
        self.bcs = [_tile(K, es, "%sbcs%d" % (tag, i), [128, 512], F32) for i in range(2)]
        self.r_bcs = [fw.res() for _ in range(2)]
        self.sbanks, self.obanks, self.bbanks = [0, 1, 2, 3, 4], [5, 6], [7]
        self.si = self.oi = self.bi = self.pi = self.ci = 0


NEG = -30000.0
FOX_SCALE = 0.125
MLA_SCALE = 96 ** -0.5


def phase_fox(K, HT, AT, w_in, nfb_col, C):
    fw = K.fw
    with ExitStack() as es:
        A = AttnCtx(K, es, "fx")
        wp = [_tile(K, es, "fxwp%d" % i, [128, 8, 384], BF16) for i in range(2)]
        ht = [_tile(K, es, "fxht%d" % i, [128, 8, 512], BF16) for i in range(2)]
        qa = [_tile(K, es, "fxqa%d" % i, [128, 2, SEQ], BF16) for i in range(2)]
        ka = [_tile(K, es, "fxka%d" % i, [128, 2, SEQ], BF16) for i in range(2)]
        VE = [_tile(K, es, "fxVE%d" % i, [128, 16, 65], BF16) for i in range(2)]
        VO = [_tile(K, es, "fxVO%d" % i, [128, 16, 128], BF16) for i in range(2)]
        ao = [_tile(K, es, "fxao%d" % i, [128, SEQ], BF16) for i in range(2)]
        wf = _tile(K, es, "fxwf", [128, 8, 72], BF16)
        selq = _tile(K, es, "fxselq", [128, 8, 70], BF16)
        selk = _tile(K, es, "fxselk", [128, 8, 70], BF16)
        maskT = _tile(K, es, "fxmask", [128, 128], BF16)
        Fp = _tile(K, es, "fxFp", [128, SEQ], BF16)
        Ff = [_tile(K, es, "fxFf%d" % i, [128, 512], F32) for i in range(2)]
        sp = _tile(K, es, "fxsp", [128, 512], F32)
        ee = _tile(K, es, "fxee", [128, 512], F32)
        HI = _tile(K, es, "fxHI", [128, 512], BF16)
        MID = _tile(K, es, "fxMID", [128, 512], BF16)
        nfb = _tile(K, es, "fxnfb", [128, 1], F32)
        r_wp = [fw.res() for _ in range(2)]
        r_ht = [fw.res() for _ in range(2)]
        r_q = [[fw.res() for _ in range(2)] for _ in range(2)]
        r_k = [[fw.res() for _ in range(2)] for _ in range(2)]
        r_VE = [fw.res() for _ in range(2)]
        r_VO = [fw.res() for _ in range(2)]
        r_ao = [fw.res() for _ in range(2)]
        r_c, r_Fp, r_sp, r_ee, r_HI, r_MID = [fw.res() for _ in range(6)]
        r_Ff = [fw.res() for _ in range(2)]
        c_wp = [fw.chan() for _ in range(2)]
        c_ht = [fw.chan() for _ in range(2)]
        c_ao = [fw.chan() for _ in range(2)]
        c_c = fw.chan()
        w_inv = w_in.rearrange("(c p) n -> p c n", p=128)
        HTv = HT.rearrange("c p n -> p c n")
        fw.op("dve", lambda e: e.memset(wf[:], 0.0), writes=[r_c])
        for rep in range(3):
            fw.dma("pool", c_c, wf[:, :, rep * 32:rep * 32 + 8], w_inv[:, :, 1536:1544], writes=[r_c])
        fw.dma("pool", c_c, selq[0:97, :, :], C["selq"], writes=[r_c])
        fw.dma("pool", c_c, selk[0:97, :, :], C["selk"], writes=[r_c])
        fw.dma("pool", c_c, maskT[:], C["mask_causal"], writes=[r_c])
        fw.op("dve", lambda e: e.tensor_scalar(out=nfb[:], in0=nfb_col, scalar1=-1.0, scalar2=None, op0=ALU.mult), reads=[K.r_const], writes=[r_c])
        fw.op("pool", lambda e: e.memset(Fp[:], 0.0), writes=[r_Fp])
        fw.op("pool", lambda e: e.memset(Fp[96:97, :], 1.0), writes=[r_Fp])
        for b in range(2):
            fw.op("pool", lambda e: e.memset(VE[b][:, :, 64:65], 1.0), writes=[r_VE[b]])
            fw.op("pool", lambda e: e.memset(VO[b][:, :, 0:64], 0.0), writes=[r_VO[b]])
            fw.op("pool", lambda e: e.memset(VO[b][:, :, 0:1], 1.0), writes=[r_VO[b]])
        nht = 0
        npair = 0

        def load_ht(g):
            nonlocal nht
            b = nht % 2
            nht += 1
            fw.dma("sp", c_ht[b], ht[b][:], HTv[:, :, g * 512:(g + 1) * 512], reads=[K.r_HT[g]], writes=[r_ht[b]])
            return b

        for s in range(2):
            for nt in range(4):
                hb = load_ht(s * 4 + nt)
                bank = K.bank()
                for kc in range(8):
                    mm(K, K.ps[0:72, bank, :], wf[:, kc, :], ht[hb][:, kc, :], kc == 0, kc == 7, [r_c, r_ht[hb]], [K.r_ps[bank]])
                fw.op("act", lambda e: e.activation(out=ee[0:72, :], in_=K.ps[0:72, bank, :], func=AF.Exp, scale=-1.0, bias=nfb[0:72, 0:1]),
                      reads=[K.r_ps[bank], r_c], writes=[r_ee])
                fw.op("act", lambda e: e.activation(out=sp[0:72, :], in_=ee[0:72, :], func=AF.Ln, bias=K.oneb[0:72, 0:1], scale=1.0),
                      reads=[r_ee, K.r_const], writes=[r_sp])
                fb = nt % 2
                init = 0.0 if nt == 0 else Ff[1 - fb][0:72, 511:512]
                fw.op("dve", lambda e: e.tensor_tensor_scan(out=Ff[fb][0:72, :], data0=K.onesf[0:72, 0:1].broadcast_to([72, 512]), data1=sp[0:72, :],
                                                            initial=init, op0=ALU.mult, op1=ALU.subtract),
                      reads=[r_sp, K.r_const, r_Ff[1 - fb]], writes=[r_Ff[fb]])
                sl = slice(nt * 512, (nt + 1) * 512)
                fw.op("dve", lambda e: e.tensor_copy(out=HI[0:72, :], in_=Ff[fb][0:72, :]), reads=[r_Ff[fb]], writes=[r_HI])
                fw.op("dve", lambda e: e.tensor_tensor(out=sp[0:72, :], in0=Ff[fb][0:72, :], in1=HI[0:72, :], op=ALU.subtract),
                      reads=[r_Ff[fb], r_HI], writes=[r_sp])
                fw.op("dve", lambda e: e.tensor_copy(out=MID[0:72, :], in_=sp[0:72, :]), reads=[r_sp], writes=[r_MID])
                fw.op("dve", lambda e: e.tensor_tensor(out=sp[0:72, :], in0=sp[0:72, :], in1=MID[0:72, :], op=ALU.subtract),
                      reads=[r_sp, r_MID], writes=[r_sp])
                fw.op("pool", lambda e: e.tensor_copy(out=Fp[0:8, sl], in_=HI[0:8, :]), reads=[r_HI], writes=[r_Fp])
                fw.op("pool", lambda e: e.tensor_copy(out=Fp[32:40, sl], in_=MID[32:40, :]), reads=[r_MID], writes=[r_Fp])
                fw.op("pool", lambda e: e.tensor_copy(out=Fp[64:72, sl], in_=sp[64:72, :]), reads=[r_sp], writes=[r_Fp])
            for j in range(K.dbg_pairs):
                pb = npair % 2
                npair += 1
                fw.dma("pool", c_wp[pb], wp[pb][:, :, 0:128], w_inv[:, :, j * 128:(j + 1) * 128], writes=[r_wp[pb]])
                fw.dma("pool", c_wp[pb], wp[pb][:, :, 128:256], w_inv[:, :, 512 + j * 128:512 + (j + 1) * 128], writes=[r_wp[pb]])
                fw.dma("pool", c_wp[pb], wp[pb][:, :, 256:384], w_inv[:, :, 1024 + j * 128:1024 + (j + 1) * 128], writes=[r_wp[pb]])
                for nt in range(4):
                    hb = load_ht(s * 4 + nt)
                    sl = slice(nt * 512, (nt + 1) * 512)
                    for hh in range(2):
                        h = 2 * j + hh
                        bq = K.bank()
                        for kc in range(8):
                            mm(K, K.ps[0:64, bq, :], wp[pb][:, kc, hh * 64:(hh + 1) * 64], ht[hb][:, kc, :], kc == 0, kc == 7,
                               [r_wp[pb], r_ht[hb]], [K.r_ps[bq]])
                        fw.op("act", lambda e: e.activation(out=qa[pb][0:64, hh, sl], in_=K.ps[0:64, bq, :], func=AF.Copy, scale=FOX_SCALE),
                              reads=[K.r_ps[bq]], writes=[r_q[pb][hh]])
                        bk = K.bank()
                        for kc in range(8):
                            mm(K, K.ps[0:64, bk, :], wp[pb][:, kc, 128 + hh * 64:128 + (hh + 1) * 64], ht[hb][:, kc, :], kc == 0, kc == 7,
                               [r_wp[pb], r_ht[hb]], [K.r_ps[bk]])
                        fw.op("dve", lambda e: e.tensor_copy(out=ka[pb][0:64, hh, sl], in_=K.ps[0:64, bk, :]),
                              reads=[K.r_ps[bk]], writes=[r_k[pb][hh]])
                        if "sel" in K.dbg_skip:
                            continue
                        ba = K.bank()
                        mm(K, K.ps[0:70, ba, :], selq[0:97, h, :], Fp[0:97, sl], True, True, [r_c, r_Fp], [K.r_ps[ba]])
                        fw.op("dve", lambda e: e.tensor_copy(out=qa[pb][64:70, hh, sl], in_=K.ps[64:70, ba, :]),
                              reads=[K.r_ps[ba]], writes=[r_q[pb][hh]])
                        ba = K.bank()
                        mm(K, K.ps[0:70, ba, :], selk[0:97, h, :], Fp[0:97, sl], True, True, [r_c, r_Fp], [K.r_ps[ba]])
                        fw.op("dve", lambda e: e.tensor_copy(out=ka[pb][64:70, hh, sl], in_=K.ps[64:70, ba, :]),
                              reads=[K.r_ps[ba]], writes=[r_k[pb][hh]])
                    if "v" in K.dbg_skip:
                        continue
                    bv = K.bank()
                    for it in range(4):
                        for kc in range(8):
                            mm(K, K.ps[:, bv, it * 128:(it + 1) * 128], ht[hb][:, kc, it * 128:(it + 1) * 128], wp[pb][:, kc, 256:384],
                               kc == 0, kc == 7, [r_wp[pb], r_ht[hb]], [K.r_ps[bv]])
                    src = K.ps[:, bv, :].rearrange("p (t c) -> p t c", t=4)
                    if "vevac" in K.dbg_skip:
                        continue
                    fw.op("act", lambda e: e.activation(out=VE[pb][:, nt * 4:nt * 4 + 4, 0:64], in_=src[:, :, 0:64], func=AF.Copy),
                          reads=[K.r_ps[bv]], writes=[r_VE[pb]])
                    if "vevac2" in K.dbg_skip:
                        continue
                    fw.op("act", lambda e: e.activation(out=VO[pb][:, nt * 4:nt * 4 + 4, 64:128], in_=src[:, :, 64:128], func=AF.Copy),
                          reads=[K.r_ps[bv]], writes=[r_VO[pb]])
                for hh in range(K.dbg_heads):
                    rq = [r_q[pb][hh], r_k[pb][hh], r_VE[pb] if hh == 0 else r_VO[pb]]
                    if hh == 0:
                        attention_head(K, A, qa[pb][:, 0, :], ka[pb][:, 0, :], 70, lambda i: VE[pb][:, i, 0:65], 65, 0, 64, maskT[:, :], rq + [r_c],
                                       lambda G: ao[pb][0:64, G * 512:(G + 1) * 512], lambda G: r_ao[pb])
                    else:
                        attention_head(K, A, qa[pb][:, 1, :], ka[pb][:, 1, :], 70, lambda i: VO[pb][:, i, 0:128], 128, 64, 0, maskT[:, :], rq + [r_c],
                                       lambda G: ao[pb][64:128, G * 512:(G + 1) * 512], lambda G: r_ao[pb])
                if "at" not in K.dbg_skip:
                    fw.dma("sp", c_ao[pb], AT[j, :, s * SEQ:(s + 1) * SEQ], ao[pb][:], reads=[r_ao[pb]], writes=[K.r_AT[s]])
        fw.barrier(c_wp + c_ht + c_ao + [c_c])


def phase_outproj(K, XT, AT, w_out):
    fw = K.fw
    with ExitStack() as es:
        wo = _tile(K, es, "opw", [128, 8, D], BF16)
        at = [_tile(K, es, "opat%d" % i, [128, 8, 512], BF16) for i in range(2)]
        xg = [_tile(K, es, "opxg%d" % i, [128, 8, 512], F32) for i in range(2)]
        r_wo = fw.res()
        r_at = [fw.res() for _ in range(2)]
        r_xg = [[fw.res() for _ in range(8)] for _ in range(2)]
        c_wo = fw.chan()
        c_at = [fw.chan() for _ in range(2)]
        c_x = [fw.chan() for _ in range(2)]
        c_xs = [fw.chan() for _ in range(2)]
        XTv = XT.rearrange("c p n -> p c n")
        ATv = AT.rearrange("c p n -> p c n")
        fw.dma("pool", c_wo, wo[:], w_out.rearrange("(c p) n -> p c n", p=128), writes=[r_wo])
        for g in range(NT // 512):
            b = g % 2
            fw.dma("sp", c_at[b], at[b][:], ATv[:, :, g * 512:(g + 1) * 512], reads=[K.r_AT[g // 4]], writes=[r_at[b]])
            fw.dma("sp", c_x[b], xg[b][:], XTv[:, :, g * 512:(g + 1) * 512], reads=[K.r_XT[g]], writes=r_xg[b])
            for dmc in range(8):
                by = K.bank()
                for c in range(8):
                    mm(K, K.ps[:, by, :], wo[:, c, dmc * 128:(dmc + 1) * 128], at[b][:, c, :], c == 0, c == 7, [r_wo, r_at[b]], [K.r_ps[by]])
                fw.op("dve", lambda e: e.tensor_tensor(out=xg[b][:, dmc, :], in0=K.ps[:, by, :], in1=xg[b][:, dmc, :], op=ALU.add),
                      reads=[K.r_ps[by], r_xg[b][dmc]], writes=[r_xg[b][dmc]])
            fw.dma("sp", c_xs[b], XTv[:, :, g * 512:(g + 1) * 512], xg[b][:], reads=r_xg[b], writes=[K.r_XT[g]])
        fw.barrier([c_wo] + c_at + c_x + c_xs)


def phase_mla(K, HT, AT, w_in, w_uq, w_uqs, w_ukv, qn_col, kvn_col, C):
    fw = K.fw
    with ExitStack() as es:
        A = AttnCtx(K, es, "ml")
        ht = [_tile(K, es, "mlht%d" % i, [128, 8, 512], BF16) for i in range(2)]
        qa = [_tile(K, es, "mlqa%d" % i, [128, 2, SEQ], BF16) for i in range(2)]
        ka = [_tile(K, es, "mlka%d" % i, [128, 2, SEQ], BF16) for i in range(2)]
        VE = [_tile(K, es, "mlVE%d" % i, [128, 16, 65], BF16) for i in range(2)]
        VO = [_tile(K, es, "mlVO%d" % i, [128, 16, 128], BF16) for i in range(2)]
        ao = [_tile(K, es, "mlao%d" % i, [128, SEQ], BF16) for i in range(2)]
        wlat = _tile(K, es, "mlwlat", [128, 8, 384], BF16)
        wkpe = _tile(K, es, "mlwkpe", [128, 8, 96], BF16)
        wkpes = _tile(K, es, "mlwkpes", [128, 8, 96], BF16)
        wuq = _tile(K, es, "mlwuq", [128, 2, 768], BF16)
        wuqs = _tile(K, es, "mlwuqs", [128, 2, 768], BF16)
        wukv = _tile(K, es, "mlwukv", [128, 1024], BF16)
        maskT = _tile(K, es, "mlmask", [128, 128], BF16)
        cos2 = _tile(K, es, "mlcos", [128, SEQ], F32)
        sin2 = _tile(K, es, "mlsin", [128, SEQ], F32)
        cqn = _tile(K, es, "mlcqn", [128, 2, SEQ], BF16)
        ckvn = _tile(K, es, "mlckvn", [128, SEQ], BF16)
        kpe = _tile(K, es, "mlkpe", [128, SEQ], BF16)
        cqf = _tile(K, es, "mlcqf", [128, 3, 512], F32)
        sq = _tile(K, es, "mlsq", [128, 3, 512], BF16)
        rstd = [_tile(K, es, "mlrstd%d" % i, [128, 512], F32) for i in range(2)]
        tmp = _tile(K, es, "mltmp", [128, 512], F32)
        t1 = [_tile(K, es, "mlt1%d" % i, [128, 512], F32) for i in range(2)]
        t2 = [_tile(K, es, "mlt2%d" % i, [128, 512], F32) for i in range(2)]
        r_ht = [fw.res() for _ in range(2)]
        r_q = [[fw.res() for _ in range(2)] for _ in range(2)]
        r_k = [[fw.res() for _ in range(2)] for _ in range(2)]
        r_VE = [fw.res() for _ in range(2)]
        r_VO = [fw.res() for _ in range(2)]
        r_ao = [fw.res() for _ in range(2)]
        r_c, r_cqn, r_ckvn, r_kpe, r_cqf, r_sq, r_tmp = [fw.res() for _ in range(7)]
        r_rstd = [fw.res() for _ in range(2)]
        r_t1 = [fw.res() for _ in range(2)]
        r_t2 = [fw.res() for _ in range(2)]
        c_ht = [fw.chan() for _ in range(2)]
        c_ao = [fw.chan() for _ in range(2)]
        c_c = fw.chan()
        w_inv = w_in.rearrange("(c p) n -> p c n", p=128)
        HTv = HT.rearrange("c p n -> p c n")
        fw.dma("pool", c_c, wlat[:], w_inv[:, :, 1544:1928], writes=[r_c])
        fw.op("dve", lambda e: e.memset(wkpe[:], 0.0), writes=[r_c])
        fw.op("dve", lambda e: e.memset(wkpes[:], 0.0), writes=[r_c])
        fw.dma("pool", c_c, wkpe[:, :, 64:96], w_inv[:, :, 1928:1960], writes=[r_c])
        fw.dma("pool", c_c, wkpes[:, :, 64:80], w_inv[:, :, 1944:1960], writes=[r_c])
        fw.dma("pool", c_c, wkpes[:, :, 80:96], w_inv[:, :, 1928:1944], writes=[r_c])
        fw.dma("pool", c_c, wuq[:], w_uq.rearrange("(c p) n -> p c n", p=128), writes=[r_c])
        fw.dma("pool", c_c, wuqs[:], w_uqs.rearrange("(c p) n -> p c n", p=128), writes=[r_c])
        fw.dma("pool", c_c, wukv[:], w_ukv, writes=[r_c])
        fw.dma("pool", c_c, maskT[:], C["mask_chunk"], writes=[r_c])
        fw.dma("sp", c_c, cos2[64:96, :], C["cos2"], writes=[r_c])
        fw.dma("sp", c_c, sin2[64:96, :], C["sin2s"], writes=[r_c])
        for b in range(2):
            fw.op("pool", lambda e: e.memset(VE[b][:, :, 64:65], 1.0), writes=[r_VE[b]])
            fw.op("pool", lambda e: e.memset(VO[b][:, :, 0:64], 0.0), writes=[r_VO[b]])
            fw.op("pool", lambda e: e.memset(VO[b][:, :, 0:1], 1.0), writes=[r_VO[b]])
        nht = 0
        npair = 0
        ti = 0
        for s in range(2):
            for nt in range(4):
                hb = nht % 2
                nht += 1
                g = s * 4 + nt
                sl = slice(nt * 512, (nt + 1) * 512)
                fw.dma("sp", c_ht[hb], ht[hb][:], HTv[:, :, g * 512:(g + 1) * 512], reads=[K.r_HT[g]], writes=[r_ht[hb]])
                for c in range(3):
                    bank = K.bank()
                    for kc in range(8):
                        mm(K, K.ps[:, bank, :], wlat[:, kc, c * 128:(c + 1) * 128], ht[hb][:, kc, :], kc == 0, kc == 7, [r_c, r_ht[hb]], [K.r_ps[bank]])
                    fw.op("act", lambda e: e.activation(out=cqf[:, c, :], in_=K.ps[:, bank, :], func=AF.Copy), reads=[K.r_ps[bank]], writes=[r_cqf])
                rms_rstd(K, cqf[:, 0:2, :], 2, 0, 512, sq[:, 0:2, :], rstd[0][:, :], [r_cqf], [r_sq] * 2, r_rstd[0], tmp, r_tmp, 256)
                rms_rstd(K, cqf[:, 2:3, :], 1, 0, 512, sq[:, 2:3, :], rstd[1][:, :], [r_cqf], [r_sq], r_rstd[1], tmp, r_tmp, 128)
                for c in range(2):
                    fw.op("dve", lambda e: e.scalar_tensor_tensor(out=cqn[:, c, sl], in0=cqf[:, c, :], scalar=qn_col[:, c:c + 1], in1=rstd[0][:, :],
                                                                   op0=ALU.mult, op1=ALU.mult), reads=[r_cqf, r_rstd[0], K.r_const], writes=[r_cqn])
                fw.op("dve", lambda e: e.scalar_tensor_tensor(out=ckvn[:, sl], in0=cqf[:, 2, :], scalar=kvn_col[:, 0:1], in1=rstd[1][:, :],
                                                               op0=ALU.mult, op1=ALU.mult), reads=[r_cqf, r_rstd[1], K.r_const], writes=[r_ckvn])
                bA, bB = K.bank(), K.bank()
                for kc in range(8):
                    mm(K, K.ps[0:96, bA, :], wkpe[:, kc, :], ht[hb][:, kc, :], kc == 0, kc == 7, [r_c, r_ht[hb]], [K.r_ps[bA]])
                for kc in range(8):
                    mm(K, K.ps[0:96, bB, :], wkpes[:, kc, :], ht[hb][:, kc, :], kc == 0, kc == 7, [r_c, r_ht[hb]], [K.r_ps[bB]])
                tb = ti % 2
                ti += 1
                fw.op("dve", lambda e: e.tensor_tensor(out=t1[tb][64:96, :], in0=K.ps[64:96, bA, :], in1=cos2[64:96, sl], op=ALU.mult),
                      reads=[K.r_ps[bA], r_c], writes=[r_t1[tb]])
                fw.op("dve", lambda e: e.tensor_tensor(out=t2[tb][64:96, :], in0=K.ps[64:96, bB, :], in1=sin2[64:96, sl], op=ALU.mult),
                      reads=[K.r_ps[bB], r_c], writes=[r_t2[tb]])
                fw.op("pool", lambda e: e.tensor_tensor(out=kpe[64:96, sl], in0=t1[tb][64:96, :], in1=t2[tb][64:96, :], op=ALU.add),
                      reads=[r_t1[tb], r_t2[tb]], writes=[r_kpe])
            for j in range(4):
                pb = npair % 2
                npair += 1
                for nt in range(4):
                    sl = slice(nt * 512, (nt + 1) * 512)
                    for hh in range(2):
                        h = 2 * j + hh
                        bA, bB = K.bank(), K.bank()
                        for c in range(2):
                            mm(K, K.ps[0:96, bA, :], wuq[:, c, h * 96:(h + 1) * 96], cqn[:, c, sl], c == 0, c == 1, [r_c, r_cqn], [K.r_ps[bA]])
                        for c in range(2):
                            mm(K, K.ps[0:96, bB, :], wuqs[:, c, h * 96:(h + 1) * 96], cqn[:, c, sl], c == 0, c == 1, [r_c, r_cqn], [K.r_ps[bB]])
                        fw.op("act", lambda e: e.activation(out=qa[pb][0:64, hh, sl], in_=K.ps[0:64, bA, :], func=AF.Copy, scale=MLA_SCALE),
                              reads=[K.r_ps[bA]], writes=[r_q[pb][hh]])
                        tb = ti % 2
                        ti += 1
                        fw.op("dve", lambda e: e.scalar_tensor_tensor(out=t1[tb][64:96, :], in0=K.ps[64:96, bA, :], scalar=MLA_SCALE, in1=cos2[64:96, sl],
                                                                       op0=ALU.mult, op1=ALU.mult), reads=[K.r_ps[bA], r_c], writes=[r_t1[tb]])
                        fw.op("dve", lambda e: e.scalar_tensor_tensor(out=t2[tb][64:96, :], in0=K.ps[64:96, bB, :], scalar=MLA_SCALE, in1=sin2[64:96, sl],
                                                                       op0=ALU.mult, op1=ALU.mult), reads=[K.r_ps[bB], r_c], writes=[r_t2[tb]])
                        fw.op("pool", lambda e: e.tensor_tensor(out=qa[pb][64:96, hh, sl], in0=t1[tb][64:96, :], in1=t2[tb][64:96, :], op=ALU.add),
                              reads=[r_t1[tb], r_t2[tb]], writes=[r_q[pb][hh]])
                        bk = K.bank()
                        mm(K, K.ps[0:64, bk, :], wukv[:, h * 128:h * 128 + 64], ckvn[:, sl], True, True, [r_c, r_ckvn], [K.r_ps[bk]])
                        fw.op("act", lambda e: e.activation(out=ka[pb][0:64, hh, sl], in_=K.ps[0:64, bk, :], func=AF.Copy),
                              reads=[K.r_ps[bk]], writes=[r_k[pb][hh]])
                        fw.op("pool", lambda e: e.tensor_copy(out=ka[pb][64:96, hh, sl], in_=kpe[64:96, sl]), reads=[r_kpe], writes=[r_k[pb][hh]])
                    bv = K.bank()
                    vcols = wukv[:, 2 * j * 128:(2 * j + 2) * 128].rearrange("p (h c) -> p h c", h=2)[:, :, 64:128]
                    for it in range(4):
                        i = nt * 4 + it
                        mm(K, K.ps[:, bv, it * 128:(it + 1) * 128], ckvn[:, i * 128:(i + 1) * 128], vcols, True, True, [r_c, r_ckvn], [K.r_ps[bv]])
                    src = K.ps[:, bv, :].rearrange("p (t c) -> p t c", t=4)
                    fw.op("act", lambda e: e.activation(out=VE[pb][:, nt * 4:nt * 4 + 4, 0:64], in_=src[:, :, 0:64], func=AF.Copy),
                          reads=[K.r_ps[bv]], writes=[r_VE[pb]])
                    fw.op("act", lambda e: e.activation(out=VO[pb][:, nt * 4:nt * 4 + 4, 64:128], in_=src[:, :, 64:128], func=AF.Copy),
                          reads=[K.r_ps[bv]], writes=[r_VO[pb]])
                for hh in range(2):
                    rq = [r_q[pb][hh], r_k[pb][hh], r_VE[pb] if hh == 0 else r_VO[pb], r_c]
                    if hh == 0:
                        attention_head(K, A, qa[pb][:, 0, :], ka[pb][:, 0, :], 96, lambda i: VE[pb][:, i, 0:65], 65, 0, 64, maskT[:, :], rq,
                                       lambda G: ao[pb][0:64, G * 512:(G + 1) * 512], lambda G: r_ao[pb])
                    else:
                        attention_head(K, A, qa[pb][:, 1, :], ka[pb][:, 1, :], 96, lambda i: VO[pb][:, i, 0:128], 128, 64, 0, maskT[:, :], rq,
                                       lambda G: ao[pb][64:128, G * 512:(G + 1) * 512], lambda G: r_ao[pb])
                fw.dma("sp", c_ao[pb], AT[4 + j, :, s * SEQ:(s + 1) * SEQ], ao[pb][:], reads=[r_ao[pb]], writes=[K.r_AT[s]])
        fw.barrier(c_ht + c_ao + [c_c])


GDN_QSCALE = 128 ** -0.5


def phase_gdn_proj(K, HT, w_in, convw_col, dtb_bc, alog_bc, S):
    fw = K.fw
    with ExitStack() as es:
        hts = _tile(K, es, "gpht", [128, 8, NT], BF16)
        wblk = [_tile(K, es, "gpw%d" % i, [128, 8, 512], BF16) for i in range(2)]
        wba = _tile(K, es, "gpwba", [128, 8, 16], BF16)
        dg = [_tile(K, es, "gpdg%d" % i, [128, 4, 4, 128], BF16) for i in range(2)]
        xc = [_tile(K, es, "gpxc%d" % i, [128, 515], BF16) for i in range(4)]
        qs = [_tile(K, es, "gpqs%d" % i, [128, 512], F32) for i in range(2)]
        sq = [_tile(K, es, "gpsq%d" % i, [128, 512], BF16) for i in range(2)]
        tmp = [_tile(K, es, "gptmp%d" % i, [128, 512], F32) for i in range(2)]
        rinv = [_tile(K, es, "gprinv%d" % i, [128, 512], F32) for i in range(2)]
        qn = [_tile(K, es, "gpqn%d" % i, [128, 512], BF16) for i in range(3)]
        tt = [_tile(K, es, "gptt%d" % i, [128, 4, 128], BF16) for i in range(2)]
        gt = [_tile(K, es, "gpgt%d" % i, [128, D], BF16) for i in range(2)]
        bgt = [_tile(K, es, "gpbg%d" % i, [128, 16], F32) for i in range(2)]
        e1 = _tile(K, es, "gpe1", [128, 8], F32)
        nexpa = _tile(K, es, "gpnexpa", [128, 8], F32)
        r_ht = [fw.res() for _ in range(NT // 512)]
        r_w = [fw.res() for _ in range(2)]
        r_dg = [fw.res() for _ in range(2)]
        r_xc = [fw.res() for _ in range(4)]
        r_qs = [fw.res() for _ in range(2)]
        r_sq = [fw.res() for _ in range(2)]
        r_tmp = [fw.res() for _ in range(2)]
        r_rinv = [fw.res() for _ in range(2)]
        r_qn = [fw.res() for _ in range(3)]
        r_tt = [fw.res() for _ in range(2)]
        r_gt = [fw.res() for _ in range(2)]
        r_bg = [fw.res() for _ in range(2)]
        r_c, r_e1 = fw.res(), fw.res()
        c_ht = fw.chan()
        c_w = [fw.chan() for _ in range(2)]
        c_c = fw.chan()
        c_qn = [fw.chan() for _ in range(3)]
        c_tt = [fw.chan() for _ in range(2)]
        c_gt = [fw.chan() for _ in range(2)]
        c_bg = [fw.chan() for _ in range(2)]
        HTv = HT.rearrange("c p n -> p c n")
        w_inv = w_in.rearrange("(c p) n -> p c n", p=128)
        for g in range(NT // 512):
            fw.dma("sp", c_ht, hts[:, :, g * 512:(g + 1) * 512], HTv[:, :, g * 512:(g + 1) * 512], reads=[K.r_HT[g]], writes=[r_ht[g]])
        fw.dma("pool", c_c, wba[:], w_inv[:, :, 3072:3088], writes=[r_c])
        fw.op("act", lambda e: e.activation(out=nexpa[:], in_=alog_bc, func=AF.Exp), reads=[K.r_const], writes=[r_c])
        fw.op("dve", lambda e: e.tensor_scalar(out=nexpa[:], in0=nexpa[:], scalar1=-1.0, scalar2=None, op0=ALU.mult), reads=[r_c], writes=[r_c])
        nw = 0
        ci = 0
        qi = 0
        ti = 0

        def load_w(col0):
            nonlocal nw
            b = nw % 2
            nw += 1
            fw.dma("pool", c_w[b], wblk[b][:], w_inv[:, :, col0:col0 + 512], writes=[r_w[b]])
            return b

        wq = {0: load_w(0)}
        for blk in range(6):
            wb = wq[blk]
            if blk + 1 < 6:
                wq[blk + 1] = load_w((blk + 1) * 512)
            db = blk % 2
            for tap in range(4):
                for fcl in range(4):
                    fc = blk * 4 + fcl
                    col = tap * 24 + fc
                    fw.op("pool", lambda e: e.tensor_scalar(out=dg[db][:, tap, fcl, :], in0=K.identf[:, :], scalar1=convw_col[:, col:col + 1], scalar2=None,
                                                            op0=ALU.mult), reads=[K.r_const], writes=[r_dg[db]])
            kind = "q" if blk < 2 else ("k" if blk < 4 else "v")
            for s in range(2):
                for nt in range(4):
                    g = s * 4 + nt
                    for fcl in range(4):
                        fc = blk * 4 + fcl
                        hd = fc % 8
                        bank = K.bank()
                        for kc in range(8):
                            mm(K, K.ps[:, bank, :], wblk[wb][:, kc, fcl * 128:(fcl + 1) * 128], hts[:, kc, g * 512:(g + 1) * 512], kc == 0, kc == 7,
                               [r_w[wb], r_ht[g]], [K.r_ps[bank]])
                        x = xc[fcl]
                        if nt == 0:
                            fw.op("pool", lambda e: e.memset(x[:, 0:3], 0.0), writes=[r_xc[fcl]])
                        else:
                            fw.op("pool", lambda e: e.tensor_copy(out=x[:, 0:3], in_=x[:, 512:515]), reads=[r_xc[fcl]], writes=[r_xc[fcl]])
                        fw.op("act", lambda e: e.activation(out=x[:, 3:515], in_=K.ps[:, bank, :], func=AF.Copy), reads=[K.r_ps[bank]], writes=[r_xc[fcl]])
                        cb = K.bank()
                        for tap in range(4):
                            mm(K, K.ps[:, cb, :], dg[db][:, tap, fcl, :], x[:, tap:tap + 512], tap == 0, tap == 3, [r_dg[db], r_xc[fcl]], [K.r_ps[cb]])
                        if kind == "v":
                            n_ = qi % 3
                            qi += 1
                            fw.op("act", lambda e: e.activation(out=qn[n_][:], in_=K.ps[:, cb, :], func=AF.Silu), reads=[K.r_ps[cb]], writes=[r_qn[n_]])
                        else:
                            a_ = ci % 2
                            ci += 1
                            fw.op("act", lambda e: e.activation(out=qs[a_][:], in_=K.ps[:, cb, :], func=AF.Silu), reads=[K.r_ps[cb]], writes=[r_qs[a_]])
                            fw.op("pool", lambda e: e.tensor_tensor(out=sq[a_][:], in0=qs[a_][:], in1=qs[a_][:], op=ALU.mult), reads=[r_qs[a_]], writes=[r_sq[a_]])
                            sb = K.bank()
                            mm(K, K.ps[:, sb, :], K.onesb[:, :], sq[a_][:], True, True, [r_sq[a_], K.r_const], [K.r_ps[sb]])
                            fw.op("act", lambda e: e.activation(out=tmp[a_][:], in_=K.ps[:, sb, :], func=AF.Sqrt, bias=K.epsb[:, 0:1], scale=1.0),
                                  reads=[K.r_ps[sb], K.r_const], writes=[r_tmp[a_]])
                            fw.op("dve", lambda e: e.reciprocal(out=rinv[a_][:], in_=tmp[a_][:]), reads=[r_tmp[a_]], writes=[r_rinv[a_]])
                            n_ = qi % 3
                            qi += 1
                            sc = GDN_QSCALE if kind == "q" else 1.0
                            fw.op("dve", lambda e: e.scalar_tensor_tensor(out=qn[n_][:], in0=qs[a_][:], scalar=sc, in1=rinv[a_][:], op0=ALU.mult, op1=ALU.mult),
                                  reads=[r_qs[a_], r_rinv[a_]], writes=[r_qn[n_]])
                        if kind in ("q", "k"):
                            dst = S["QT"] if kind == "q" else S["KT"]
                            fw.dma("sp", c_qn[n_], dst[hd, :, g * 512:(g + 1) * 512], qn[n_][:], reads=[r_qn[n_]], writes=[S["r_QK"][s]])
                        if kind in ("k", "v"):
                            t_ = ti % 2
                            ti += 1
                            tb = K.bank()
                            pbf = K.ps[:, tb, :].bitcast(BF16)
                            for it in range(4):
                                fw.op("pe", lambda e: e.transpose(out=pbf[:, it * 128:(it + 1) * 128], in_=qn[n_][:, it * 128:(it + 1) * 128], identity=K.identb[:]),
                                      reads=[r_qn[n_], K.r_const], writes=[K.r_ps[tb]])
                            fw.op("act", lambda e: e.activation(out=tt[t_][:], in_=pbf[:, 0:512].rearrange("p (t c) -> p t c", t=4), func=AF.Copy),
                                  reads=[K.r_ps[tb]], writes=[r_tt[t_]])
                            dst = S["Ktok"] if kind == "k" else S["Vtok"]
                            fw.dma("sp", c_tt[t_], dst[g * 512:(g + 1) * 512, hd * 128:(hd + 1) * 128].rearrange("(t p) c -> p t c", p=128), tt[t_][:],
                                   reads=[r_tt[t_]], writes=[S["r_tok"][s]])
        wg = [load_w(3088), load_w(3088 + 512)]
        gi = 0
        for g in range(NT // 512):
            s = g // 4
            for it in range(4):
                tok0 = g * 512 + it * 128
                b_ = gi % 2
                gi += 1
                for half in range(2):
                    bank = K.bank()
                    for kc in range(8):
                        mm(K, K.ps[:, bank, :], hts[:, kc, tok0:tok0 + 128], wblk[wg[half]][:, kc, :], kc == 0, kc == 7, [r_w[wg[half]], r_ht[g]], [K.r_ps[bank]])
                    fw.op("act", lambda e: e.activation(out=gt[b_][:, half * 512:(half + 1) * 512], in_=K.ps[:, bank, :], func=AF.Silu),
                          reads=[K.r_ps[bank]], writes=[r_gt[b_]])
                fw.dma("sp", c_gt[b_], S["Gtok"][tok0:tok0 + 128, :], gt[b_][:], reads=[r_gt[b_]], writes=[S["r_tok"][s]])
                bank = K.bank()
                for kc in range(8):
                    mm(K, K.ps[:, bank, 0:16], hts[:, kc, tok0:tok0 + 128], wba[:, kc, :], kc == 0, kc == 7, [r_c, r_ht[g]], [K.r_ps[bank]])
                fw.op("act", lambda e: e.activation(out=bgt[b_][:, 0:8], in_=K.ps[:, bank, 0:8], func=AF.Sigmoid), reads=[K.r_ps[bank]], writes=[r_bg[b_]])
                fw.op("dve", lambda e: e.tensor_tensor(out=e1[:], in0=K.ps[:, bank, 8:16], in1=dtb_bc, op=ALU.add), reads=[K.r_ps[bank], K.r_const], writes=[r_e1])
                fw.op("act", lambda e: e.activation(out=e1[:], in_=e1[:], func=AF.Exp), reads=[r_e1], writes=[r_e1])
                fw.op("act", lambda e: e.activation(out=e1[:], in_=e1[:], func=AF.Ln, bias=K.oneb[:, 0:1], scale=1.0), reads=[r_e1, K.r_const], writes=[r_e1])
                fw.op("dve", lambda e: e.tensor_tensor(out=bgt[b_][:, 8:16], in0=e1[:], in1=nexpa[:], op=ALU.mult), reads=[r_e1, r_c], writes=[r_bg[b_]])
                fw.dma("sp", c_bg[b_], S["BG"][tok0:tok0 + 128, :], bgt[b_][:], reads=[r_bg[b_]], writes=[S["r_tok"][s]])
        fw.barrier([c_ht, c_c] + c_w + c_qn + c_tt + c_gt + c_bg)


BIG = 30000.0


def phase_gdn_core(K, S, AT, onorm_bc, C):
    fw = K.fw
    with ExitStack() as es:
        def T(name, dt=F32, n=1, shape=(128, 8, 128)):
            return [_tile(K, es, "gc%s%d" % (name, i), list(shape), dt) for i in range(n)]
        qT, kT = T("q", BF16, 2), T("k", BF16, 2)
        ktok, vtok, gtok = T("kt", BF16, 2), T("vt", BF16, 2), T("gt", BF16, 2)
        bg = T("bg", F32, 2, (128, 16))
        U, M1, SM = T("U", F32, 1, (128, 128))[0], T("M1", F32, 1, (128, 128))[0], T("SM", F32, 1, (128, 128))[0]
        St, Sb = T("S")[0], T("Sb", BF16)[0]
        NGU, ET, Esb = T("NGU")[0], T("ET")[0], T("Esb")[0]
        Aa, Ab, Pp = T("A", F32, 2), T("AT", F32, 2), T("P", F32, 2)
        vb, kbd, usb, ot, sqo, og = T("vb")[0], T("kbd")[0], T("usb")[0], T("ot")[0], T("sqo")[0], T("og")[0]
        ktl, wT, inT, vnw, ogb = T("ktl", BF16)[0], T("wT", BF16)[0], T("inT", BF16)[0], T("vnw", BF16)[0], T("ogb", BF16)[0]
        att = T("att", BF16, 2)
        sc = {n: T(n, F32, 1, (128, 8))[0] for n in ("gam", "eg", "gl", "dl", "et", "beg", "nbeta", "ng", "ss", "rstd")}

        def RL(n):
            return [fw.res() for _ in range(n)]
        r_in, r_att = RL(2), RL(2)
        r_c, r_sc, r_S, r_Sb = fw.res(), fw.res(), fw.res(), fw.res()
        r_NGU = fw.res()
        r_ET, r_Esb = RL(2), RL(2)
        r_A, r_B, r_P = [RL(2) for _ in range(2)], [RL(2) for _ in range(2)], [RL(2) for _ in range(2)]
        r_vb, r_kbd, r_ktl = fw.res(), fw.res(), fw.res()
        r_usb, r_wT, r_inT, r_vnw, r_ot = RL(2), RL(2), RL(2), RL(2), RL(2)
        r_sq, r_og, r_ogb = fw.res(), fw.res(), fw.res()
        c_in = [fw.chan() for _ in range(2)]
        c_att = [fw.chan() for _ in range(2)]
        c_c = fw.chan()
        fw.dma("sp", c_c, U[:], C["U"], writes=[r_c])
        fw.dma("sp", c_c, M1[:], C["gmask1"], writes=[r_c])
        fw.dma("sp", c_c, SM[:], C["gstrict"], writes=[r_c])
        QTv = S["QT"].rearrange("h p n -> p h n")
        KTv = S["KT"].rearrange("h p n -> p h n")
        ATv = AT.rearrange("h p n -> p h n")

        def bc_h(ap8, lo, n=4):
            return ap8[:, lo:lo + n].unsqueeze(2).broadcast_to([128, n, 128])

        def bc_m(ap, n=4):
            return ap.unsqueeze(1).broadcast_to([128, n, 128])

        def stage(mmfn):
            banks = [K.bank(), K.bank()]
            for h in range(8):
                mmfn(h, K.ps[:, banks[h // 4], (h % 4) * 128:(h % 4 + 1) * 128], banks[h // 4])
            return banks

        def pv(bank):
            return K.ps[:, bank, :].rearrange("p (h c) -> p h c", h=4)

        def hs(hf):
            return slice(hf * 4, hf * 4 + 4)

        step = 0
        for s in range(2):
            fw.op("pool", lambda e: e.memset(St[:], 0.0), writes=[r_S])
            fw.op("pool", lambda e: e.memset(Sb[:], 0.0), writes=[r_Sb])
            for b in range(SEQ // 128):
                tok0 = s * SEQ + b * 128
                ib_ = step % 2
                step += 1
                rin = r_in[ib_]
                q_, k_, kt_, vt_, gt_, bg_ = qT[ib_], kT[ib_], ktok[ib_], vtok[ib_], gtok[ib_], bg[ib_]
                fw.dma("sp", c_in[ib_], q_[:], QTv[:, :, tok0:tok0 + 128], reads=[S["r_QK"][s]], writes=[rin])
                fw.dma("sp", c_in[ib_], k_[:], KTv[:, :, tok0:tok0 + 128], reads=[S["r_QK"][s]], writes=[rin])
                fw.dma("sp", c_in[ib_], kt_[:], S["Ktok"][tok0:tok0 + 128, :].rearrange("p (h c) -> p h c", h=8), reads=[S["r_tok"][s]], writes=[rin])
                fw.dma("sp", c_in[ib_], vt_[:], S["Vtok"][tok0:tok0 + 128, :].rearrange("p (h c) -> p h c", h=8), reads=[S["r_tok"][s]], writes=[rin])
                fw.dma("sp", c_in[ib_], gt_[:], S["Gtok"][tok0:tok0 + 128, :].rearrange("p (h c) -> p h c", h=8), reads=[S["r_tok"][s]], writes=[rin])
                fw.dma("sp", c_in[ib_], bg_[:], S["BG"][tok0:tok0 + 128, :], reads=[S["r_tok"][s]], writes=[rin])
                b1, b2 = K.bank(), K.bank()
                mm(K, K.ps[:, b1, 0:8], U[:, :], bg_[:, 8:16], True, True, [r_c, rin], [K.r_ps[b1]])
                mm(K, K.ps[:, b2, 0:8], K.onesf[:, :], bg_[:, 8:16], True, True, [K.r_const, rin], [K.r_ps[b2]])
                fw.op("dve", lambda e: e.tensor_copy(out=sc["gam"][:], in_=K.ps[:, b1, 0:8]), reads=[K.r_ps[b1]], writes=[r_sc])
                fw.op("act", lambda e: e.activation(out=sc["eg"][:], in_=K.ps[:, b1, 0:8], func=AF.Exp), reads=[K.r_ps[b1]], writes=[r_sc])
                fw.op("act", lambda e: e.activation(out=sc["gl"][:], in_=K.ps[:, b2, 0:8], func=AF.Exp), reads=[K.r_ps[b2]], writes=[r_sc])
                fw.op("dve", lambda e: e.tensor_tensor(out=sc["dl"][:], in0=K.ps[:, b2, 0:8], in1=sc["gam"][:], op=ALU.subtract),
                      reads=[K.r_ps[b2], r_sc], writes=[r_sc])
                fw.op("act", lambda e: e.activation(out=sc["et"][:], in_=sc["dl"][:], func=AF.Exp), reads=[r_sc], writes=[r_sc])
                fw.op("dve", lambda e: e.tensor_tensor(out=sc["beg"][:], in0=bg_[:, 0:8], in1=sc["eg"][:], op=ALU.mult), reads=[rin, r_sc], writes=[r_sc])
                fw.op("dve", lambda e: e.tensor_scalar(out=sc["nbeta"][:], in0=bg_[:, 0:8], scalar1=-1.0, scalar2=None, op0=ALU.mult), reads=[rin], writes=[r_sc])
                fw.op("dve", lambda e: e.tensor_scalar(out=sc["ng"][:], in0=bg_[:, 8:16], scalar1=-1.0, scalar2=None, op0=ALU.mult), reads=[rin], writes=[r_sc])
                fw.op("pool", lambda e: e.tensor_tensor(out=NGU[:], in0=bc_m(U[:, :], 8), in1=bc_h(sc["ng"], 0, 8), op=ALU.mult),
                      reads=[r_c, r_sc], writes=[r_NGU])

                def mm_p1(h, o, bk):
                    gb = bg_[:, 8 + h:9 + h].broadcast_to([128, 128])
                    mm(K, o, gb, U[:, :], True, False, [rin, r_c], [K.r_ps[bk]])
                    mm(K, o, NGU[:, h, :], K.onesf[:, :], False, False, [r_NGU, K.r_const], [K.r_ps[bk]])
                    mm(K, o, K.identf[:, :], M1[:, :], False, True, [K.r_const, r_c], [K.r_ps[bk]])
                bks = stage(mm_p1)
                for hf in range(2):
                    fw.op("act", lambda e: e.activation(out=ET[:, hs(hf), :], in_=pv(bks[hf]), func=AF.Exp), reads=[K.r_ps[bks[hf]]], writes=[r_ET[hf]])

                def mm_tr(h, o, bk):
                    fw.op("pe", lambda e: e.transpose(out=o, in_=ET[:, h, :], identity=K.identf[:]), reads=[r_ET[h // 4], K.r_const], writes=[K.r_ps[bk]])
                bks = stage(mm_tr)
                for hf in range(2):
                    fw.op("dve", lambda e: e.tensor_tensor(out=Esb[:, hs(hf), :], in0=pv(bks[hf]), in1=bc_m(SM[:, :]), op=ALU.mult),
                          reads=[K.r_ps[bks[hf]], r_c], writes=[r_Esb[hf]])
                    fw.op("pool", lambda e: e.tensor_tensor(out=Esb[:, hs(hf), :], in0=Esb[:, hs(hf), :], in1=bc_h(sc["nbeta"], hf * 4), op=ALU.mult),
                          reads=[r_Esb[hf], r_sc], writes=[r_Esb[hf]])

                def mm_kk(h, o, bk):
                    mm(K, o, k_[:, h, :], k_[:, h, :], True, True, [rin], [K.r_ps[bk]])
                bks = stage(mm_kk)
                ca = 0
                for hf in range(2):
                    fw.op("dve", lambda e: e.tensor_tensor(out=Aa[ca][:, hs(hf), :], in0=pv(bks[hf]), in1=Esb[:, hs(hf), :], op=ALU.mult),
                          reads=[K.r_ps[bks[hf]], r_Esb[hf]], writes=[r_A[ca][hf]])

                def mm_at(h, o, bk):
                    fw.op("pe", lambda e: e.transpose(out=o, in_=Aa[ca][:, h, :], identity=K.identf[:]), reads=[r_A[ca][h // 4], K.r_const], writes=[K.r_ps[bk]])
                bks = stage(mm_at)
                cb, cp = 0, 0
                for hf in range(2):
                    fw.op("act", lambda e: e.activation(out=Ab[cb][:, hs(hf), :], in_=pv(bks[hf]), func=AF.Copy), reads=[K.r_ps[bks[hf]]], writes=[r_B[cb][hf]])
                    fw.op("dve", lambda e: e.tensor_tensor(out=Pp[cp][:, hs(hf), :], in0=pv(bks[hf]), in1=bc_m(K.identf[:, :]), op=ALU.add),
                          reads=[K.r_ps[bks[hf]], K.r_const], writes=[r_P[cp][hf]])
                for lev in range(6):
                    na, nb_, np_ = 1 - ca, 1 - cb, 1 - cp

                    def mm_a2(h, o, bk):
                        mm(K, o, Ab[cb][:, h, :], Aa[ca][:, h, :], True, True, [r_B[cb][h // 4], r_A[ca][h // 4]], [K.r_ps[bk]])
                    bks = stage(mm_a2)
                    if lev < 5:
                        def mm_b2(h, o, bk):
                            mm(K, o, Aa[ca][:, h, :], Ab[cb][:, h, :], True, True, [r_B[cb][h // 4], r_A[ca][h // 4]], [K.r_ps[bk]])
                        bks2 = stage(mm_b2)
                    for hf in range(2):
                        fw.op("act", lambda e: e.activation(out=Aa[na][:, hs(hf), :], in_=pv(bks[hf]), func=AF.Copy), reads=[K.r_ps[bks[hf]]], writes=[r_A[na][hf]])
                    if lev < 5:
                        for hf in range(2):
                            fw.op("dve", lambda e: e.tensor_copy(out=Ab[nb_][:, hs(hf), :], in_=pv(bks2[hf])), reads=[K.r_ps[bks2[hf]]], writes=[r_B[nb_][hf]])

                    def mm_p(h, o, bk):
                        mm(K, o, Aa[na][:, h, :], Pp[cp][:, h, :], True, True, [r_A[na][h // 4], r_P[cp][h // 4]], [K.r_ps[bk]])
                    bks3 = stage(mm_p)
                    for hf in range(2):
                        fw.op("dve", lambda e: e.tensor_tensor(out=Pp[np_][:, hs(hf), :], in0=pv(bks3[hf]), in1=Pp[cp][:, hs(hf), :], op=ALU.add),
                              reads=[K.r_ps[bks3[hf]], r_P[cp][hf]], writes=[r_P[np_][hf]])
                    ca, cp = na, np_
                    if lev < 5:
                        cb = nb_
                fw.op("pool", lambda e: e.tensor_tensor(out=vb[:], in0=vt_[:], in1=bc_h(bg_, 0, 8), op=ALU.mult), reads=[rin], writes=[r_vb])
                fw.op("pool", lambda e: e.tensor_tensor(out=kbd[:], in0=kt_[:], in1=bc_h(sc["beg"], 0, 8), op=ALU.mult), reads=[rin, r_sc], writes=[r_kbd])
                fw.op("pool", lambda e: e.tensor_tensor(out=ktl[:], in0=kt_[:], in1=bc_h(sc["et"], 0, 8), op=ALU.mult), reads=[rin, r_sc], writes=[r_ktl])

                def mm_u(h, o, bk):
                    mm(K, o, Pp[cp][:, h, :], vb[:, h, :], True, True, [r_P[cp][h // 4], r_vb], [K.r_ps[bk]])
                bks = stage(mm_u)

                def mm_w(h, o, bk):
                    mm(K, o, kbd[:, h, :], Pp[cp][:, h, :], True, True, [r_P[cp][h // 4], r_kbd], [K.r_ps[bk]])
                bks2 = stage(mm_w)
                for hf in range(2):
                    fw.op("act", lambda e: e.activation(out=usb[:, hs(hf), :], in_=pv(bks[hf]), func=AF.Copy), reads=[K.r_ps[bks[hf]]], writes=[r_usb[hf]])
                for hf in range(2):
                    fw.op("act", lambda e: e.activation(out=wT[:, hs(hf), :], in_=pv(bks2[hf]), func=AF.Copy), reads=[K.r_ps[bks2[hf]]], writes=[r_wT[hf]])

                def mm_kq(h, o, bk):
                    mm(K, o, k_[:, h, :], q_[:, h, :], True, True, [rin], [K.r_ps[bk]])
                bks = stage(mm_kq)
                for hf in range(2):
                    fw.op("dve", lambda e: e.tensor_tensor(out=inT[:, hs(hf), :], in0=pv(bks[hf]), in1=ET[:, hs(hf), :], op=ALU.mult),
                          reads=[K.r_ps[bks[hf]], r_ET[hf]], writes=[r_inT[hf]])

                def mm_ws(h, o, bk):
                    mm(K, o, wT[:, h, :], Sb[:, h, :], True, True, [r_wT[h // 4], r_Sb], [K.r_ps[bk]])
                bks = stage(mm_ws)
                for hf in range(2):
                    fw.op("dve", lambda e: e.tensor_tensor(out=vnw[:, hs(hf), :], in0=usb[:, hs(hf), :], in1=pv(bks[hf]), op=ALU.subtract),
                          reads=[K.r_ps[bks[hf]], r_usb[hf]], writes=[r_vnw[hf]])

                def mm_qs(h, o, bk):
                    mm(K, o, q_[:, h, :], Sb[:, h, :], True, True, [rin, r_Sb], [K.r_ps[bk]])
                bks = stage(mm_qs)
                for hf in range(2):
                    fw.op("dve", lambda e: e.tensor_tensor(out=ot[:, hs(hf), :], in0=pv(bks[hf]), in1=bc_h(sc["eg"], hf * 4), op=ALU.mult),
                          reads=[K.r_ps[bks[hf]], r_sc], writes=[r_ot[hf]])

                def mm_iv(h, o, bk):
                    mm(K, o, inT[:, h, :], vnw[:, h, :], True, True, [r_inT[h // 4], r_vnw[h // 4]], [K.r_ps[bk]])
                bks = stage(mm_iv)
                for hf in range(2):
                    fw.op("dve", lambda e: e.tensor_tensor(out=ot[:, hs(hf), :], in0=pv(bks[hf]), in1=ot[:, hs(hf), :], op=ALU.add),
                          reads=[K.r_ps[bks[hf]], r_ot[hf]], writes=[r_ot[hf]])

                def mm_kv(h, o, bk):
                    mm(K, o, ktl[:, h, :], vnw[:, h, :], True, True, [r_ktl, r_vnw[h // 4]], [K.r_ps[bk]])
                bks = stage(mm_kv)
                fw.op("pool", lambda e: e.tensor_tensor(out=St[:], in0=St[:], in1=bc_h(sc["gl"], 0, 8), op=ALU.mult), reads=[r_S, r_sc], writes=[r_S])
                for hf in range(2):
                    fw.op("dve", lambda e: e.tensor_tensor(out=St[:, hs(hf), :], in0=pv(bks[hf]), in1=St[:, hs(hf), :], op=ALU.add),
                          reads=[K.r_ps[bks[hf]], r_S], writes=[r_S])
                fw.op("pool", lambda e: e.tensor_copy(out=Sb[:], in_=St[:]), reads=[r_S], writes=[r_Sb])
                fw.op("pool", lambda e: e.tensor_tensor(out=sqo[:], in0=ot[:], in1=ot[:], op=ALU.mult), reads=r_ot, writes=[r_sq])
                fw.op("dve", lambda e: e.tensor_reduce(out=sc["ss"][:], in_=sqo[:], axis=mybir.AxisListType.X, op=ALU.add), reads=[r_sq], writes=[r_sc])
                fw.op("act", lambda e: e.activation(out=sc["rstd"][:], in_=sc["ss"][:], func=AF.Sqrt, bias=K.epsb[:, 0:1], scale=1.0 / 128),
                      reads=[r_sc, K.r_const], writes=[r_sc])
                fw.op("dve", lambda e: e.reciprocal(out=sc["rstd"][:], in_=sc["rstd"][:]), reads=[r_sc], writes=[r_sc])
                fw.op("pool", lambda e: e.tensor_tensor(out=og[:], in0=ot[:], in1=bc_h(sc["rstd"], 0, 8), op=ALU.mult), reads=r_ot + [r_sc], writes=[r_og])
                fw.op("pool", lambda e: e.tensor_tensor(out=og[:], in0=og[:], in1=bc_m(onorm_bc, 8), op=ALU.mult), reads=[r_og, K.r_const], writes=[r_og])
                fw.op("pool", lambda e: e.tensor_tensor(out=ogb[:], in0=og[:], in1=gt_[:], op=ALU.mult), reads=[r_og, rin], writes=[r_ogb])
                tb = K.bank()
                ptb = K.ps[:, tb, :].bitcast(BF16)
                for h in range(8):
                    fw.op("pe", lambda e: e.transpose(out=ptb[:, h * 128:(h + 1) * 128], in_=ogb[:, h, :], identity=K.identb[:]),
                          reads=[r_ogb, K.r_const], writes=[K.r_ps[tb]])
                ob_ = ib_
                fw.op("act", lambda e: e.activation(out=att[ob_][:], in_=ptb[:, :].rearrange("p (h c) -> p h c", h=8), func=AF.Copy),
                      reads=[K.r_ps[tb]], writes=[r_att[ob_]])
                fw.dma("sp", c_att[ob_], ATv[:, :, tok0:tok0 + 128], att[ob_][:], reads=[r_att[ob_]], writes=[K.r_AT[s]])
        fw.barrier(c_in + c_att + [c_c])
```

```python
import numpy as np
from contextlib import ExitStack
import concourse.bass as bass
import concourse.mybir as mybir
from concourse.bass_utils import run_bass_kernel_spmd

F32 = mybir.dt.float32
BF16 = mybir.dt.bfloat16
AF = mybir.ActivationFunctionType
ALU = mybir.AluOpType

N_CORES = 8
D = 1024
NT = 4096
SEQ = 2048
DFF = 2816
EPS = 1e-6


class Res:
    __slots__ = ("name", "w", "r", "excl")

    def __init__(self, name):
        self.name = name
        self.excl = False
        self.w = None
        self.r = {}


class Chan:
    __slots__ = ("sem", "cnt", "key")

    def __init__(self, sem, key):
        self.sem = sem
        self.cnt = 0
        self.key = key


class FW:
    SELF_SYNC = {"pe": False, "act": True, "dve": True, "pool": True, "sp": False}

    def __init__(self, nc, es):
        self.nc = nc
        self.es = es
        self.engs = {"pe": nc.tensor, "act": nc.scalar, "dve": nc.vector, "pool": nc.gpsimd, "sp": nc.sync}
        self.sem = {k: es.enter_context(nc.semaphore("s_" + k)) for k in self.engs}
        self.cnt = {k: 0 for k in self.engs}
        self.seen = {k: {} for k in self.engs}
        self.nchan = 0
        self.nres = 0

    def res(self, name=None):
        self.nres += 1
        return Res(name or ("r%d" % self.nres))

    def chan(self):
        self.nchan += 1
        key = "c%d" % self.nchan
        return Chan(self.es.enter_context(self.nc.semaphore(key)), key)

    def _wait(self, e, reads, writes):
        need = {}

        def add(ev):
            if ev is None:
                return
            key, h, v = ev
            if key == e and not self.SELF_SYNC[e]:
                return
            if self.seen[e].get(key, 0) >= v:
                return
            if key not in need or need[key][1] < v:
                need[key] = (h, v)

        for r in reads:
            add(r.w)
        for r in writes:
            add(r.w)
            for ev in r.r.values():
                add(ev)
        for key, (h, v) in need.items():
            self.engs[e].wait_ge(h, v)
            self.seen[e][key] = v

    def _mark(self, ev, reads, writes):
        key = ev[0]
        for r in reads:
            old = r.r.get(key)
            if old is None or old[2] < ev[2]:
                r.r[key] = ev
        for r in writes:
            r.w = ev
            r.r = {}

    def op(self, e, fn, reads=(), writes=()):
        ex = [r for r in reads if r.excl]
        if ex:
            reads = [r for r in reads if not r.excl]
            writes = list(writes) + ex
        self._wait(e, reads, writes)
        ins = fn(self.engs[e])
        self.cnt[e] += 1
        ins.then_inc(self.sem[e], 1)
        self._mark((e, self.sem[e], self.cnt[e]), reads, writes)
        return ins

    def dma(self, e, chan, out, in_, reads=(), writes=()):
        self._wait(e, reads, writes)
        ins = self.engs[e].dma_start(out=out, in_=in_)
        chan.cnt += 16
        ins.then_inc(chan.sem, 16)
        self._mark((chan.key, chan.sem, chan.cnt), reads, writes)
        return ins

    def finish(self, e, resources):
        self._wait(e, resources, ())

    def barrier(self, chans=()):
        for e in self.engs:
            for k in self.engs:
                if k == e or self.cnt[k] == 0:
                    continue
                if self.seen[e].get(k, 0) < self.cnt[k]:
                    self.engs[e].wait_ge(self.sem[k], self.cnt[k])
                    self.seen[e][k] = self.cnt[k]
            for c in chans:
                if c.cnt and self.seen[e].get(c.key, 0) < c.cnt:
                    self.engs[e].wait_ge(c.sem, c.cnt)
                    self.seen[e][c.key] = c.cnt


class Ctx:
    pass


def _tile(K, es, name, shape, dt):
    K.uid = getattr(K, "uid", 0) + 1
    return es.enter_context(K.nc.sbuf_tensor("%s_%d" % (name, K.uid), shape, dt))


def mm(K, out, lhsT, rhs, start, stop, reads, writes):
    return K.fw.op("pe", lambda e: e.matmul(out, lhsT=lhsT, rhs=rhs, start=start, stop=stop), reads=reads, writes=writes)


def phase_input(K, x_dram, XT):
    fw, nc = K.fw, K.nc
    with ExitStack() as es:
        xin = [_tile(K, es, "xin%d" % i, [128, D], F32) for i in range(3)]
        xo = [_tile(K, es, "xo%d" % i, [128, 8, 512], F32) for i in range(2)]
        r_in = [fw.res() for _ in range(3)]
        r_o = [fw.res() for _ in range(2)]
        c_in = [fw.chan() for _ in range(3)]
        c_o = [fw.chan() for _ in range(2)]
        XTv = XT.rearrange("c p n -> p c n")
        ti = 0
        for g in range(NT // 512):
            ob = g % 2
            for j in range(4):
                t = g * 4 + j
                ib = ti % 3
                ti += 1
                fw.dma("sp", c_in[ib], xin[ib][:], x_dram[t * 128:(t + 1) * 128, :], writes=[r_in[ib]])
                for half in range(2):
                    bank = K.bank()
                    for q in range(4):
                        c = half * 4 + q
                        fw.op("pe", lambda e: e.transpose(out=K.ps[:, bank, q * 128:(q + 1) * 128],
                                                          in_=xin[ib][:, c * 128:(c + 1) * 128], identity=K.identf[:]),
                              reads=[r_in[ib], K.r_const], writes=[K.r_ps[bank]])
                    src = K.ps[:, bank, :].rearrange("p (c n) -> p c n", c=4)
                    dst = xo[ob][:, half * 4:half * 4 + 4, j * 128:(j + 1) * 128]
                    if half == 0:
                        fw.op("act", lambda e: e.activation(out=dst, in_=src, func=AF.Copy), reads=[K.r_ps[bank]], writes=[r_o[ob]])
                    else:
                        fw.op("dve", lambda e: e.tensor_copy(out=dst, in_=src), reads=[K.r_ps[bank]], writes=[r_o[ob]])
            fw.dma("sp", c_o[ob], XTv[:, :, g * 512:(g + 1) * 512], xo[ob][:], reads=[r_o[ob]], writes=[K.r_XT[g]])
        fw.barrier(c_in + c_o)


def rms_rstd(K, x_tile, nchunk, n0, n, sq_tile, rstd_out, r_x, r_sq, r_rstd, tmp, r_tmp, dim):
    fw = K.fw
    fw.op("act", lambda e: e.activation(out=sq_tile[:, 0:nchunk, 0:n], in_=x_tile[:, 0:nchunk, n0:n0 + n], func=AF.Square),
          reads=r_x, writes=r_sq)
    bank = K.bank()
    for c in range(nchunk):
        mm(K, K.ps[:, bank, 0:n], K.onesb[:, :], sq_tile[:, c, 0:n], c == 0, c == nchunk - 1, [r_sq[c], K.r_const], [K.r_ps[bank]])
    fw.op("act", lambda e: e.activation(out=tmp[:, 0:n], in_=K.ps[:, bank, 0:n], func=AF.Sqrt, bias=K.epsb[:, 0:1], scale=1.0 / dim),
          reads=[K.r_ps[bank], K.r_const], writes=[r_tmp])
    fw.op("dve", lambda e: e.reciprocal(out=rstd_out, in_=tmp[:, 0:n]), reads=[r_tmp], writes=[r_rstd])


def phase_ffn(K, XT, gain_col, w13, w2):
    fw, nc = K.fw, K.nc
    TG = 1024
    NKC = 8
    NFC = DFF // 128
    with ExitStack() as es:
        xg = _tile(K, es, "xg", [128, 8, TG], F32)
        hT = _tile(K, es, "hT", [128, 8, TG], BF16)
        gT = _tile(K, es, "gT", [128, NFC, TG], BF16)
        rstd = _tile(K, es, "rstd", [128, TG], F32)
        tmp = _tile(K, es, "tmp", [128, 512], F32)
        wa = [_tile(K, es, "wa%d" % i, [128, 8, 512], BF16) for i in range(2)]
        wb = [_tile(K, es, "wb%d" % i, [128, 8, 512], BF16) for i in range(2)]
        w2t = [_tile(K, es, "w2t%d" % i, [128, NFC, 256], BF16) for i in range(2)]
        sa = [_tile(K, es, "sa%d" % i, [128, 512], F32) for i in range(3)]
        r_rstd, r_tmp = [fw.res() for _ in range(2)]
        r_xgc = [fw.res() for _ in range(16)]
        r_hTc = [fw.res() for _ in range(16)]
        r_gTc = [fw.res() for _ in range(NFC * 2)]
        r_w13 = [fw.res() for _ in range(2)]
        r_w2 = [fw.res() for _ in range(2)]
        r_sa = [fw.res() for _ in range(3)]
        c_x, c_xs = fw.chan(), fw.chan()
        c_w13 = [fw.chan() for _ in range(2)]
        c_w2 = [fw.chan() for _ in range(2)]
        XTv = XT.rearrange("c p n -> p c n")
        fblocks = [(b * 512, 512) for b in range(5)] + [(2560, 256)]
        w13v = w13.rearrange("(c p) n -> p c n", p=128)
        w2v = w2.rearrange("(c p) n -> p c n", p=128)
        nw13 = 0
        nw2 = 0
        sai = 0

        def load_w13(bi):
            nonlocal nw13
            f0, fwid = fblocks[bi]
            b = nw13 % 2
            nw13 += 1
            fw.dma("pool", c_w13[b], wa[b][:, :, 0:fwid], w13v[:, :, f0:f0 + fwid], writes=[r_w13[b]])
            fw.dma("pool", c_w13[b], wb[b][:, :, 0:fwid], w13v[:, :, DFF + f0:DFF + f0 + fwid], writes=[r_w13[b]])
            return b

        def load_w2(di):
            nonlocal nw2
            b = nw2 % 2
            nw2 += 1
            fw.dma("pool", c_w2[b], w2t[b][:, :, :], w2v[:, :, di * 256:(di + 1) * 256], writes=[r_w2[b]])
            return b

        for g in range(NT // TG):
            t0 = g * TG
            fw.dma("sp", c_x, xg[:], XTv[:, :, t0:t0 + TG], reads=K.r_XT[2 * g:2 * g + 2], writes=r_xgc)
            wbuf = {0: load_w13(0), 1: load_w13(1)}
            for nt in range(TG // 512):
                rms_rstd(K, xg, 8, nt * 512, 512, gT, rstd[:, nt * 512:(nt + 1) * 512], [r_xgc[c * 2 + nt] for c in range(8)],
                         [r_gTc[c * 2] for c in range(8)], r_rstd, tmp, r_tmp, D)
                for c in range(8):
                    fw.op("dve", lambda e: e.scalar_tensor_tensor(out=hT[:, c, nt * 512:(nt + 1) * 512], in0=xg[:, c, nt * 512:(nt + 1) * 512],
                                                                   scalar=gain_col[:, c:c + 1], in1=rstd[:, nt * 512:(nt + 1) * 512],
                                                                   op0=ALU.mult, op1=ALU.mult),
                          reads=[r_xgc[c * 2 + nt], r_rstd, K.r_const], writes=[r_hTc[c * 2 + nt]])
            w2buf = {}
            for bi, (f0, fwid) in enumerate(fblocks):
                b = wbuf[bi]
                for fcl in range(fwid // 128):
                    fc = f0 // 128 + fcl
                    for nt in range(TG // 512):
                        ba, bb = K.bank(), K.bank()
                        for kc in range(NKC):
                            mm(K, K.ps[:, ba, :], wa[b][:, kc, fcl * 128:(fcl + 1) * 128], hT[:, kc, nt * 512:(nt + 1) * 512],
                               kc == 0, kc == NKC - 1, [r_w13[b], r_hTc[kc * 2 + nt]], [K.r_ps[ba]])
                        for kc in range(NKC):
                            mm(K, K.ps[:, bb, :], wb[b][:, kc, fcl * 128:(fcl + 1) * 128], hT[:, kc, nt * 512:(nt + 1) * 512],
                               kc == 0, kc == NKC - 1, [r_w13[b], r_hTc[kc * 2 + nt]], [K.r_ps[bb]])
                        s = sai % 3
                        sai += 1
                        fw.op("act", lambda e: e.activation(out=sa[s][:], in_=K.ps[:, ba, :], func=AF.Silu), reads=[K.r_ps[ba]], writes=[r_sa[s]])
                        rg = r_gTc[fc * 2 + nt]
                        fw.op("dve", lambda e: e.tensor_tensor(out=gT[:, fc, nt * 512:(nt + 1) * 512], in0=K.ps[:, bb, :], in1=sa[s][:], op=ALU.mult),
                              reads=[K.r_ps[bb], r_sa[s]], writes=[rg])
                if bi + 2 < len(fblocks):
                    wbuf[bi + 2] = load_w13(bi + 2)
                if bi == 3:
                    w2buf[0] = load_w2(0)
                if bi == 4:
                    w2buf[1] = load_w2(1)
            for di in range(4):
                b = w2buf[di]
                for dcl in range(2):
                    dmc = di * 2 + dcl
                    for nt in range(TG // 512):
                        by = K.bank()
                        for fc in range(NFC):
                            mm(K, K.ps[:, by, :], w2t[b][:, fc, dcl * 128:(dcl + 1) * 128], gT[:, fc, nt * 512:(nt + 1) * 512],
                               fc == 0, fc == NFC - 1, [r_w2[b], r_gTc[fc * 2 + nt]], [K.r_ps[by]])
                        fw.op("dve", lambda e: e.scalar_tensor_tensor(out=xg[:, dmc, nt * 512:(nt + 1) * 512], in0=K.ps[:, by, :], scalar=0.5,
                                                                       in1=xg[:, dmc, nt * 512:(nt + 1) * 512], op0=ALU.mult, op1=ALU.add),
                              reads=[K.r_ps[by], r_xgc[dmc * 2 + nt]], writes=[r_xgc[dmc * 2 + nt]])
                if di + 2 < 4:
                    w2buf[di + 2] = load_w2(di + 2)
            fw.dma("sp", c_xs, XTv[:, :, t0:t0 + TG], xg[:], reads=r_xgc, writes=K.r_XT[2 * g:2 * g + 2])
        fw.barrier([c_x, c_xs] + c_w13 + c_w2)


def phase_final(K, XT, gain_col, out_dram):
    fw, nc = K.fw, K.nc
    with ExitStack() as es:
        xg = [_tile(K, es, "fxg%d" % i, [128, 8, 512], F32) for i in range(2)]
        sq = _tile(K, es, "fsq", [128, 8, 512], BF16)
        yn = _tile(K, es, "fyn", [128, 8, 512], F32)
        rstd = _tile(K, es, "frstd", [128, 512], F32)
        tmp = _tile(K, es, "ftmp", [128, 512], F32)
        yo = [_tile(K, es, "fyo%d" % i, [128, D], F32) for i in range(2)]
        r_xg = [fw.res() for _ in range(2)]
        r_yo = [fw.res() for _ in range(2)]
        r_sq, r_yn, r_rstd, r_tmp = [fw.res() for _ in range(4)]
        c_x = [fw.chan() for _ in range(2)]
        c_o = [fw.chan() for _ in range(2)]
        XTv = XT.rearrange("c p n -> p c n")
        oi = 0
        for g in range(NT // 512):
            b = g % 2
            fw.dma("sp", c_x[b], xg[b][:], XTv[:, :, g * 512:(g + 1) * 512], reads=[K.r_XT[g]], writes=[r_xg[b]])
            rms_rstd(K, xg[b], 8, 0, 512, sq, rstd[:, :], [r_xg[b]], [r_sq] * 8, r_rstd, tmp, r_tmp, D)
            for c in range(8):
                fw.op("dve", lambda e: e.scalar_tensor_tensor(out=yn[:, c, :], in0=xg[b][:, c, :], scalar=gain_col[:, c:c + 1], in1=rstd[:, :],
                                                               op0=ALU.mult, op1=ALU.mult),
                      reads=[r_xg[b], r_rstd, K.r_const], writes=[r_yn])
            for j in range(4):
                ob = oi % 2
                oi += 1
                for half in range(2):
                    bank = K.bank()
                    for q in range(4):
                        c = half * 4 + q
                        fw.op("pe", lambda e: e.transpose(out=K.ps[:, bank, q * 128:(q + 1) * 128], in_=yn[:, c, j * 128:(j + 1) * 128],
                                                          identity=K.identf[:]),
                              reads=[r_yn, K.r_const], writes=[K.r_ps[bank]])
                    if half == 0:
                        fw.op("act", lambda e: e.activation(out=yo[ob][:, 0:512], in_=K.ps[:, bank, :], func=AF.Copy),
                              reads=[K.r_ps[bank]], writes=[r_yo[ob]])
                    else:
                        fw.op("dve", lambda e: e.tensor_copy(out=yo[ob][:, 512:1024], in_=K.ps[:, bank, :]),
                              reads=[K.r_ps[bank]], writes=[r_yo[ob]])
                t = g * 4 + j
                fw.dma("sp", c_o[ob], out_dram[t * 128:(t + 1) * 128, :], yo[ob][:], reads=[r_yo[ob]], writes=[K.r_out])
        fw.barrier(c_x + c_o)


VEC_COLS = {}


def _vec_layout():
    off = 0
    lay = {}
    for name, n in [("ffn1_norm", 16), ("mix_norm", 16), ("ffn2_norm", 16), ("final_norm", 8), ("fbias3", 1), ("mla_q_norm", 2),
                    ("mla_kv_norm", 1), ("gdn_conv", 96), ("gdn_dtb", 8), ("gdn_alog", 8), ("gdn_onorm", 128)]:
        lay[name] = (off, n)
        off += n
    return lay, off


def pack_vecs(inp):
    lay, nv = _vec_layout()
    v = np.zeros((128, nv), np.float32)

    def fm(a):
        a = np.asarray(a, np.float32).reshape(-1, 8, 128)
        return a.transpose(2, 0, 1).reshape(128, -1)
    for name in ["ffn1_norm", "mix_norm", "ffn2_norm", "final_norm"]:
        o, n = lay[name]
        v[:, o:o + n] = fm(inp[name])
    fb = np.asarray(inp["fox_f_bias"], np.float32)[0]
    for rep in range(3):
        v[rep * 32:rep * 32 + 8, lay["fbias3"][0]] = fb
    v[:, lay["mla_q_norm"][0]:lay["mla_q_norm"][0] + 2] = np.asarray(inp["mla_q_norm"], np.float32)[0].reshape(2, 128).T
    v[:, lay["mla_kv_norm"][0]] = np.asarray(inp["mla_kv_norm"], np.float32)[0]
    cw = np.asarray(inp["gdn_conv_w"], np.float32)[0].reshape(4, 24, 128)
    v[:, lay["gdn_conv"][0]:lay["gdn_conv"][0] + 96] = cw.transpose(2, 0, 1).reshape(128, 96)
    v[:, lay["gdn_dtb"][0]:lay["gdn_dtb"][0] + 8] = np.asarray(inp["gdn_dt_bias"], np.float32)[0][None, :]
    v[:, lay["gdn_alog"][0]:lay["gdn_alog"][0] + 8] = np.asarray(inp["gdn_a_log"], np.float32)[0][None, :]
    v[:, lay["gdn_onorm"][0]:lay["gdn_onorm"][0] + 128] = np.asarray(inp["gdn_out_norm"], np.float32)[0][None, :]
    return v


def host_consts(inp):
    c = {}
    selq = np.zeros((97, 8, 70), np.float32)
    selk = np.zeros((97, 8, 70), np.float32)
    for h in range(8):
        for p in range(3):
            selq[p * 32 + h, h, 64 + p] = 1.0
            selk[p * 32 + h, h, 67 + p] = -1.0
        selq[96, h, 67:70] = 1.0
        selk[96, h, 64:67] = 1.0
    c["selq"], c["selk"] = selq, selk
    s_idx = np.arange(128)[:, None]
    t_idx = np.arange(128)[None, :]
    c["mask_causal"] = np.where(s_idx <= t_idx, 0.0, NEG).astype(np.float32)
    c["mask_chunk"] = np.where((s_idx // 64) <= (t_idx // 64), 0.0, NEG).astype(np.float32)
    inv = 10000.0 ** (-np.arange(16, dtype=np.float32) / 16)
    ang = np.arange(SEQ, dtype=np.float32)[None, :] * inv[:, None]
    cos, sin = np.cos(ang).astype(np.float32), np.sin(ang).astype(np.float32)
    c["cos2"] = np.concatenate([cos, cos], 0)
    c["sin2s"] = np.concatenate([-sin, sin], 0)
    c["U"] = (s_idx <= t_idx).astype(np.float32)
    c["gmask1"] = np.where(t_idx >= s_idx, 0.0, NEG).astype(np.float32)
    c["gmask2"] = np.where(s_idx > t_idx, 0.0, BIG).astype(np.float32)
    c["gstrict"] = (s_idx > t_idx).astype(np.float32)
    wuq = np.asarray(inp["mla_w_uq"], np.float32)[0].reshape(256, 8, 96)
    c["mla_w_uqs"] = np.ascontiguousarray(np.concatenate([wuq[:, :, 0:64], wuq[:, :, 80:96], wuq[:, :, 64:80]], axis=2).reshape(256, 768))
    return c


CONST_SHAPES = {"selq": [97, 8, 70], "selk": [97, 8, 70], "mask_causal": [128, 128], "mask_chunk": [128, 128], "cos2": [32, SEQ],
                "sin2s": [32, SEQ], "mla_w_uqs": [256, 768], "U": [128, 128], "gmask1": [128, 128], "gmask2": [128, 128],
                "gstrict": [128, 128]}


W_SHAPES = {
    "ffn1_w13": [2, D, 2 * DFF], "ffn1_w2": [2, DFF, D], "ffn2_w13": [2, D, 2 * DFF], "ffn2_w2": [2, DFF, D],
    "attn_w_in": [1, D, 1960], "mla_w_uq": [1, 256, 768], "mla_w_ukv": [1, 128, 1024], "attn_w_out": [1, D, D],
    "gdn_w_in": [1, D, 4112], "gdn_w_out": [1, D, D],
}


def build(phases=("input", "ffn1_0", "final"), dbg=False):
    nc = bass.Bass("TRN2", target_bir_lowering=False)
    K = Ctx()
    K.nc = nc
    import os
    K.dbg_pairs = int(os.environ.get("DBG_PAIRS", "4"))
    K.dbg_heads = int(os.environ.get("DBG_HEADS", "2"))
    K.dbg_skip = os.environ.get("DBG_SKIP", "").split(",")
    lay, nv = _vec_layout()
    x = nc.dram_tensor("x", [NT, D], F32, kind="ExternalInput").ap()
    vecs_d = nc.dram_tensor("vecs", [128, nv], F32, kind="ExternalInput").ap()
    ident_d = nc.dram_tensor("identf", [128, 128], F32, kind="ExternalInput").ap()
    W = {k: nc.dram_tensor(k, shp, F32, kind="ExternalInput").ap() for k, shp in W_SHAPES.items()}
    C = {k: nc.dram_tensor(k, shp, F32, kind="ExternalInput").ap() for k, shp in CONST_SHAPES.items()}
    out = nc.dram_tensor("out", [NT, D], F32, kind="ExternalOutput").ap()
    XT = nc.dram_tensor("XT", [8, 128, NT], F32, kind="Internal").ap()
    HT = nc.dram_tensor("HT", [8, 128, NT], BF16, kind="Internal").ap()
    AT = nc.dram_tensor("AT", [8, 128, NT], BF16, kind="Internal").ap()
    S = {"QT": nc.dram_tensor("gQT", [8, 128, NT], BF16, kind="Internal").ap(),
         "KT": nc.dram_tensor("gKT", [8, 128, NT], BF16, kind="Internal").ap(),
         "Ktok": nc.dram_tensor("gKtok", [NT, D], BF16, kind="Internal").ap(),
         "Vtok": nc.dram_tensor("gVtok", [NT, D], BF16, kind="Internal").ap(),
         "Gtok": nc.dram_tensor("gGtok", [NT, D], BF16, kind="Internal").ap(),
         "BG": nc.dram_tensor("gBG", [NT, 16], F32, kind="Internal").ap()}
    with ExitStack() as es:
        fw = FW(nc, es)
        K.fw = fw
        K.ps = es.enter_context(nc.psum_tensor("ps", [128, 8, 512], F32))
        K.r_ps = [fw.res("ps%d" % i) for i in range(8)]
        for r_ in K.r_ps:
            r_.excl = True
        K._bank = 0

        def bank():
            b = K._bank
            K._bank = (b + 1) % 8
            return b
        K.bank = bank
        K.r_XT = [fw.res("XT%d" % i) for i in range(NT // 512)]
        K.r_out = fw.res("out")
        K.r_HT = [fw.res("HT%d" % i) for i in range(NT // 512)]
        K.r_AT = [fw.res("AT%d" % i) for i in range(2)]
        S["r_QK"] = [fw.res() for _ in range(2)]
        S["r_tok"] = [fw.res() for _ in range(2)]
        K.r_const = fw.res("const")
        K.identf = _tile(K, es, "identf_sb", [128, 128], F32)
        K.onesb = _tile(K, es, "onesb", [128, 128], BF16)
        K.epsb = _tile(K, es, "epsb", [128, 1], F32)
        K.oneb = _tile(K, es, "oneb", [128, 1], F32)
        K.onesf = _tile(K, es, "onesf", [128, 128], F32)
        K.identb = _tile(K, es, "identb", [128, 128], BF16)
        K.vecs = _tile(K, es, "vecs_sb", [128, nv], F32)
        c0 = fw.chan()
        fw.dma("sp", c0, K.identf[:], ident_d, writes=[K.r_const])
        fw.dma("sp", c0, K.vecs[:], vecs_d, writes=[K.r_const])
        fw.op("dve", lambda e: e.memset(K.onesb[:], 1.0), writes=[K.r_const])
        fw.op("dve", lambda e: e.memset(K.epsb[:], EPS), writes=[K.r_const])
        fw.op("dve", lambda e: e.memset(K.oneb[:], 1.0), writes=[K.r_const])
        fw.op("dve", lambda e: e.memset(K.onesf[:], 1.0), writes=[K.r_const])
        fw.op("dve", lambda e: e.tensor_copy(out=K.identb[:], in_=K.identf[:]), reads=[K.r_const], writes=[K.r_const])
        fw.barrier([c0])

        def vcol(name, layer=0, n=8):
            o, _ = lay[name]
            return K.vecs[:, o + layer * n:o + (layer + 1) * n]

        for ph in phases:
            if ph == "input":
                phase_input(K, x, XT)
            elif ph.startswith("ffn"):
                which, layer = ph[:4], int(ph[5:])
                phase_ffn(K, XT, vcol(which + "_norm", layer), W[which + "_w13"][layer], W[which + "_w2"][layer])
            elif ph.startswith("norm"):
                layer = int(ph[4:])
                phase_norm_to_ht(K, XT, vcol("mix_norm", layer), HT)
            elif ph == "fox":
                o = lay["fbias3"][0]
                phase_fox(K, HT, AT, W["attn_w_in"][0], K.vecs[:, o:o + 1], C)
            elif ph == "mla":
                oq, okv = lay["mla_q_norm"][0], lay["mla_kv_norm"][0]
                phase_mla(K, HT, AT, W["attn_w_in"][0], W["mla_w_uq"][0], C["mla_w_uqs"], W["mla_w_ukv"][0],
                          K.vecs[:, oq:oq + 2], K.vecs[:, okv:okv + 1], C)
            elif ph == "oproj0":
                phase_outproj(K, XT, AT, W["attn_w_out"][0])
            elif ph == "gdnp":
                oc, od, oa = lay["gdn_conv"][0], lay["gdn_dtb"][0], lay["gdn_alog"][0]
                phase_gdn_proj(K, HT, W["gdn_w_in"][0], K.vecs[:, oc:oc + 96], K.vecs[:, od:od + 8], K.vecs[:, oa:oa + 8], S)
            elif ph == "gdnc":
                oo = lay["gdn_onorm"][0]
                phase_gdn_core(K, S, AT, K.vecs[:, oo:oo + 128], C)
            elif ph == "oproj1":
                phase_outproj(K, XT, AT, W["gdn_w_out"][0])
            elif ph == "final":
                phase_final(K, XT, vcol("final_norm"), out)
        fw.finish("sp", [K.r_out])
    return nc


ALL_PHASES = ("input", "ffn1_0", "norm0", "fox", "mla", "oproj0", "ffn2_0", "ffn1_1", "norm1", "gdnp", "gdnc", "oproj1", "ffn2_1", "final")


def make_in_maps(inputs, n_cores=N_CORES):
    x = np.ascontiguousarray(np.asarray(inputs["x"], np.float32)).reshape(n_cores, NT, D)
    shared = {"vecs": pack_vecs(inputs), "identf": np.eye(128, dtype=np.float32)}
    shared.update(host_consts(inputs))
    for k in W_SHAPES:
        shared[k] = np.ascontiguousarray(np.asarray(inputs[k], np.float32))
    return [dict(shared, x=x[i]) for i in range(n_cores)]


def kernel(**inputs):
    nc = build(ALL_PHASES)
    in_maps = make_in_maps(inputs)
    res = run_bass_kernel_spmd(nc, in_maps, core_ids=list(range(N_CORES)))
    out = np.stack([np.asarray(r["out"]) for r in res.results], axis=0)
    return out.reshape(16, SEQ, D).astype(np.float32)


def phase_norm_to_ht(K, XT, gain_col, HT):
    fw = K.fw
    with ExitStack() as es:
        xg = [_tile(K, es, "nxg%d" % i, [128, 8, 512], F32) for i in range(2)]
        hb = [_tile(K, es, "nhb%d" % i, [128, 8, 512], BF16) for i in range(2)]
        sq = _tile(K, es, "nsq", [128, 8, 512], BF16)
        rstd = _tile(K, es, "nrstd", [128, 512], F32)
        tmp = _tile(K, es, "ntmp", [128, 512], F32)
        r_xg = [fw.res() for _ in range(2)]
        r_hb = [fw.res() for _ in range(2)]
        r_sq, r_rstd, r_tmp = [fw.res() for _ in range(3)]
        c_x = [fw.chan() for _ in range(2)]
        c_h = [fw.chan() for _ in range(2)]
        XTv = XT.rearrange("c p n -> p c n")
        HTv = HT.rearrange("c p n -> p c n")
        for g in range(NT // 512):
            b = g % 2
            fw.dma("sp", c_x[b], xg[b][:], XTv[:, :, g * 512:(g + 1) * 512], reads=[K.r_XT[g]], writes=[r_xg[b]])
            rms_rstd(K, xg[b], 8, 0, 512, sq, rstd[:, :], [r_xg[b]], [r_sq] * 8, r_rstd, tmp, r_tmp, D)
            for c in range(8):
                fw.op("dve", lambda e: e.scalar_tensor_tensor(out=hb[b][:, c, :], in0=xg[b][:, c, :], scalar=gain_col[:, c:c + 1], in1=rstd[:, :],
                                                               op0=ALU.mult, op1=ALU.mult),
                      reads=[r_xg[b], r_rstd, K.r_const], writes=[r_hb[b]])
            fw.dma("sp", c_h[b], HTv[:, :, g * 512:(g + 1) * 512], hb[b][:], reads=[r_hb[b]], writes=[K.r_HT[g]])
        fw.barrier(c_x + c_h)


def attention_head(K, A, qa, ka, KR, vlhs, M, obase, drow, maskT, reads_qkv, out_ap, out_res):
    fw = K.fw
    LOOK = 2
    for G in range(SEQ // 512):
        ob = A.obanks[A.oi % len(A.obanks)]
        A.oi += 1
        nkt = 4 * G + 4
        pend = []

        def score(i):
            r = i - 4 * G
            q0 = max(r, 0) * 128
            N = 512 - q0
            sb = A.sbanks[A.si % len(A.sbanks)]
            A.si += 1
            mm(K, K.ps[:, sb, 0:N], ka[0:KR, i * 128:(i + 1) * 128], qa[0:KR, G * 512 + q0:(G + 1) * 512], True, r < 0,
               reads_qkv, [K.r_ps[sb]])
            if r >= 0:
                mm(K, K.ps[:, sb, 0:128], K.identb[:, :], maskT, False, True, [K.r_const], [K.r_ps[sb]])
            pb = A.pi % len(A.pt)
            A.pi += 1
            fw.op("act", lambda e: e.activation(out=A.pt[pb][:, 0:N], in_=K.ps[:, sb, 0:N], func=AF.Exp),
                  reads=[K.r_ps[sb]], writes=[A.r_pt[pb]])
            pend.append((i, q0, N, pb))

        def pv():
            i, q0, N, pb = pend.pop(0)
            mm(K, K.ps[0:M, ob, q0:512], vlhs(i), A.pt[pb][:, 0:N], i == 0, i == nkt - 1, reads_qkv + [A.r_pt[pb]], [K.r_ps[ob]])

        for i in range(nkt):
            score(i)
            if len(pend) > LOOK:
                pv()
        while pend:
            pv()
        fw.op("dve", lambda e: e.reciprocal(out=A.rd[drow:drow + 1, :], in_=K.ps[drow:drow + 1, ob, :]), reads=[K.r_ps[ob]], writes=[A.r_rd])
        bb = A.bbanks[A.bi % len(A.bbanks)]
        A.bi += 1
        mm(K, K.ps[obase:obase + 64, bb, :], K.onesf[drow:drow + 1, 0:64], A.rd[drow:drow + 1, :], True, True, [A.r_rd, K.r_const], [K.r_ps[bb]])
        cb = A.ci % len(A.bcs)
        A.ci += 1
        fw.op("act", lambda e: e.activation(out=A.bcs[cb][obase:obase + 64, :], in_=K.ps[obase:obase + 64, bb, :], func=AF.Copy),
              reads=[K.r_ps[bb]], writes=[A.r_bcs[cb]])
        fw.op("dve", lambda e: e.tensor_tensor(out=out_ap(G), in0=K.ps[obase:obase + 64, ob, :], in1=A.bcs[cb][obase:obase + 64, :], op=ALU.mult),
              reads=[K.r_ps[ob], A.r_bcs[cb]], writes=[out_res(G)])


class AttnCtx:
    def __init__(self, K, es, tag):
        fw = K.fw
        self.pt = [_tile(K, es, "%spt%d" % (tag, i), [128, 512], BF16) for i in range(6)]
        self.r_pt = [fw.res() for _ in range(6)]
        self.rd = _tile(K, es, tag + "rd", [128, 512], F32)
        self.r_rd = fw.res()
        self.bcs = [_tile(K, es, "%sbcs%d" % (tag, i), [128, 512], F32) for i in range(2)]
        self.r_bcs = [fw.res() for _ in range(2)]
        self.sbanks, self.obanks, self.bbanks = [0, 1, 2, 3, 4], [5, 6], [7]
        self.si = self.oi = self.bi = self.pi = self.ci = 0


NEG = -30000.0
FOX_SCALE = 0.125
MLA_SCALE = 96 ** -0.5


def phase_fox(K, HT, AT, w_in, nfb_col, C):
    fw = K.fw
    with ExitStack() as es:
        A = AttnCtx(K, es, "fx")
        wp = [_tile(K, es, "fxwp%d" % i, [128, 8, 384], BF16) for i in range(2)]
        ht = [_tile(K, es, "fxht%d" % i, [128, 8, 512], BF16) for i in range(2)]
        qa = [_tile(K, es, "fxqa%d" % i, [128, 2, SEQ], BF16) for i in range(2)]
        ka = [_tile(K, es, "fxka%d" % i, [128, 2, SEQ], BF16) for i in range(2)]
        VE = [_tile(K, es, "fxVE%d" % i, [128, 16, 65], BF16) for i in range(2)]
        VO = [_tile(K, es, "fxVO%d" % i, [128, 16, 128], BF16) for i in range(2)]
        ao = [_tile(K, es, "fxao%d" % i, [128, SEQ], BF16) for i in range(2)]
        wf = _tile(K, es, "fxwf", [128, 8, 72], BF16)
        selq = _tile(K, es, "fxselq", [128, 8, 70], BF16)
        selk = _tile(K, es, "fxselk", [128, 8, 70], BF16)
        maskT = _tile(K, es, "fxmask", [128, 128], BF16)
        Fp = _tile(K, es, "fxFp", [128, SEQ], BF16)
        Ff = [_tile(K, es, "fxFf%d" % i, [128, 512], F32) for i in range(2)]
        sp = _tile(K, es, "fxsp", [128, 512], F32)
        ee = _tile(K, es, "fxee", [128, 512], F32)
        HI = _tile(K, es, "fxHI", [128, 512], BF16)
        MID = _tile(K, es, "fxMID", [128, 512], BF16)
        nfb = _tile(K, es, "fxnfb", [128, 1], F32)
        r_wp = [fw.res() for _ in range(2)]
        r_ht = [fw.res() for _ in range(2)]
        r_q = [[fw.res() for _ in range(2)] for _ in range(2)]
        r_k = [[fw.res() for _ in range(2)] for _ in range(2)]
        r_VE = [fw.res() for _ in range(2)]
        r_VO = [fw.res() for _ in range(2)]
        r_ao = [fw.res() for _ in range(2)]
        r_c, r_Fp, r_sp, r_ee, r_HI, r_MID = [fw.res() for _ in range(6)]
        r_Ff = [fw.res() for _ in range(2)]
        c_wp = [fw.chan() for _ in range(2)]
        c_ht = [fw.chan() for _ in range(2)]
        c_ao = [fw.chan() for _ in range(2)]
        c_c = fw.chan()
        w_inv = w_in.rearrange("(c p) n -> p c n", p=128)
        HTv = HT.rearrange("c p n -> p c n")
        fw.op("dve", lambda e: e.memset(wf[:], 0.0), writes=[r_c])
        for rep in range(3):
            fw.dma("pool", c_c, wf[:, :, rep * 32:rep * 32 + 8], w_inv[:, :, 1536:1544], writes=[r_c])
        fw.dma("pool", c_c, selq[0:97, :, :], C["selq"], writes=[r_c])
        fw.dma("pool", c_c, selk[0:97, :, :], C["selk"], writes=[r_c])
        fw.dma("pool", c_c, maskT[:], C["mask_causal"], writes=[r_c])
        fw.op("dve", lambda e: e.tensor_scalar(out=nfb[:], in0=nfb_col, scalar1=-1.0, scalar2=None, op0=ALU.mult), reads=[K.r_const], writes=[r_c])
        fw.op("pool", lambda e: e.memset(Fp[:], 0.0), writes=[r_Fp])
        fw.op("pool", lambda e: e.memset(Fp[96:97, :], 1.0), writes=[r_Fp])
        for b in range(2):
            fw.op("pool", lambda e: e.memset(VE[b][:, :, 64:65], 1.0), writes=[r_VE[b]])
            fw.op("pool", lambda e: e.memset(VO[b][:, :, 0:64], 0.0), writes=[r_VO[b]])
            fw.op("pool", lambda e: e.memset(VO[b][:, :, 0:1], 1.0), writes=[r_VO[b]])
        nht = 0
        npair = 0

        def load_ht(g):
            nonlocal nht
            b = nht % 2
            nht += 1
            fw.dma("sp", c_ht[b], ht[b][:], HTv[:, :, g * 512:(g + 1) * 512], reads=[K.r_HT[g]], writes=[r_ht[b]])
            return b

        for s in range(2):
            for nt in range(4):
                hb = load_ht(s * 4 + nt)
                bank = K.bank()
                for kc in range(8):
                    mm(K, K.ps[0:72, bank, :], wf[:, kc, :], ht[hb][:, kc, :], kc == 0, kc == 7, [r_c, r_ht[hb]], [K.r_ps[bank]])
                fw.op("act", lambda e: e.activation(out=ee[0:72, :], in_=K.ps[0:72, bank, :], func=AF.Exp, scale=-1.0, bias=nfb[0:72, 0:1]),
                      reads=[K.r_ps[bank], r_c], writes=[r_ee])
                fw.op("act", lambda e: e.activation(out=sp[0:72, :], in_=ee[0:72, :], func=AF.Ln, bias=K.oneb[0:72, 0:1], scale=1.0),
                      reads=[r_ee, K.r_const], writes=[r_sp])
                fb = nt % 2
                init = 0.0 if nt == 0 else Ff[1 - fb][0:72, 511:512]
                fw.op("dve", lambda e: e.tensor_tensor_scan(out=Ff[fb][0:72, :], data0=K.onesf[0:72, 0:1].broadcast_to([72, 512]), data1=sp[0:72, :],
                                                            initial=init, op0=ALU.mult, op1=ALU.subtract),
                      reads=[r_sp, K.r_const, r_Ff[1 - fb]], writes=[r_Ff[fb]])
                sl = slice(nt * 512, (nt + 1) * 512)
                fw.op("dve", lambda e: e.tensor_copy(out=HI[0:72, :], in_=Ff[fb][0:72, :]), reads=[r_Ff[fb]], writes=[r_HI])
                fw.op("dve", lambda e: e.tensor_tensor(out=sp[0:72, :], in0=Ff[fb][0:72, :], in1=HI[0:72, :], op=ALU.subtract),
                      reads=[r_Ff[fb], r_HI], writes=[r_sp])
                fw.op("dve", lambda e: e.tensor_copy(out=MID[0:72, :], in_=sp[0:72, :]), reads=[r_sp], writes=[r_MID])
                fw.op("dve", lambda e: e.tensor_tensor(out=sp[0:72, :], in0=sp[0:72, :], in1=MID[0:72, :], op=ALU.subtract),
                      reads=[r_sp, r_MID], writes=[r_sp])
                fw.op("pool", lambda e: e.tensor_copy(out=Fp[0:8, sl], in_=HI[0:8, :]), reads=[r_HI], writes=[r_Fp])
                fw.op("pool", lambda e: e.tensor_copy(out=Fp[32:40, sl], in_=MID[32:40, :]), reads=[r_MID], writes=[r_Fp])
                fw.op("pool", lambda e: e.tensor_copy(out=Fp[64:72, sl], in_=sp[64:72, :]), reads=[r_sp], writes=[r_Fp])
            for j in range(K.dbg_pairs):
                pb = npair % 2
                npair += 1
                fw.dma("pool", c_wp[pb], wp[pb][:, :, 0:128], w_inv[:, :, j * 128:(j + 1) * 128], writes=[r_wp[pb]])
                fw.dma("pool", c_wp[pb], wp[pb][:, :, 128:256], w_inv[:, :, 512 + j * 128:512 + (j + 1) * 128], writes=[r_wp[pb]])
                fw.dma("pool", c_wp[pb], wp[pb][:, :, 256:384], w_inv[:, :, 1024 + j * 128:1024 + (j + 1) * 128], writes=[r_wp[pb]])
                for nt in range(4):
                    hb = load_ht(s * 4 + nt)
                    sl = slice(nt * 512, (nt + 1) * 512)
                    for hh in range(2):
                        h = 2 * j + hh
                        bq = K.bank()
                        for kc in range(8):
                            mm(K, K.ps[0:64, bq, :], wp[pb][:, kc, hh * 64:(hh + 1) * 64], ht[hb][:, kc, :], kc == 0, kc == 7,
                               [r_wp[pb], r_ht[hb]], [K.r_ps[bq]])
                        fw.op("act", lambda e: e.activation(out=qa[pb][0:64, hh, sl], in_=K.ps[0:64, bq, :], func=AF.Copy, scale=FOX_SCALE),
                              reads=[K.r_ps[bq]], writes=[r_q[pb][hh]])
                        bk = K.bank()
                        for kc in range(8):
                            mm(K, K.ps[0:64, bk, :], wp[pb][:, kc, 128 + hh * 64:128 + (hh + 1) * 64], ht[hb][:, kc, :], kc == 0, kc == 7,
                               [r_wp[pb], r_ht[hb]], [K.r_ps[bk]])
                        fw.op("dve", lambda e: e.tensor_copy(out=ka[pb][0:64, hh, sl], in_=K.ps[0:64, bk, :]),
                              reads=[K.r_ps[bk]], writes=[r_k[pb][hh]])
                        if "sel" in K.dbg_skip:
                            continue
                        ba = K.bank()
                        mm(K, K.ps[0:70, ba, :], selq[0:97, h, :], Fp[0:97, sl], True, True, [r_c, r_Fp], [K.r_ps[ba]])
                        fw.op("dve", lambda e: e.tensor_copy(out=qa[pb][64:70, hh, sl], in_=K.ps[64:70, ba, :]),
                              reads=[K.r_ps[ba]], writes=[r_q[pb][hh]])
                        ba = K.bank()
                        mm(K, K.ps[0:70, ba, :], selk[0:97, h, :], Fp[0:97, sl], True, True, [r_c, r_Fp], [K.r_ps[ba]])
                        fw.op("dve", lambda e: e.tensor_copy(out=ka[pb][64:70, hh, sl], in_=K.ps[64:70, ba, :]),
                              reads=[K.r_ps[ba]], writes=[r_k[pb][hh]])
                    if "v" in K.dbg_skip:
                        continue
                    bv = K.bank()
                    for it in range(4):
                        for kc in range(8):
                            mm(K, K.ps[:, bv, it * 128:(it + 1) * 128], ht[hb][:, kc, it * 128:(it + 1) * 128], wp[pb][:, kc, 256:384],
                               kc == 0, kc == 7, [r_wp[pb], r_ht[hb]], [K.r_ps[bv]])
                    src = K.ps[:, bv, :].rearrange("p (t c) -> p t c", t=4)
                    if "vevac" in K.dbg_skip:
                        continue
                    fw.op("act", lambda e: e.activation(out=VE[pb][:, nt * 4:nt * 4 + 4, 0:64], in_=src[:, :, 0:64], func=AF.Copy),
                          reads=[K.r_ps[bv]], writes=[r_VE[pb]])
                    if "vevac2" in K.dbg_skip:
                        continue
                    fw.op("act", lambda e: e.activation(out=VO[pb][:, nt * 4:nt * 4 + 4, 64:128], in_=src[:, :, 64:128], func=AF.Copy),
                          reads=[K.r_ps[bv]], writes=[r_VO[pb]])
                for hh in range(K.dbg_heads):
                    rq = [r_q[pb][hh], r_k[pb][hh], r_VE[pb] if hh == 0 else r_VO[pb]]
                    if hh == 0:
                        attention_head(K, A, qa[pb][:, 0, :], ka[pb][:, 0, :], 70, lambda i: VE[pb][:, i, 0:65], 65, 0, 64, maskT[:, :], rq + [r_c],
                                       lambda G: ao[pb][0:64, G * 512:(G + 1) * 512], lambda G: r_ao[pb])
                    else:
                        attention_head(K, A, qa[pb][:, 1, :], ka[pb][:, 1, :], 70, lambda i: VO[pb][:, i, 0:128], 128, 64, 0, maskT[:, :], rq + [r_c],
                                       lambda G: ao[pb][64:128, G * 512:(G + 1) * 512], lambda G: r_ao[pb])
                if "at" not in K.dbg_skip:
                    fw.dma("sp", c_ao[pb], AT[j, :, s * SEQ:(s + 1) * SEQ], ao[pb][:], reads=[r_ao[pb]], writes=[K.r_AT[s]])
        fw.barrier(c_wp + c_ht + c_ao + [c_c])


def phase_outproj(K, XT, AT, w_out):
    fw = K.fw
    with ExitStack() as es:
        wo = _tile(K, es, "opw", [128, 8, D], BF16)
        at = [_tile(K, es, "opat%d" % i, [128, 8, 512], BF16) for i in range(2)]
        xg = [_tile(K, es, "opxg%d" % i, [128, 8, 512], F32) for i in range(2)]
        r_wo = fw.res()
        r_at = [fw.res() for _ in range(2)]
        r_xg = [[fw.res() for _ in range(8)] for _ in range(2)]
        c_wo = fw.chan()
        c_at = [fw.chan() for _ in range(2)]
        c_x = [fw.chan() for _ in range(2)]
        c_xs = [fw.chan() for _ in range(2)]
        XTv = XT.rearrange("c p n -> p c n")
        ATv = AT.rearrange("c p n -> p c n")
        fw.dma("pool", c_wo, wo[:], w_out.rearrange("(c p) n -> p c n", p=128), writes=[r_wo])
        for g in range(NT // 512):
            b = g % 2
            fw.dma("sp", c_at[b], at[b][:], ATv[:, :, g * 512:(g + 1) * 512], reads=[K.r_AT[g // 4]], writes=[r_at[b]])
            fw.dma("sp", c_x[b], xg[b][:], XTv[:, :, g * 512:(g + 1) * 512], reads=[K.r_XT[g]], writes=r_xg[b])
            for dmc in range(8):
                by = K.bank()
                for c in range(8):
                    mm(K, K.ps[:, by, :], wo[:, c, dmc * 128:(dmc + 1) * 128], at[b][:, c, :], c == 0, c == 7, [r_wo, r_at[b]], [K.r_ps[by]])
                fw.op("dve", lambda e: e.tensor_tensor(out=xg[b][:, dmc, :], in0=K.ps[:, by, :], in1=xg[b][:, dmc, :], op=ALU.add),
                      reads=[K.r_ps[by], r_xg[b][dmc]], writes=[r_xg[b][dmc]])
            fw.dma("sp", c_xs[b], XTv[:, :, g * 512:(g + 1) * 512], xg[b][:], reads=r_xg[b], writes=[K.r_XT[g]])
        fw.barrier([c_wo] + c_at + c_x + c_xs)


def phase_mla(K, HT, AT, w_in, w_uq, w_uqs, w_ukv, qn_col, kvn_col, C):
    fw = K.fw
    with ExitStack() as es:
        A = AttnCtx(K, es, "ml")
        ht = [_tile(K, es, "mlht%d" % i, [128, 8, 512], BF16) for i in range(2)]
        qa = [_tile(K, es, "mlqa%d" % i, [128, 2, SEQ], BF16) for i in range(2)]
        ka = [_tile(K, es, "mlka%d" % i, [128, 2, SEQ], BF16) for i in range(2)]
        VE = [_tile(K, es, "mlVE%d" % i, [128, 16, 65], BF16) for i in range(2)]
        VO = [_tile(K, es, "mlVO%d" % i, [128, 16, 128], BF16) for i in range(2)]
        ao = [_tile(K, es, "mlao%d" % i, [128, SEQ], BF16) for i in range(2)]
        wlat = _tile(K, es, "mlwlat", [128, 8, 384], BF16)
        wkpe = _tile(K, es, "mlwkpe", [128, 8, 96], BF16)
        wkpes = _tile(K, es, "mlwkpes", [128, 8, 96], BF16)
        wuq = _tile(K, es, "mlwuq", [128, 2, 768], BF16)
        wuqs = _tile(K, es, "mlwuqs", [128, 2, 768], BF16)
        wukv = _tile(K, es, "mlwukv", [128, 1024], BF16)
        maskT = _tile(K, es, "mlmask", [128, 128], BF16)
        cos2 = _tile(K, es, "mlcos", [128, SEQ], F32)
        sin2 = _tile(K, es, "mlsin", [128, SEQ], F32)
        cqn = _tile(K, es, "mlcqn", [128, 2, SEQ], BF16)
        ckvn = _tile(K, es, "mlckvn", [128, SEQ], BF16)
        kpe = _tile(K, es, "mlkpe", [128, SEQ], BF16)
        cqf = _tile(K, es, "mlcqf", [128, 3, 512], F32)
        sq = _tile(K, es, "mlsq", [128, 3, 512], BF16)
        rstd = [_tile(K, es, "mlrstd%d" % i, [128, 512], F32) for i in range(2)]
        tmp = _tile(K, es, "mltmp", [128, 512], F32)
        t1 = [_tile(K, es, "mlt1%d" % i, [128, 512], F32) for i in range(2)]
        t2 = [_tile(K, es, "mlt2%d" % i, [128, 512], F32) for i in range(2)]
        r_ht = [fw.res() for _ in range(2)]
        r_q = [[fw.res() for _ in range(2)] for _ in range(2)]
        r_k = [[fw.res() for _ in range(2)] for _ in range(2)]
        r_VE = [fw.res() for _ in range(2)]
        r_VO = [fw.res() for _ in range(2)]
        r_ao = [fw.res() for _ in range(2)]
        r_c, r_cqn, r_ckvn, r_kpe, r_cqf, r_sq, r_tmp = [fw.res() for _ in range(7)]
        r_rstd = [fw.res() for _ in range(2)]
        r_t1 = [fw.res() for _ in range(2)]
        r_t2 = [fw.res() for _ in range(2)]
        c_ht = [fw.chan() for _ in range(2)]
        c_ao = [fw.chan() for _ in range(2)]
        c_c = fw.chan()
        w_inv = w_in.rearrange("(c p) n -> p c n", p=128)
        HTv = HT.rearrange("c p n -> p c n")
        fw.dma("pool", c_c, wlat[:], w_inv[:, :, 1544:1928], writes=[r_c])
        fw.op("dve", lambda e: e.memset(wkpe[:], 0.0), writes=[r_c])
        fw.op("dve", lambda e: e.memset(wkpes[:], 0.0), writes=[r_c])
        fw.dma("pool", c_c, wkpe[:, :, 64:96], w_inv[:, :, 1928:1960], writes=[r_c])
        fw.dma("pool", c_c, wkpes[:, :, 64:80], w_inv[:, :, 1944:1960], writes=[r_c])
        fw.dma("pool", c_c, wkpes[:, :, 80:96], w_inv[:, :, 1928:1944], writes=[r_c])
        fw.dma("pool", c_c, wuq[:], w_uq.rearrange("(c p) n -> p c n", p=128), writes=[r_c])
        fw.dma("pool", c_c, wuqs[:], w_uqs.rearrange("(c p) n -> p c n", p=128), writes=[r_c])
        fw.dma("pool", c_c, wukv[:], w_ukv, writes=[r_c])
        fw.dma("pool", c_c, maskT[:], C["mask_chunk"], writes=[r_c])
        fw.dma("sp", c_c, cos2[64:96, :], C["cos2"], writes=[r_c])
        fw.dma("sp", c_c, sin2[64:96, :], C["sin2s"], writes=[r_c])
        for b in range(2):
            fw.op("pool", lambda e: e.memset(VE[b][:, :, 64:65], 1.0), writes=[r_VE[b]])
            fw.op("pool", lambda e: e.memset(VO[b][:, :, 0:64], 0.0), writes=[r_VO[b]])
            fw.op("pool", lambda e: e.memset(VO[b][:, :, 0:1], 1.0), writes=[r_VO[b]])
        nht = 0
        npair = 0
        ti = 0
        for s in range(2):
            for nt in range(4):
                hb = nht % 2
                nht += 1
                g = s * 4 + nt
                sl = slice(nt * 512, (nt + 1) * 512)
                fw.dma("sp", c_ht[hb], ht[hb][:], HTv[:, :, g * 512:(g + 1) * 512], reads=[K.r_HT[g]], writes=[r_ht[hb]])
                for c in range(3):
                    bank = K.bank()
                    for kc in range(8):
                        mm(K, K.ps[:, bank, :], wlat[:, kc, c * 128:(c + 1) * 128], ht[hb][:, kc, :], kc == 0, kc == 7, [r_c, r_ht[hb]], [K.r_ps[bank]])
                    fw.op("act", lambda e: e.activation(out=cqf[:, c, :], in_=K.ps[:, bank, :], func=AF.Copy), reads=[K.r_ps[bank]], writes=[r_cqf])
                rms_rstd(K, cqf[:, 0:2, :], 2, 0, 512, sq[:, 0:2, :], rstd[0][:, :], [r_cqf], [r_sq] * 2, r_rstd[0], tmp, r_tmp, 256)
                rms_rstd(K, cqf[:, 2:3, :], 1, 0, 512, sq[:, 2:3, :], rstd[1][:, :], [r_cqf], [r_sq], r_rstd[1], tmp, r_tmp, 128)
                for c in range(2):
                    fw.op("dve", lambda e: e.scalar_tensor_tensor(out=cqn[:, c, sl], in0=cqf[:, c, :], scalar=qn_col[:, c:c + 1], in1=rstd[0][:, :],
                                                                   op0=ALU.mult, op1=ALU.mult), reads=[r_cqf, r_rstd[0], K.r_const], writes=[r_cqn])
                fw.op("dve", lambda e: e.scalar_tensor_tensor(out=ckvn[:, sl], in0=cqf[:, 2, :], scalar=kvn_col[:, 0:1], in1=rstd[1][:, :],
                                                               op0=ALU.mult, op1=ALU.mult), reads=[r_cqf, r_rstd[1], K.r_const], writes=[r_ckvn])
                bA, bB = K.bank(), K.bank()
                for kc in range(8):
                    mm(K, K.ps[0:96, bA, :], wkpe[:, kc, :], ht[hb][:, kc, :], kc == 0, kc == 7, [r_c, r_ht[hb]], [K.r_ps[bA]])
                for kc in range(8):
                    mm(K, K.ps[0:96, bB, :], wkpes[:, kc, :], ht[hb][:, kc, :], kc == 0, kc == 7, [r_c, r_ht[hb]], [K.r_ps[bB]])
                tb = ti % 2
                ti += 1
                fw.op("dve", lambda e: e.tensor_tensor(out=t1[tb][64:96, :], in0=K.ps[64:96, bA, :], in1=cos2[64:96, sl], op=ALU.mult),
                      reads=[K.r_ps[bA], r_c], writes=[r_t1[tb]])
                fw.op("dve", lambda e: e.tensor_tensor(out=t2[tb][64:96, :], in0=K.ps[64:96, bB, :], in1=sin2[64:96, sl], op=ALU.mult),
                      reads=[K.r_ps[bB], r_c], writes=[r_t2[tb]])
                fw.op("pool", lambda e: e.tensor_tensor(out=kpe[64:96, sl], in0=t1[tb][64:96, :], in1=t2[tb][64:96, :], op=ALU.add),
                      reads=[r_t1[tb], r_t2[tb]], writes=[r_kpe])
            for j in range(4):
                pb = npair % 2
                npair += 1
                for nt in range(4):
                    sl = slice(nt * 512, (nt + 1) * 512)
                    for hh in range(2):
                        h = 2 * j + hh
                        bA, bB = K.bank(), K.bank()
                        for c in range(2):
                            mm(K, K.ps[0:96, bA, :], wuq[:, c, h * 96:(h + 1) * 96], cqn[:, c, sl], c == 0, c == 1, [r_c, r_cqn], [K.r_ps[bA]])
                        for c in range(2):
                            mm(K, K.ps[0:96, bB, :], wuqs[:, c, h * 96:(h + 1) * 96], cqn[:, c, sl], c == 0, c == 1, [r_c, r_cqn], [K.r_ps[bB]])
                        fw.op("act", lambda e: e.activation(out=qa[pb][0:64, hh, sl], in_=K.ps[0:64, bA, :], func=AF.Copy, scale=MLA_SCALE),
                              reads=[K.r_ps[bA]], writes=[r_q[pb][hh]])
                        tb = ti % 2
                        ti += 1
                        fw.op("dve", lambda e: e.scalar_tensor_tensor(out=t1[tb][64:96, :], in0=K.ps[64:96, bA, :], scalar=MLA_SCALE, in1=cos2[64:96, sl],
                                                                       op0=ALU.mult, op1=ALU.mult), reads=[K.r_ps[bA], r_c], writes=[r_t1[tb]])
                        fw.op("dve", lambda e: e.scalar_tensor_tensor(out=t2[tb][64:96, :], in0=K.ps[64:96, bB, :], scalar=MLA_SCALE, in1=sin2[64:96, sl],
                                                                       op0=ALU.mult, op1=ALU.mult), reads=[K.r_ps[bB], r_c], writes=[r_t2[tb]])
                        fw.op("pool", lambda e: e.tensor_tensor(out=qa[pb][64:96, hh, sl], in0=t1[tb][64:96, :], in1=t2[tb][64:96, :], op=ALU.add),
                              reads=[r_t1[tb], r_t2[tb]], writes=[r_q[pb][hh]])
                        bk = K.bank()
                        mm(K, K.ps[0:64, bk, :], wukv[:, h * 128:h * 128 + 64], ckvn[:, sl], True, True, [r_c, r_ckvn], [K.r_ps[bk]])
                        fw.op("act", lambda e: e.activation(out=ka[pb][0:64, hh, sl], in_=K.ps[0:64, bk, :], func=AF.Copy),
                              reads=[K.r_ps[bk]], writes=[r_k[pb][hh]])
                        fw.op("pool", lambda e: e.tensor_copy(out=ka[pb][64:96, hh, sl], in_=kpe[64:96, sl]), reads=[r_kpe], writes=[r_k[pb][hh]])
                    bv = K.bank()
                    vcols = wukv[:, 2 * j * 128:(2 * j + 2) * 128].rearrange("p (h c) -> p h c", h=2)[:, :, 64:128]
                    for it in range(4):
                        i = nt * 4 + it
                        mm(K, K.ps[:, bv, it * 128:(it + 1) * 128], ckvn[:, i * 128:(i + 1) * 128], vcols, True, True, [r_c, r_ckvn], [K.r_ps[bv]])
                    src = K.ps[:, bv, :].rearrange("p (t c) -> p t c", t=4)
                    fw.op("act", lambda e: e.activation(out=VE[pb][:, nt * 4:nt * 4 + 4, 0:64], in_=src[:, :, 0:64], func=AF.Copy),
                          reads=[K.r_ps[bv]], writes=[r_VE[pb]])
                    fw.op("act", lambda e: e.activation(out=VO[pb][:, nt * 4:nt * 4 + 4, 64:128], in_=src[:, :, 64:128], func=AF.Copy),
                          reads=[K.r_ps[bv]], writes=[r_VO[pb]])
                for hh in range(2):
                    rq = [r_q[pb][hh], r_k[pb][hh], r_VE[pb] if hh == 0 else r_VO[pb], r_c]
                    if hh == 0:
                        attention_head(K, A, qa[pb][:, 0, :], ka[pb][:, 0, :], 96, lambda i: VE[pb][:, i, 0:65], 65, 0, 64, maskT[:, :], rq,
                                       lambda G: ao[pb][0:64, G * 512:(G + 1) * 512], lambda G: r_ao[pb])
                    else:
                        attention_head(K, A, qa[pb][:, 1, :], ka[pb][:, 1, :], 96, lambda i: VO[pb][:, i, 0:128], 128, 64, 0, maskT[:, :], rq,
                                       lambda G: ao[pb][64:128, G * 512:(G + 1) * 512], lambda G: r_ao[pb])
                fw.dma("sp", c_ao[pb], AT[4 + j, :, s * SEQ:(s + 1) * SEQ], ao[pb][:], reads=[r_ao[pb]], writes=[K.r_AT[s]])
        fw.barrier(c_ht + c_ao + [c_c])


GDN_QSCALE = 128 ** -0.5


def phase_gdn_proj(K, HT, w_in, convw_col, dtb_bc, alog_bc, S):
    fw = K.fw
    with ExitStack() as es:
        hts = _tile(K, es, "gpht", [128, 8, NT], BF16)
        wblk = [_tile(K, es, "gpw%d" % i, [128, 8, 512], BF16) for i in range(2)]
        wba = _tile(K, es, "gpwba", [128, 8, 16], BF16)
        dg = [_tile(K, es, "gpdg%d" % i, [128, 4, 4, 128], BF16) for i in range(2)]
        xc = [_tile(K, es, "gpxc%d" % i, [128, 515], BF16) for i in range(4)]
        qs = [_tile(K, es, "gpqs%d" % i, [128, 512], F32) for i in range(4)]
        sq = [_tile(K, es, "gpsq%d" % i, [128, 512], BF16) for i in range(4)]
        tmp = [_tile(K, es, "gptmp%d" % i, [128, 512], F32) for i in range(4)]
        rinv = [_tile(K, es, "gprinv%d" % i, [128, 512], F32) for i in range(4)]
        qn = [_tile(K, es, "gpqn%d" % i, [128, 512], BF16) for i in range(4)]
        tt = [_tile(K, es, "gptt%d" % i, [128, 4, 128], BF16) for i in range(4)]
        gt = [_tile(K, es, "gpgt%d" % i, [128, D], BF16) for i in range(2)]
        bgt = [_tile(K, es, "gpbg%d" % i, [128, 16], F32) for i in range(2)]
        e1 = _tile(K, es, "gpe1", [128, 8], F32)
        nexpa = _tile(K, es, "gpnexpa", [128, 8], F32)
        r_ht = [fw.res() for _ in range(NT // 512)]
        r_w = [fw.res() for _ in range(2)]
        r_dg = [fw.res() for _ in range(2)]
        r_xc = [fw.res() for _ in range(4)]
        r_qs = [fw.res() for _ in range(4)]
        r_sq = [fw.res() for _ in range(4)]
        r_tmp = [fw.res() for _ in range(4)]
        r_rinv = [fw.res() for _ in range(4)]
        r_qn = [fw.res() for _ in range(4)]
        r_tt = [fw.res() for _ in range(4)]
        r_gt = [fw.res() for _ in range(2)]
        r_bg = [fw.res() for _ in range(2)]
        r_c, r_e1 = fw.res(), fw.res()
        c_ht = fw.chan()
        c_w = [fw.chan() for _ in range(2)]
        c_c = fw.chan()
        c_qn = [fw.chan() for _ in range(4)]
        c_tt = [fw.chan() for _ in range(4)]
        c_gt = [fw.chan() for _ in range(2)]
        c_bg = [fw.chan() for _ in range(2)]
        HTv = HT.rearrange("c p n -> p c n")
        w_inv = w_in.rearrange("(c p) n -> p c n", p=128)
        for g in range(NT // 512):
            fw.dma("sp", c_ht, hts[:, :, g * 512:(g + 1) * 512], HTv[:, :, g * 512:(g + 1) * 512], reads=[K.r_HT[g]], writes=[r_ht[g]])
        fw.dma("pool", c_c, wba[:], w_inv[:, :, 3072:3088], writes=[r_c])
        fw.op("act", lambda e: e.activation(out=nexpa[:], in_=alog_bc, func=AF.Exp), reads=[K.r_const], writes=[r_c])
        fw.op("dve", lambda e: e.tensor_scalar(out=nexpa[:], in0=nexpa[:], scalar1=-1.0, scalar2=None, op0=ALU.mult), reads=[r_c], writes=[r_c])
        nw = 0
        ci = 0
        qi = 0
        ti = 0

        def load_w(col0):
            nonlocal nw
            b = nw % 2
            nw += 1
            fw.dma("pool", c_w[b], wblk[b][:], w_inv[:, :, col0:col0 + 512], writes=[r_w[b]])
            return b

        wq = {0: load_w(0)}
        for blk in range(6):
            wb = wq[blk]
            if blk + 1 < 6:
                wq[blk + 1] = load_w((blk + 1) * 512)
            db = blk % 2
            for tap in range(4):
                for fcl in range(4):
                    fc = blk * 4 + fcl
                    col = tap * 24 + fc
                    fw.op("pool", lambda e: e.tensor_scalar(out=dg[db][:, tap, fcl, :], in0=K.identf[:, :], scalar1=convw_col[:, col:col + 1], scalar2=None,
                                                            op0=ALU.mult), reads=[K.r_const], writes=[r_dg[db]])
            kind = "q" if blk < 2 else ("k" if blk < 4 else "v")
            for s in range(2):
                for nt in range(4):
                    g = s * 4 + nt
                    pbk = []
                    for fcl in range(4):
                        bank = K.bank()
                        pbk.append(bank)
                        for kc in range(8):
                            mm(K, K.ps[:, bank, :], wblk[wb][:, kc, fcl * 128:(fcl + 1) * 128], hts[:, kc, g * 512:(g + 1) * 512], kc == 0, kc == 7,
                               [r_w[wb], r_ht[g]], [K.r_ps[bank]])
                    for fcl in range(4):
                        x = xc[fcl]
                        if nt == 0:
                            fw.op("pool", lambda e: e.memset(x[:, 0:3], 0.0), writes=[r_xc[fcl]])
                        else:
                            fw.op("pool", lambda e: e.tensor_copy(out=x[:, 0:3], in_=x[:, 512:515]), reads=[r_xc[fcl]], writes=[r_xc[fcl]])
                        fw.op("act", lambda e: e.activation(out=x[:, 3:515], in_=K.ps[:, pbk[fcl], :], func=AF.Copy), reads=[K.r_ps[pbk[fcl]]], writes=[r_xc[fcl]])
                    cbk = []
                    for fcl in range(4):
                        cb = K.bank()
                        cbk.append(cb)
                        for tap in range(4):
                            mm(K, K.ps[:, cb, :], dg[db][:, tap, fcl, :], xc[fcl][:, tap:tap + 512], tap == 0, tap == 3, [r_dg[db], r_xc[fcl]], [K.r_ps[cb]])
                    for fcl in range(4):
                        if kind == "v":
                            fw.op("act", lambda e: e.activation(out=qn[fcl][:], in_=K.ps[:, cbk[fcl], :], func=AF.Silu), reads=[K.r_ps[cbk[fcl]]], writes=[r_qn[fcl]])
                        else:
                            fw.op("act", lambda e: e.activation(out=qs[fcl][:], in_=K.ps[:, cbk[fcl], :], func=AF.Silu), reads=[K.r_ps[cbk[fcl]]], writes=[r_qs[fcl]])
                    if kind != "v":
                        sbk = []
                        for fcl in range(4):
                            fw.op("pool", lambda e: e.tensor_tensor(out=sq[fcl][:], in0=qs[fcl][:], in1=qs[fcl][:], op=ALU.mult), reads=[r_qs[fcl]], writes=[r_sq[fcl]])
                        for fcl in range(4):
                            sb = K.bank()
                            sbk.append(sb)
                            mm(K, K.ps[:, sb, :], K.onesb[:, :], sq[fcl][:], True, True, [r_sq[fcl], K.r_const], [K.r_ps[sb]])
                        for fcl in range(4):
                            fw.op("act", lambda e: e.activation(out=tmp[fcl][:], in_=K.ps[:, sbk[fcl], :], func=AF.Sqrt, bias=K.epsb[:, 0:1], scale=1.0),
                                  reads=[K.r_ps[sbk[fcl]], K.r_const], writes=[r_tmp[fcl]])
                        sc = GDN_QSCALE if kind == "q" else 1.0
                        for fcl in range(4):
                            fw.op("dve", lambda e: e.reciprocal(out=rinv[fcl][:], in_=tmp[fcl][:]), reads=[r_tmp[fcl]], writes=[r_rinv[fcl]])
                        for fcl in range(4):
                            fw.op("dve", lambda e: e.scalar_tensor_tensor(out=qn[fcl][:], in0=qs[fcl][:], scalar=sc, in1=rinv[fcl][:], op0=ALU.mult, op1=ALU.mult),
                                  reads=[r_qs[fcl], r_rinv[fcl]], writes=[r_qn[fcl]])
                    for fcl in range(4):
                        hd = (blk * 4 + fcl) % 8
                        if kind in ("q", "k"):
                            dst = S["QT"] if kind == "q" else S["KT"]
                            fw.dma("sp", c_qn[fcl], dst[hd, :, g * 512:(g + 1) * 512], qn[fcl][:], reads=[r_qn[fcl]], writes=[S["r_QK"][s]])
                    if kind in ("k", "v"):
                        tbk = []
                        for fcl in range(4):
                            tb = K.bank()
                            tbk.append(tb)
                            pbf = K.ps[:, tb, :].bitcast(BF16)
                            for it in range(4):
                                fw.op("pe", lambda e: e.transpose(out=pbf[:, it * 128:(it + 1) * 128], in_=qn[fcl][:, it * 128:(it + 1) * 128], identity=K.identb[:]),
                                      reads=[r_qn[fcl], K.r_const], writes=[K.r_ps[tb]])
                        for fcl in range(4):
                            hd = (blk * 4 + fcl) % 8
                            pbf = K.ps[:, tbk[fcl], :].bitcast(BF16)
                            fw.op("act", lambda e: e.activation(out=tt[fcl][:], in_=pbf[:, 0:512].rearrange("p (t c) -> p t c", t=4), func=AF.Copy),
                                  reads=[K.r_ps[tbk[fcl]]], writes=[r_tt[fcl]])
                            dst = S["Ktok"] if kind == "k" else S["Vtok"]
                            fw.dma("sp", c_tt[fcl], dst[g * 512:(g + 1) * 512, hd * 128:(hd + 1) * 128].rearrange("(t p) c -> p t c", p=128), tt[fcl][:],
                                   reads=[r_tt[fcl]], writes=[S["r_tok"][s]])
        wg = [load_w(3088), load_w(3088 + 512)]
        gi = 0
        for g in range(NT // 512):
            s = g // 4
            for it in range(4):
                tok0 = g * 512 + it * 128
                b_ = gi % 2
                gi += 1
                for half in range(2):
                    bank = K.bank()
                    for kc in range(8):
                        mm(K, K.ps[:, bank, :], hts[:, kc, tok0:tok0 + 128], wblk[wg[half]][:, kc, :], kc == 0, kc == 7, [r_w[wg[half]], r_ht[g]], [K.r_ps[bank]])
                    fw.op("act", lambda e: e.activation(out=gt[b_][:, half * 512:(half + 1) * 512], in_=K.ps[:, bank, :], func=AF.Silu),
                          reads=[K.r_ps[bank]], writes=[r_gt[b_]])
                fw.dma("sp", c_gt[b_], S["Gtok"][tok0:tok0 + 128, :], gt[b_][:], reads=[r_gt[b_]], writes=[S["r_tok"][s]])
                bank = K.bank()
                for kc in range(8):
                    mm(K, K.ps[:, bank, 0:16], hts[:, kc, tok0:tok0 + 128], wba[:, kc, :], kc == 0, kc == 7, [r_c, r_ht[g]], [K.r_ps[bank]])
                fw.op("act", lambda e: e.activation(out=bgt[b_][:, 0:8], in_=K.ps[:, bank, 0:8], func=AF.Sigmoid), reads=[K.r_ps[bank]], writes=[r_bg[b_]])
                fw.op("dve", lambda e: e.tensor_tensor(out=e1[:], in0=K.ps[:, bank, 8:16], in1=dtb_bc, op=ALU.add), reads=[K.r_ps[bank], K.r_const], writes=[r_e1])
                fw.op("act", lambda e: e.activation(out=e1[:], in_=e1[:], func=AF.Exp), reads=[r_e1], writes=[r_e1])
                fw.op("act", lambda e: e.activation(out=e1[:], in_=e1[:], func=AF.Ln, bias=K.oneb[:, 0:1], scale=1.0), reads=[r_e1, K.r_const], writes=[r_e1])
                fw.op("dve", lambda e: e.tensor_tensor(out=bgt[b_][:, 8:16], in0=e1[:], in1=nexpa[:], op=ALU.mult), reads=[r_e1, r_c], writes=[r_bg[b_]])
                fw.dma("sp", c_bg[b_], S["BG"][tok0:tok0 + 128, :], bgt[b_][:], reads=[r_bg[b_]], writes=[S["r_tok"][s]])
        fw.barrier([c_ht, c_c] + c_w + c_qn + c_tt + c_gt + c_bg)


BIG = 30000.0


def phase_gdn_core(K, S, AT, onorm_bc, C):
    fw = K.fw
    with ExitStack() as es:
        def T(name, dt=F32, n=1, shape=(128, 8, 128)):
            return [_tile(K, es, "gc%s%d" % (name, i), list(shape), dt) for i in range(n)]
        U, M1, SM = T("U", F32, 1, (128, 128))[0], T("M1", F32, 1, (128, 128))[0], T("SM", F32, 1, (128, 128))[0]
        def RL(n):
            return [fw.res() for _ in range(n)]
        allc = []
        r_c = fw.res()
        c_c = fw.chan()
        fw.dma("sp", c_c, U[:], C["U"], writes=[r_c])
        fw.dma("sp", c_c, M1[:], C["gmask1"], writes=[r_c])
        fw.dma("sp", c_c, SM[:], C["gstrict"], writes=[r_c])
        QTv = S["QT"].rearrange("h p n -> p h n")
        KTv = S["KT"].rearrange("h p n -> p h n")
        ATv = AT.rearrange("h p n -> p h n")

        def bc_h(ap8, lo, n=4):
            return ap8[:, lo:lo + n].unsqueeze(2).broadcast_to([128, n, 128])

        def bc_m(ap, n=4):
            return ap.unsqueeze(1).broadcast_to([128, n, 128])

        def stage(mmfn):
            banks = [K.bank(), K.bank()]
            for h in range(8):
                mmfn(h, K.ps[:, banks[h // 4], (h % 4) * 128:(h % 4 + 1) * 128], banks[h // 4])
            return banks

        def pv(bank):
            return K.ps[:, bank, :].rearrange("p (h c) -> p h c", h=4)

        def hs(hf):
            return slice(hf * 4, hf * 4 + 4)

        def chain(s):
            qT, kT = T("q", BF16, 2), T("k", BF16, 2)
            ktok, vtok, gtok = T("kt", BF16, 2), T("vt", BF16, 2), T("gt", BF16, 2)
            bg = T("bg", F32, 2, (128, 16))
            St, Sb = T("S")[0], T("Sb", BF16)[0]
            NGU, ET, Esb = T("NGU")[0], T("ET")[0], T("Esb")[0]
            F32R = mybir.dt.float32r
            Aa, Ab, Pp = T("A", F32R, 2), T("AT", F32R, 2), T("P", F32R, 2)
            vb, kbd, usb, ot, sqo, og = T("vb")[0], T("kbd")[0], T("usb")[0], T("ot")[0], T("sqo")[0], T("og")[0]
            ktl, wT, inT, vnw, ogb = T("ktl", BF16)[0], T("wT", BF16)[0], T("inT", BF16)[0], T("vnw", BF16)[0], T("ogb", BF16)[0]
            att = T("att", BF16, 2)
            sc = {n: T(n, F32, 1, (128, 8))[0] for n in ("gam", "eg", "gl", "dl", "et", "beg", "nbeta", "ng", "ss", "rstd")}

            r_in, r_att = RL(2), RL(2)
            r_sc, r_S, r_Sb = fw.res(), fw.res(), fw.res()
            r_NGU = fw.res()
            r_ET, r_Esb = RL(2), RL(2)
            r_A, r_B, r_P = [RL(2) for _ in range(2)], [RL(2) for _ in range(2)], [RL(2) for _ in range(2)]
            r_vb, r_kbd, r_ktl = fw.res(), fw.res(), fw.res()
            r_usb, r_wT, r_inT, r_vnw, r_ot = RL(2), RL(2), RL(2), RL(2), RL(2)
            r_sq, r_og, r_ogb = fw.res(), fw.res(), fw.res()
            c_in = [fw.chan() for _ in range(2)]
            c_att = [fw.chan() for _ in range(2)]
            allc.extend(c_in + c_att)
            step = 0
            fw.op("pool", lambda e: e.memset(St[:], 0.0), writes=[r_S])
            fw.op("pool", lambda e: e.memset(Sb[:], 0.0), writes=[r_Sb])
            for b in range(SEQ // 128):
                tok0 = s * SEQ + b * 128
                ib_ = step % 2
                step += 1
                rin = r_in[ib_]
                q_, k_, kt_, vt_, gt_, bg_ = qT[ib_], kT[ib_], ktok[ib_], vtok[ib_], gtok[ib_], bg[ib_]
                fw.dma("sp", c_in[ib_], q_[:], QTv[:, :, tok0:tok0 + 128], reads=[S["r_QK"][s]], writes=[rin])
                fw.dma("sp", c_in[ib_], k_[:], KTv[:, :, tok0:tok0 + 128], reads=[S["r_QK"][s]], writes=[rin])
                fw.dma("sp", c_in[ib_], kt_[:], S["Ktok"][tok0:tok0 + 128, :].rearrange("p (h c) -> p h c", h=8), reads=[S["r_tok"][s]], writes=[rin])
                fw.dma("sp", c_in[ib_], vt_[:], S["Vtok"][tok0:tok0 + 128, :].rearrange("p (h c) -> p h c", h=8), reads=[S["r_tok"][s]], writes=[rin])
                fw.dma("sp", c_in[ib_], gt_[:], S["Gtok"][tok0:tok0 + 128, :].rearrange("p (h c) -> p h c", h=8), reads=[S["r_tok"][s]], writes=[rin])
                fw.dma("sp", c_in[ib_], bg_[:], S["BG"][tok0:tok0 + 128, :], reads=[S["r_tok"][s]], writes=[rin])
                b1, b2 = K.bank(), K.bank()
                mm(K, K.ps[:, b1, 0:8], U[:, :], bg_[:, 8:16], True, True, [r_c, rin], [K.r_ps[b1]])
                mm(K, K.ps[:, b2, 0:8], K.onesf[:, :], bg_[:, 8:16], True, True, [K.r_const, rin], [K.r_ps[b2]])
                fw.op("dve", lambda e: e.tensor_copy(out=sc["gam"][:], in_=K.ps[:, b1, 0:8]), reads=[K.r_ps[b1]], writes=[r_sc])
                fw.op("act", lambda e: e.activation(out=sc["eg"][:], in_=K.ps[:, b1, 0:8], func=AF.Exp), reads=[K.r_ps[b1]], writes=[r_sc])
                fw.op("act", lambda e: e.activation(out=sc["gl"][:], in_=K.ps[:, b2, 0:8], func=AF.Exp), reads=[K.r_ps[b2]], writes=[r_sc])
                fw.op("dve", lambda e: e.tensor_tensor(out=sc["dl"][:], in0=K.ps[:, b2, 0:8], in1=sc["gam"][:], op=ALU.subtract),
                      reads=[K.r_ps[b2], r_sc], writes=[r_sc])
                fw.op("act", lambda e: e.activation(out=sc["et"][:], in_=sc["dl"][:], func=AF.Exp), reads=[r_sc], writes=[r_sc])
                fw.op("dve", lambda e: e.tensor_tensor(out=sc["beg"][:], in0=bg_[:, 0:8], in1=sc["eg"][:], op=ALU.mult), reads=[rin, r_sc], writes=[r_sc])
                fw.op("dve", lambda e: e.tensor_scalar(out=sc["nbeta"][:], in0=bg_[:, 0:8], scalar1=-1.0, scalar2=None, op0=ALU.mult), reads=[rin], writes=[r_sc])
                fw.op("dve", lambda e: e.tensor_scalar(out=sc["ng"][:], in0=bg_[:, 8:16], scalar1=-1.0, scalar2=None, op0=ALU.mult), reads=[rin], writes=[r_sc])
                fw.op("pool", lambda e: e.tensor_tensor(out=NGU[:], in0=bc_m(U[:, :], 8), in1=bc_h(sc["ng"], 0, 8), op=ALU.mult),
                      reads=[r_c, r_sc], writes=[r_NGU])

                yield
                def mm_p1(h, o, bk):
                    gb = bg_[:, 8 + h:9 + h].broadcast_to([128, 128])
                    mm(K, o, gb, U[:, :], True, False, [rin, r_c], [K.r_ps[bk]])
                    mm(K, o, NGU[:, h, :], K.onesf[:, :], False, False, [r_NGU, K.r_const], [K.r_ps[bk]])
                    mm(K, o, K.identf[:, :], M1[:, :], False, True, [K.r_const, r_c], [K.r_ps[bk]])
                bks = stage(mm_p1)
                for hf in range(2):
                    fw.op("act", lambda e: e.activation(out=ET[:, hs(hf), :], in_=pv(bks[hf]), func=AF.Exp), reads=[K.r_ps[bks[hf]]], writes=[r_ET[hf]])

                yield
                def mm_tr(h, o, bk):
                    fw.op("pe", lambda e: e.transpose(out=o, in_=ET[:, h, :], identity=K.identf[:]), reads=[r_ET[h // 4], K.r_const], writes=[K.r_ps[bk]])
                bks = stage(mm_tr)
                for hf in range(2):
                    fw.op("dve", lambda e: e.tensor_tensor(out=Esb[:, hs(hf), :], in0=pv(bks[hf]), in1=bc_m(SM[:, :]), op=ALU.mult),
                          reads=[K.r_ps[bks[hf]], r_c], writes=[r_Esb[hf]])
                    fw.op("pool", lambda e: e.tensor_tensor(out=Esb[:, hs(hf), :], in0=Esb[:, hs(hf), :], in1=bc_h(sc["nbeta"], hf * 4), op=ALU.mult),
                          reads=[r_Esb[hf], r_sc], writes=[r_Esb[hf]])

                yield
                def mm_kk(h, o, bk):
                    mm(K, o, k_[:, h, :], k_[:, h, :], True, True, [rin], [K.r_ps[bk]])
                bks = stage(mm_kk)
                ca = 0
                for hf in range(2):
                    fw.op("dve", lambda e: e.tensor_tensor(out=Aa[ca][:, hs(hf), :], in0=pv(bks[hf]), in1=Esb[:, hs(hf), :], op=ALU.mult),
                          reads=[K.r_ps[bks[hf]], r_Esb[hf]], writes=[r_A[ca][hf]])

                yield
                def mm_at(h, o, bk):
                    fw.op("pe", lambda e: e.transpose(out=o, in_=Aa[ca][:, h, :].bitcast(F32), identity=K.identf[:]), reads=[r_A[ca][h // 4], K.r_const], writes=[K.r_ps[bk]])
                bks = stage(mm_at)
                cb, cp = 0, 0
                for hf in range(2):
                    fw.op("act", lambda e: e.activation(out=Ab[cb][:, hs(hf), :], in_=pv(bks[hf]), func=AF.Copy), reads=[K.r_ps[bks[hf]]], writes=[r_B[cb][hf]])
                    fw.op("dve", lambda e: e.tensor_tensor(out=Pp[cp][:, hs(hf), :], in0=pv(bks[hf]), in1=bc_m(K.identf[:, :]), op=ALU.add),
                          reads=[K.r_ps[bks[hf]], K.r_const], writes=[r_P[cp][hf]])
                for lev in range(6):
                    na, nb_, np_ = 1 - ca, 1 - cb, 1 - cp

                    def mm_a2(h, o, bk):
                        mm(K, o, Ab[cb][:, h, :], Aa[ca][:, h, :], True, True, [r_B[cb][h // 4], r_A[ca][h // 4]], [K.r_ps[bk]])
                    bks = stage(mm_a2)
                    if lev < 5:
                        def mm_b2(h, o, bk):
                            mm(K, o, Aa[ca][:, h, :], Ab[cb][:, h, :], True, True, [r_B[cb][h // 4], r_A[ca][h // 4]], [K.r_ps[bk]])
                        bks2 = stage(mm_b2)
                    for hf in range(2):
                        fw.op("act", lambda e: e.activation(out=Aa[na][:, hs(hf), :], in_=pv(bks[hf]), func=AF.Copy), reads=[K.r_ps[bks[hf]]], writes=[r_A[na][hf]])
                    if lev < 5:
                        for hf in range(2):
                            fw.op("dve", lambda e: e.tensor_copy(out=Ab[nb_][:, hs(hf), :], in_=pv(bks2[hf])), reads=[K.r_ps[bks2[hf]]], writes=[r_B[nb_][hf]])

                    def mm_p(h, o, bk):
                        mm(K, o, Aa[na][:, h, :], Pp[cp][:, h, :], True, True, [r_A[na][h // 4], r_P[cp][h // 4]], [K.r_ps[bk]])
                    bks3 = stage(mm_p)
                    for hf in range(2):
                        fw.op("dve", lambda e: e.tensor_tensor(out=Pp[np_][:, hs(hf), :], in0=pv(bks3[hf]), in1=Pp[cp][:, hs(hf), :].bitcast(F32), op=ALU.add),
                              reads=[K.r_ps[bks3[hf]], r_P[cp][hf]], writes=[r_P[np_][hf]])
                    ca, cp = na, np_
                    yield
                    if lev < 5:
                        cb = nb_
                yield
                fw.op("pool", lambda e: e.tensor_tensor(out=vb[:], in0=vt_[:], in1=bc_h(bg_, 0, 8), op=ALU.mult), reads=[rin], writes=[r_vb])
                fw.op("pool", lambda e: e.tensor_tensor(out=kbd[:], in0=kt_[:], in1=bc_h(sc["beg"], 0, 8), op=ALU.mult), reads=[rin, r_sc], writes=[r_kbd])
                fw.op("pool", lambda e: e.tensor_tensor(out=ktl[:], in0=kt_[:], in1=bc_h(sc["et"], 0, 8), op=ALU.mult), reads=[rin, r_sc], writes=[r_ktl])

                yield
                def mm_u(h, o, bk):
                    mm(K, o, Pp[cp][:, h, :].bitcast(F32), vb[:, h, :], True, True, [r_P[cp][h // 4], r_vb], [K.r_ps[bk]])
                bks = stage(mm_u)

                def mm_w(h, o, bk):
                    mm(K, o, kbd[:, h, :], Pp[cp][:, h, :].bitcast(F32), True, True, [r_P[cp][h // 4], r_kbd], [K.r_ps[bk]])
                bks2 = stage(mm_w)
                for hf in range(2):
                    fw.op("act", lambda e: e.activation(out=usb[:, hs(hf), :], in_=pv(bks[hf]), func=AF.Copy), reads=[K.r_ps[bks[hf]]], writes=[r_usb[hf]])
                for hf in range(2):
                    fw.op("act", lambda e: e.activation(out=wT[:, hs(hf), :], in_=pv(bks2[hf]), func=AF.Copy), reads=[K.r_ps[bks2[hf]]], writes=[r_wT[hf]])

                yield
                def mm_kq(h, o, bk):
                    mm(K, o, k_[:, h, :], q_[:, h, :], True, True, [rin], [K.r_ps[bk]])
                bks = stage(mm_kq)
                for hf in range(2):
                    fw.op("dve", lambda e: e.tensor_tensor(out=inT[:, hs(hf), :], in0=pv(bks[hf]), in1=ET[:, hs(hf), :], op=ALU.mult),
                          reads=[K.r_ps[bks[hf]], r_ET[hf]], writes=[r_inT[hf]])

                yield
                def mm_ws(h, o, bk):
                    mm(K, o, wT[:, h, :], Sb[:, h, :], True, True, [r_wT[h // 4], r_Sb], [K.r_ps[bk]])
                bks = stage(mm_ws)
                for hf in range(2):
                    fw.op("dve", lambda e: e.tensor_tensor(out=vnw[:, hs(hf), :], in0=usb[:, hs(hf), :], in1=pv(bks[hf]), op=ALU.subtract),
                          reads=[K.r_ps[bks[hf]], r_usb[hf]], writes=[r_vnw[hf]])

                def mm_qs(h, o, bk):
                    mm(K, o, q_[:, h, :], Sb[:, h, :], True, True, [rin, r_Sb], [K.r_ps[bk]])
                bks = stage(mm_qs)
                for hf in range(2):
                    fw.op("dve", lambda e: e.tensor_tensor(out=ot[:, hs(hf), :], in0=pv(bks[hf]), in1=bc_h(sc["eg"], hf * 4), op=ALU.mult),
                          reads=[K.r_ps[bks[hf]], r_sc], writes=[r_ot[hf]])

                def mm_iv(h, o, bk):
                    mm(K, o, inT[:, h, :], vnw[:, h, :], True, True, [r_inT[h // 4], r_vnw[h // 4]], [K.r_ps[bk]])
                bks = stage(mm_iv)
                for hf in range(2):
                    fw.op("dve", lambda e: e.tensor_tensor(out=ot[:, hs(hf), :], in0=pv(bks[hf]), in1=ot[:, hs(hf), :], op=ALU.add),
                          reads=[K.r_ps[bks[hf]], r_ot[hf]], writes=[r_ot[hf]])

                def mm_kv(h, o, bk):
                    mm(K, o, ktl[:, h, :], vnw[:, h, :], True, True, [r_ktl, r_vnw[h // 4]], [K.r_ps[bk]])
                bks = stage(mm_kv)
                fw.op("pool", lambda e: e.tensor_tensor(out=St[:], in0=St[:], in1=bc_h(sc["gl"], 0, 8), op=ALU.mult), reads=[r_S, r_sc], writes=[r_S])
                for hf in range(2):
                    fw.op("dve", lambda e: e.tensor_tensor(out=St[:, hs(hf), :], in0=pv(bks[hf]), in1=St[:, hs(hf), :], op=ALU.add),
                          reads=[K.r_ps[bks[hf]], r_S], writes=[r_S])
                fw.op("pool", lambda e: e.tensor_copy(out=Sb[:], in_=St[:]), reads=[r_S], writes=[r_Sb])
                yield
                fw.op("pool", lambda e: e.tensor_tensor(out=sqo[:], in0=ot[:], in1=ot[:], op=ALU.mult), reads=r_ot, writes=[r_sq])
                fw.op("dve", lambda e: e.tensor_reduce(out=sc["ss"][:], in_=sqo[:], axis=mybir.AxisListType.X, op=ALU.add), reads=[r_sq], writes=[r_sc])
                fw.op("act", lambda e: e.activation(out=sc["rstd"][:], in_=sc["ss"][:], func=AF.Sqrt, bias=K.epsb[:, 0:1], scale=1.0 / 128),
                      reads=[r_sc, K.r_const], writes=[r_sc])
                fw.op("dve", lambda e: e.reciprocal(out=sc["rstd"][:], in_=sc["rstd"][:]), reads=[r_sc], writes=[r_sc])
                fw.op("pool", lambda e: e.tensor_tensor(out=og[:], in0=ot[:], in1=bc_h(sc["rstd"], 0, 8), op=ALU.mult), reads=r_ot + [r_sc], writes=[r_og])
                fw.op("pool", lambda e: e.tensor_tensor(out=og[:], in0=og[:], in1=bc_m(onorm_bc, 8), op=ALU.mult), reads=[r_og, K.r_const], writes=[r_og])
                fw.op("pool", lambda e: e.tensor_tensor(out=ogb[:], in0=og[:], in1=gt_[:], op=ALU.mult), reads=[r_og, rin], writes=[r_ogb])
                tb = K.bank()
                ptb = K.ps[:, tb, :].bitcast(BF16)
                for h in range(8):
                    fw.op("pe", lambda e: e.transpose(out=ptb[:, h * 128:(h + 1) * 128], in_=ogb[:, h, :], identity=K.identb[:]),
                          reads=[r_ogb, K.r_const], writes=[K.r_ps[tb]])
                ob_ = ib_
                fw.op("act", lambda e: e.activation(out=att[ob_][:], in_=ptb[:, :].rearrange("p (h c) -> p h c", h=8), func=AF.Copy),
                      reads=[K.r_ps[tb]], writes=[r_att[ob_]])
                fw.dma("sp", c_att[ob_], ATv[:, :, tok0:tok0 + 128], att[ob_][:], reads=[r_att[ob_]], writes=[K.r_AT[s]])
        gens = [chain(0), chain(1)]
        while gens:
            for g_ in list(gens):
                try:
                    next(g_)
                except StopIteration:
                    gens.remove(g_)
        fw.barrier(allc + [c_c])
```

```python
import numpy as np
from contextlib import ExitStack
import concourse.bass as bass
import concourse.mybir as mybir
from concourse.bass_utils import run_bass_kernel_spmd

F32 = mybir.dt.float32
BF16 = mybir.dt.bfloat16
AF = mybir.ActivationFunctionType
ALU = mybir.AluOpType

N_CORES = 8
D = 1024
NT = 4096
SEQ = 2048
DFF = 2816
EPS = 1e-6


class Res:
    __slots__ = ("name", "w", "r", "excl")

    def __init__(self, name):
        self.name = name
        self.excl = False
        self.w = None
        self.r = {}


class Chan:
    __slots__ = ("sem", "cnt", "key")

    def __init__(self, sem, key):
        self.sem = sem
        self.cnt = 0
        self.key = key


class FW:
    SELF_SYNC = {"pe": False, "act": True, "dve": True, "pool": True, "sp": False}

    def __init__(self, nc, es):
        self.nc = nc
        self.es = es
        self.engs = {"pe": nc.tensor, "act": nc.scalar, "dve": nc.vector, "pool": nc.gpsimd, "sp": nc.sync}
        self.sem = {k: es.enter_context(nc.semaphore("s_" + k)) for k in self.engs}
        self.cnt = {k: 0 for k in self.engs}
        self.seen = {k: {} for k in self.engs}
        self.nchan = 0
        self.nres = 0

    def res(self, name=None):
        self.nres += 1
        return Res(name or ("r%d" % self.nres))

    def chan(self):
        if getattr(self, "free_chans", None):
            return self.free_chans.pop()
        self.nchan += 1
        key = "c%d" % self.nchan
        return Chan(self.es.enter_context(self.nc.semaphore(key)), key)

    def _wait(self, e, reads, writes):
        need = {}

        def add(ev):
            if ev is None:
                return
            key, h, v = ev
            if key == e and not self.SELF_SYNC[e]:
                return
            if self.seen[e].get(key, 0) >= v:
                return
            if key not in need or need[key][1] < v:
                need[key] = (h, v)

        for r in reads:
            add(r.w)
        for r in writes:
            add(r.w)
            for ev in r.r.values():
                add(ev)
        for key, (h, v) in need.items():
            self.engs[e].wait_ge(h, v)
            self.seen[e][key] = v

    def _mark(self, ev, reads, writes):
        key = ev[0]
        for r in reads:
            old = r.r.get(key)
            if old is None or old[2] < ev[2]:
                r.r[key] = ev
        for r in writes:
            r.w = ev
            r.r = {}

    def op(self, e, fn, reads=(), writes=()):
        ex = [r for r in reads if r.excl]
        if ex:
            reads = [r for r in reads if not r.excl]
            writes = list(writes) + ex
        self._wait(e, reads, writes)
        ins = fn(self.engs[e])
        self.cnt[e] += 1
        ins.then_inc(self.sem[e], 1)
        self._mark((e, self.sem[e], self.cnt[e]), reads, writes)
        return ins

    def dma(self, e, chan, out, in_, reads=(), writes=()):
        self._wait(e, reads, writes)
        ins = self.engs[e].dma_start(out=out, in_=in_)
        chan.cnt += 16
        ins.then_inc(chan.sem, 16)
        self._mark((chan.key, chan.sem, chan.cnt), reads, writes)
        return ins

    def finish(self, e, resources):
        self._wait(e, resources, ())

    def barrier(self, chans=()):
        for e in self.engs:
            for k in self.engs:
                if k == e or self.cnt[k] == 0:
                    continue
                if self.seen[e].get(k, 0) < self.cnt[k]:
                    self.engs[e].wait_ge(self.sem[k], self.cnt[k])
                    self.seen[e][k] = self.cnt[k]
            for c in chans:
                if c.cnt and self.seen[e].get(c.key, 0) < c.cnt:
                    self.engs[e].wait_ge(c.sem, c.cnt)
                    self.seen[e][c.key] = c.cnt
        if not hasattr(self, "free_chans"):
            self.free_chans = []
        for c in chans:
            if c not in self.free_chans:
                self.free_chans.append(c)


class Ctx:
    pass


def _tile(K, es, name, shape, dt):
    K.uid = getattr(K, "uid", 0) + 1
    return es.enter_context(K.nc.sbuf_tensor("%s_%d" % (name, K.uid), shape, dt))


def mm(K, out, lhsT, rhs, start, stop, reads, writes):
    return K.fw.op("pe", lambda e: e.matmul(out, lhsT=lhsT, rhs=rhs, start=start, stop=stop), reads=reads, writes=writes)


def phase_input(K, x_dram, XT):
    fw, nc = K.fw, K.nc
    with ExitStack() as es:
        xin = [_tile(K, es, "xin%d" % i, [128, D], F32) for i in range(3)]
        xo = [_tile(K, es, "xo%d" % i, [128, 8, 512], F32) for i in range(2)]
        r_in = [fw.res() for _ in range(3)]
        r_o = [fw.res() for _ in range(2)]
        c_in = [fw.chan() for _ in range(3)]
        c_o = [fw.chan() for _ in range(2)]
        XTv = XT.rearrange("c p n -> p c n")
        ti = 0
        for g in range(NT // 512):
            ob = g % 2
            for j in range(4):
                t = g * 4 + j
                ib = ti % 3
                ti += 1
                fw.dma("sp", c_in[ib], xin[ib][:], x_dram[t * 128:(t + 1) * 128, :], writes=[r_in[ib]])
                for half in range(2):
                    bank = K.bank()
                    for q in range(4):
                        c = half * 4 + q
                        fw.op("pe", lambda e: e.transpose(out=K.ps[:, bank, q * 128:(q + 1) * 128],
                                                          in_=xin[ib][:, c * 128:(c + 1) * 128], identity=K.identf[:]),
                              reads=[r_in[ib], K.r_const], writes=[K.r_ps[bank]])
                    src = K.ps[:, bank, :].rearrange("p (c n) -> p c n", c=4)
                    dst = xo[ob][:, half * 4:half * 4 + 4, j * 128:(j + 1) * 128]
                    if half == 0:
                        fw.op("act", lambda e: e.activation(out=dst, in_=src, func=AF.Copy), reads=[K.r_ps[bank]], writes=[r_o[ob]])
                    else:
                        fw.op("dve", lambda e: e.tensor_copy(out=dst, in_=src), reads=[K.r_ps[bank]], writes=[r_o[ob]])
            fw.dma("pool", c_o[ob], XTv[:, :, g * 512:(g + 1) * 512], xo[ob][:], reads=[r_o[ob]], writes=[K.r_XT[g]])
        fw.barrier(c_in + c_o)


def rms_rstd(K, x_tile, nchunk, n0, n, sq_tile, rstd_out, r_x, r_sq, r_rstd, tmp, r_tmp, dim):
    fw = K.fw
    fw.op("act", lambda e: e.activation(out=sq_tile[:, 0:nchunk, 0:n], in_=x_tile[:, 0:nchunk, n0:n0 + n], func=AF.Square),
          reads=r_x, writes=r_sq)
    bank = K.bank()
    for c in range(nchunk):
        mm(K, K.ps[:, bank, 0:n], K.onesb[:, :], sq_tile[:, c, 0:n], c == 0, c == nchunk - 1, [r_sq[c], K.r_const], [K.r_ps[bank]])
    fw.op("act", lambda e: e.activation(out=tmp[:, 0:n], in_=K.ps[:, bank, 0:n], func=AF.Sqrt, bias=K.epsb[:, 0:1], scale=1.0 / dim),
          reads=[K.r_ps[bank], K.r_const], writes=[r_tmp])
    fw.op("dve", lambda e: e.reciprocal(out=rstd_out, in_=tmp[:, 0:n]), reads=[r_tmp], writes=[r_rstd])


def phase_ffn(K, XT, gain_col, w13, w2):
    fw, nc = K.fw, K.nc
    TG = 1024
    NKC = 8
    NFC = DFF // 128
    NG = NT // TG
    with ExitStack() as es:
        xg = [_tile(K, es, "xg%d" % i, [128, 8, TG], F32) for i in range(2)]
        hT = _tile(K, es, "hT", [128, 8, TG], BF16)
        gT = _tile(K, es, "gT", [128, NFC, TG], BF16)
        sqs = _tile(K, es, "sqs", [128, 8, 512], BF16)
        rstd = _tile(K, es, "rstd", [128, TG], F32)
        tmp = _tile(K, es, "tmp", [128, 512], F32)
        wa = [_tile(K, es, "wa%d" % i, [128, 8, 512], BF16) for i in range(2)]
        wb = [_tile(K, es, "wb%d" % i, [128, 8, 512], BF16) for i in range(2)]
        NW2 = 3
        w2t = [_tile(K, es, "w2t%d" % i, [128, NFC, 128], BF16) for i in range(NW2)]
        sa = [_tile(K, es, "sa%d" % i, [128, 512], F32) for i in range(3)]
        r_rstd, r_tmp, r_sq = [fw.res() for _ in range(3)]
        r_xgc = [[fw.res() for _ in range(16)] for _ in range(2)]
        r_hTc = [fw.res() for _ in range(16)]
        r_gTc = [fw.res() for _ in range(NFC * 2)]
        r_w13 = [fw.res() for _ in range(2)]
        r_w2 = [fw.res() for _ in range(NW2)]
        r_sa = [fw.res() for _ in range(3)]
        c_x = [fw.chan() for _ in range(2)]
        c_xs = [fw.chan() for _ in range(2)]
        c_w13 = [fw.chan() for _ in range(2)]
        c_w2 = [fw.chan() for _ in range(NW2)]
        XTv = XT.rearrange("c p n -> p c n")
        fblocks = [(b * 512, 512) for b in range(5)] + [(2560, 256)]
        w13v = w13.rearrange("(c p) n -> p c n", p=128)
        w2v = w2.rearrange("(c p) n -> p c n", p=128)
        st = {"nw13": 0, "nw2": 0, "sai": 0}

        def load_w13(bi):
            f0, fwid = fblocks[bi]
            b = st["nw13"] % 2
            st["nw13"] += 1
            fw.dma("pool", c_w13[b], wa[b][:, :, 0:fwid], w13v[:, :, f0:f0 + fwid], writes=[r_w13[b]])
            fw.dma("pool", c_w13[b], wb[b][:, :, 0:fwid], w13v[:, :, DFF + f0:DFF + f0 + fwid], writes=[r_w13[b]])
            return b

        def load_w2(dmc):
            b = st["nw2"] % NW2
            st["nw2"] += 1
            fw.dma("pool", c_w2[b], w2t[b][:, :, :], w2v[:, :, dmc * 128:(dmc + 1) * 128], writes=[r_w2[b]])
            return b

        def load_x(g):
            xb = g % 2
            fw.dma("sp", c_x[xb], xg[xb][:], XTv[:, :, g * TG:(g + 1) * TG], reads=K.r_XT[2 * g:2 * g + 2], writes=r_xgc[xb])

        def norm(g):
            xb = g % 2
            for nt in range(TG // 512):
                rms_rstd(K, xg[xb], 8, nt * 512, 512, sqs, rstd[:, nt * 512:(nt + 1) * 512], [r_xgc[xb][c * 2 + nt] for c in range(8)],
                         [r_sq] * 8, r_rstd, tmp, r_tmp, D)
                for c in range(8):
                    fw.op("dve", lambda e: e.scalar_tensor_tensor(out=hT[:, c, nt * 512:(nt + 1) * 512], in0=xg[xb][:, c, nt * 512:(nt + 1) * 512],
                                                                   scalar=gain_col[:, c:c + 1], in1=rstd[:, nt * 512:(nt + 1) * 512],
                                                                   op0=ALU.mult, op1=ALU.mult),
                          reads=[r_xgc[xb][c * 2 + nt], r_rstd, K.r_const], writes=[r_hTc[c * 2 + nt]])

        load_x(0)
        wbuf = {0: load_w13(0), 1: load_w13(1)}
        norm(0)
        for g in range(NG):
            xb = g % 2
            if g + 1 < NG:
                load_x(g + 1)
            w2buf = {}
            for bi, (f0, fwid) in enumerate(fblocks):
                b = wbuf[bi]
                for fcl in range(fwid // 128):
                    fc = f0 // 128 + fcl
                    for nt in range(TG // 512):
                        ba, bb = K.bank(), K.bank()
                        for kc in range(NKC):
                            mm(K, K.ps[:, ba, :], wa[b][:, kc, fcl * 128:(fcl + 1) * 128], hT[:, kc, nt * 512:(nt + 1) * 512],
                               kc == 0, kc == NKC - 1, [r_w13[b], r_hTc[kc * 2 + nt]], [K.r_ps[ba]])
                        for kc in range(NKC):
                            mm(K, K.ps[:, bb, :], wb[b][:, kc, fcl * 128:(fcl + 1) * 128], hT[:, kc, nt * 512:(nt + 1) * 512],
                               kc == 0, kc == NKC - 1, [r_w13[b], r_hTc[kc * 2 + nt]], [K.r_ps[bb]])
                        s = st["sai"] % 3
                        st["sai"] += 1
                        fw.op("act", lambda e: e.activation(out=sa[s][:], in_=K.ps[:, ba, :], func=AF.Silu), reads=[K.r_ps[ba]], writes=[r_sa[s]])
                        rg = r_gTc[fc * 2 + nt]
                        fw.op("dve", lambda e: e.tensor_tensor(out=gT[:, fc, nt * 512:(nt + 1) * 512], in0=K.ps[:, bb, :], in1=sa[s][:], op=ALU.mult),
                              reads=[K.r_ps[bb], r_sa[s]], writes=[rg])
                if bi + 2 < len(fblocks):
                    wbuf[bi + 2] = load_w13(bi + 2)
                if bi >= 3:
                    w2buf[bi - 3] = load_w2(bi - 3)
            if g + 1 < NG:
                wbuf = {0: load_w13(0), 1: load_w13(1)}
                norm(g + 1)
            for dmc in range(8):
                b = w2buf[dmc]
                for nt in range(TG // 512):
                    by = K.bank()
                    for fc in range(NFC):
                        mm(K, K.ps[:, by, :], w2t[b][:, fc, :], gT[:, fc, nt * 512:(nt + 1) * 512],
                           fc == 0, fc == NFC - 1, [r_w2[b], r_gTc[fc * 2 + nt]], [K.r_ps[by]])
                    fw.op("dve", lambda e: e.scalar_tensor_tensor(out=xg[xb][:, dmc, nt * 512:(nt + 1) * 512], in0=K.ps[:, by, :], scalar=0.5,
                                                                   in1=xg[xb][:, dmc, nt * 512:(nt + 1) * 512], op0=ALU.mult, op1=ALU.add),
                          reads=[K.r_ps[by], r_xgc[xb][dmc * 2 + nt]], writes=[r_xgc[xb][dmc * 2 + nt]])
                if dmc + 3 < 8:
                    w2buf[dmc + 3] = load_w2(dmc + 3)
            fw.dma("sp", c_xs[xb], XTv[:, :, g * TG:(g + 1) * TG], xg[xb][:], reads=r_xgc[xb], writes=K.r_XT[2 * g:2 * g + 2])
        fw.barrier(c_x + c_xs + c_w13 + c_w2)


def phase_final(K, XT, gain_col, out_dram):
    fw, nc = K.fw, K.nc
    with ExitStack() as es:
        xg = [_tile(K, es, "fxg%d" % i, [128, 8, 512], F32) for i in range(2)]
        sq = _tile(K, es, "fsq", [128, 8, 512], BF16)
        yn = _tile(K, es, "fyn", [128, 8, 512], F32)
        rstd = _tile(K, es, "frstd", [128, 512], F32)
        tmp = _tile(K, es, "ftmp", [128, 512], F32)
        yo = [_tile(K, es, "fyo%d" % i, [128, D], F32) for i in range(2)]
        r_xg = [fw.res() for _ in range(2)]
        r_yo = [fw.res() for _ in range(2)]
        r_sq, r_yn, r_rstd, r_tmp = [fw.res() for _ in range(4)]
        c_x = [fw.chan() for _ in range(2)]
        c_o = [fw.chan() for _ in range(2)]
        XTv = XT.rearrange("c p n -> p c n")
        oi = 0
        for g in range(NT // 512):
            b = g % 2
            fw.dma("sp", c_x[b], xg[b][:], XTv[:, :, g * 512:(g + 1) * 512], reads=[K.r_XT[g]], writes=[r_xg[b]])
            rms_rstd(K, xg[b], 8, 0, 512, sq, rstd[:, :], [r_xg[b]], [r_sq] * 8, r_rstd, tmp, r_tmp, D)
            for c in range(8):
                fw.op("dve", lambda e: e.scalar_tensor_tensor(out=yn[:, c, :], in0=xg[b][:, c, :], scalar=gain_col[:, c:c + 1], in1=rstd[:, :],
                                                               op0=ALU.mult, op1=ALU.mult),
                      reads=[r_xg[b], r_rstd, K.r_const], writes=[r_yn])
            for j in range(4):
                ob = oi % 2
                oi += 1
                for half in range(2):
                    bank = K.bank()
                    for q in range(4):
                        c = half * 4 + q
                        fw.op("pe", lambda e: e.transpose(out=K.ps[:, bank, q * 128:(q + 1) * 128], in_=yn[:, c, j * 128:(j + 1) * 128],
                                                          identity=K.identf[:]),
                              reads=[r_yn, K.r_const], writes=[K.r_ps[bank]])
                    if half == 0:
                        fw.op("act", lambda e: e.activation(out=yo[ob][:, 0:512], in_=K.ps[:, bank, :], func=AF.Copy),
                              reads=[K.r_ps[bank]], writes=[r_yo[ob]])
                    else:
                        fw.op("dve", lambda e: e.tensor_copy(out=yo[ob][:, 512:1024], in_=K.ps[:, bank, :]),
                              reads=[K.r_ps[bank]], writes=[r_yo[ob]])
                t = g * 4 + j
                fw.dma("pool", c_o[ob], out_dram[t * 128:(t + 1) * 128, :], yo[ob][:], reads=[r_yo[ob]], writes=[K.r_out])
        fw.barrier(c_x + c_o)


VEC_COLS = {}


def _vec_layout():
    off = 0
    lay = {}
    for name, n in [("ffn1_norm", 16), ("mix_norm", 16), ("ffn2_norm", 16), ("final_norm", 8), ("fbias3", 1), ("mla_q_norm", 2),
                    ("mla_kv_norm", 1), ("gdn_conv", 96), ("gdn_dtb", 8), ("gdn_alog", 8), ("gdn_onorm", 128)]:
        lay[name] = (off, n)
        off += n
    return lay, off


def pack_vecs(inp):
    lay, nv = _vec_layout()
    v = np.zeros((128, nv), np.float32)

    def fm(a):
        a = np.asarray(a, np.float32).reshape(-1, 8, 128)
        return a.transpose(2, 0, 1).reshape(128, -1)
    for name in ["ffn1_norm", "mix_norm", "ffn2_norm", "final_norm"]:
        o, n = lay[name]
        v[:, o:o + n] = fm(inp[name])
    fb = np.asarray(inp["fox_f_bias"], np.float32)[0]
    for rep in range(3):
        v[rep * 32:rep * 32 + 8, lay["fbias3"][0]] = fb
    v[:, lay["mla_q_norm"][0]:lay["mla_q_norm"][0] + 2] = np.asarray(inp["mla_q_norm"], np.float32)[0].reshape(2, 128).T
    v[:, lay["mla_kv_norm"][0]] = np.asarray(inp["mla_kv_norm"], np.float32)[0]
    cw = np.asarray(inp["gdn_conv_w"], np.float32)[0].reshape(4, 24, 128)
    v[:, lay["gdn_conv"][0]:lay["gdn_conv"][0] + 96] = cw.transpose(2, 0, 1).reshape(128, 96)
    v[:, lay["gdn_dtb"][0]:lay["gdn_dtb"][0] + 8] = np.asarray(inp["gdn_dt_bias"], np.float32)[0][None, :]
    v[:, lay["gdn_alog"][0]:lay["gdn_alog"][0] + 8] = np.asarray(inp["gdn_a_log"], np.float32)[0][None, :]
    v[:, lay["gdn_onorm"][0]:lay["gdn_onorm"][0] + 128] = np.asarray(inp["gdn_out_norm"], np.float32)[0][None, :]
    return v


def host_consts(inp):
    c = {}
    selq = np.zeros((97, 8, 70), np.float32)
    selk = np.zeros((97, 8, 70), np.float32)
    for h in range(8):
        for p in range(3):
            selq[p * 32 + h, h, 64 + p] = 1.0
            selk[p * 32 + h, h, 67 + p] = -1.0
        selq[96, h, 67:70] = 1.0
        selk[96, h, 64:67] = 1.0
    c["selq"], c["selk"] = selq, selk
    s_idx = np.arange(128)[:, None]
    t_idx = np.arange(128)[None, :]
    c["mask_causal"] = np.where(s_idx <= t_idx, 0.0, NEG).astype(np.float32)
    c["mask_chunk"] = np.where((s_idx // 64) <= (t_idx // 64), 0.0, NEG).astype(np.float32)
    inv = 10000.0 ** (-np.arange(16, dtype=np.float32) / 16)
    ang = np.arange(SEQ, dtype=np.float32)[None, :] * inv[:, None]
    cos, sin = np.cos(ang).astype(np.float32), np.sin(ang).astype(np.float32)
    c["cos2"] = np.concatenate([cos, cos], 0)
    c["sin2s"] = np.concatenate([-sin, sin], 0)
    c["U"] = (s_idx <= t_idx).astype(np.float32)
    c["gmask1"] = np.where(t_idx >= s_idx, 0.0, NEG).astype(np.float32)
    c["gmask2"] = np.where(s_idx > t_idx, 0.0, BIG).astype(np.float32)
    c["gstrict"] = (s_idx > t_idx).astype(np.float32)
    wuq = np.asarray(inp["mla_w_uq"], np.float32)[0].reshape(256, 8, 96)
    c["mla_w_uqs"] = np.ascontiguousarray(np.concatenate([wuq[:, :, 0:64], wuq[:, :, 80:96], wuq[:, :, 64:80]], axis=2).reshape(256, 768))
    return c


CONST_SHAPES = {"selq": [97, 8, 70], "selk": [97, 8, 70], "mask_causal": [128, 128], "mask_chunk": [128, 128], "cos2": [32, SEQ],
                "sin2s": [32, SEQ], "mla_w_uqs": [256, 768], "U": [128, 128], "gmask1": [128, 128], "gmask2": [128, 128],
                "gstrict": [128, 128]}


W_SHAPES = {
    "ffn1_w13": [2, D, 2 * DFF], "ffn1_w2": [2, DFF, D], "ffn2_w13": [2, D, 2 * DFF], "ffn2_w2": [2, DFF, D],
    "attn_w_in": [1, D, 1960], "mla_w_uq": [1, 256, 768], "mla_w_ukv": [1, 128, 1024], "attn_w_out": [1, D, D],
    "gdn_w_in": [1, D, 4112], "gdn_w_out": [1, D, D],
}


def build(phases=("input", "ffn1_0", "final"), dbg=False):
    nc = bass.Bass("TRN2", target_bir_lowering=False)
    K = Ctx()
    K.nc = nc
    import os
    K.dbg_pairs = int(os.environ.get("DBG_PAIRS", "4"))
    K.dbg_heads = int(os.environ.get("DBG_HEADS", "2"))
    K.dbg_skip = os.environ.get("DBG_SKIP", "").split(",")
    lay, nv = _vec_layout()
    x = nc.dram_tensor("x", [NT, D], F32, kind="ExternalInput").ap()
    vecs_d = nc.dram_tensor("vecs", [128, nv], F32, kind="ExternalInput").ap()
    ident_d = nc.dram_tensor("identf", [128, 128], F32, kind="ExternalInput").ap()
    W = {k: nc.dram_tensor(k, shp, F32, kind="ExternalInput").ap() for k, shp in W_SHAPES.items()}
    C = {k: nc.dram_tensor(k, shp, F32, kind="ExternalInput").ap() for k, shp in CONST_SHAPES.items()}
    out = nc.dram_tensor("out", [NT, D], F32, kind="ExternalOutput").ap()
    XT = nc.dram_tensor("XT", [8, 128, NT], F32, kind="Internal").ap()
    HT = nc.dram_tensor("HT", [8, 128, NT], BF16, kind="Internal").ap()
    AT = nc.dram_tensor("AT", [8, 128, NT], BF16, kind="Internal").ap()
    S = {"QT": nc.dram_tensor("gQT", [8, 128, NT], BF16, kind="Internal").ap(),
         "KT": nc.dram_tensor("gKT", [8, 128, NT], BF16, kind="Internal").ap(),
         "Ktok": nc.dram_tensor("gKtok", [NT, D], BF16, kind="Internal").ap(),
         "Vtok": nc.dram_tensor("gVtok", [NT, D], BF16, kind="Internal").ap(),
         "Gtok": nc.dram_tensor("gGtok", [NT, D], BF16, kind="Internal").ap(),
         "BG": nc.dram_tensor("gBG", [NT, 16], F32, kind="Internal").ap()}
    with ExitStack() as es:
        fw = FW(nc, es)
        K.fw = fw
        K.ps = es.enter_context(nc.psum_tensor("ps", [128, 8, 512], F32))
        K.r_ps = [fw.res("ps%d" % i) for i in range(8)]
        for r_ in K.r_ps:
            r_.excl = True
        K._bank = 0

        def bank():
            b = K._bank
            K._bank = (b + 1) % 8
            return b
        K.bank = bank
        K.r_XT = [fw.res("XT%d" % i) for i in range(NT // 512)]
        K.r_out = fw.res("out")
        K.r_HT = [fw.res("HT%d" % i) for i in range(NT // 512)]
        K.r_AT = [fw.res("AT%d" % i) for i in range(2)]
        S["r_QK"] = [fw.res() for _ in range(2)]
        S["r_tok"] = [fw.res() for _ in range(2)]
        K.r_const = fw.res("const")
        K.identf = _tile(K, es, "identf_sb", [128, 128], F32)
        K.onesb = _tile(K, es, "onesb", [128, 128], BF16)
        K.epsb = _tile(K, es, "epsb", [128, 1], F32)
        K.oneb = _tile(K, es, "oneb", [128, 1], F32)
        K.onesf = _tile(K, es, "onesf", [128, 128], F32)
        K.identb = _tile(K, es, "identb", [128, 128], BF16)
        K.vecs = _tile(K, es, "vecs_sb", [128, nv], F32)
        c0 = fw.chan()
        fw.dma("sp", c0, K.identf[:], ident_d, writes=[K.r_const])
        fw.dma("sp", c0, K.vecs[:], vecs_d, writes=[K.r_const])
        fw.op("dve", lambda e: e.memset(K.onesb[:], 1.0), writes=[K.r_const])
        fw.op("dve", lambda e: e.memset(K.epsb[:], EPS), writes=[K.r_const])
        fw.op("dve", lambda e: e.memset(K.oneb[:], 1.0), writes=[K.r_const])
        fw.op("dve", lambda e: e.memset(K.onesf[:], 1.0), writes=[K.r_const])
        fw.op("dve", lambda e: e.tensor_copy(out=K.identb[:], in_=K.identf[:]), reads=[K.r_const], writes=[K.r_const])
        fw.barrier([c0])

        def vcol(name, layer=0, n=8):
            o, _ = lay[name]
            return K.vecs[:, o + layer * n:o + (layer + 1) * n]

        for ph in phases:
            if ph == "input":
                phase_input(K, x, XT)
            elif ph.startswith("ffn"):
                which, layer = ph[:4], int(ph[5:])
                phase_ffn(K, XT, vcol(which + "_norm", layer), W[which + "_w13"][layer], W[which + "_w2"][layer])
            elif ph.startswith("norm"):
                layer = int(ph[4:])
                phase_norm_to_ht(K, XT, vcol("mix_norm", layer), HT)
            elif ph == "fox":
                o = lay["fbias3"][0]
                phase_fox(K, HT, AT, W["attn_w_in"][0], K.vecs[:, o:o + 1], C)
            elif ph == "mla":
                oq, okv = lay["mla_q_norm"][0], lay["mla_kv_norm"][0]
                phase_mla(K, HT, AT, W["attn_w_in"][0], W["mla_w_uq"][0], C["mla_w_uqs"], W["mla_w_ukv"][0],
                          K.vecs[:, oq:oq + 2], K.vecs[:, okv:okv + 1], C)
            elif ph == "oproj0":
                phase_outproj(K, XT, AT, W["attn_w_out"][0])
            elif ph == "gdnp":
                oc, od, oa = lay["gdn_conv"][0], lay["gdn_dtb"][0], lay["gdn_alog"][0]
                phase_gdn_proj(K, HT, W["gdn_w_in"][0], K.vecs[:, oc:oc + 96], K.vecs[:, od:od + 8], K.vecs[:, oa:oa + 8], S)
            elif ph == "gdnc":
                oo = lay["gdn_onorm"][0]
                phase_gdn_core(K, S, AT, K.vecs[:, oo:oo + 128], C)
            elif ph == "oproj1":
                phase_outproj(K, XT, AT, W["gdn_w_out"][0])
            elif ph == "final":
                phase_final(K, XT, vcol("final_norm"), out)
        fw.finish("sp", [K.r_out])
    return nc


ALL_PHASES = ("input", "ffn1_0", "norm0", "fox", "mla", "oproj0", "ffn2_0", "ffn1_1", "norm1", "gdnp", "gdnc", "oproj1", "ffn2_1", "final")


def make_in_maps(inputs, n_cores=N_CORES):
    x = np.ascontiguousarray(np.asarray(inputs["x"], np.float32)).reshape(n_cores, NT, D)
    shared = {"vecs": pack_vecs(inputs), "identf": np.eye(128, dtype=np.float32)}
    shared.update(host_consts(inputs))
    for k in W_SHAPES:
        shared[k] = np.ascontiguousarray(np.asarray(inputs[k], np.float32))
    return [dict(shared, x=x[i]) for i in range(n_cores)]


def kernel(**inputs):
    nc = build(ALL_PHASES)
    in_maps = make_in_maps(inputs)
    res = run_bass_kernel_spmd(nc, in_maps, core_ids=list(range(N_CORES)))
    out = np.stack([np.asarray(r["out"]) for r in res.results], axis=0)
    return out.reshape(16, SEQ, D).astype(np.float32)


def phase_norm_to_ht(K, XT, gain_col, HT):
    fw = K.fw
    with ExitStack() as es:
        xg = [_tile(K, es, "nxg%d" % i, [128, 8, 512], F32) for i in range(2)]
        hb = [_tile(K, es, "nhb%d" % i, [128, 8, 512], BF16) for i in range(2)]
        sq = _tile(K, es, "nsq", [128, 8, 512], BF16)
        rstd = _tile(K, es, "nrstd", [128, 512], F32)
        tmp = _tile(K, es, "ntmp", [128, 512], F32)
        r_xg = [fw.res() for _ in range(2)]
        r_hb = [fw.res() for _ in range(2)]
        r_sq, r_rstd, r_tmp = [fw.res() for _ in range(3)]
        c_x = [fw.chan() for _ in range(2)]
        c_h = [fw.chan() for _ in range(2)]
        XTv = XT.rearrange("c p n -> p c n")
        HTv = HT.rearrange("c p n -> p c n")
        for g in range(NT // 512):
            b = g % 2
            fw.dma("sp", c_x[b], xg[b][:], XTv[:, :, g * 512:(g + 1) * 512], reads=[K.r_XT[g]], writes=[r_xg[b]])
            rms_rstd(K, xg[b], 8, 0, 512, sq, rstd[:, :], [r_xg[b]], [r_sq] * 8, r_rstd, tmp, r_tmp, D)
            for c in range(8):
                fw.op("dve", lambda e: e.scalar_tensor_tensor(out=hb[b][:, c, :], in0=xg[b][:, c, :], scalar=gain_col[:, c:c + 1], in1=rstd[:, :],
                                                               op0=ALU.mult, op1=ALU.mult),
                      reads=[r_xg[b], r_rstd, K.r_const], writes=[r_hb[b]])
            fw.dma("pool", c_h[b], HTv[:, :, g * 512:(g + 1) * 512], hb[b][:], reads=[r_hb[b]], writes=[K.r_HT[g]])
        fw.barrier(c_x + c_h)


def attention_head(K, A, qa, ka, KR, vlhs, M, obase, drow, maskT, reads_qkv, out_ap, out_res):
    fw = K.fw
    LOOK = 2
    for G in range(SEQ // 512):
        ob = A.obanks[A.oi % len(A.obanks)]
        A.oi += 1
        nkt = 4 * G + 4
        pend = []

        def score(i):
            r = i - 4 * G
            q0 = max(r, 0) * 128
            N = 512 - q0
            sb = A.sbanks[A.si % len(A.sbanks)]
            A.si += 1
            mm(K, K.ps[:, sb, 0:N], ka[0:KR, i * 128:(i + 1) * 128], qa[0:KR, G * 512 + q0:(G + 1) * 512], True, r < 0,
               reads_qkv, [K.r_ps[sb]])
            if r >= 0:
                mm(K, K.ps[:, sb, 0:128], K.identb[:, :], maskT, False, True, [K.r_const], [K.r_ps[sb]])
            pb = A.pi % len(A.pt)
            A.pi += 1
            fw.op("act", lambda e: e.activation(out=A.pt[pb][:, 0:N], in_=K.ps[:, sb, 0:N], func=AF.Exp),
                  reads=[K.r_ps[sb]], writes=[A.r_pt[pb]])
            pend.append((i, q0, N, pb))

        def pv():
            i, q0, N, pb = pend.pop(0)
            mm(K, K.ps[0:M, ob, q0:512], vlhs(i), A.pt[pb][:, 0:N], i == 0, i == nkt - 1, reads_qkv + [A.r_pt[pb]], [K.r_ps[ob]])

        for i in range(nkt):
            score(i)
            if len(pend) > LOOK:
                pv()
        while pend:
            pv()
        fw.op("dve", lambda e: e.reciprocal(out=A.rd[drow:drow + 1, :], in_=K.ps[drow:drow + 1, ob, :]), reads=[K.r_ps[ob]], writes=[A.r_rd])
        bb = A.bbanks[A.bi % len(A.bbanks)]
        A.bi += 1
        mm(K, K.ps[obase:obase + 64, bb, :], K.onesf[drow:drow + 1, 0:64], A.rd[drow:drow + 1, :], True, True, [A.r_rd, K.r_const], [K.r_ps[bb]])
        cb = A.ci % len(A.bcs)
        A.ci += 1
        fw.op("act", lambda e: e.activation(out=A.bcs[cb][obase:obase + 64, :], in_=K.ps[obase:obase + 64, bb, :], func=AF.Copy),
              reads=[K.r_ps[bb]], writes=[A.r_bcs[cb]])
        fw.op("dve", lambda e: e.tensor_tensor(out=out_ap(G), in0=K.ps[obase:obase + 64, ob, :], in1=A.bcs[cb][obase:obase + 64, :], op=ALU.mult),
              reads=[K.r_ps[ob], A.r_bcs[cb]], writes=[out_res(G)])


class AttnCtx:
    def __init__(self, K, es, tag):
        fw = K.fw
        self.pt = [_tile(K, es, "%spt%d" % (tag, i), [128, 512], BF16) for i in range(6)]
        self.r_pt = [fw.res() for _ in range(6)]
        self.rd = _tile(K, es, tag + "rd", [128, 512], F32)
        self.r_rd = fw.res()
        self.bcs = [_tile(K, es, "%sbcs%d" % (tag, i), [128, 512], F32) for i in range(2)]
        self.r_bcs = [fw.res() for _ in range(2)]
        self.sbanks, self.obanks, self.bbanks = [0, 1, 2, 3, 4], [5, 6], [7]
        self.si = self.oi = self.bi = self.pi = self.ci = 0


NEG = -30000.0
FOX_SCALE = 0.125
MLA_SCALE = 96 ** -0.5


def phase_fox(K, HT, AT, w_in, nfb_col, C):
    fw = K.fw
    with ExitStack() as es:
        A = AttnCtx(K, es, "fx")
        wp = [_tile(K, es, "fxwp%d" % i, [128, 8, 384], BF16) for i in range(2)]
        ht = [_tile(K, es, "fxht%d" % i, [128, 8, 512], BF16) for i in range(2)]
        qa = [_tile(K, es, "fxqa%d" % i, [128, 2, SEQ], BF16) for i in range(2)]
        ka = [_tile(K, es, "fxka%d" % i, [128, 2, SEQ], BF16) for i in range(2)]
        VE = [_tile(K, es, "fxVE%d" % i, [128, 16, 65], BF16) for i in range(2)]
        VO = [_tile(K, es, "fxVO%d" % i, [128, 16, 128], BF16) for i in range(2)]
        ao = [_tile(K, es, "fxao%d" % i, [128, SEQ], BF16) for i in range(2)]
        wf = _tile(K, es, "fxwf", [128, 8, 72], BF16)
        selq = _tile(K, es, "fxselq", [128, 8, 70], BF16)
        selk = _tile(K, es, "fxselk", [128, 8, 70], BF16)
        maskT = _tile(K, es, "fxmask", [128, 128], BF16)
        Fp = _tile(K, es, "fxFp", [128, SEQ], BF16)
        Ff = [_tile(K, es, "fxFf%d" % i, [128, 512], F32) for i in range(2)]
        sp = _tile(K, es, "fxsp", [128, 512], F32)
        ee = _tile(K, es, "fxee", [128, 512], F32)
        HI = _tile(K, es, "fxHI", [128, 512], BF16)
        MID = _tile(K, es, "fxMID", [128, 512], BF16)
        nfb = _tile(K, es, "fxnfb", [128, 1], F32)
        r_wp = [fw.res() for _ in range(2)]
        r_ht = [fw.res() for _ in range(2)]
        r_q = [[fw.res() for _ in range(2)] for _ in range(2)]
        r_k = [[fw.res() for _ in range(2)] for _ in range(2)]
        r_VE = [fw.res() for _ in range(2)]
        r_VO = [fw.res() for _ in range(2)]
        r_ao = [fw.res() for _ in range(2)]
        r_c, r_Fp, r_sp, r_ee, r_HI, r_MID = [fw.res() for _ in range(6)]
        r_Ff = [fw.res() for _ in range(2)]
        c_wp = [fw.chan() for _ in range(2)]
        c_ht = [fw.chan() for _ in range(2)]
        c_ao = [fw.chan() for _ in range(2)]
        c_c = fw.chan()
        w_inv = w_in.rearrange("(c p) n -> p c n", p=128)
        HTv = HT.rearrange("c p n -> p c n")
        fw.op("dve", lambda e: e.memset(wf[:], 0.0), writes=[r_c])
        for rep in range(3):
            fw.dma("pool", c_c, wf[:, :, rep * 32:rep * 32 + 8], w_inv[:, :, 1536:1544], writes=[r_c])
        fw.dma("pool", c_c, selq[0:97, :, :], C["selq"], writes=[r_c])
        fw.dma("pool", c_c, selk[0:97, :, :], C["selk"], writes=[r_c])
        fw.dma("pool", c_c, maskT[:], C["mask_causal"], writes=[r_c])
        fw.op("dve", lambda e: e.tensor_scalar(out=nfb[:], in0=nfb_col, scalar1=-1.0, scalar2=None, op0=ALU.mult), reads=[K.r_const], writes=[r_c])
        fw.op("pool", lambda e: e.memset(Fp[:], 0.0), writes=[r_Fp])
        fw.op("pool", lambda e: e.memset(Fp[96:97, :], 1.0), writes=[r_Fp])
        for b in range(2):
            fw.op("pool", lambda e: e.memset(VE[b][:, :, 64:65], 1.0), writes=[r_VE[b]])
            fw.op("pool", lambda e: e.memset(VO[b][:, :, 0:64], 0.0), writes=[r_VO[b]])
            fw.op("pool", lambda e: e.memset(VO[b][:, :, 0:1], 1.0), writes=[r_VO[b]])
        nht = 0
        npair = 0

        def load_ht(g):
            nonlocal nht
            b = nht % 2
            nht += 1
            fw.dma("sp", c_ht[b], ht[b][:], HTv[:, :, g * 512:(g + 1) * 512], reads=[K.r_HT[g]], writes=[r_ht[b]])
            return b

        for s in range(2):
            for nt in range(4):
                hb = load_ht(s * 4 + nt)
                bank = K.bank()
                for kc in range(8):
                    mm(K, K.ps[0:72, bank, :], wf[:, kc, :], ht[hb][:, kc, :], kc == 0, kc == 7, [r_c, r_ht[hb]], [K.r_ps[bank]])
                fw.op("act", lambda e: e.activation(out=ee[0:72, :], in_=K.ps[0:72, bank, :], func=AF.Exp, scale=-1.0, bias=nfb[0:72, 0:1]),
                      reads=[K.r_ps[bank], r_c], writes=[r_ee])
                fw.op("act", lambda e: e.activation(out=sp[0:72, :], in_=ee[0:72, :], func=AF.Ln, bias=K.oneb[0:72, 0:1], scale=1.0),
                      reads=[r_ee, K.r_const], writes=[r_sp])
                fb = nt % 2
                init = 0.0 if nt == 0 else Ff[1 - fb][0:72, 511:512]
                fw.op("dve", lambda e: e.tensor_tensor_scan(out=Ff[fb][0:72, :], data0=K.onesf[0:72, 0:1].broadcast_to([72, 512]), data1=sp[0:72, :],
                                                            initial=init, op0=ALU.mult, op1=ALU.subtract),
                      reads=[r_sp, K.r_const, r_Ff[1 - fb]], writes=[r_Ff[fb]])
                sl = slice(nt * 512, (nt + 1) * 512)
                fw.op("dve", lambda e: e.tensor_copy(out=HI[0:72, :], in_=Ff[fb][0:72, :]), reads=[r_Ff[fb]], writes=[r_HI])
                fw.op("dve", lambda e: e.tensor_tensor(out=sp[0:72, :], in0=Ff[fb][0:72, :], in1=HI[0:72, :], op=ALU.subtract),
                      reads=[r_Ff[fb], r_HI], writes=[r_sp])
                fw.op("dve", lambda e: e.tensor_copy(out=MID[0:72, :], in_=sp[0:72, :]), reads=[r_sp], writes=[r_MID])
                fw.op("dve", lambda e: e.tensor_tensor(out=sp[0:72, :], in0=sp[0:72, :], in1=MID[0:72, :], op=ALU.subtract),
                      reads=[r_sp, r_MID], writes=[r_sp])
                fw.op("pool", lambda e: e.tensor_copy(out=Fp[0:8, sl], in_=HI[0:8, :]), reads=[r_HI], writes=[r_Fp])
                fw.op("pool", lambda e: e.tensor_copy(out=Fp[32:40, sl], in_=MID[32:40, :]), reads=[r_MID], writes=[r_Fp])
                fw.op("pool", lambda e: e.tensor_copy(out=Fp[64:72, sl], in_=sp[64:72, :]), reads=[r_sp], writes=[r_Fp])
            for j in range(K.dbg_pairs):
                pb = npair % 2
                npair += 1
                fw.dma("pool", c_wp[pb], wp[pb][:, :, 0:128], w_inv[:, :, j * 128:(j + 1) * 128], writes=[r_wp[pb]])
                fw.dma("pool", c_wp[pb], wp[pb][:, :, 128:256], w_inv[:, :, 512 + j * 128:512 + (j + 1) * 128], writes=[r_wp[pb]])
                fw.dma("pool", c_wp[pb], wp[pb][:, :, 256:384], w_inv[:, :, 1024 + j * 128:1024 + (j + 1) * 128], writes=[r_wp[pb]])
                for nt in range(4):
                    hb = load_ht(s * 4 + nt)
                    sl = slice(nt * 512, (nt + 1) * 512)
                    for hh in range(2):
                        h = 2 * j + hh
                        bq = K.bank()
                        for kc in range(8):
                            mm(K, K.ps[0:64, bq, :], wp[pb][:, kc, hh * 64:(hh + 1) * 64], ht[hb][:, kc, :], kc == 0, kc == 7,
                               [r_wp[pb], r_ht[hb]], [K.r_ps[bq]])
                        fw.op("act", lambda e: e.activation(out=qa[pb][0:64, hh, sl], in_=K.ps[0:64, bq, :], func=AF.Copy, scale=FOX_SCALE),
                              reads=[K.r_ps[bq]], writes=[r_q[pb][hh]])
                        bk = K.bank()
                        for kc in range(8):
                            mm(K, K.ps[0:64, bk, :], wp[pb][:, kc, 128 + hh * 64:128 + (hh + 1) * 64], ht[hb][:, kc, :], kc == 0, kc == 7,
                               [r_wp[pb], r_ht[hb]], [K.r_ps[bk]])
                        fw.op("dve", lambda e: e.tensor_copy(out=ka[pb][0:64, hh, sl], in_=K.ps[0:64, bk, :]),
                              reads=[K.r_ps[bk]], writes=[r_k[pb][hh]])
                        if "sel" in K.dbg_skip:
                            continue
                        ba = K.bank()
                        mm(K, K.ps[0:70, ba, :], selq[0:97, h, :], Fp[0:97, sl], True, True, [r_c, r_Fp], [K.r_ps[ba]])
                        fw.op("dve", lambda e: e.tensor_copy(out=qa[pb][64:70, hh, sl], in_=K.ps[64:70, ba, :]),
                              reads=[K.r_ps[ba]], writes=[r_q[pb][hh]])
                        ba = K.bank()
                        mm(K, K.ps[0:70, ba, :], selk[0:97, h, :], Fp[0:97, sl], True, True, [r_c, r_Fp], [K.r_ps[ba]])
                        fw.op("dve", lambda e: e.tensor_copy(out=ka[pb][64:70, hh, sl], in_=K.ps[64:70, ba, :]),
                              reads=[K.r_ps[ba]], writes=[r_k[pb][hh]])
                    if "v" in K.dbg_skip:
                        continue
                    bv = K.bank()
                    for it in range(4):
                        for kc in range(8):
                            mm(K, K.ps[:, bv, it * 128:(it + 1) * 128], ht[hb][:, kc, it * 128:(it + 1) * 128], wp[pb][:, kc, 256:384],
                               kc == 0, kc == 7, [r_wp[pb], r_ht[hb]], [K.r_ps[bv]])
                    src = K.ps[:, bv, :].rearrange("p (t c) -> p t c", t=4)
                    if "vevac" in K.dbg_skip:
                        continue
                    fw.op("act", lambda e: e.activation(out=VE[pb][:, nt * 4:nt * 4 + 4, 0:64], in_=src[:, :, 0:64], func=AF.Copy),
                          reads=[K.r_ps[bv]], writes=[r_VE[pb]])
                    if "vevac2" in K.dbg_skip:
                        continue
                    fw.op("act", lambda e: e.activation(out=VO[pb][:, nt * 4:nt * 4 + 4, 64:128], in_=src[:, :, 64:128], func=AF.Copy),
                          reads=[K.r_ps[bv]], writes=[r_VO[pb]])
                for hh in range(K.dbg_heads):
                    rq = [r_q[pb][hh], r_k[pb][hh], r_VE[pb] if hh == 0 else r_VO[pb]]
                    if hh == 0:
                        attention_head(K, A, qa[pb][:, 0, :], ka[pb][:, 0, :], 70, lambda i: VE[pb][:, i, 0:65], 65, 0, 64, maskT[:, :], rq + [r_c],
                                       lambda G: ao[pb][0:64, G * 512:(G + 1) * 512], lambda G: r_ao[pb])
                    else:
                        attention_head(K, A, qa[pb][:, 1, :], ka[pb][:, 1, :], 70, lambda i: VO[pb][:, i, 0:128], 128, 64, 0, maskT[:, :], rq + [r_c],
                                       lambda G: ao[pb][64:128, G * 512:(G + 1) * 512], lambda G: r_ao[pb])
                if "at" not in K.dbg_skip:
                    fw.dma("sp", c_ao[pb], AT[j, :, s * SEQ:(s + 1) * SEQ], ao[pb][:], reads=[r_ao[pb]], writes=[K.r_AT[s]])
        fw.barrier(c_wp + c_ht + c_ao + [c_c])


def phase_outproj(K, XT, AT, w_out):
    fw = K.fw
    with ExitStack() as es:
        wo = _tile(K, es, "opw", [128, 8, D], BF16)
        at = [_tile(K, es, "opat%d" % i, [128, 8, 512], BF16) for i in range(2)]
        xg = [_tile(K, es, "opxg%d" % i, [128, 8, 512], F32) for i in range(2)]
        r_wo = fw.res()
        r_at = [fw.res() for _ in range(2)]
        r_xg = [[fw.res() for _ in range(8)] for _ in range(2)]
        c_wo = fw.chan()
        c_at = [fw.chan() for _ in range(2)]
        c_x = [fw.chan() for _ in range(2)]
        c_xs = [fw.chan() for _ in range(2)]
        XTv = XT.rearrange("c p n -> p c n")
        ATv = AT.rearrange("c p n -> p c n")
        fw.dma("pool", c_wo, wo[:], w_out.rearrange("(c p) n -> p c n", p=128), writes=[r_wo])
        for g in range(NT // 512):
            b = g % 2
            fw.dma("sp", c_at[b], at[b][:], ATv[:, :, g * 512:(g + 1) * 512], reads=[K.r_AT[g // 4]], writes=[r_at[b]])
            fw.dma("sp", c_x[b], xg[b][:], XTv[:, :, g * 512:(g + 1) * 512], reads=[K.r_XT[g]], writes=r_xg[b])
            for dmc in range(8):
                by = K.bank()
                for c in range(8):
                    mm(K, K.ps[:, by, :], wo[:, c, dmc * 128:(dmc + 1) * 128], at[b][:, c, :], c == 0, c == 7, [r_wo, r_at[b]], [K.r_ps[by]])
                fw.op("dve", lambda e: e.tensor_tensor(out=xg[b][:, dmc, :], in0=K.ps[:, by, :], in1=xg[b][:, dmc, :], op=ALU.add),
                      reads=[K.r_ps[by], r_xg[b][dmc]], writes=[r_xg[b][dmc]])
            fw.dma("pool", c_xs[b], XTv[:, :, g * 512:(g + 1) * 512], xg[b][:], reads=r_xg[b], writes=[K.r_XT[g]])
        fw.barrier([c_wo] + c_at + c_x + c_xs)


def phase_mla(K, HT, AT, w_in, w_uq, w_uqs, w_ukv, qn_col, kvn_col, C):
    fw = K.fw
    with ExitStack() as es:
        A = AttnCtx(K, es, "ml")
        ht = [_tile(K, es, "mlht%d" % i, [128, 8, 512], BF16) for i in range(2)]
        qa = [_tile(K, es, "mlqa%d" % i, [128, 2, SEQ], BF16) for i in range(2)]
        ka = [_tile(K, es, "mlka%d" % i, [128, 2, SEQ], BF16) for i in range(2)]
        VE = [_tile(K, es, "mlVE%d" % i, [128, 16, 65], BF16) for i in range(2)]
        VO = [_tile(K, es, "mlVO%d" % i, [128, 16, 128], BF16) for i in range(2)]
        ao = [_tile(K, es, "mlao%d" % i, [128, SEQ], BF16) for i in range(2)]
        wlat = _tile(K, es, "mlwlat", [128, 8, 384], BF16)
        wkpe = _tile(K, es, "mlwkpe", [128, 8, 96], BF16)
        wkpes = _tile(K, es, "mlwkpes", [128, 8, 96], BF16)
        wuq = _tile(K, es, "mlwuq", [128, 2, 768], BF16)
        wuqs = _tile(K, es, "mlwuqs", [128, 2, 768], BF16)
        wukv = _tile(K, es, "mlwukv", [128, 1024], BF16)
        maskT = _tile(K, es, "mlmask", [128, 128], BF16)
        cos2 = _tile(K, es, "mlcos", [128, SEQ], F32)
        sin2 = _tile(K, es, "mlsin", [128, SEQ], F32)
        cqn = _tile(K, es, "mlcqn", [128, 2, SEQ], BF16)
        ckvn = _tile(K, es, "mlckvn", [128, SEQ], BF16)
        kpe = _tile(K, es, "mlkpe", [128, SEQ], BF16)
        cqf = _tile(K, es, "mlcqf", [128, 3, 512], F32)
        sq = _tile(K, es, "mlsq", [128, 3, 512], BF16)
        rstd = [_tile(K, es, "mlrstd%d" % i, [128, 512], F32) for i in range(2)]
        tmp = _tile(K, es, "mltmp", [128, 512], F32)
        t1 = [_tile(K, es, "mlt1%d" % i, [128, 512], F32) for i in range(2)]
        t2 = [_tile(K, es, "mlt2%d" % i, [128, 512], F32) for i in range(2)]
        r_ht = [fw.res() for _ in range(2)]
        r_q = [[fw.res() for _ in range(2)] for _ in range(2)]
        r_k = [[fw.res() for _ in range(2)] for _ in range(2)]
        r_VE = [fw.res() for _ in range(2)]
        r_VO = [fw.res() for _ in range(2)]
        r_ao = [fw.res() for _ in range(2)]
        r_c, r_cqn, r_ckvn, r_kpe, r_cqf, r_sq, r_tmp = [fw.res() for _ in range(7)]
        r_rstd = [fw.res() for _ in range(2)]
        r_t1 = [fw.res() for _ in range(2)]
        r_t2 = [fw.res() for _ in range(2)]
        c_ht = [fw.chan() for _ in range(2)]
        c_ao = [fw.chan() for _ in range(2)]
        c_c = fw.chan()
        w_inv = w_in.rearrange("(c p) n -> p c n", p=128)
        HTv = HT.rearrange("c p n -> p c n")
        fw.dma("pool", c_c, wlat[:], w_inv[:, :, 1544:1928], writes=[r_c])
        fw.op("dve", lambda e: e.memset(wkpe[:], 0.0), writes=[r_c])
        fw.op("dve", lambda e: e.memset(wkpes[:], 0.0), writes=[r_c])
        fw.dma("pool", c_c, wkpe[:, :, 64:96], w_inv[:, :, 1928:1960], writes=[r_c])
        fw.dma("pool", c_c, wkpes[:, :, 64:80], w_inv[:, :, 1944:1960], writes=[r_c])
        fw.dma("pool", c_c, wkpes[:, :, 80:96], w_inv[:, :, 1928:1944], writes=[r_c])
        fw.dma("pool", c_c, wuq[:], w_uq.rearrange("(c p) n -> p c n", p=128), writes=[r_c])
        fw.dma("pool", c_c, wuqs[:], w_uqs.rearrange("(c p) n -> p c n", p=128), writes=[r_c])
        fw.dma("pool", c_c, wukv[:], w_ukv, writes=[r_c])
        fw.dma("pool", c_c, maskT[:], C["mask_chunk"], writes=[r_c])
        fw.dma("sp", c_c, cos2[64:96, :], C["cos2"], writes=[r_c])
        fw.dma("sp", c_c, sin2[64:96, :], C["sin2s"], writes=[r_c])
        for b in range(2):
            fw.op("pool", lambda e: e.memset(VE[b][:, :, 64:65], 1.0), writes=[r_VE[b]])
            fw.op("pool", lambda e: e.memset(VO[b][:, :, 0:64], 0.0), writes=[r_VO[b]])
            fw.op("pool", lambda e: e.memset(VO[b][:, :, 0:1], 1.0), writes=[r_VO[b]])
        nht = 0
        npair = 0
        ti = 0
        for s in range(2):
            for nt in range(4):
                hb = nht % 2
                nht += 1
                g = s * 4 + nt
                sl = slice(nt * 512, (nt + 1) * 512)
                fw.dma("sp", c_ht[hb], ht[hb][:], HTv[:, :, g * 512:(g + 1) * 512], reads=[K.r_HT[g]], writes=[r_ht[hb]])
                for c in range(3):
                    bank = K.bank()
                    for kc in range(8):
                        mm(K, K.ps[:, bank, :], wlat[:, kc, c * 128:(c + 1) * 128], ht[hb][:, kc, :], kc == 0, kc == 7, [r_c, r_ht[hb]], [K.r_ps[bank]])
                    fw.op("act", lambda e: e.activation(out=cqf[:, c, :], in_=K.ps[:, bank, :], func=AF.Copy), reads=[K.r_ps[bank]], writes=[r_cqf])
                rms_rstd(K, cqf[:, 0:2, :], 2, 0, 512, sq[:, 0:2, :], rstd[0][:, :], [r_cqf], [r_sq] * 2, r_rstd[0], tmp, r_tmp, 256)
                rms_rstd(K, cqf[:, 2:3, :], 1, 0, 512, sq[:, 2:3, :], rstd[1][:, :], [r_cqf], [r_sq], r_rstd[1], tmp, r_tmp, 128)
                for c in range(2):
                    fw.op("dve", lambda e: e.scalar_tensor_tensor(out=cqn[:, c, sl], in0=cqf[:, c, :], scalar=qn_col[:, c:c + 1], in1=rstd[0][:, :],
                                                                   op0=ALU.mult, op1=ALU.mult), reads=[r_cqf, r_rstd[0], K.r_const], writes=[r_cqn])
                fw.op("dve", lambda e: e.scalar_tensor_tensor(out=ckvn[:, sl], in0=cqf[:, 2, :], scalar=kvn_col[:, 0:1], in1=rstd[1][:, :],
                                                               op0=ALU.mult, op1=ALU.mult), reads=[r_cqf, r_rstd[1], K.r_const], writes=[r_ckvn])
                bA, bB = K.bank(), K.bank()
                for kc in range(8):
                    mm(K, K.ps[0:96, bA, :], wkpe[:, kc, :], ht[hb][:, kc, :], kc == 0, kc == 7, [r_c, r_ht[hb]], [K.r_ps[bA]])
                for kc in range(8):
                    mm(K, K.ps[0:96, bB, :], wkpes[:, kc, :], ht[hb][:, kc, :], kc == 0, kc == 7, [r_c, r_ht[hb]], [K.r_ps[bB]])
                tb = ti % 2
                ti += 1
                fw.op("dve", lambda e: e.tensor_tensor(out=t1[tb][64:96, :], in0=K.ps[64:96, bA, :], in1=cos2[64:96, sl], op=ALU.mult),
                      reads=[K.r_ps[bA], r_c], writes=[r_t1[tb]])
                fw.op("dve", lambda e: e.tensor_tensor(out=t2[tb][64:96, :], in0=K.ps[64:96, bB, :], in1=sin2[64:96, sl], op=ALU.mult),
                      reads=[K.r_ps[bB], r_c], writes=[r_t2[tb]])
                fw.op("pool", lambda e: e.tensor_tensor(out=kpe[64:96, sl], in0=t1[tb][64:96, :], in1=t2[tb][64:96, :], op=ALU.add),
                      reads=[r_t1[tb], r_t2[tb]], writes=[r_kpe])
            for j in range(4):
                pb = npair % 2
                npair += 1
                for nt in range(4):
                    sl = slice(nt * 512, (nt + 1) * 512)
                    for hh in range(2):
                        h = 2 * j + hh
                        bA, bB = K.bank(), K.bank()
                        for c in range(2):
                            mm(K, K.ps[0:96, bA, :], wuq[:, c, h * 96:(h + 1) * 96], cqn[:, c, sl], c == 0, c == 1, [r_c, r_cqn], [K.r_ps[bA]])
                        for c in range(2):
                            mm(K, K.ps[0:96, bB, :], wuqs[:, c, h * 96:(h + 1) * 96], cqn[:, c, sl], c == 0, c == 1, [r_c, r_cqn], [K.r_ps[bB]])
                        fw.op("act", lambda e: e.activation(out=qa[pb][0:64, hh, sl], in_=K.ps[0:64, bA, :], func=AF.Copy, scale=MLA_SCALE),
                              reads=[K.r_ps[bA]], writes=[r_q[pb][hh]])
                        tb = ti % 2
                        ti += 1
                        fw.op("dve", lambda e: e.scalar_tensor_tensor(out=t1[tb][64:96, :], in0=K.ps[64:96, bA, :], scalar=MLA_SCALE, in1=cos2[64:96, sl],
                                                                       op0=ALU.mult, op1=ALU.mult), reads=[K.r_ps[bA], r_c], writes=[r_t1[tb]])
                        fw.op("dve", lambda e: e.scalar_tensor_tensor(out=t2[tb][64:96, :], in0=K.ps[64:96, bB, :], scalar=MLA_SCALE, in1=sin2[64:96, sl],
                                                                       op0=ALU.mult, op1=ALU.mult), reads=[K.r_ps[bB], r_c], writes=[r_t2[tb]])
                        fw.op("pool", lambda e: e.tensor_tensor(out=qa[pb][64:96, hh, sl], in0=t1[tb][64:96, :], in1=t2[tb][64:96, :], op=ALU.add),
                              reads=[r_t1[tb], r_t2[tb]], writes=[r_q[pb][hh]])
                        bk = K.bank()
                        mm(K, K.ps[0:64, bk, :], wukv[:, h * 128:h * 128 + 64], ckvn[:, sl], True, True, [r_c, r_ckvn], [K.r_ps[bk]])
                        fw.op("act", lambda e: e.activation(out=ka[pb][0:64, hh, sl], in_=K.ps[0:64, bk, :], func=AF.Copy),
                              reads=[K.r_ps[bk]], writes=[r_k[pb][hh]])
                        fw.op("pool", lambda e: e.tensor_copy(out=ka[pb][64:96, hh, sl], in_=kpe[64:96, sl]), reads=[r_kpe], writes=[r_k[pb][hh]])
                    bv = K.bank()
                    vcols = wukv[:, 2 * j * 128:(2 * j + 2) * 128].rearrange("p (h c) -> p h c", h=2)[:, :, 64:128]
                    for it in range(4):
                        i = nt * 4 + it
                        mm(K, K.ps[:, bv, it * 128:(it + 1) * 128], ckvn[:, i * 128:(i + 1) * 128], vcols, True, True, [r_c, r_ckvn], [K.r_ps[bv]])
                    src = K.ps[:, bv, :].rearrange("p (t c) -> p t c", t=4)
                    fw.op("act", lambda e: e.activation(out=VE[pb][:, nt * 4:nt * 4 + 4, 0:64], in_=src[:, :, 0:64], func=AF.Copy),
                          reads=[K.r_ps[bv]], writes=[r_VE[pb]])
                    fw.op("act", lambda e: e.activation(out=VO[pb][:, nt * 4:nt * 4 + 4, 64:128], in_=src[:, :, 64:128], func=AF.Copy),
                          reads=[K.r_ps[bv]], writes=[r_VO[pb]])
                for hh in range(2):
                    rq = [r_q[pb][hh], r_k[pb][hh], r_VE[pb] if hh == 0 else r_VO[pb], r_c]
                    if hh == 0:
                        attention_head(K, A, qa[pb][:, 0, :], ka[pb][:, 0, :], 96, lambda i: VE[pb][:, i, 0:65], 65, 0, 64, maskT[:, :], rq,
                                       lambda G: ao[pb][0:64, G * 512:(G + 1) * 512], lambda G: r_ao[pb])
                    else:
                        attention_head(K, A, qa[pb][:, 1, :], ka[pb][:, 1, :], 96, lambda i: VO[pb][:, i, 0:128], 128, 64, 0, maskT[:, :], rq,
                                       lambda G: ao[pb][64:128, G * 512:(G + 1) * 512], lambda G: r_ao[pb])
                fw.dma("sp", c_ao[pb], AT[4 + j, :, s * SEQ:(s + 1) * SEQ], ao[pb][:], reads=[r_ao[pb]], writes=[K.r_AT[s]])
        fw.barrier(c_ht + c_ao + [c_c])


GDN_QSCALE = 128 ** -0.5


def phase_gdn_proj(K, HT, w_in, convw_col, dtb_bc, alog_bc, S):
    fw = K.fw
    with ExitStack() as es:
        hts = _tile(K, es, "gpht", [128, 8, NT], BF16)
        wblk = [_tile(K, es, "gpw%d" % i, [128, 8, 512], BF16) for i in range(2)]
        wba = _tile(K, es, "gpwba", [128, 8, 16], BF16)
        dg = [_tile(K, es, "gpdg%d" % i, [128, 4, 4, 128], BF16) for i in range(2)]
        xc = [_tile(K, es, "gpxc%d" % i, [128, 515], BF16) for i in range(4)]
        qs = [_tile(K, es, "gpqs%d" % i, [128, 512], F32) for i in range(4)]
        sq = [_tile(K, es, "gpsq%d" % i, [128, 512], BF16) for i in range(4)]
        tmp = [_tile(K, es, "gptmp%d" % i, [128, 512], F32) for i in range(4)]
        rinv = [_tile(K, es, "gprinv%d" % i, [128, 512], F32) for i in range(4)]
        qn = [_tile(K, es, "gpqn%d" % i, [128, 512], BF16) for i in range(4)]
        tt = [_tile(K, es, "gptt%d" % i, [128, 4, 128], BF16) for i in range(4)]
        gt = [_tile(K, es, "gpgt%d" % i, [128, D], BF16) for i in range(2)]
        bgt = [_tile(K, es, "gpbg%d" % i, [128, 16], F32) for i in range(2)]
        e1 = _tile(K, es, "gpe1", [128, 8], F32)
        nexpa = _tile(K, es, "gpnexpa", [128, 8], F32)
        r_ht = [fw.res() for _ in range(NT // 512)]
        r_w = [fw.res() for _ in range(2)]
        r_dg = [fw.res() for _ in range(2)]
        r_xc = [fw.res() for _ in range(4)]
        r_qs = [fw.res() for _ in range(4)]
        r_sq = [fw.res() for _ in range(4)]
        r_tmp = [fw.res() for _ in range(4)]
        r_rinv = [fw.res() for _ in range(4)]
        r_qn = [fw.res() for _ in range(4)]
        r_tt = [fw.res() for _ in range(4)]
        r_gt = [fw.res() for _ in range(2)]
        r_bg = [fw.res() for _ in range(2)]
        r_c, r_e1 = fw.res(), fw.res()
        c_ht = fw.chan()
        c_w = [fw.chan() for _ in range(2)]
        c_c = fw.chan()
        c_qn = [fw.chan() for _ in range(4)]
        c_tt = [fw.chan() for _ in range(4)]
        c_gt = [fw.chan() for _ in range(2)]
        c_bg = [fw.chan() for _ in range(2)]
        HTv = HT.rearrange("c p n -> p c n")
        w_inv = w_in.rearrange("(c p) n -> p c n", p=128)
        for g in range(NT // 512):
            fw.dma("sp", c_ht, hts[:, :, g * 512:(g + 1) * 512], HTv[:, :, g * 512:(g + 1) * 512], reads=[K.r_HT[g]], writes=[r_ht[g]])
        fw.dma("pool", c_c, wba[:], w_inv[:, :, 3072:3088], writes=[r_c])
        fw.op("act", lambda e: e.activation(out=nexpa[:], in_=alog_bc, func=AF.Exp), reads=[K.r_const], writes=[r_c])
        fw.op("dve", lambda e: e.tensor_scalar(out=nexpa[:], in0=nexpa[:], scalar1=-1.0, scalar2=None, op0=ALU.mult), reads=[r_c], writes=[r_c])
        nw = 0
        ci = 0
        qi = 0
        ti = 0

        def load_w(col0):
            nonlocal nw
            b = nw % 2
            nw += 1
            fw.dma("pool", c_w[b], wblk[b][:], w_inv[:, :, col0:col0 + 512], writes=[r_w[b]])
            return b

        wq = {0: load_w(0)}
        for blk in range(6):
            wb = wq[blk]
            if blk + 1 < 6:
                wq[blk + 1] = load_w((blk + 1) * 512)
            db = blk % 2
            for tap in range(4):
                for fcl in range(4):
                    fc = blk * 4 + fcl
                    col = tap * 24 + fc
                    fw.op("pool", lambda e: e.tensor_scalar(out=dg[db][:, tap, fcl, :], in0=K.identf[:, :], scalar1=convw_col[:, col:col + 1], scalar2=None,
                                                            op0=ALU.mult), reads=[K.r_const], writes=[r_dg[db]])
            kind = "q" if blk < 2 else ("k" if blk < 4 else "v")
            for s in range(2):
                for nt in range(4):
                    g = s * 4 + nt
                    pbk = []
                    for fcl in range(4):
                        bank = K.bank()
                        pbk.append(bank)
                        for kc in range(8):
                            mm(K, K.ps[:, bank, :], wblk[wb][:, kc, fcl * 128:(fcl + 1) * 128], hts[:, kc, g * 512:(g + 1) * 512], kc == 0, kc == 7,
                               [r_w[wb], r_ht[g]], [K.r_ps[bank]])
                    for fcl in range(4):
                        x = xc[fcl]
                        if nt == 0:
                            fw.op("pool", lambda e: e.memset(x[:, 0:3], 0.0), writes=[r_xc[fcl]])
                        else:
                            fw.op("pool", lambda e: e.tensor_copy(out=x[:, 0:3], in_=x[:, 512:515]), reads=[r_xc[fcl]], writes=[r_xc[fcl]])
                        fw.op("act", lambda e: e.activation(out=x[:, 3:515], in_=K.ps[:, pbk[fcl], :], func=AF.Copy), reads=[K.r_ps[pbk[fcl]]], writes=[r_xc[fcl]])
                    cbk = []
                    for fcl in range(4):
                        cb = K.bank()
                        cbk.append(cb)
                        for tap in range(4):
                            mm(K, K.ps[:, cb, :], dg[db][:, tap, fcl, :], xc[fcl][:, tap:tap + 512], tap == 0, tap == 3, [r_dg[db], r_xc[fcl]], [K.r_ps[cb]])
                    for fcl in range(4):
                        if kind == "v":
                            fw.op("act", lambda e: e.activation(out=qn[fcl][:], in_=K.ps[:, cbk[fcl], :], func=AF.Silu), reads=[K.r_ps[cbk[fcl]]], writes=[r_qn[fcl]])
                        else:
                            fw.op("act", lambda e: e.activation(out=qs[fcl][:], in_=K.ps[:, cbk[fcl], :], func=AF.Silu), reads=[K.r_ps[cbk[fcl]]], writes=[r_qs[fcl]])
                    if kind != "v":
                        sbk = []
                        for fcl in range(4):
                            fw.op("pool", lambda e: e.tensor_tensor(out=sq[fcl][:], in0=qs[fcl][:], in1=qs[fcl][:], op=ALU.mult), reads=[r_qs[fcl]], writes=[r_sq[fcl]])
                        for fcl in range(4):
                            sb = K.bank()
                            sbk.append(sb)
                            mm(K, K.ps[:, sb, :], K.onesb[:, :], sq[fcl][:], True, True, [r_sq[fcl], K.r_const], [K.r_ps[sb]])
                        for fcl in range(4):
                            fw.op("act", lambda e: e.activation(out=tmp[fcl][:], in_=K.ps[:, sbk[fcl], :], func=AF.Sqrt, bias=K.epsb[:, 0:1], scale=1.0),
                                  reads=[K.r_ps[sbk[fcl]], K.r_const], writes=[r_tmp[fcl]])
                        sc = GDN_QSCALE if kind == "q" else 1.0
                        for fcl in range(4):
                            fw.op("dve", lambda e: e.reciprocal(out=rinv[fcl][:], in_=tmp[fcl][:]), reads=[r_tmp[fcl]], writes=[r_rinv[fcl]])
                        for fcl in range(4):
                            fw.op("dve", lambda e: e.scalar_tensor_tensor(out=qn[fcl][:], in0=qs[fcl][:], scalar=sc, in1=rinv[fcl][:], op0=ALU.mult, op1=ALU.mult),
                                  reads=[r_qs[fcl], r_rinv[fcl]], writes=[r_qn[fcl]])
                    for fcl in range(4):
                        hd = (blk * 4 + fcl) % 8
                        if kind in ("q", "k"):
                            dst = S["QT"] if kind == "q" else S["KT"]
                            fw.dma("sp", c_qn[fcl], dst[hd, :, g * 512:(g + 1) * 512], qn[fcl][:], reads=[r_qn[fcl]], writes=[S["r_QK"][s]])
                    if kind in ("k", "v"):
                        tbk = []
                        for fcl in range(4):
                            tb = K.bank()
                            tbk.append(tb)
                            pbf = K.ps[:, tb, :].bitcast(BF16)
                            for it in range(4):
                                fw.op("pe", lambda e: e.transpose(out=pbf[:, it * 128:(it + 1) * 128], in_=qn[fcl][:, it * 128:(it + 1) * 128], identity=K.identb[:]),
                                      reads=[r_qn[fcl], K.r_const], writes=[K.r_ps[tb]])
                        for fcl in range(4):
                            hd = (blk * 4 + fcl) % 8
                            pbf = K.ps[:, tbk[fcl], :].bitcast(BF16)
                            fw.op("act", lambda e: e.activation(out=tt[fcl][:], in_=pbf[:, 0:512].rearrange("p (t c) -> p t c", t=4), func=AF.Copy),
                                  reads=[K.r_ps[tbk[fcl]]], writes=[r_tt[fcl]])
                            dst = S["Ktok"] if kind == "k" else S["Vtok"]
                            fw.dma("sp", c_tt[fcl], dst[g * 512:(g + 1) * 512, hd * 128:(hd + 1) * 128].rearrange("(t p) c -> p t c", p=128), tt[fcl][:],
                                   reads=[r_tt[fcl]], writes=[S["r_tok"][s]])
        wg = [load_w(3088), load_w(3088 + 512)]
        gi = 0
        for g in range(NT // 512):
            s = g // 4
            for it in range(4):
                tok0 = g * 512 + it * 128
                b_ = gi % 2
                gi += 1
                for half in range(2):
                    bank = K.bank()
                    for kc in range(8):
                        mm(K, K.ps[:, bank, :], hts[:, kc, tok0:tok0 + 128], wblk[wg[half]][:, kc, :], kc == 0, kc == 7, [r_w[wg[half]], r_ht[g]], [K.r_ps[bank]])
                    fw.op("act", lambda e: e.activation(out=gt[b_][:, half * 512:(half + 1) * 512], in_=K.ps[:, bank, :], func=AF.Silu),
                          reads=[K.r_ps[bank]], writes=[r_gt[b_]])
                fw.dma("sp", c_gt[b_], S["Gtok"][tok0:tok0 + 128, :], gt[b_][:], reads=[r_gt[b_]], writes=[S["r_tok"][s]])
                bank = K.bank()
                for kc in range(8):
                    mm(K, K.ps[:, bank, 0:16], hts[:, kc, tok0:tok0 + 128], wba[:, kc, :], kc == 0, kc == 7, [r_c, r_ht[g]], [K.r_ps[bank]])
                fw.op("act", lambda e: e.activation(out=bgt[b_][:, 0:8], in_=K.ps[:, bank, 0:8], func=AF.Sigmoid), reads=[K.r_ps[bank]], writes=[r_bg[b_]])
                fw.op("dve", lambda e: e.tensor_tensor(out=e1[:], in0=K.ps[:, bank, 8:16], in1=dtb_bc, op=ALU.add), reads=[K.r_ps[bank], K.r_const], writes=[r_e1])
                fw.op("act", lambda e: e.activation(out=e1[:], in_=e1[:], func=AF.Exp), reads=[r_e1], writes=[r_e1])
                fw.op("act", lambda e: e.activation(out=e1[:], in_=e1[:], func=AF.Ln, bias=K.oneb[:, 0:1], scale=1.0), reads=[r_e1, K.r_const], writes=[r_e1])
                fw.op("dve", lambda e: e.tensor_tensor(out=bgt[b_][:, 8:16], in0=e1[:], in1=nexpa[:], op=ALU.mult), reads=[r_e1, r_c], writes=[r_bg[b_]])
                fw.dma("sp", c_bg[b_], S["BG"][tok0:tok0 + 128, :], bgt[b_][:], reads=[r_bg[b_]], writes=[S["r_tok"][s]])
        fw.barrier([c_ht, c_c] + c_w + c_qn + c_tt + c_gt + c_bg)


BIG = 30000.0


def phase_gdn_core(K, S, AT, onorm_bc, C):
    fw = K.fw
    with ExitStack() as es:
        def T(name, dt=F32, n=1, shape=(128, 8, 128)):
            return [_tile(K, es, "gc%s%d" % (name, i), list(shape), dt) for i in range(n)]
        U, M1, SM = T("U", F32, 1, (128, 128))[0], T("M1", F32, 1, (128, 128))[0], T("SM", F32, 1, (128, 128))[0]
        def RL(n):
            return [fw.res() for _ in range(n)]
        allc = []
        r_c = fw.res()
        c_c = fw.chan()
        fw.dma("sp", c_c, U[:], C["U"], writes=[r_c])
        fw.dma("sp", c_c, M1[:], C["gmask1"], writes=[r_c])
        fw.dma("sp", c_c, SM[:], C["gstrict"], writes=[r_c])
        QTv = S["QT"].rearrange("h p n -> p h n")
        KTv = S["KT"].rearrange("h p n -> p h n")
        ATv = AT.rearrange("h p n -> p h n")

        def bc_h(ap8, lo, n=4):
            return ap8[:, lo:lo + n].unsqueeze(2).broadcast_to([128, n, 128])

        def bc_m(ap, n=4):
            return ap.unsqueeze(1).broadcast_to([128, n, 128])

        def stage(mmfn):
            banks = [K.bank(), K.bank()]
            for h in range(8):
                mmfn(h, K.ps[:, banks[h // 4], (h % 4) * 128:(h % 4 + 1) * 128], banks[h // 4])
            return banks

        def pv(bank):
            return K.ps[:, bank, :].rearrange("p (h c) -> p h c", h=4)

        def hs(hf):
            return slice(hf * 4, hf * 4 + 4)

        def chain(s):
            qT, kT = T("q", BF16, 2), T("k", BF16, 2)
            ktok, vtok, gtok = T("kt", BF16, 2), T("vt", BF16, 2), T("gt", BF16, 2)
            bg = T("bg", F32, 2, (128, 16))
            St, Sb = T("S")[0], T("Sb", BF16)[0]
            NGU, ET, Esb = T("NGU")[0], T("ET")[0], T("Esb")[0]
            F32R = mybir.dt.float32r
            Aa, Ab, Pp = T("A", F32R, 2), T("AT", F32R, 2), T("P", F32R, 2)
            vb, kbd, usb, ot, sqo, og = T("vb")[0], T("kbd")[0], T("usb")[0], T("ot")[0], T("sqo")[0], T("og")[0]
            ktl, wT, inT, vnw, ogb = T("ktl", BF16)[0], T("wT", BF16)[0], T("inT", BF16)[0], T("vnw", BF16)[0], T("ogb", BF16)[0]
            att = T("att", BF16, 2)
            sc = {n: T(n, F32, 1, (128, 8))[0] for n in ("gam", "eg", "gl", "dl", "et", "beg", "nbeta", "ng", "ss", "rstd")}

            r_in, r_att = RL(2), RL(2)
            r_sc, r_S, r_Sb = fw.res(), fw.res(), fw.res()
            r_NGU = fw.res()
            r_ET, r_Esb = RL(2), RL(2)
            r_A, r_B, r_P = [RL(2) for _ in range(2)], [RL(2) for _ in range(2)], [RL(2) for _ in range(2)]
            r_vb, r_kbd, r_ktl = fw.res(), fw.res(), fw.res()
            r_usb, r_wT, r_inT, r_vnw, r_ot = RL(2), RL(2), RL(2), RL(2), RL(2)
            r_sq, r_og, r_ogb = fw.res(), fw.res(), fw.res()
            c_in = [fw.chan() for _ in range(2)]
            c_att = [fw.chan() for _ in range(2)]
            allc.extend(c_in + c_att)
            def loads(bb):
                i_ = bb % 2
                t0_ = s * SEQ + bb * 128
                fw.dma("sp", c_in[i_], qT[i_][:], QTv[:, :, t0_:t0_ + 128], reads=[S["r_QK"][s]], writes=[r_in[i_]])
                fw.dma("sp", c_in[i_], kT[i_][:], KTv[:, :, t0_:t0_ + 128], reads=[S["r_QK"][s]], writes=[r_in[i_]])
                fw.dma("sp", c_in[i_], ktok[i_][:], S["Ktok"][t0_:t0_ + 128, :].rearrange("p (h c) -> p h c", h=8), reads=[S["r_tok"][s]], writes=[r_in[i_]])
                fw.dma("sp", c_in[i_], vtok[i_][:], S["Vtok"][t0_:t0_ + 128, :].rearrange("p (h c) -> p h c", h=8), reads=[S["r_tok"][s]], writes=[r_in[i_]])
                fw.dma("sp", c_in[i_], gtok[i_][:], S["Gtok"][t0_:t0_ + 128, :].rearrange("p (h c) -> p h c", h=8), reads=[S["r_tok"][s]], writes=[r_in[i_]])
                fw.dma("sp", c_in[i_], bg[i_][:], S["BG"][t0_:t0_ + 128, :], reads=[S["r_tok"][s]], writes=[r_in[i_]])

            fw.op("pool", lambda e: e.memset(St[:], 0.0), writes=[r_S])
            fw.op("pool", lambda e: e.memset(Sb[:], 0.0), writes=[r_Sb])
            for b in range(SEQ // 128):
                tok0 = s * SEQ + b * 128
                ib_ = b % 2
                rin = r_in[ib_]
                q_, k_, kt_, vt_, gt_, bg_ = qT[ib_], kT[ib_], ktok[ib_], vtok[ib_], gtok[ib_], bg[ib_]
                if b == 0:
                    loads(0)
                if b + 1 < SEQ // 128:
                    loads(b + 1)
                b1, b2 = K.bank(), K.bank()
                mm(K, K.ps[:, b1, 0:8], U[:, :], bg_[:, 8:16], True, True, [r_c, rin], [K.r_ps[b1]])
                mm(K, K.ps[:, b2, 0:8], K.onesf[:, :], bg_[:, 8:16], True, True, [K.r_const, rin], [K.r_ps[b2]])
                fw.op("dve", lambda e: e.tensor_copy(out=sc["gam"][:], in_=K.ps[:, b1, 0:8]), reads=[K.r_ps[b1]], writes=[r_sc])
                fw.op("act", lambda e: e.activation(out=sc["eg"][:], in_=K.ps[:, b1, 0:8], func=AF.Exp), reads=[K.r_ps[b1]], writes=[r_sc])
                fw.op("act", lambda e: e.activation(out=sc["gl"][:], in_=K.ps[:, b2, 0:8], func=AF.Exp), reads=[K.r_ps[b2]], writes=[r_sc])
                fw.op("dve", lambda e: e.tensor_tensor(out=sc["dl"][:], in0=K.ps[:, b2, 0:8], in1=sc["gam"][:], op=ALU.subtract),
                      reads=[K.r_ps[b2], r_sc], writes=[r_sc])
                fw.op("act", lambda e: e.activation(out=sc["et"][:], in_=sc["dl"][:], func=AF.Exp), reads=[r_sc], writes=[r_sc])
                fw.op("dve", lambda e: e.tensor_tensor(out=sc["beg"][:], in0=bg_[:, 0:8], in1=sc["eg"][:], op=ALU.mult), reads=[rin, r_sc], writes=[r_sc])
                fw.op("dve", lambda e: e.tensor_scalar(out=sc["nbeta"][:], in0=bg_[:, 0:8], scalar1=-1.0, scalar2=None, op0=ALU.mult), reads=[rin], writes=[r_sc])
                fw.op("dve", lambda e: e.tensor_scalar(out=sc["ng"][:], in0=bg_[:, 8:16], scalar1=-1.0, scalar2=None, op0=ALU.mult), reads=[rin], writes=[r_sc])
                fw.op("pool", lambda e: e.tensor_tensor(out=NGU[:], in0=bc_m(U[:, :], 8), in1=bc_h(sc["ng"], 0, 8), op=ALU.mult),
                      reads=[r_c, r_sc], writes=[r_NGU])

                yield
                def mm_p1(h, o, bk):
                    gb = bg_[:, 8 + h:9 + h].broadcast_to([128, 128])
                    mm(K, o, gb, U[:, :], True, False, [rin, r_c], [K.r_ps[bk]])
                    mm(K, o, NGU[:, h, :], K.onesf[:, :], False, False, [r_NGU, K.r_const], [K.r_ps[bk]])
                    mm(K, o, K.identf[:, :], M1[:, :], False, True, [K.r_const, r_c], [K.r_ps[bk]])
                bks = stage(mm_p1)
                for hf in range(2):
                    fw.op("act", lambda e: e.activation(out=ET[:, hs(hf), :], in_=pv(bks[hf]), func=AF.Exp), reads=[K.r_ps[bks[hf]]], writes=[r_ET[hf]])

                yield
                def mm_tr(h, o, bk):
                    fw.op("pe", lambda e: e.transpose(out=o, in_=ET[:, h, :], identity=K.identf[:]), reads=[r_ET[h // 4], K.r_const], writes=[K.r_ps[bk]])
                bks = stage(mm_tr)
                for hf in range(2):
                    fw.op("dve", lambda e: e.tensor_tensor(out=Esb[:, hs(hf), :], in0=pv(bks[hf]), in1=bc_m(SM[:, :]), op=ALU.mult),
                          reads=[K.r_ps[bks[hf]], r_c], writes=[r_Esb[hf]])
                    fw.op("pool", lambda e: e.tensor_tensor(out=Esb[:, hs(hf), :], in0=Esb[:, hs(hf), :], in1=bc_h(sc["nbeta"], hf * 4), op=ALU.mult),
                          reads=[r_Esb[hf], r_sc], writes=[r_Esb[hf]])

                yield
                def mm_kk(h, o, bk):
                    mm(K, o, k_[:, h, :], k_[:, h, :], True, True, [rin], [K.r_ps[bk]])
                bks = stage(mm_kk)
                ca = 0
                for hf in range(2):
                    fw.op("dve", lambda e: e.tensor_tensor(out=Aa[ca][:, hs(hf), :], in0=pv(bks[hf]), in1=Esb[:, hs(hf), :], op=ALU.mult),
                          reads=[K.r_ps[bks[hf]], r_Esb[hf]], writes=[r_A[ca][hf]])

                yield
                def mm_at(h, o, bk):
                    fw.op("pe", lambda e: e.transpose(out=o, in_=Aa[ca][:, h, :].bitcast(F32), identity=K.identf[:]), reads=[r_A[ca][h // 4], K.r_const], writes=[K.r_ps[bk]])
                bks = stage(mm_at)
                cb, cp = 0, 0
                for hf in range(2):
                    fw.op("act", lambda e: e.activation(out=Ab[cb][:, hs(hf), :], in_=pv(bks[hf]), func=AF.Copy), reads=[K.r_ps[bks[hf]]], writes=[r_B[cb][hf]])
                    fw.op("dve", lambda e: e.tensor_tensor(out=Pp[cp][:, hs(hf), :], in0=pv(bks[hf]), in1=bc_m(K.identf[:, :]), op=ALU.add),
                          reads=[K.r_ps[bks[hf]], K.r_const], writes=[r_P[cp][hf]])
                for lev in range(6):
                    na, nb_, np_ = 1 - ca, 1 - cb, 1 - cp

                    def mm_a2(h, o, bk):
                        mm(K, o, Ab[cb][:, h, :], Aa[ca][:, h, :], True, True, [r_B[cb][h // 4], r_A[ca][h // 4]], [K.r_ps[bk]])
                    bks = stage(mm_a2)
                    if lev < 5:
                        def mm_b2(h, o, bk):
                            mm(K, o, Aa[ca][:, h, :], Ab[cb][:, h, :], True, True, [r_B[cb][h // 4], r_A[ca][h // 4]], [K.r_ps[bk]])
                        bks2 = stage(mm_b2)
                    for hf in range(2):
                        fw.op("act", lambda e: e.activation(out=Aa[na][:, hs(hf), :], in_=pv(bks[hf]), func=AF.Copy), reads=[K.r_ps[bks[hf]]], writes=[r_A[na][hf]])
                    if lev < 5:
                        for hf in range(2):
                            fw.op("dve", lambda e: e.tensor_copy(out=Ab[nb_][:, hs(hf), :], in_=pv(bks2[hf])), reads=[K.r_ps[bks2[hf]]], writes=[r_B[nb_][hf]])

                    def mm_p(h, o, bk):
                        mm(K, o, Aa[na][:, h, :], Pp[cp][:, h, :], True, True, [r_A[na][h // 4], r_P[cp][h // 4]], [K.r_ps[bk]])
                    bks3 = stage(mm_p)
                    for hf in range(2):
                        fw.op("dve", lambda e: e.tensor_tensor(out=Pp[np_][:, hs(hf), :], in0=pv(bks3[hf]), in1=Pp[cp][:, hs(hf), :].bitcast(F32), op=ALU.add),
                              reads=[K.r_ps[bks3[hf]], r_P[cp][hf]], writes=[r_P[np_][hf]])
                    ca, cp = na, np_
                    yield
                    if lev < 5:
                        cb = nb_
                yield
                fw.op("pool", lambda e: e.tensor_tensor(out=vb[:], in0=vt_[:], in1=bc_h(bg_, 0, 8), op=ALU.mult), reads=[rin], writes=[r_vb])
                fw.op("pool", lambda e: e.tensor_tensor(out=kbd[:], in0=kt_[:], in1=bc_h(sc["beg"], 0, 8), op=ALU.mult), reads=[rin, r_sc], writes=[r_kbd])
                fw.op("pool", lambda e: e.tensor_tensor(out=ktl[:], in0=kt_[:], in1=bc_h(sc["et"], 0, 8), op=ALU.mult), reads=[rin, r_sc], writes=[r_ktl])

                yield
                def mm_u(h, o, bk):
                    mm(K, o, Pp[cp][:, h, :].bitcast(F32), vb[:, h, :], True, True, [r_P[cp][h // 4], r_vb], [K.r_ps[bk]])
                bks = stage(mm_u)

                def mm_w(h, o, bk):
                    mm(K, o, kbd[:, h, :], Pp[cp][:, h, :].bitcast(F32), True, True, [r_P[cp][h // 4], r_kbd], [K.r_ps[bk]])
                bks2 = stage(mm_w)
                for hf in range(2):
                    fw.op("act", lambda e: e.activation(out=usb[:, hs(hf), :], in_=pv(bks[hf]), func=AF.Copy), reads=[K.r_ps[bks[hf]]], writes=[r_usb[hf]])
                for hf in range(2):
                    fw.op("act", lambda e: e.activation(out=wT[:, hs(hf), :], in_=pv(bks2[hf]), func=AF.Copy), reads=[K.r_ps[bks2[hf]]], writes=[r_wT[hf]])

                yield
                def mm_kq(h, o, bk):
                    mm(K, o, k_[:, h, :], q_[:, h, :], True, True, [rin], [K.r_ps[bk]])
                bks = stage(mm_kq)
                for hf in range(2):
                    fw.op("dve", lambda e: e.tensor_tensor(out=inT[:, hs(hf), :], in0=pv(bks[hf]), in1=ET[:, hs(hf), :], op=ALU.mult),
                          reads=[K.r_ps[bks[hf]], r_ET[hf]], writes=[r_inT[hf]])

                yield
                def mm_ws(h, o, bk):
                    mm(K, o, wT[:, h, :], Sb[:, h, :], True, True, [r_wT[h // 4], r_Sb], [K.r_ps[bk]])
                bks = stage(mm_ws)
                for hf in range(2):
                    fw.op("dve", lambda e: e.tensor_tensor(out=vnw[:, hs(hf), :], in0=usb[:, hs(hf), :], in1=pv(bks[hf]), op=ALU.subtract),
                          reads=[K.r_ps[bks[hf]], r_usb[hf]], writes=[r_vnw[hf]])

                def mm_qs(h, o, bk):
                    mm(K, o, q_[:, h, :], Sb[:, h, :], True, True, [rin, r_Sb], [K.r_ps[bk]])
                bks = stage(mm_qs)
                for hf in range(2):
                    fw.op("dve", lambda e: e.tensor_tensor(out=ot[:, hs(hf), :], in0=pv(bks[hf]), in1=bc_h(sc["eg"], hf * 4), op=ALU.mult),
                          reads=[K.r_ps[bks[hf]], r_sc], writes=[r_ot[hf]])

                def mm_iv(h, o, bk):
                    mm(K, o, inT[:, h, :], vnw[:, h, :], True, True, [r_inT[h // 4], r_vnw[h // 4]], [K.r_ps[bk]])
                bks = stage(mm_iv)
                for hf in range(2):
                    fw.op("dve", lambda e: e.tensor_tensor(out=ot[:, hs(hf), :], in0=pv(bks[hf]), in1=ot[:, hs(hf), :], op=ALU.add),
                          reads=[K.r_ps[bks[hf]], r_ot[hf]], writes=[r_ot[hf]])

                def mm_kv(h, o, bk):
                    mm(K, o, ktl[:, h, :], vnw[:, h, :], True, True, [r_ktl, r_vnw[h // 4]], [K.r_ps[bk]])
                bks = stage(mm_kv)
                fw.op("pool", lambda e: e.tensor_tensor(out=St[:], in0=St[:], in1=bc_h(sc["gl"], 0, 8), op=ALU.mult), reads=[r_S, r_sc], writes=[r_S])
                for hf in range(2):
                    fw.op("dve", lambda e: e.tensor_tensor(out=St[:, hs(hf), :], in0=pv(bks[hf]), in1=St[:, hs(hf), :], op=ALU.add),
                          reads=[K.r_ps[bks[hf]], r_S], writes=[r_S])
                fw.op("pool", lambda e: e.tensor_copy(out=Sb[:], in_=St[:]), reads=[r_S], writes=[r_Sb])
                yield
                fw.op("pool", lambda e: e.tensor_tensor(out=sqo[:], in0=ot[:], in1=ot[:], op=ALU.mult), reads=r_ot, writes=[r_sq])
                fw.op("dve", lambda e: e.tensor_reduce(out=sc["ss"][:], in_=sqo[:], axis=mybir.AxisListType.X, op=ALU.add), reads=[r_sq], writes=[r_sc])
                fw.op("act", lambda e: e.activation(out=sc["rstd"][:], in_=sc["ss"][:], func=AF.Sqrt, bias=K.epsb[:, 0:1], scale=1.0 / 128),
                      reads=[r_sc, K.r_const], writes=[r_sc])
                fw.op("dve", lambda e: e.reciprocal(out=sc["rstd"][:], in_=sc["rstd"][:]), reads=[r_sc], writes=[r_sc])
                fw.op("pool", lambda e: e.tensor_tensor(out=og[:], in0=ot[:], in1=bc_h(sc["rstd"], 0, 8), op=ALU.mult), reads=r_ot + [r_sc], writes=[r_og])
                fw.op("pool", lambda e: e.tensor_tensor(out=og[:], in0=og[:], in1=bc_m(onorm_bc, 8), op=ALU.mult), reads=[r_og, K.r_const], writes=[r_og])
                fw.op("pool", lambda e: e.tensor_tensor(out=ogb[:], in0=og[:], in1=gt_[:], op=ALU.mult), reads=[r_og, rin], writes=[r_ogb])
                tb = K.bank()
                ptb = K.ps[:, tb, :].bitcast(BF16)
                for h in range(8):
                    fw.op("pe", lambda e: e.transpose(out=ptb[:, h * 128:(h + 1) * 128], in_=ogb[:, h, :], identity=K.identb[:]),
                          reads=[r_ogb, K.r_const], writes=[K.r_ps[tb]])
                ob_ = ib_
                fw.op("act", lambda e: e.activation(out=att[ob_][:], in_=ptb[:, :].rearrange("p (h c) -> p h c", h=8), func=AF.Copy),
                      reads=[K.r_ps[tb]], writes=[r_att[ob_]])
                fw.dma("pool", c_att[ob_], ATv[:, :, tok0:tok0 + 128], att[ob_][:], reads=[r_att[ob_]], writes=[K.r_AT[s]])
        gens = [chain(0), chain(1)]
        while gens:
            for g_ in list(gens):
                try:
                    next(g_)
                except StopIteration:
                    gens.remove(g_)
        fw.barrier(allc + [c_c])
```

```python
import numpy as np
from contextlib import ExitStack
import concourse.bass as bass
import concourse.mybir as mybir
from concourse.bass_utils import run_bass_kernel_spmd

F32 = mybir.dt.float32
BF16 = mybir.dt.bfloat16
AF = mybir.ActivationFunctionType
ALU = mybir.AluOpType

N_CORES = 8
D = 1024
NT = 4096
SEQ = 2048
DFF = 2816
EPS = 1e-6


class Res:
    __slots__ = ("name", "w", "r", "excl")

    def __init__(self, name):
        self.name = name
        self.excl = False
        self.w = None
        self.r = {}


class Chan:
    __slots__ = ("sem", "cnt", "key")

    def __init__(self, sem, key):
        self.sem = sem
        self.cnt = 0
        self.key = key


class FW:
    SELF_SYNC = {"pe": False, "act": True, "dve": True, "pool": True, "sp": False}

    def __init__(self, nc, es):
        self.nc = nc
        self.es = es
        self.engs = {"pe": nc.tensor, "act": nc.scalar, "dve": nc.vector, "pool": nc.gpsimd, "sp": nc.sync}
        self.sem = {k: es.enter_context(nc.semaphore("s_" + k)) for k in self.engs}
        self.cnt = {k: 0 for k in self.engs}
        self.seen = {k: {} for k in self.engs}
        self.nchan = 0
        self.nres = 0

    def res(self, name=None):
        self.nres += 1
        return Res(name or ("r%d" % self.nres))

    def chan(self):
        if getattr(self, "free_chans", None):
            return self.free_chans.pop()
        self.nchan += 1
        key = "c%d" % self.nchan
        return Chan(self.es.enter_context(self.nc.semaphore(key)), key)

    def _wait(self, e, reads, writes):
        need = {}

        def add(ev):
            if ev is None:
                return
            key, h, v = ev
            if key == e and not self.SELF_SYNC[e]:
                return
            if self.seen[e].get(key, 0) >= v:
                return
            if key not in need or need[key][1] < v:
                need[key] = (h, v)

        for r in reads:
            add(r.w)
        for r in writes:
            add(r.w)
            for ev in r.r.values():
                add(ev)
        for key, (h, v) in need.items():
            self.engs[e].wait_ge(h, v)
            self.seen[e][key] = v

    def _mark(self, ev, reads, writes):
        key = ev[0]
        for r in reads:
            old = r.r.get(key)
            if old is None or old[2] < ev[2]:
                r.r[key] = ev
        for r in writes:
            r.w = ev
            r.r = {}

    def op(self, e, fn, reads=(), writes=()):
        ex = [r for r in reads if r.excl]
        if ex:
            reads = [r for r in reads if not r.excl]
            writes = list(writes) + ex
        self._wait(e, reads, writes)
        ins = fn(self.engs[e])
        self.cnt[e] += 1
        ins.then_inc(self.sem[e], 1)
        self._mark((e, self.sem[e], self.cnt[e]), reads, writes)
        return ins

    def dma(self, e, chan, out, in_, reads=(), writes=()):
        self._wait(e, reads, writes)
        ins = self.engs[e].dma_start(out=out, in_=in_)
        chan.cnt += 16
        ins.then_inc(chan.sem, 16)
        self._mark((chan.key, chan.sem, chan.cnt), reads, writes)
        return ins

    def finish(self, e, resources):
        self._wait(e, resources, ())

    def barrier(self, chans=()):
        for e in self.engs:
            for k in self.engs:
                if k == e or self.cnt[k] == 0:
                    continue
                if self.seen[e].get(k, 0) < self.cnt[k]:
                    self.engs[e].wait_ge(self.sem[k], self.cnt[k])
                    self.seen[e][k] = self.cnt[k]
            for c in chans:
                if c.cnt and self.seen[e].get(c.key, 0) < c.cnt:
                    self.engs[e].wait_ge(c.sem, c.cnt)
                    self.seen[e][c.key] = c.cnt
        if not hasattr(self, "free_chans"):
            self.free_chans = []
        for c in chans:
            if c not in self.free_chans:
                self.free_chans.append(c)


class Ctx:
    pass


def _tile(K, es, name, shape, dt):
    K.uid = getattr(K, "uid", 0) + 1
    return es.enter_context(K.nc.sbuf_tensor("%s_%d" % (name, K.uid), shape, dt))


def mm(K, out, lhsT, rhs, start, stop, reads, writes):
    return K.fw.op("pe", lambda e: e.matmul(out, lhsT=lhsT, rhs=rhs, start=start, stop=stop), reads=reads, writes=writes)


def phase_input(K, x_dram, XT):
    fw, nc = K.fw, K.nc
    with ExitStack() as es:
        xin = [_tile(K, es, "xin%d" % i, [128, D], F32) for i in range(3)]
        xo = [_tile(K, es, "xo%d" % i, [128, 8, 512], F32) for i in range(2)]
        r_in = [fw.res() for _ in range(3)]
        r_o = [fw.res() for _ in range(2)]
        c_in = [fw.chan() for _ in range(3)]
        c_o = [fw.chan() for _ in range(2)]
        XTv = XT.rearrange("c p n -> p c n")
        ti = 0
        for g in range(NT // 512):
            ob = g % 2
            for j in range(4):
                t = g * 4 + j
                ib = ti % 3
                ti += 1
                fw.dma("sp", c_in[ib], xin[ib][:], x_dram[t * 128:(t + 1) * 128, :], writes=[r_in[ib]])
                for half in range(2):
                    bank = K.bank()
                    for q in range(4):
                        c = half * 4 + q
                        fw.op("pe", lambda e: e.transpose(out=K.ps[:, bank, q * 128:(q + 1) * 128],
                                                          in_=xin[ib][:, c * 128:(c + 1) * 128], identity=K.identf[:]),
                              reads=[r_in[ib], K.r_const], writes=[K.r_ps[bank]])
                    src = K.ps[:, bank, :].rearrange("p (c n) -> p c n", c=4)
                    dst = xo[ob][:, half * 4:half * 4 + 4, j * 128:(j + 1) * 128]
                    if half == 0:
                        fw.op("act", lambda e: e.activation(out=dst, in_=src, func=AF.Copy), reads=[K.r_ps[bank]], writes=[r_o[ob]])
                    else:
                        fw.op("dve", lambda e: e.tensor_copy(out=dst, in_=src), reads=[K.r_ps[bank]], writes=[r_o[ob]])
            fw.dma("pool", c_o[ob], XTv[:, :, g * 512:(g + 1) * 512], xo[ob][:], reads=[r_o[ob]], writes=[K.r_XT[g]])
        fw.barrier(c_in + c_o)


def rms_rstd(K, x_tile, nchunk, n0, n, sq_tile, rstd_out, r_x, r_sq, r_rstd, tmp, r_tmp, dim):
    fw = K.fw
    fw.op("act", lambda e: e.activation(out=sq_tile[:, 0:nchunk, 0:n], in_=x_tile[:, 0:nchunk, n0:n0 + n], func=AF.Square),
          reads=r_x, writes=r_sq)
    bank = K.bank()
    for c in range(nchunk):
        mm(K, K.ps[:, bank, 0:n], K.onesb[:, :], sq_tile[:, c, 0:n], c == 0, c == nchunk - 1, [r_sq[c], K.r_const], [K.r_ps[bank]])
    fw.op("act", lambda e: e.activation(out=tmp[:, 0:n], in_=K.ps[:, bank, 0:n], func=AF.Sqrt, bias=K.epsb[:, 0:1], scale=1.0 / dim),
          reads=[K.r_ps[bank], K.r_const], writes=[r_tmp])
    fw.op("dve", lambda e: e.reciprocal(out=rstd_out, in_=tmp[:, 0:n]), reads=[r_tmp], writes=[r_rstd])


def phase_ffn(K, XT, gain_col, w13, w2, post_norm=None, final=None):
    fw, nc = K.fw, K.nc
    TG = 1024
    NKC = 8
    NFC = DFF // 128
    NG = NT // TG
    with ExitStack() as es:
        xg = [_tile(K, es, "xg%d" % i, [128, 8, TG], F32) for i in range(2)]
        hT = _tile(K, es, "hT", [128, 8, TG], BF16)
        gT = _tile(K, es, "gT", [128, NFC, TG], BF16)
        sqs = _tile(K, es, "sqs", [128, 8, 512], BF16)
        rstd = _tile(K, es, "rstd", [128, TG], F32)
        tmp = _tile(K, es, "tmp", [128, 512], F32)
        wa = [_tile(K, es, "wa%d" % i, [128, 8, 512], BF16) for i in range(2)]
        wb = [_tile(K, es, "wb%d" % i, [128, 8, 512], BF16) for i in range(2)]
        NW2 = 3 if final is None else 2
        w2t = [_tile(K, es, "w2t%d" % i, [128, NFC, 128], BF16) for i in range(NW2)]
        sa = [_tile(K, es, "sa%d" % i, [128, 512], F32) for i in range(3)]
        if post_norm is not None or final is not None:
            hb = [_tile(K, es, "pnhb%d" % i, [128, 8, 256], BF16 if final is None else F32) for i in range(2 if final is None else 1)]
            prs = _tile(K, es, "pnrstd", [128, 256], F32)
            r_hb = [fw.res() for _ in range(2)]
            r_prs = fw.res()
            c_hb = [fw.chan() for _ in range(2)]
        else:
            c_hb = []
        if final is not None:
            yo = [_tile(K, es, "fyo%d" % i, [128, D], F32) for i in range(1)]
            r_yo = [fw.res() for _ in range(1)]
            c_yo = [fw.chan() for _ in range(1)]
        else:
            c_yo = []
        r_rstd, r_tmp, r_sq = [fw.res() for _ in range(3)]
        r_xgc = [[fw.res() for _ in range(16)] for _ in range(2)]
        r_hTc = [fw.res() for _ in range(16)]
        r_gTc = [fw.res() for _ in range(NFC * 2)]
        r_w13 = [fw.res() for _ in range(2)]
        r_w2 = [fw.res() for _ in range(NW2)]
        r_sa = [fw.res() for _ in range(3)]
        c_x = [fw.chan() for _ in range(2)]
        c_xs = [fw.chan() for _ in range(2)]
        c_w13 = [fw.chan() for _ in range(2)]
        c_w2 = [fw.chan() for _ in range(NW2)]
        XTv = XT.rearrange("c p n -> p c n")
        fblocks = [(b * 512, 512) for b in range(5)] + [(2560, 256)]
        w13v = w13.rearrange("(c p) n -> p c n", p=128)
        w2v = w2.rearrange("(c p) n -> p c n", p=128)
        st = {"nw13": 0, "nw2": 0, "sai": 0}

        def load_w13(bi):
            f0, fwid = fblocks[bi]
            b = st["nw13"] % 2
            st["nw13"] += 1
            fw.dma("pool", c_w13[b], wa[b][:, :, 0:fwid], w13v[:, :, f0:f0 + fwid], writes=[r_w13[b]])
            fw.dma("pool", c_w13[b], wb[b][:, :, 0:fwid], w13v[:, :, DFF + f0:DFF + f0 + fwid], writes=[r_w13[b]])
            return b

        def load_w2(dmc):
            b = st["nw2"] % NW2
            st["nw2"] += 1
            fw.dma("pool", c_w2[b], w2t[b][:, :, :], w2v[:, :, dmc * 128:(dmc + 1) * 128], writes=[r_w2[b]])
            return b

        def load_x(g):
            xb = g % 2
            fw.dma("sp", c_x[xb], xg[xb][:], XTv[:, :, g * TG:(g + 1) * TG], reads=K.r_XT[2 * g:2 * g + 2], writes=r_xgc[xb])

        def norm(g):
            xb = g % 2
            for nt in range(TG // 512):
                rms_rstd(K, xg[xb], 8, nt * 512, 512, sqs, rstd[:, nt * 512:(nt + 1) * 512], [r_xgc[xb][c * 2 + nt] for c in range(8)],
                         [r_sq] * 8, r_rstd, tmp, r_tmp, D)
                for c in range(8):
                    fw.op("dve", lambda e: e.scalar_tensor_tensor(out=hT[:, c, nt * 512:(nt + 1) * 512], in0=xg[xb][:, c, nt * 512:(nt + 1) * 512],
                                                                   scalar=gain_col[:, c:c + 1], in1=rstd[:, nt * 512:(nt + 1) * 512],
                                                                   op0=ALU.mult, op1=ALU.mult),
                          reads=[r_xgc[xb][c * 2 + nt], r_rstd, K.r_const], writes=[r_hTc[c * 2 + nt]])

        load_x(0)
        wbuf = {0: load_w13(0), 1: load_w13(1)}
        norm(0)
        for g in range(NG):
            xb = g % 2
            if g + 1 < NG:
                load_x(g + 1)
            w2buf = {}
            for bi, (f0, fwid) in enumerate(fblocks):
                b = wbuf[bi]
                for fcl in range(fwid // 128):
                    fc = f0 // 128 + fcl
                    for nt in range(TG // 512):
                        ba, bb = K.bank(), K.bank()
                        for kc in range(NKC):
                            mm(K, K.ps[:, ba, :], wa[b][:, kc, fcl * 128:(fcl + 1) * 128], hT[:, kc, nt * 512:(nt + 1) * 512],
                               kc == 0, kc == NKC - 1, [r_w13[b], r_hTc[kc * 2 + nt]], [K.r_ps[ba]])
                        for kc in range(NKC):
                            mm(K, K.ps[:, bb, :], wb[b][:, kc, fcl * 128:(fcl + 1) * 128], hT[:, kc, nt * 512:(nt + 1) * 512],
                               kc == 0, kc == NKC - 1, [r_w13[b], r_hTc[kc * 2 + nt]], [K.r_ps[bb]])
                        s = st["sai"] % 3
                        st["sai"] += 1
                        fw.op("act", lambda e: e.activation(out=sa[s][:], in_=K.ps[:, ba, :], func=AF.Silu), reads=[K.r_ps[ba]], writes=[r_sa[s]])
                        rg = r_gTc[fc * 2 + nt]
                        fw.op("dve", lambda e: e.tensor_tensor(out=gT[:, fc, nt * 512:(nt + 1) * 512], in0=K.ps[:, bb, :], in1=sa[s][:], op=ALU.mult),
                              reads=[K.r_ps[bb], r_sa[s]], writes=[rg])
                if bi + 2 < len(fblocks):
                    wbuf[bi + 2] = load_w13(bi + 2)
                if bi >= 3 and bi - 3 < NW2:
                    w2buf[bi - 3] = load_w2(bi - 3)
            if g + 1 < NG:
                wbuf = {0: load_w13(0), 1: load_w13(1)}
                norm(g + 1)
            for dmc in range(8):
                b = w2buf[dmc]
                for nt in range(TG // 512):
                    by = K.bank()
                    for fc in range(NFC):
                        mm(K, K.ps[:, by, :], w2t[b][:, fc, :], gT[:, fc, nt * 512:(nt + 1) * 512],
                           fc == 0, fc == NFC - 1, [r_w2[b], r_gTc[fc * 2 + nt]], [K.r_ps[by]])
                    fw.op("dve", lambda e: e.scalar_tensor_tensor(out=xg[xb][:, dmc, nt * 512:(nt + 1) * 512], in0=K.ps[:, by, :], scalar=0.5,
                                                                   in1=xg[xb][:, dmc, nt * 512:(nt + 1) * 512], op0=ALU.mult, op1=ALU.add),
                          reads=[K.r_ps[by], r_xgc[xb][dmc * 2 + nt]], writes=[r_xgc[xb][dmc * 2 + nt]])
                if dmc + NW2 < 8:
                    w2buf[dmc + NW2] = load_w2(dmc + NW2)
            if final is None:
                fw.dma("sp", c_xs[xb], XTv[:, :, g * TG:(g + 1) * TG], xg[xb][:], reads=r_xgc[xb], writes=K.r_XT[2 * g:2 * g + 2])
            if post_norm is not None or final is not None:
                pg, dst = post_norm if final is None else final
                for nt in range(TG // 256):
                    tg = g * (TG // 256) + nt
                    hbi = tg % len(hb)
                    nsl = slice(nt * 256, (nt + 1) * 256)
                    rxs = [r_xgc[xb][c * 2 + nt // 2] for c in range(8)]
                    rms_rstd(K, xg[xb], 8, nt * 256, 256, sqs, prs[:, :], rxs, [r_sq] * 8, r_prs, tmp, r_tmp, D)
                    for c in range(8):
                        fw.op("dve", lambda e: e.scalar_tensor_tensor(out=hb[hbi][:, c, :], in0=xg[xb][:, c, nsl], scalar=pg[:, c:c + 1],
                                                                       in1=prs[:, :], op0=ALU.mult, op1=ALU.mult),
                              reads=[rxs[c], r_prs, K.r_const], writes=[r_hb[hbi]])
                    if final is None:
                        fw.dma("pool", c_hb[hbi], dst.rearrange("c p n -> p c n")[:, :, tg * 256:(tg + 1) * 256], hb[hbi][:], reads=[r_hb[hbi]],
                               writes=[K.r_HT[tg // 2]])
                    else:
                        for j in range(2):
                            ob = 0
                            for half in range(2):
                                bank = K.bank()
                                for q in range(4):
                                    c = half * 4 + q
                                    fw.op("pe", lambda e: e.transpose(out=K.ps[:, bank, q * 128:(q + 1) * 128], in_=hb[hbi][:, c, j * 128:(j + 1) * 128],
                                                                      identity=K.identf[:]), reads=[r_hb[hbi], K.r_const], writes=[K.r_ps[bank]])
                                if half == 0:
                                    fw.op("act", lambda e: e.activation(out=yo[ob][:, 0:512], in_=K.ps[:, bank, :], func=AF.Copy),
                                          reads=[K.r_ps[bank]], writes=[r_yo[ob]])
                                else:
                                    fw.op("dve", lambda e: e.tensor_copy(out=yo[ob][:, 512:1024], in_=K.ps[:, bank, :]),
                                          reads=[K.r_ps[bank]], writes=[r_yo[ob]])
                            t = tg * 2 + j
                            fw.dma("pool", c_yo[ob], dst[t * 128:(t + 1) * 128, :], yo[ob][:], reads=[r_yo[ob]], writes=[K.r_out])
        fw.barrier(c_x + c_xs + c_w13 + c_w2 + c_hb + c_yo)


def phase_final(K, XT, gain_col, out_dram):
    fw, nc = K.fw, K.nc
    with ExitStack() as es:
        xg = [_tile(K, es, "fxg%d" % i, [128, 8, 512], F32) for i in range(2)]
        sq = _tile(K, es, "fsq", [128, 8, 512], BF16)
        yn = _tile(K, es, "fyn", [128, 8, 512], F32)
        rstd = _tile(K, es, "frstd", [128, 512], F32)
        tmp = _tile(K, es, "ftmp", [128, 512], F32)
        yo = [_tile(K, es, "fyo%d" % i, [128, D], F32) for i in range(2)]
        r_xg = [fw.res() for _ in range(2)]
        r_yo = [fw.res() for _ in range(2)]
        r_sq, r_yn, r_rstd, r_tmp = [fw.res() for _ in range(4)]
        c_x = [fw.chan() for _ in range(2)]
        c_o = [fw.chan() for _ in range(2)]
        XTv = XT.rearrange("c p n -> p c n")
        oi = 0
        for g in range(NT // 512):
            b = g % 2
            fw.dma("sp", c_x[b], xg[b][:], XTv[:, :, g * 512:(g + 1) * 512], reads=[K.r_XT[g]], writes=[r_xg[b]])
            rms_rstd(K, xg[b], 8, 0, 512, sq, rstd[:, :], [r_xg[b]], [r_sq] * 8, r_rstd, tmp, r_tmp, D)
            for c in range(8):
                fw.op("dve", lambda e: e.scalar_tensor_tensor(out=yn[:, c, :], in0=xg[b][:, c, :], scalar=gain_col[:, c:c + 1], in1=rstd[:, :],
                                                               op0=ALU.mult, op1=ALU.mult),
                      reads=[r_xg[b], r_rstd, K.r_const], writes=[r_yn])
            for j in range(4):
                ob = oi % 2
                oi += 1
                for half in range(2):
                    bank = K.bank()
                    for q in range(4):
                        c = half * 4 + q
                        fw.op("pe", lambda e: e.transpose(out=K.ps[:, bank, q * 128:(q + 1) * 128], in_=yn[:, c, j * 128:(j + 1) * 128],
                                                          identity=K.identf[:]),
                              reads=[r_yn, K.r_const], writes=[K.r_ps[bank]])
                    if half == 0:
                        fw.op("act", lambda e: e.activation(out=yo[ob][:, 0:512], in_=K.ps[:, bank, :], func=AF.Copy),
                              reads=[K.r_ps[bank]], writes=[r_yo[ob]])
                    else:
                        fw.op("dve", lambda e: e.tensor_copy(out=yo[ob][:, 512:1024], in_=K.ps[:, bank, :]),
                              reads=[K.r_ps[bank]], writes=[r_yo[ob]])
                t = g * 4 + j
                fw.dma("pool", c_o[ob], out_dram[t * 128:(t + 1) * 128, :], yo[ob][:], reads=[r_yo[ob]], writes=[K.r_out])
        fw.barrier(c_x + c_o)


VEC_COLS = {}


def _vec_layout():
    off = 0
    lay = {}
    for name, n in [("ffn1_norm", 16), ("mix_norm", 16), ("ffn2_norm", 16), ("final_norm", 8), ("fbias3", 1), ("mla_q_norm", 2),
                    ("mla_kv_norm", 1), ("gdn_conv", 96), ("gdn_dtb", 8), ("gdn_alog", 8), ("gdn_onorm", 128)]:
        lay[name] = (off, n)
        off += n
    return lay, off


def pack_vecs(inp):
    lay, nv = _vec_layout()
    v = np.zeros((128, nv), np.float32)

    def fm(a):
        a = np.asarray(a, np.float32).reshape(-1, 8, 128)
        return a.transpose(2, 0, 1).reshape(128, -1)
    for name in ["ffn1_norm", "mix_norm", "ffn2_norm", "final_norm"]:
        o, n = lay[name]
        v[:, o:o + n] = fm(inp[name])
    fb = np.asarray(inp["fox_f_bias"], np.float32)[0]
    for rep in range(3):
        v[rep * 32:rep * 32 + 8, lay["fbias3"][0]] = fb
    v[:, lay["mla_q_norm"][0]:lay["mla_q_norm"][0] + 2] = np.asarray(inp["mla_q_norm"], np.float32)[0].reshape(2, 128).T
    v[:, lay["mla_kv_norm"][0]] = np.asarray(inp["mla_kv_norm"], np.float32)[0]
    cw = np.asarray(inp["gdn_conv_w"], np.float32)[0].reshape(4, 24, 128)
    v[:, lay["gdn_conv"][0]:lay["gdn_conv"][0] + 96] = cw.transpose(2, 0, 1).reshape(128, 96)
    v[:, lay["gdn_dtb"][0]:lay["gdn_dtb"][0] + 8] = np.asarray(inp["gdn_dt_bias"], np.float32)[0][None, :]
    v[:, lay["gdn_alog"][0]:lay["gdn_alog"][0] + 8] = np.asarray(inp["gdn_a_log"], np.float32)[0][None, :]
    v[:, lay["gdn_onorm"][0]:lay["gdn_onorm"][0] + 128] = np.asarray(inp["gdn_out_norm"], np.float32)[0][None, :]
    return v


def host_consts(inp):
    c = {}
    selq = np.zeros((97, 8, 70), np.float32)
    selk = np.zeros((97, 8, 70), np.float32)
    for h in range(8):
        for p in range(3):
            selq[p * 32 + h, h, 64 + p] = 1.0
            selk[p * 32 + h, h, 67 + p] = -1.0
        selq[96, h, 67:70] = 1.0
        selk[96, h, 64:67] = 1.0
    c["selq"], c["selk"] = selq, selk
    s_idx = np.arange(128)[:, None]
    t_idx = np.arange(128)[None, :]
    c["mask_causal"] = np.where(s_idx <= t_idx, 0.0, NEG).astype(np.float32)
    c["mask_chunk"] = np.where((s_idx // 64) <= (t_idx // 64), 0.0, NEG).astype(np.float32)
    inv = 10000.0 ** (-np.arange(16, dtype=np.float32) / 16)
    ang = np.arange(SEQ, dtype=np.float32)[None, :] * inv[:, None]
    cos, sin = np.cos(ang).astype(np.float32), np.sin(ang).astype(np.float32)
    c["cos2"] = np.concatenate([cos, cos], 0)
    c["sin2s"] = np.concatenate([-sin, sin], 0)
    c["U"] = (s_idx <= t_idx).astype(np.float32)
    c["gmask1"] = np.where(t_idx >= s_idx, 0.0, NEG).astype(np.float32)
    c["gmask2"] = np.where(s_idx > t_idx, 0.0, BIG).astype(np.float32)
    c["gstrict"] = (s_idx > t_idx).astype(np.float32)
    wuq = np.asarray(inp["mla_w_uq"], np.float32)[0].reshape(256, 8, 96)
    c["mla_w_uqs"] = np.ascontiguousarray(np.concatenate([wuq[:, :, 0:64], wuq[:, :, 80:96], wuq[:, :, 64:80]], axis=2).reshape(256, 768))
    return c


CONST_SHAPES = {"selq": [97, 8, 70], "selk": [97, 8, 70], "mask_causal": [128, 128], "mask_chunk": [128, 128], "cos2": [32, SEQ],
                "sin2s": [32, SEQ], "mla_w_uqs": [256, 768], "U": [128, 128], "gmask1": [128, 128], "gmask2": [128, 128],
                "gstrict": [128, 128]}


W_SHAPES = {
    "ffn1_w13": [2, D, 2 * DFF], "ffn1_w2": [2, DFF, D], "ffn2_w13": [2, D, 2 * DFF], "ffn2_w2": [2, DFF, D],
    "attn_w_in": [1, D, 1960], "mla_w_uq": [1, 256, 768], "mla_w_ukv": [1, 128, 1024], "attn_w_out": [1, D, D],
    "gdn_w_in": [1, D, 4112], "gdn_w_out": [1, D, D],
}


def build(phases=("input", "ffn1_0", "final"), dbg=False):
    nc = bass.Bass("TRN2", target_bir_lowering=False)
    K = Ctx()
    K.nc = nc
    import os
    K.dbg_pairs = int(os.environ.get("DBG_PAIRS", "4"))
    K.dbg_heads = int(os.environ.get("DBG_HEADS", "2"))
    K.dbg_skip = os.environ.get("DBG_SKIP", "").split(",")
    lay, nv = _vec_layout()
    x = nc.dram_tensor("x", [NT, D], F32, kind="ExternalInput").ap()
    vecs_d = nc.dram_tensor("vecs", [128, nv], F32, kind="ExternalInput").ap()
    ident_d = nc.dram_tensor("identf", [128, 128], F32, kind="ExternalInput").ap()
    W = {k: nc.dram_tensor(k, shp, F32, kind="ExternalInput").ap() for k, shp in W_SHAPES.items()}
    C = {k: nc.dram_tensor(k, shp, F32, kind="ExternalInput").ap() for k, shp in CONST_SHAPES.items()}
    out = nc.dram_tensor("out", [NT, D], F32, kind="ExternalOutput").ap()
    XT = nc.dram_tensor("XT", [8, 128, NT], F32, kind="Internal").ap()
    HT = nc.dram_tensor("HT", [8, 128, NT], BF16, kind="Internal").ap()
    AT = nc.dram_tensor("AT", [8, 128, NT], BF16, kind="Internal").ap()
    S = {"QT": nc.dram_tensor("gQT", [8, 128, NT], BF16, kind="Internal").ap(),
         "KT": nc.dram_tensor("gKT", [8, 128, NT], BF16, kind="Internal").ap(),
         "Ktok": nc.dram_tensor("gKtok", [NT, D], BF16, kind="Internal").ap(),
         "Vtok": nc.dram_tensor("gVtok", [NT, D], BF16, kind="Internal").ap(),
         "Gtok": nc.dram_tensor("gGtok", [NT, D], BF16, kind="Internal").ap(),
         "BG": nc.dram_tensor("gBG", [NT, 16], F32, kind="Internal").ap()}
    with ExitStack() as es:
        fw = FW(nc, es)
        K.fw = fw
        K.ps = es.enter_context(nc.psum_tensor("ps", [128, 8, 512], F32))
        K.r_ps = [fw.res("ps%d" % i) for i in range(8)]
        for r_ in K.r_ps:
            r_.excl = True
        K._bank = 0

        def bank():
            b = K._bank
            K._bank = (b + 1) % 8
            return b
        K.bank = bank
        K.r_XT = [fw.res("XT%d" % i) for i in range(NT // 512)]
        K.r_out = fw.res("out")
        K.r_HT = [fw.res("HT%d" % i) for i in range(NT // 512)]
        K.r_AT = [fw.res("AT%d" % i) for i in range(2)]
        S["r_QK"] = [fw.res() for _ in range(2)]
        S["r_tok"] = [fw.res() for _ in range(2)]
        K.r_const = fw.res("const")
        K.identf = _tile(K, es, "identf_sb", [128, 128], F32)
        K.onesb = _tile(K, es, "onesb", [128, 128], BF16)
        K.epsb = _tile(K, es, "epsb", [128, 1], F32)
        K.oneb = _tile(K, es, "oneb", [128, 1], F32)
        K.onesf = _tile(K, es, "onesf", [128, 128], F32)
        K.identb = _tile(K, es, "identb", [128, 128], BF16)
        K.vecs = _tile(K, es, "vecs_sb", [128, nv], F32)
        c0 = fw.chan()
        fw.dma("sp", c0, K.identf[:], ident_d, writes=[K.r_const])
        fw.dma("sp", c0, K.vecs[:], vecs_d, writes=[K.r_const])
        fw.op("dve", lambda e: e.memset(K.onesb[:], 1.0), writes=[K.r_const])
        fw.op("dve", lambda e: e.memset(K.epsb[:], EPS), writes=[K.r_const])
        fw.op("dve", lambda e: e.memset(K.oneb[:], 1.0), writes=[K.r_const])
        fw.op("dve", lambda e: e.memset(K.onesf[:], 1.0), writes=[K.r_const])
        fw.op("dve", lambda e: e.tensor_copy(out=K.identb[:], in_=K.identf[:]), reads=[K.r_const], writes=[K.r_const])
        fw.barrier([c0])

        def vcol(name, layer=0, n=8):
            o, _ = lay[name]
            return K.vecs[:, o + layer * n:o + (layer + 1) * n]

        skip = set()
        for pi, ph in enumerate(phases):
            if pi in skip:
                continue
            if ph == "input":
                phase_input(K, x, XT)
            elif ph.startswith("ffn"):
                which, layer = ph[:4], int(ph[5:])
                nxt = phases[pi + 1] if pi + 1 < len(phases) else None
                pn = fin = None
                if nxt is not None and nxt.startswith("norm") and FUSE_NORM:
                    pn = (vcol("mix_norm", int(nxt[4:])), HT)
                    skip.add(pi + 1)
                if nxt == "final" and FUSE_NORM:
                    fin = (vcol("final_norm"), out)
                    skip.add(pi + 1)
                phase_ffn(K, XT, vcol(which + "_norm", layer), W[which + "_w13"][layer], W[which + "_w2"][layer], post_norm=pn, final=fin)
            elif ph.startswith("norm"):
                layer = int(ph[4:])
                phase_norm_to_ht(K, XT, vcol("mix_norm", layer), HT)
            elif ph == "fox":
                o = lay["fbias3"][0]
                phase_fox(K, HT, AT, W["attn_w_in"][0], K.vecs[:, o:o + 1], C)
            elif ph == "mla":
                oq, okv = lay["mla_q_norm"][0], lay["mla_kv_norm"][0]
                phase_mla(K, HT, AT, W["attn_w_in"][0], W["mla_w_uq"][0], C["mla_w_uqs"], W["mla_w_ukv"][0],
                          K.vecs[:, oq:oq + 2], K.vecs[:, okv:okv + 1], C)
            elif ph == "oproj0":
                phase_outproj(K, XT, AT, W["attn_w_out"][0])
            elif ph == "gdnp":
                oc, od, oa = lay["gdn_conv"][0], lay["gdn_dtb"][0], lay["gdn_alog"][0]
                phase_gdn_proj(K, HT, W["gdn_w_in"][0], K.vecs[:, oc:oc + 96], K.vecs[:, od:od + 8], K.vecs[:, oa:oa + 8], S)
            elif ph == "gdnc":
                oo = lay["gdn_onorm"][0]
                phase_gdn_core(K, S, AT, K.vecs[:, oo:oo + 128], C)
            elif ph == "oproj1":
                phase_outproj(K, XT, AT, W["gdn_w_out"][0])
            elif ph == "final":
                phase_final(K, XT, vcol("final_norm"), out)
        fw.finish("sp", [K.r_out])
    return nc


ALL_PHASES = ("input", "ffn1_0", "norm0", "fox", "mla", "oproj0", "ffn2_0", "ffn1_1", "norm1", "gdnp", "gdnc", "oproj1", "ffn2_1", "final")


def make_in_maps(inputs, n_cores=N_CORES):
    x = np.ascontiguousarray(np.asarray(inputs["x"], np.float32)).reshape(n_cores, NT, D)
    shared = {"vecs": pack_vecs(inputs), "identf": np.eye(128, dtype=np.float32)}
    shared.update(host_consts(inputs))
    for k in W_SHAPES:
        shared[k] = np.ascontiguousarray(np.asarray(inputs[k], np.float32))
    return [dict(shared, x=x[i]) for i in range(n_cores)]


def kernel(**inputs):
    nc = build(ALL_PHASES)
    in_maps = make_in_maps(inputs)
    res = run_bass_kernel_spmd(nc, in_maps, core_ids=list(range(N_CORES)))
    out = np.stack([np.asarray(r["out"]) for r in res.results], axis=0)
    return out.reshape(16, SEQ, D).astype(np.float32)


def phase_norm_to_ht(K, XT, gain_col, HT):
    fw = K.fw
    with ExitStack() as es:
        xg = [_tile(K, es, "nxg%d" % i, [128, 8, 512], F32) for i in range(2)]
        hb = [_tile(K, es, "nhb%d" % i, [128, 8, 512], BF16) for i in range(2)]
        sq = _tile(K, es, "nsq", [128, 8, 512], BF16)
        rstd = _tile(K, es, "nrstd", [128, 512], F32)
        tmp = _tile(K, es, "ntmp", [128, 512], F32)
        r_xg = [fw.res() for _ in range(2)]
        r_hb = [fw.res() for _ in range(2)]
        r_sq, r_rstd, r_tmp = [fw.res() for _ in range(3)]
        c_x = [fw.chan() for _ in range(2)]
        c_h = [fw.chan() for _ in range(2)]
        XTv = XT.rearrange("c p n -> p c n")
        HTv = HT.rearrange("c p n -> p c n")
        for g in range(NT // 512):
            b = g % 2
            fw.dma("sp", c_x[b], xg[b][:], XTv[:, :, g * 512:(g + 1) * 512], reads=[K.r_XT[g]], writes=[r_xg[b]])
            rms_rstd(K, xg[b], 8, 0, 512, sq, rstd[:, :], [r_xg[b]], [r_sq] * 8, r_rstd, tmp, r_tmp, D)
            for c in range(8):
                fw.op("dve", lambda e: e.scalar_tensor_tensor(out=hb[b][:, c, :], in0=xg[b][:, c, :], scalar=gain_col[:, c:c + 1], in1=rstd[:, :],
                                                               op0=ALU.mult, op1=ALU.mult),
                      reads=[r_xg[b], r_rstd, K.r_const], writes=[r_hb[b]])
            fw.dma("pool", c_h[b], HTv[:, :, g * 512:(g + 1) * 512], hb[b][:], reads=[r_hb[b]], writes=[K.r_HT[g]])
        fw.barrier(c_x + c_h)


def attention_head(K, A, qa, ka, KR, vlhs, M, obase, drow, maskT, reads_qkv, out_ap, out_res):
    fw = K.fw
    LOOK = 2
    for G in range(SEQ // 512):
        ob = A.obanks[A.oi % len(A.obanks)]
        A.oi += 1
        nkt = 4 * G + 4
        pend = []

        def score(i):
            r = i - 4 * G
            q0 = max(r, 0) * 128
            N = 512 - q0
            sb = A.sbanks[A.si % len(A.sbanks)]
            A.si += 1
            mm(K, K.ps[:, sb, 0:N], ka[0:KR, i * 128:(i + 1) * 128], qa[0:KR, G * 512 + q0:(G + 1) * 512], True, r < 0,
               reads_qkv, [K.r_ps[sb]])
            if r >= 0:
                mm(K, K.ps[:, sb, 0:128], K.identb[:, :], maskT, False, True, [K.r_const], [K.r_ps[sb]])
            pb = A.pi % len(A.pt)
            A.pi += 1
            fw.op("act", lambda e: e.activation(out=A.pt[pb][:, 0:N], in_=K.ps[:, sb, 0:N], func=AF.Exp),
                  reads=[K.r_ps[sb]], writes=[A.r_pt[pb]])
            pend.append((i, q0, N, pb))

        def pv():
            i, q0, N, pb = pend.pop(0)
            mm(K, K.ps[0:M, ob, q0:512], vlhs(i), A.pt[pb][:, 0:N], i == 0, i == nkt - 1, reads_qkv + [A.r_pt[pb]], [K.r_ps[ob]])

        for i in range(nkt):
            score(i)
            if len(pend) > LOOK:
                pv()
        while pend:
            pv()
        fw.op("dve", lambda e: e.reciprocal(out=A.rd[drow:drow + 1, :], in_=K.ps[drow:drow + 1, ob, :]), reads=[K.r_ps[ob]], writes=[A.r_rd])
        bb = A.bbanks[A.bi % len(A.bbanks)]
        A.bi += 1
        mm(K, K.ps[obase:obase + 64, bb, :], K.onesf[drow:drow + 1, 0:64], A.rd[drow:drow + 1, :], True, True, [A.r_rd, K.r_const], [K.r_ps[bb]])
        cb = A.ci % len(A.bcs)
        A.ci += 1
        fw.op("act", lambda e: e.activation(out=A.bcs[cb][obase:obase + 64, :], in_=K.ps[obase:obase + 64, bb, :], func=AF.Copy),
              reads=[K.r_ps[bb]], writes=[A.r_bcs[cb]])
        fw.op("dve", lambda e: e.tensor_tensor(out=out_ap(G), in0=K.ps[obase:obase + 64, ob, :], in1=A.bcs[cb][obase:obase + 64, :], op=ALU.mult),
              reads=[K.r_ps[ob], A.r_bcs[cb]], writes=[out_res(G)])


class AttnCtx:
    def __init__(self, K, es, tag):
        fw = K.fw
        self.pt = [_tile(K, es, "%spt%d" % (tag, i), [128, 512], BF16) for i in range(6)]
        self.r_pt = [fw.res() for _ in range(6)]
        self.rd = _tile(K, es, tag + "rd", [128, 512], F32)
        self.r_rd = fw.res()
        self.bcs = [_tile(K, es, "%sbcs%d" % (tag, i), [128, 512], F32) for i in range(2)]
        self.r_bcs = [fw.res() for _ in range(2)]
        self.sbanks, self.obanks, self.bbanks = [0, 1, 2, 3, 4], [5, 6], [7]
        self.si = self.oi = self.bi = self.pi = self.ci = 0


FUSE_NORM = True
NEG = -30000.0
FOX_SCALE = 0.125
MLA_SCALE = 96 ** -0.5


def phase_fox(K, HT, AT, w_in, nfb_col, C):
    fw = K.fw
    with ExitStack() as es:
        A = AttnCtx(K, es, "fx")
        wp = [_tile(K, es, "fxwp%d" % i, [128, 8, 384], BF16) for i in range(2)]
        ht = [_tile(K, es, "fxht%d" % i, [128, 8, 512], BF16) for i in range(2)]
        qa = [_tile(K, es, "fxqa%d" % i, [128, 2, SEQ], BF16) for i in range(2)]
        ka = [_tile(K, es, "fxka%d" % i, [128, 2, SEQ], BF16) for i in range(2)]
        VE = [_tile(K, es, "fxVE%d" % i, [128, 16, 65], BF16) for i in range(2)]
        VO = [_tile(K, es, "fxVO%d" % i, [128, 16, 128], BF16) for i in range(2)]
        ao = [_tile(K, es, "fxao%d" % i, [128, SEQ], BF16) for i in range(2)]
        wf = _tile(K, es, "fxwf", [128, 8, 72], BF16)
        selq = _tile(K, es, "fxselq", [128, 8, 70], BF16)
        selk = _tile(K, es, "fxselk", [128, 8, 70], BF16)
        maskT = _tile(K, es, "fxmask", [128, 128], BF16)
        Fp = _tile(K, es, "fxFp", [128, SEQ], BF16)
        Ff = [_tile(K, es, "fxFf%d" % i, [128, 512], F32) for i in range(2)]
        sp = _tile(K, es, "fxsp", [128, 512], F32)
        ee = _tile(K, es, "fxee", [128, 512], F32)
        HI = _tile(K, es, "fxHI", [128, 512], BF16)
        MID = _tile(K, es, "fxMID", [128, 512], BF16)
        nfb = _tile(K, es, "fxnfb", [128, 1], F32)
        r_wp = [fw.res() for _ in range(2)]
        r_ht = [fw.res() for _ in range(2)]
        r_q = [[fw.res() for _ in range(2)] for _ in range(2)]
        r_k = [[fw.res() for _ in range(2)] for _ in range(2)]
        r_VE = [fw.res() for _ in range(2)]
        r_VO = [fw.res() for _ in range(2)]
        r_ao = [fw.res() for _ in range(2)]
        r_c, r_Fp, r_sp, r_ee, r_HI, r_MID = [fw.res() for _ in range(6)]
        r_Ff = [fw.res() for _ in range(2)]
        c_wp = [fw.chan() for _ in range(2)]
        c_ht = [fw.chan() for _ in range(2)]
        c_ao = [fw.chan() for _ in range(2)]
        c_c = fw.chan()
        w_inv = w_in.rearrange("(c p) n -> p c n", p=128)
        HTv = HT.rearrange("c p n -> p c n")
        fw.op("dve", lambda e: e.memset(wf[:], 0.0), writes=[r_c])
        for rep in range(3):
            fw.dma("pool", c_c, wf[:, :, rep * 32:rep * 32 + 8], w_inv[:, :, 1536:1544], writes=[r_c])
        fw.dma("pool", c_c, selq[0:97, :, :], C["selq"], writes=[r_c])
        fw.dma("pool", c_c, selk[0:97, :, :], C["selk"], writes=[r_c])
        fw.dma("pool", c_c, maskT[:], C["mask_causal"], writes=[r_c])
        fw.op("dve", lambda e: e.tensor_scalar(out=nfb[:], in0=nfb_col, scalar1=-1.0, scalar2=None, op0=ALU.mult), reads=[K.r_const], writes=[r_c])
        fw.op("pool", lambda e: e.memset(Fp[:], 0.0), writes=[r_Fp])
        fw.op("pool", lambda e: e.memset(Fp[96:97, :], 1.0), writes=[r_Fp])
        for b in range(2):
            fw.op("pool", lambda e: e.memset(VE[b][:, :, 64:65], 1.0), writes=[r_VE[b]])
            fw.op("pool", lambda e: e.memset(VO[b][:, :, 0:64], 0.0), writes=[r_VO[b]])
            fw.op("pool", lambda e: e.memset(VO[b][:, :, 0:1], 1.0), writes=[r_VO[b]])
        nht = 0
        npair = 0

        def load_ht(g):
            nonlocal nht
            b = nht % 2
            nht += 1
            fw.dma("sp", c_ht[b], ht[b][:], HTv[:, :, g * 512:(g + 1) * 512], reads=[K.r_HT[g]], writes=[r_ht[b]])
            return b

        for s in range(2):
            for nt in range(4):
                hb = load_ht(s * 4 + nt)
                bank = K.bank()
                for kc in range(8):
                    mm(K, K.ps[0:72, bank, :], wf[:, kc, :], ht[hb][:, kc, :], kc == 0, kc == 7, [r_c, r_ht[hb]], [K.r_ps[bank]])
                fw.op("act", lambda e: e.activation(out=ee[0:72, :], in_=K.ps[0:72, bank, :], func=AF.Exp, scale=-1.0, bias=nfb[0:72, 0:1]),
                      reads=[K.r_ps[bank], r_c], writes=[r_ee])
                fw.op("act", lambda e: e.activation(out=sp[0:72, :], in_=ee[0:72, :], func=AF.Ln, bias=K.oneb[0:72, 0:1], scale=1.0),
                      reads=[r_ee, K.r_const], writes=[r_sp])
                fb = nt % 2
                init = 0.0 if nt == 0 else Ff[1 - fb][0:72, 511:512]
                fw.op("dve", lambda e: e.tensor_tensor_scan(out=Ff[fb][0:72, :], data0=K.onesf[0:72, 0:1].broadcast_to([72, 512]), data1=sp[0:72, :],
                                                            initial=init, op0=ALU.mult, op1=ALU.subtract),
                      reads=[r_sp, K.r_const, r_Ff[1 - fb]], writes=[r_Ff[fb]])
                sl = slice(nt * 512, (nt + 1) * 512)
                fw.op("dve", lambda e: e.tensor_copy(out=HI[0:72, :], in_=Ff[fb][0:72, :]), reads=[r_Ff[fb]], writes=[r_HI])
                fw.op("dve", lambda e: e.tensor_tensor(out=sp[0:72, :], in0=Ff[fb][0:72, :], in1=HI[0:72, :], op=ALU.subtract),
                      reads=[r_Ff[fb], r_HI], writes=[r_sp])
                fw.op("dve", lambda e: e.tensor_copy(out=MID[0:72, :], in_=sp[0:72, :]), reads=[r_sp], writes=[r_MID])
                fw.op("dve", lambda e: e.tensor_tensor(out=sp[0:72, :], in0=sp[0:72, :], in1=MID[0:72, :], op=ALU.subtract),
                      reads=[r_sp, r_MID], writes=[r_sp])
                fw.op("pool", lambda e: e.tensor_copy(out=Fp[0:8, sl], in_=HI[0:8, :]), reads=[r_HI], writes=[r_Fp])
                fw.op("pool", lambda e: e.tensor_copy(out=Fp[32:40, sl], in_=MID[32:40, :]), reads=[r_MID], writes=[r_Fp])
                fw.op("pool", lambda e: e.tensor_copy(out=Fp[64:72, sl], in_=sp[64:72, :]), reads=[r_sp], writes=[r_Fp])
            for j in range(K.dbg_pairs):
                pb = npair % 2
                npair += 1
                fw.dma("pool", c_wp[pb], wp[pb][:, :, 0:128], w_inv[:, :, j * 128:(j + 1) * 128], writes=[r_wp[pb]])
                fw.dma("pool", c_wp[pb], wp[pb][:, :, 128:256], w_inv[:, :, 512 + j * 128:512 + (j + 1) * 128], writes=[r_wp[pb]])
                fw.dma("pool", c_wp[pb], wp[pb][:, :, 256:384], w_inv[:, :, 1024 + j * 128:1024 + (j + 1) * 128], writes=[r_wp[pb]])
                for nt in range(4):
                    hb = load_ht(s * 4 + nt)
                    sl = slice(nt * 512, (nt + 1) * 512)
                    for hh in range(2):
                        h = 2 * j + hh
                        bq = K.bank()
                        for kc in range(8):
                            mm(K, K.ps[0:64, bq, :], wp[pb][:, kc, hh * 64:(hh + 1) * 64], ht[hb][:, kc, :], kc == 0, kc == 7,
                               [r_wp[pb], r_ht[hb]], [K.r_ps[bq]])
                        fw.op("act", lambda e: e.activation(out=qa[pb][0:64, hh, sl], in_=K.ps[0:64, bq, :], func=AF.Copy, scale=FOX_SCALE),
                              reads=[K.r_ps[bq]], writes=[r_q[pb][hh]])
                        bk = K.bank()
                        for kc in range(8):
                            mm(K, K.ps[0:64, bk, :], wp[pb][:, kc, 128 + hh * 64:128 + (hh + 1) * 64], ht[hb][:, kc, :], kc == 0, kc == 7,
                               [r_wp[pb], r_ht[hb]], [K.r_ps[bk]])
                        fw.op("dve", lambda e: e.tensor_copy(out=ka[pb][0:64, hh, sl], in_=K.ps[0:64, bk, :]),
                              reads=[K.r_ps[bk]], writes=[r_k[pb][hh]])
                        if "sel" in K.dbg_skip:
                            continue
                        ba = K.bank()
                        mm(K, K.ps[0:70, ba, :], selq[0:97, h, :], Fp[0:97, sl], True, True, [r_c, r_Fp], [K.r_ps[ba]])
                        fw.op("dve", lambda e: e.tensor_copy(out=qa[pb][64:70, hh, sl], in_=K.ps[64:70, ba, :]),
                              reads=[K.r_ps[ba]], writes=[r_q[pb][hh]])
                        ba = K.bank()
                        mm(K, K.ps[0:70, ba, :], selk[0:97, h, :], Fp[0:97, sl], True, True, [r_c, r_Fp], [K.r_ps[ba]])
                        fw.op("dve", lambda e: e.tensor_copy(out=ka[pb][64:70, hh, sl], in_=K.ps[64:70, ba, :]),
                              reads=[K.r_ps[ba]], writes=[r_k[pb][hh]])
                    if "v" in K.dbg_skip:
                        continue
                    bv = K.bank()
                    for it in range(4):
                        for kc in range(8):
                            mm(K, K.ps[:, bv, it * 128:(it + 1) * 128], ht[hb][:, kc, it * 128:(it + 1) * 128], wp[pb][:, kc, 256:384],
                               kc == 0, kc == 7, [r_wp[pb], r_ht[hb]], [K.r_ps[bv]])
                    src = K.ps[:, bv, :].rearrange("p (t c) -> p t c", t=4)
                    if "vevac" in K.dbg_skip:
                        continue
                    fw.op("act", lambda e: e.activation(out=VE[pb][:, nt * 4:nt * 4 + 4, 0:64], in_=src[:, :, 0:64], func=AF.Copy),
                          reads=[K.r_ps[bv]], writes=[r_VE[pb]])
                    if "vevac2" in K.dbg_skip:
                        continue
                    fw.op("act", lambda e: e.activation(out=VO[pb][:, nt * 4:nt * 4 + 4, 64:128], in_=src[:, :, 64:128], func=AF.Copy),
                          reads=[K.r_ps[bv]], writes=[r_VO[pb]])
                for hh in range(K.dbg_heads):
                    rq = [r_q[pb][hh], r_k[pb][hh], r_VE[pb] if hh == 0 else r_VO[pb]]
                    if hh == 0:
                        attention_head(K, A, qa[pb][:, 0, :], ka[pb][:, 0, :], 70, lambda i: VE[pb][:, i, 0:65], 65, 0, 64, maskT[:, :], rq + [r_c],
                                       lambda G: ao[pb][0:64, G * 512:(G + 1) * 512], lambda G: r_ao[pb])
                    else:
                        attention_head(K, A, qa[pb][:, 1, :], ka[pb][:, 1, :], 70, lambda i: VO[pb][:, i, 0:128], 128, 64, 0, maskT[:, :], rq + [r_c],
                                       lambda G: ao[pb][64:128, G * 512:(G + 1) * 512], lambda G: r_ao[pb])
                if "at" not in K.dbg_skip:
                    fw.dma("sp", c_ao[pb], AT[j, :, s * SEQ:(s + 1) * SEQ], ao[pb][:], reads=[r_ao[pb]], writes=[K.r_AT[s]])
        fw.barrier(c_wp + c_ht + c_ao + [c_c])


def phase_outproj(K, XT, AT, w_out):
    fw = K.fw
    with ExitStack() as es:
        wo = _tile(K, es, "opw", [128, 8, D], BF16)
        at = [_tile(K, es, "opat%d" % i, [128, 8, 512], BF16) for i in range(2)]
        xg = [_tile(K, es, "opxg%d" % i, [128, 8, 512], F32) for i in range(2)]
        r_wo = fw.res()
        r_at = [fw.res() for _ in range(2)]
        r_xg = [[fw.res() for _ in range(8)] for _ in range(2)]
        c_wo = fw.chan()
        c_at = [fw.chan() for _ in range(2)]
        c_x = [fw.chan() for _ in range(2)]
        c_xs = [fw.chan() for _ in range(2)]
        XTv = XT.rearrange("c p n -> p c n")
        ATv = AT.rearrange("c p n -> p c n")
        fw.dma("pool", c_wo, wo[:], w_out.rearrange("(c p) n -> p c n", p=128), writes=[r_wo])
        for g in range(NT // 512):
            b = g % 2
            fw.dma("sp", c_at[b], at[b][:], ATv[:, :, g * 512:(g + 1) * 512], reads=[K.r_AT[g // 4]], writes=[r_at[b]])
            fw.dma("sp", c_x[b], xg[b][:], XTv[:, :, g * 512:(g + 1) * 512], reads=[K.r_XT[g]], writes=r_xg[b])
            for dmc in range(8):
                by = K.bank()
                for c in range(8):
                    mm(K, K.ps[:, by, :], wo[:, c, dmc * 128:(dmc + 1) * 128], at[b][:, c, :], c == 0, c == 7, [r_wo, r_at[b]], [K.r_ps[by]])
                fw.op("dve", lambda e: e.tensor_tensor(out=xg[b][:, dmc, :], in0=K.ps[:, by, :], in1=xg[b][:, dmc, :], op=ALU.add),
                      reads=[K.r_ps[by], r_xg[b][dmc]], writes=[r_xg[b][dmc]])
            fw.dma("pool", c_xs[b], XTv[:, :, g * 512:(g + 1) * 512], xg[b][:], reads=r_xg[b], writes=[K.r_XT[g]])
        fw.barrier([c_wo] + c_at + c_x + c_xs)


def phase_mla(K, HT, AT, w_in, w_uq, w_uqs, w_ukv, qn_col, kvn_col, C):
    fw = K.fw
    with ExitStack() as es:
        A = AttnCtx(K, es, "ml")
        ht = [_tile(K, es, "mlht%d" % i, [128, 8, 512], BF16) for i in range(2)]
        qa = [_tile(K, es, "mlqa%d" % i, [128, 2, SEQ], BF16) for i in range(2)]
        ka = [_tile(K, es, "mlka%d" % i, [128, 2, SEQ], BF16) for i in range(2)]
        VE = [_tile(K, es, "mlVE%d" % i, [128, 16, 65], BF16) for i in range(2)]
        VO = [_tile(K, es, "mlVO%d" % i, [128, 16, 128], BF16) for i in range(2)]
        ao = [_tile(K, es, "mlao%d" % i, [128, SEQ], BF16) for i in range(2)]
        wlat = _tile(K, es, "mlwlat", [128, 8, 384], BF16)
        wkpe = _tile(K, es, "mlwkpe", [128, 8, 96], BF16)
        wkpes = _tile(K, es, "mlwkpes", [128, 8, 96], BF16)
        wuq = _tile(K, es, "mlwuq", [128, 2, 768], BF16)
        wuqs = _tile(K, es, "mlwuqs", [128, 2, 768], BF16)
        wukv = _tile(K, es, "mlwukv", [128, 1024], BF16)
        maskT = _tile(K, es, "mlmask", [128, 128], BF16)
        cos2 = _tile(K, es, "mlcos", [128, SEQ], F32)
        sin2 = _tile(K, es, "mlsin", [128, SEQ], F32)
        cqn = _tile(K, es, "mlcqn", [128, 2, SEQ], BF16)
        ckvn = _tile(K, es, "mlckvn", [128, SEQ], BF16)
        kpe = _tile(K, es, "mlkpe", [128, SEQ], BF16)
        cqf = _tile(K, es, "mlcqf", [128, 3, 512], F32)
        sq = _tile(K, es, "mlsq", [128, 3, 512], BF16)
        rstd = [_tile(K, es, "mlrstd%d" % i, [128, 512], F32) for i in range(2)]
        tmp = _tile(K, es, "mltmp", [128, 512], F32)
        t1 = [_tile(K, es, "mlt1%d" % i, [128, 512], F32) for i in range(2)]
        t2 = [_tile(K, es, "mlt2%d" % i, [128, 512], F32) for i in range(2)]
        r_ht = [fw.res() for _ in range(2)]
        r_q = [[fw.res() for _ in range(2)] for _ in range(2)]
        r_k = [[fw.res() for _ in range(2)] for _ in range(2)]
        r_VE = [fw.res() for _ in range(2)]
        r_VO = [fw.res() for _ in range(2)]
        r_ao = [fw.res() for _ in range(2)]
        r_c, r_cqn, r_ckvn, r_kpe, r_cqf, r_sq, r_tmp = [fw.res() for _ in range(7)]
        r_rstd = [fw.res() for _ in range(2)]
        r_t1 = [fw.res() for _ in range(2)]
        r_t2 = [fw.res() for _ in range(2)]
        c_ht = [fw.chan() for _ in range(2)]
        c_ao = [fw.chan() for _ in range(2)]
        c_c = fw.chan()
        w_inv = w_in.rearrange("(c p) n -> p c n", p=128)
        HTv = HT.rearrange("c p n -> p c n")
        fw.dma("pool", c_c, wlat[:], w_inv[:, :, 1544:1928], writes=[r_c])
        fw.op("dve", lambda e: e.memset(wkpe[:], 0.0), writes=[r_c])
        fw.op("dve", lambda e: e.memset(wkpes[:], 0.0), writes=[r_c])
        fw.dma("pool", c_c, wkpe[:, :, 64:96], w_inv[:, :, 1928:1960], writes=[r_c])
        fw.dma("pool", c_c, wkpes[:, :, 64:80], w_inv[:, :, 1944:1960], writes=[r_c])
        fw.dma("pool", c_c, wkpes[:, :, 80:96], w_inv[:, :, 1928:1944], writes=[r_c])
        fw.dma("pool", c_c, wuq[:], w_uq.rearrange("(c p) n -> p c n", p=128), writes=[r_c])
        fw.dma("pool", c_c, wuqs[:], w_uqs.rearrange("(c p) n -> p c n", p=128), writes=[r_c])
        fw.dma("pool", c_c, wukv[:], w_ukv, writes=[r_c])
        fw.dma("pool", c_c, maskT[:], C["mask_chunk"], writes=[r_c])
        fw.dma("sp", c_c, cos2[64:96, :], C["cos2"], writes=[r_c])
        fw.dma("sp", c_c, sin2[64:96, :], C["sin2s"], writes=[r_c])
        for b in range(2):
            fw.op("pool", lambda e: e.memset(VE[b][:, :, 64:65], 1.0), writes=[r_VE[b]])
            fw.op("pool", lambda e: e.memset(VO[b][:, :, 0:64], 0.0), writes=[r_VO[b]])
            fw.op("pool", lambda e: e.memset(VO[b][:, :, 0:1], 1.0), writes=[r_VO[b]])
        nht = 0
        npair = 0
        ti = 0
        for s in range(2):
            for nt in range(4):
                hb = nht % 2
                nht += 1
                g = s * 4 + nt
                sl = slice(nt * 512, (nt + 1) * 512)
                fw.dma("sp", c_ht[hb], ht[hb][:], HTv[:, :, g * 512:(g + 1) * 512], reads=[K.r_HT[g]], writes=[r_ht[hb]])
                for c in range(3):
                    bank = K.bank()
                    for kc in range(8):
                        mm(K, K.ps[:, bank, :], wlat[:, kc, c * 128:(c + 1) * 128], ht[hb][:, kc, :], kc == 0, kc == 7, [r_c, r_ht[hb]], [K.r_ps[bank]])
                    fw.op("act", lambda e: e.activation(out=cqf[:, c, :], in_=K.ps[:, bank, :], func=AF.Copy), reads=[K.r_ps[bank]], writes=[r_cqf])
                rms_rstd(K, cqf[:, 0:2, :], 2, 0, 512, sq[:, 0:2, :], rstd[0][:, :], [r_cqf], [r_sq] * 2, r_rstd[0], tmp, r_tmp, 256)
                rms_rstd(K, cqf[:, 2:3, :], 1, 0, 512, sq[:, 2:3, :], rstd[1][:, :], [r_cqf], [r_sq], r_rstd[1], tmp, r_tmp, 128)
                for c in range(2):
                    fw.op("dve", lambda e: e.scalar_tensor_tensor(out=cqn[:, c, sl], in0=cqf[:, c, :], scalar=qn_col[:, c:c + 1], in1=rstd[0][:, :],
                                                                   op0=ALU.mult, op1=ALU.mult), reads=[r_cqf, r_rstd[0], K.r_const], writes=[r_cqn])
                fw.op("dve", lambda e: e.scalar_tensor_tensor(out=ckvn[:, sl], in0=cqf[:, 2, :], scalar=kvn_col[:, 0:1], in1=rstd[1][:, :],
                                                               op0=ALU.mult, op1=ALU.mult), reads=[r_cqf, r_rstd[1], K.r_const], writes=[r_ckvn])
                bA, bB = K.bank(), K.bank()
                for kc in range(8):
                    mm(K, K.ps[0:96, bA, :], wkpe[:, kc, :], ht[hb][:, kc, :], kc == 0, kc == 7, [r_c, r_ht[hb]], [K.r_ps[bA]])
                for kc in range(8):
                    mm(K, K.ps[0:96, bB, :], wkpes[:, kc, :], ht[hb][:, kc, :], kc == 0, kc == 7, [r_c, r_ht[hb]], [K.r_ps[bB]])
                tb = ti % 2
                ti += 1
                fw.op("dve", lambda e: e.tensor_tensor(out=t1[tb][64:96, :], in0=K.ps[64:96, bA, :], in1=cos2[64:96, sl], op=ALU.mult),
                      reads=[K.r_ps[bA], r_c], writes=[r_t1[tb]])
                fw.op("dve", lambda e: e.tensor_tensor(out=t2[tb][64:96, :], in0=K.ps[64:96, bB, :], in1=sin2[64:96, sl], op=ALU.mult),
                      reads=[K.r_ps[bB], r_c], writes=[r_t2[tb]])
                fw.op("pool", lambda e: e.tensor_tensor(out=kpe[64:96, sl], in0=t1[tb][64:96, :], in1=t2[tb][64:96, :], op=ALU.add),
                      reads=[r_t1[tb], r_t2[tb]], writes=[r_kpe])
            for j in range(4):
                pb = npair % 2
                npair += 1
                for nt in range(4):
                    sl = slice(nt * 512, (nt + 1) * 512)
                    for hh in range(2):
                        h = 2 * j + hh
                        bA, bB = K.bank(), K.bank()
                        for c in range(2):
                            mm(K, K.ps[0:96, bA, :], wuq[:, c, h * 96:(h + 1) * 96], cqn[:, c, sl], c == 0, c == 1, [r_c, r_cqn], [K.r_ps[bA]])
                        for c in range(2):
                            mm(K, K.ps[0:96, bB, :], wuqs[:, c, h * 96:(h + 1) * 96], cqn[:, c, sl], c == 0, c == 1, [r_c, r_cqn], [K.r_ps[bB]])
                        fw.op("act", lambda e: e.activation(out=qa[pb][0:64, hh, sl], in_=K.ps[0:64, bA, :], func=AF.Copy, scale=MLA_SCALE),
                              reads=[K.r_ps[bA]], writes=[r_q[pb][hh]])
                        tb = ti % 2
                        ti += 1
                        fw.op("dve", lambda e: e.scalar_tensor_tensor(out=t1[tb][64:96, :], in0=K.ps[64:96, bA, :], scalar=MLA_SCALE, in1=cos2[64:96, sl],
                                                                       op0=ALU.mult, op1=ALU.mult), reads=[K.r_ps[bA], r_c], writes=[r_t1[tb]])
                        fw.op("dve", lambda e: e.scalar_tensor_tensor(out=t2[tb][64:96, :], in0=K.ps[64:96, bB, :], scalar=MLA_SCALE, in1=sin2[64:96, sl],
                                                                       op0=ALU.mult, op1=ALU.mult), reads=[K.r_ps[bB], r_c], writes=[r_t2[tb]])
                        fw.op("pool", lambda e: e.tensor_tensor(out=qa[pb][64:96, hh, sl], in0=t1[tb][64:96, :], in1=t2[tb][64:96, :], op=ALU.add),
                              reads=[r_t1[tb], r_t2[tb]], writes=[r_q[pb][hh]])
                        bk = K.bank()
                        mm(K, K.ps[0:64, bk, :], wukv[:, h * 128:h * 128 + 64], ckvn[:, sl], True, True, [r_c, r_ckvn], [K.r_ps[bk]])
                        fw.op("act", lambda e: e.activation(out=ka[pb][0:64, hh, sl], in_=K.ps[0:64, bk, :], func=AF.Copy),
                              reads=[K.r_ps[bk]], writes=[r_k[pb][hh]])
                        fw.op("pool", lambda e: e.tensor_copy(out=ka[pb][64:96, hh, sl], in_=kpe[64:96, sl]), reads=[r_kpe], writes=[r_k[pb][hh]])
                    bv = K.bank()
                    vcols = wukv[:, 2 * j * 128:(2 * j + 2) * 128].rearrange("p (h c) -> p h c", h=2)[:, :, 64:128]
                    for it in range(4):
                        i = nt * 4 + it
                        mm(K, K.ps[:, bv, it * 128:(it + 1) * 128], ckvn[:, i * 128:(i + 1) * 128], vcols, True, True, [r_c, r_ckvn], [K.r_ps[bv]])
                    src = K.ps[:, bv, :].rearrange("p (t c) -> p t c", t=4)
                    fw.op("act", lambda e: e.activation(out=VE[pb][:, nt * 4:nt * 4 + 4, 0:64], in_=src[:, :, 0:64], func=AF.Copy),
                          reads=[K.r_ps[bv]], writes=[r_VE[pb]])
                    fw.op("act", lambda e: e.activation(out=VO[pb][:, nt * 4:nt * 4 + 4, 64:128], in_=src[:, :, 64:128], func=AF.Copy),
                          reads=[K.r_ps[bv]], writes=[r_VO[pb]])
                for hh in range(2):
                    rq = [r_q[pb][hh], r_k[pb][hh], r_VE[pb] if hh == 0 else r_VO[pb], r_c]
                    if hh == 0:
                        attention_head(K, A, qa[pb][:, 0, :], ka[pb][:, 0, :], 96, lambda i: VE[pb][:, i, 0:65], 65, 0, 64, maskT[:, :], rq,
                                       lambda G: ao[pb][0:64, G * 512:(G + 1) * 512], lambda G: r_ao[pb])
                    else:
                        attention_head(K, A, qa[pb][:, 1, :], ka[pb][:, 1, :], 96, lambda i: VO[pb][:, i, 0:128], 128, 64, 0, maskT[:, :], rq,
                                       lambda G: ao[pb][64:128, G * 512:(G + 1) * 512], lambda G: r_ao[pb])
                fw.dma("sp", c_ao[pb], AT[4 + j, :, s * SEQ:(s + 1) * SEQ], ao[pb][:], reads=[r_ao[pb]], writes=[K.r_AT[s]])
        fw.barrier(c_ht + c_ao + [c_c])


GDN_QSCALE = 128 ** -0.5


def phase_gdn_proj(K, HT, w_in, convw_col, dtb_bc, alog_bc, S):
    fw = K.fw
    with ExitStack() as es:
        hts = _tile(K, es, "gpht", [128, 8, NT], BF16)
        wblk = [_tile(K, es, "gpw%d" % i, [128, 8, 512], BF16) for i in range(2)]
        wba = _tile(K, es, "gpwba", [128, 8, 16], BF16)
        dg = [_tile(K, es, "gpdg%d" % i, [128, 4, 4, 128], BF16) for i in range(2)]
        xc = [_tile(K, es, "gpxc%d" % i, [128, 515], BF16) for i in range(4)]
        qs = [_tile(K, es, "gpqs%d" % i, [128, 512], F32) for i in range(4)]
        sq = [_tile(K, es, "gpsq%d" % i, [128, 512], BF16) for i in range(4)]
        tmp = [_tile(K, es, "gptmp%d" % i, [128, 512], F32) for i in range(4)]
        rinv = [_tile(K, es, "gprinv%d" % i, [128, 512], F32) for i in range(4)]
        qn = [_tile(K, es, "gpqn%d" % i, [128, 512], BF16) for i in range(4)]
        tt = [_tile(K, es, "gptt%d" % i, [128, 4, 128], BF16) for i in range(4)]
        gt = [_tile(K, es, "gpgt%d" % i, [128, D], BF16) for i in range(2)]
        bgt = [_tile(K, es, "gpbg%d" % i, [128, 16], F32) for i in range(2)]
        e1 = _tile(K, es, "gpe1", [128, 8], F32)
        nexpa = _tile(K, es, "gpnexpa", [128, 8], F32)
        r_ht = [fw.res() for _ in range(NT // 512)]
        r_w = [fw.res() for _ in range(2)]
        r_dg = [fw.res() for _ in range(2)]
        r_xc = [fw.res() for _ in range(4)]
        r_qs = [fw.res() for _ in range(4)]
        r_sq = [fw.res() for _ in range(4)]
        r_tmp = [fw.res() for _ in range(4)]
        r_rinv = [fw.res() for _ in range(4)]
        r_qn = [fw.res() for _ in range(4)]
        r_tt = [fw.res() for _ in range(4)]
        r_gt = [fw.res() for _ in range(2)]
        r_bg = [fw.res() for _ in range(2)]
        r_c, r_e1 = fw.res(), fw.res()
        c_ht = fw.chan()
        c_w = [fw.chan() for _ in range(2)]
        c_c = fw.chan()
        c_qn = [fw.chan() for _ in range(4)]
        c_tt = [fw.chan() for _ in range(4)]
        c_gt = [fw.chan() for _ in range(2)]
        c_bg = [fw.chan() for _ in range(2)]
        HTv = HT.rearrange("c p n -> p c n")
        w_inv = w_in.rearrange("(c p) n -> p c n", p=128)
        for g in range(NT // 512):
            fw.dma("sp", c_ht, hts[:, :, g * 512:(g + 1) * 512], HTv[:, :, g * 512:(g + 1) * 512], reads=[K.r_HT[g]], writes=[r_ht[g]])
        fw.dma("pool", c_c, wba[:], w_inv[:, :, 3072:3088], writes=[r_c])
        fw.op("act", lambda e: e.activation(out=nexpa[:], in_=alog_bc, func=AF.Exp), reads=[K.r_const], writes=[r_c])
        fw.op("dve", lambda e: e.tensor_scalar(out=nexpa[:], in0=nexpa[:], scalar1=-1.0, scalar2=None, op0=ALU.mult), reads=[r_c], writes=[r_c])
        nw = 0
        ci = 0
        qi = 0
        ti = 0

        def load_w(col0):
            nonlocal nw
            b = nw % 2
            nw += 1
            fw.dma("pool", c_w[b], wblk[b][:], w_inv[:, :, col0:col0 + 512], writes=[r_w[b]])
            return b

        wq = {0: load_w(0)}
        for blk in range(6):
            wb = wq[blk]
            if blk + 1 < 6:
                wq[blk + 1] = load_w((blk + 1) * 512)
            db = blk % 2
            for tap in range(4):
                for fcl in range(4):
                    fc = blk * 4 + fcl
                    col = tap * 24 + fc
                    fw.op("pool", lambda e: e.tensor_scalar(out=dg[db][:, tap, fcl, :], in0=K.identf[:, :], scalar1=convw_col[:, col:col + 1], scalar2=None,
                                                            op0=ALU.mult), reads=[K.r_const], writes=[r_dg[db]])
            kind = "q" if blk < 2 else ("k" if blk < 4 else "v")
            for s in range(2):
                for nt in range(4):
                    g = s * 4 + nt
                    pbk = []
                    for fcl in range(4):
                        bank = K.bank()
                        pbk.append(bank)
                        for kc in range(8):
                            mm(K, K.ps[:, bank, :], wblk[wb][:, kc, fcl * 128:(fcl + 1) * 128], hts[:, kc, g * 512:(g + 1) * 512], kc == 0, kc == 7,
                               [r_w[wb], r_ht[g]], [K.r_ps[bank]])
                    for fcl in range(4):
                        x = xc[fcl]
                        if nt == 0:
                            fw.op("pool", lambda e: e.memset(x[:, 0:3], 0.0), writes=[r_xc[fcl]])
                        else:
                            fw.op("pool", lambda e: e.tensor_copy(out=x[:, 0:3], in_=x[:, 512:515]), reads=[r_xc[fcl]], writes=[r_xc[fcl]])
                        fw.op("act", lambda e: e.activation(out=x[:, 3:515], in_=K.ps[:, pbk[fcl], :], func=AF.Copy), reads=[K.r_ps[pbk[fcl]]], writes=[r_xc[fcl]])
                    cbk = []
                    for fcl in range(4):
                        cb = K.bank()
                        cbk.append(cb)
                        for tap in range(4):
                            mm(K, K.ps[:, cb, :], dg[db][:, tap, fcl, :], xc[fcl][:, tap:tap + 512], tap == 0, tap == 3, [r_dg[db], r_xc[fcl]], [K.r_ps[cb]])
                    for fcl in range(4):
                        if kind == "v":
                            fw.op("act", lambda e: e.activation(out=qn[fcl][:], in_=K.ps[:, cbk[fcl], :], func=AF.Silu), reads=[K.r_ps[cbk[fcl]]], writes=[r_qn[fcl]])
                        else:
                            fw.op("act", lambda e: e.activation(out=qs[fcl][:], in_=K.ps[:, cbk[fcl], :], func=AF.Silu), reads=[K.r_ps[cbk[fcl]]], writes=[r_qs[fcl]])
                    if kind != "v":
                        sbk = []
                        for fcl in range(4):
                            fw.op("pool", lambda e: e.tensor_tensor(out=sq[fcl][:], in0=qs[fcl][:], in1=qs[fcl][:], op=ALU.mult), reads=[r_qs[fcl]], writes=[r_sq[fcl]])
                        for fcl in range(4):
                            sb = K.bank()
                            sbk.append(sb)
                            mm(K, K.ps[:, sb, :], K.onesb[:, :], sq[fcl][:], True, True, [r_sq[fcl], K.r_const], [K.r_ps[sb]])
                        for fcl in range(4):
                            fw.op("act", lambda e: e.activation(out=tmp[fcl][:], in_=K.ps[:, sbk[fcl], :], func=AF.Sqrt, bias=K.epsb[:, 0:1], scale=1.0),
                                  reads=[K.r_ps[sbk[fcl]], K.r_const], writes=[r_tmp[fcl]])
                        sc = GDN_QSCALE if kind == "q" else 1.0
                        for fcl in range(4):
                            fw.op("dve", lambda e: e.reciprocal(out=rinv[fcl][:], in_=tmp[fcl][:]), reads=[r_tmp[fcl]], writes=[r_rinv[fcl]])
                        for fcl in range(4):
                            fw.op("dve", lambda e: e.scalar_tensor_tensor(out=qn[fcl][:], in0=qs[fcl][:], scalar=sc, in1=rinv[fcl][:], op0=ALU.mult, op1=ALU.mult),
                                  reads=[r_qs[fcl], r_rinv[fcl]], writes=[r_qn[fcl]])
                    for fcl in range(4):
                        hd = (blk * 4 + fcl) % 8
                        if kind in ("q", "k"):
                            dst = S["QT"] if kind == "q" else S["KT"]
                            fw.dma("sp", c_qn[fcl], dst[hd, :, g * 512:(g + 1) * 512], qn[fcl][:], reads=[r_qn[fcl]], writes=[S["r_QK"][s]])
                    if kind in ("k", "v"):
                        tbk = []
                        for fcl in range(4):
                            tb = K.bank()
                            tbk.append(tb)
                            pbf = K.ps[:, tb, :].bitcast(BF16)
                            for it in range(4):
                                fw.op("pe", lambda e: e.transpose(out=pbf[:, it * 128:(it + 1) * 128], in_=qn[fcl][:, it * 128:(it + 1) * 128], identity=K.identb[:]),
                                      reads=[r_qn[fcl], K.r_const], writes=[K.r_ps[tb]])
                        for fcl in range(4):
                            hd = (blk * 4 + fcl) % 8
                            pbf = K.ps[:, tbk[fcl], :].bitcast(BF16)
                            fw.op("act", lambda e: e.activation(out=tt[fcl][:], in_=pbf[:, 0:512].rearrange("p (t c) -> p t c", t=4), func=AF.Copy),
                                  reads=[K.r_ps[tbk[fcl]]], writes=[r_tt[fcl]])
                            dst = S["Ktok"] if kind == "k" else S["Vtok"]
                            fw.dma("sp", c_tt[fcl], dst[g * 512:(g + 1) * 512, hd * 128:(hd + 1) * 128].rearrange("(t p) c -> p t c", p=128), tt[fcl][:],
                                   reads=[r_tt[fcl]], writes=[S["r_tok"][s]])
        wg = [load_w(3088), load_w(3088 + 512)]
        gi = 0
        for g in range(NT // 512):
            s = g // 4
            for it in range(4):
                tok0 = g * 512 + it * 128
                b_ = gi % 2
                gi += 1
                for half in range(2):
                    bank = K.bank()
                    for kc in range(8):
                        mm(K, K.ps[:, bank, :], hts[:, kc, tok0:tok0 + 128], wblk[wg[half]][:, kc, :], kc == 0, kc == 7, [r_w[wg[half]], r_ht[g]], [K.r_ps[bank]])
                    fw.op("act", lambda e: e.activation(out=gt[b_][:, half * 512:(half + 1) * 512], in_=K.ps[:, bank, :], func=AF.Silu),
                          reads=[K.r_ps[bank]], writes=[r_gt[b_]])
                fw.dma("sp", c_gt[b_], S["Gtok"][tok0:tok0 + 128, :], gt[b_][:], reads=[r_gt[b_]], writes=[S["r_tok"][s]])
                bank = K.bank()
                for kc in range(8):
                    mm(K, K.ps[:, bank, 0:16], hts[:, kc, tok0:tok0 + 128], wba[:, kc, :], kc == 0, kc == 7, [r_c, r_ht[g]], [K.r_ps[bank]])
                fw.op("act", lambda e: e.activation(out=bgt[b_][:, 0:8], in_=K.ps[:, bank, 0:8], func=AF.Sigmoid), reads=[K.r_ps[bank]], writes=[r_bg[b_]])
                fw.op("dve", lambda e: e.tensor_tensor(out=e1[:], in0=K.ps[:, bank, 8:16], in1=dtb_bc, op=ALU.add), reads=[K.r_ps[bank], K.r_const], writes=[r_e1])
                fw.op("act", lambda e: e.activation(out=e1[:], in_=e1[:], func=AF.Exp), reads=[r_e1], writes=[r_e1])
                fw.op("act", lambda e: e.activation(out=e1[:], in_=e1[:], func=AF.Ln, bias=K.oneb[:, 0:1], scale=1.0), reads=[r_e1, K.r_const], writes=[r_e1])
                fw.op("dve", lambda e: e.tensor_tensor(out=bgt[b_][:, 8:16], in0=e1[:], in1=nexpa[:], op=ALU.mult), reads=[r_e1, r_c], writes=[r_bg[b_]])
                fw.dma("sp", c_bg[b_], S["BG"][tok0:tok0 + 128, :], bgt[b_][:], reads=[r_bg[b_]], writes=[S["r_tok"][s]])
        fw.barrier([c_ht, c_c] + c_w + c_qn + c_tt + c_gt + c_bg)


BIG = 30000.0


def phase_gdn_core(K, S, AT, onorm_bc, C):
    fw = K.fw
    with ExitStack() as es:
        def T(name, dt=F32, n=1, shape=(128, 8, 128)):
            return [_tile(K, es, "gc%s%d" % (name, i), list(shape), dt) for i in range(n)]
        U, M1, SM = T("U", F32, 1, (128, 128))[0], T("M1", F32, 1, (128, 128))[0], T("SM", F32, 1, (128, 128))[0]
        def RL(n):
            return [fw.res() for _ in range(n)]
        allc = []
        r_c = fw.res()
        c_c = fw.chan()
        fw.dma("sp", c_c, U[:], C["U"], writes=[r_c])
        fw.dma("sp", c_c, M1[:], C["gmask1"], writes=[r_c])
        fw.dma("sp", c_c, SM[:], C["gstrict"], writes=[r_c])
        QTv = S["QT"].rearrange("h p n -> p h n")
        KTv = S["KT"].rearrange("h p n -> p h n")
        ATv = AT.rearrange("h p n -> p h n")

        def bc_h(ap8, lo, n=4):
            return ap8[:, lo:lo + n].unsqueeze(2).broadcast_to([128, n, 128])

        def bc_m(ap, n=4):
            return ap.unsqueeze(1).broadcast_to([128, n, 128])

        def stage(mmfn):
            banks = [K.bank(), K.bank()]
            for h in range(8):
                mmfn(h, K.ps[:, banks[h // 4], (h % 4) * 128:(h % 4 + 1) * 128], banks[h // 4])
            return banks

        def pv(bank):
            return K.ps[:, bank, :].rearrange("p (h c) -> p h c", h=4)

        def hs(hf):
            return slice(hf * 4, hf * 4 + 4)

        def chain(s):
            qT, kT = T("q", BF16, 2), T("k", BF16, 2)
            ktok, vtok, gtok = T("kt", BF16, 2), T("vt", BF16, 2), T("gt", BF16, 2)
            bg = T("bg", F32, 2, (128, 16))
            St, Sb = T("S")[0], T("Sb", BF16)[0]
            NGU, ET, Esb = T("NGU")[0], T("ET")[0], T("Esb")[0]
            F32R = mybir.dt.float32r
            Aa, Ab, Pp = T("A", F32R, 2), T("AT", F32R, 2), T("P", F32R, 2)
            vb, kbd, usb, ot, sqo, og = T("vb")[0], T("kbd")[0], T("usb")[0], T("ot")[0], T("sqo")[0], T("og")[0]
            ktl, wT, inT, vnw, ogb = T("ktl", BF16)[0], T("wT", BF16)[0], T("inT", BF16)[0], T("vnw", BF16)[0], T("ogb", BF16)[0]
            att = T("att", BF16, 2)
            sc = {n: T(n, F32, 1, (128, 8))[0] for n in ("gam", "eg", "gl", "dl", "et", "beg", "nbeta", "ng", "ss", "rstd")}

            r_in, r_att = RL(2), RL(2)
            r_sc, r_S, r_Sb = fw.res(), fw.res(), fw.res()
            r_NGU = fw.res()
            r_ET, r_Esb = RL(2), RL(2)
            r_A, r_B, r_P = [RL(2) for _ in range(2)], [RL(2) for _ in range(2)], [RL(2) for _ in range(2)]
            r_vb, r_kbd, r_ktl = fw.res(), fw.res(), fw.res()
            r_usb, r_wT, r_inT, r_vnw, r_ot = RL(2), RL(2), RL(2), RL(2), RL(2)
            r_sq, r_og, r_ogb = fw.res(), fw.res(), fw.res()
            c_in = [fw.chan() for _ in range(2)]
            c_att = [fw.chan() for _ in range(2)]
            allc.extend(c_in + c_att)
            def loads(bb):
                i_ = bb % 2
                t0_ = s * SEQ + bb * 128
                fw.dma("sp", c_in[i_], qT[i_][:], QTv[:, :, t0_:t0_ + 128], reads=[S["r_QK"][s]], writes=[r_in[i_]])
                fw.dma("sp", c_in[i_], kT[i_][:], KTv[:, :, t0_:t0_ + 128], reads=[S["r_QK"][s]], writes=[r_in[i_]])
                fw.dma("sp", c_in[i_], ktok[i_][:], S["Ktok"][t0_:t0_ + 128, :].rearrange("p (h c) -> p h c", h=8), reads=[S["r_tok"][s]], writes=[r_in[i_]])
                fw.dma("sp", c_in[i_], vtok[i_][:], S["Vtok"][t0_:t0_ + 128, :].rearrange("p (h c) -> p h c", h=8), reads=[S["r_tok"][s]], writes=[r_in[i_]])
                fw.dma("sp", c_in[i_], gtok[i_][:], S["Gtok"][t0_:t0_ + 128, :].rearrange("p (h c) -> p h c", h=8), reads=[S["r_tok"][s]], writes=[r_in[i_]])
                fw.dma("sp", c_in[i_], bg[i_][:], S["BG"][t0_:t0_ + 128, :], reads=[S["r_tok"][s]], writes=[r_in[i_]])

            fw.op("pool", lambda e: e.memset(St[:], 0.0), writes=[r_S])
            fw.op("pool", lambda e: e.memset(Sb[:], 0.0), writes=[r_Sb])
            for b in range(SEQ // 128):
                tok0 = s * SEQ + b * 128
                ib_ = b % 2
                rin = r_in[ib_]
                q_, k_, kt_, vt_, gt_, bg_ = qT[ib_], kT[ib_], ktok[ib_], vtok[ib_], gtok[ib_], bg[ib_]
                if b == 0:
                    loads(0)
                if b + 1 < SEQ // 128:
                    loads(b + 1)
                b1, b2 = K.bank(), K.bank()
                mm(K, K.ps[:, b1, 0:8], U[:, :], bg_[:, 8:16], True, True, [r_c, rin], [K.r_ps[b1]])
                mm(K, K.ps[:, b2, 0:8], K.onesf[:, :], bg_[:, 8:16], True, True, [K.r_const, rin], [K.r_ps[b2]])
                fw.op("dve", lambda e: e.tensor_copy(out=sc["gam"][:], in_=K.ps[:, b1, 0:8]), reads=[K.r_ps[b1]], writes=[r_sc])
                fw.op("act", lambda e: e.activation(out=sc["eg"][:], in_=K.ps[:, b1, 0:8], func=AF.Exp), reads=[K.r_ps[b1]], writes=[r_sc])
                fw.op("act", lambda e: e.activation(out=sc["gl"][:], in_=K.ps[:, b2, 0:8], func=AF.Exp), reads=[K.r_ps[b2]], writes=[r_sc])
                fw.op("dve", lambda e: e.tensor_tensor(out=sc["dl"][:], in0=K.ps[:, b2, 0:8], in1=sc["gam"][:], op=ALU.subtract),
                      reads=[K.r_ps[b2], r_sc], writes=[r_sc])
                fw.op("act", lambda e: e.activation(out=sc["et"][:], in_=sc["dl"][:], func=AF.Exp), reads=[r_sc], writes=[r_sc])
                fw.op("dve", lambda e: e.tensor_tensor(out=sc["beg"][:], in0=bg_[:, 0:8], in1=sc["eg"][:], op=ALU.mult), reads=[rin, r_sc], writes=[r_sc])
                fw.op("dve", lambda e: e.tensor_scalar(out=sc["nbeta"][:], in0=bg_[:, 0:8], scalar1=-1.0, scalar2=None, op0=ALU.mult), reads=[rin], writes=[r_sc])
                fw.op("dve", lambda e: e.tensor_scalar(out=sc["ng"][:], in0=bg_[:, 8:16], scalar1=-1.0, scalar2=None, op0=ALU.mult), reads=[rin], writes=[r_sc])
                fw.op("pool", lambda e: e.tensor_tensor(out=NGU[:], in0=bc_m(U[:, :], 8), in1=bc_h(sc["ng"], 0, 8), op=ALU.mult),
                      reads=[r_c, r_sc], writes=[r_NGU])

                yield
                def mm_p1(h, o, bk):
                    gb = bg_[:, 8 + h:9 + h].broadcast_to([128, 128])
                    mm(K, o, gb, U[:, :], True, False, [rin, r_c], [K.r_ps[bk]])
                    mm(K, o, NGU[:, h, :], K.onesf[:, :], False, False, [r_NGU, K.r_const], [K.r_ps[bk]])
                    mm(K, o, K.identf[:, :], M1[:, :], False, True, [K.r_const, r_c], [K.r_ps[bk]])
                bks = stage(mm_p1)
                for hf in range(2):
                    fw.op("act", lambda e: e.activation(out=ET[:, hs(hf), :], in_=pv(bks[hf]), func=AF.Exp), reads=[K.r_ps[bks[hf]]], writes=[r_ET[hf]])

                yield
                def mm_tr(h, o, bk):
                    fw.op("pe", lambda e: e.transpose(out=o, in_=ET[:, h, :], identity=K.identf[:]), reads=[r_ET[h // 4], K.r_const], writes=[K.r_ps[bk]])
                bks = stage(mm_tr)
                for hf in range(2):
                    fw.op("dve", lambda e: e.tensor_tensor(out=Esb[:, hs(hf), :], in0=pv(bks[hf]), in1=bc_m(SM[:, :]), op=ALU.mult),
                          reads=[K.r_ps[bks[hf]], r_c], writes=[r_Esb[hf]])
                    fw.op("pool", lambda e: e.tensor_tensor(out=Esb[:, hs(hf), :], in0=Esb[:, hs(hf), :], in1=bc_h(sc["nbeta"], hf * 4), op=ALU.mult),
                          reads=[r_Esb[hf], r_sc], writes=[r_Esb[hf]])

                yield
                def mm_kk(h, o, bk):
                    mm(K, o, k_[:, h, :], k_[:, h, :], True, True, [rin], [K.r_ps[bk]])
                bks = stage(mm_kk)
                ca = 0
                for hf in range(2):
                    fw.op("dve", lambda e: e.tensor_tensor(out=Aa[ca][:, hs(hf), :], in0=pv(bks[hf]), in1=Esb[:, hs(hf), :], op=ALU.mult),
                          reads=[K.r_ps[bks[hf]], r_Esb[hf]], writes=[r_A[ca][hf]])

                yield
                def mm_at(h, o, bk):
                    fw.op("pe", lambda e: e.transpose(out=o, in_=Aa[ca][:, h, :].bitcast(F32), identity=K.identf[:]), reads=[r_A[ca][h // 4], K.r_const], writes=[K.r_ps[bk]])
                bks = stage(mm_at)
                cb, cp = 0, 0
                for hf in range(2):
                    fw.op("act", lambda e: e.activation(out=Ab[cb][:, hs(hf), :], in_=pv(bks[hf]), func=AF.Copy), reads=[K.r_ps[bks[hf]]], writes=[r_B[cb][hf]])
                    fw.op("dve", lambda e: e.tensor_tensor(out=Pp[cp][:, hs(hf), :], in0=pv(bks[hf]), in1=bc_m(K.identf[:, :]), op=ALU.add),
                          reads=[K.r_ps[bks[hf]], K.r_const], writes=[r_P[cp][hf]])
                for lev in range(6):
                    na, nb_, np_ = 1 - ca, 1 - cb, 1 - cp

                    def mm_a2(h, o, bk):
                        mm(K, o, Ab[cb][:, h, :], Aa[ca][:, h, :], True, True, [r_B[cb][h // 4], r_A[ca][h // 4]], [K.r_ps[bk]])
                    bks = stage(mm_a2)
                    if lev < 5:
                        def mm_b2(h, o, bk):
                            mm(K, o, Aa[ca][:, h, :], Ab[cb][:, h, :], True, True, [r_B[cb][h // 4], r_A[ca][h // 4]], [K.r_ps[bk]])
                        bks2 = stage(mm_b2)
                    for hf in range(2):
                        fw.op("act", lambda e: e.activation(out=Aa[na][:, hs(hf), :], in_=pv(bks[hf]), func=AF.Copy), reads=[K.r_ps[bks[hf]]], writes=[r_A[na][hf]])
                    if lev < 5:
                        for hf in range(2):
                            fw.op("dve", lambda e: e.tensor_copy(out=Ab[nb_][:, hs(hf), :], in_=pv(bks2[hf])), reads=[K.r_ps[bks2[hf]]], writes=[r_B[nb_][hf]])

                    def mm_p(h, o, bk):
                        mm(K, o, Aa[na][:, h, :], Pp[cp][:, h, :], True, True, [r_A[na][h // 4], r_P[cp][h // 4]], [K.r_ps[bk]])
                    bks3 = stage(mm_p)
                    for hf in range(2):
                        fw.op("dve", lambda e: e.tensor_tensor(out=Pp[np_][:, hs(hf), :], in0=pv(bks3[hf]), in1=Pp[cp][:, hs(hf), :].bitcast(F32), op=ALU.add),
                              reads=[K.r_ps[bks3[hf]], r_P[cp][hf]], writes=[r_P[np_][hf]])
                    ca, cp = na, np_
                    yield
                    if lev < 5:
                        cb = nb_
                yield
                fw.op("pool", lambda e: e.tensor_tensor(out=vb[:], in0=vt_[:], in1=bc_h(bg_, 0, 8), op=ALU.mult), reads=[rin], writes=[r_vb])
                fw.op("pool", lambda e: e.tensor_tensor(out=kbd[:], in0=kt_[:], in1=bc_h(sc["beg"], 0, 8), op=ALU.mult), reads=[rin, r_sc], writes=[r_kbd])
                fw.op("pool", lambda e: e.tensor_tensor(out=ktl[:], in0=kt_[:], in1=bc_h(sc["et"], 0, 8), op=ALU.mult), reads=[rin, r_sc], writes=[r_ktl])

                yield
                def mm_u(h, o, bk):
                    mm(K, o, Pp[cp][:, h, :].bitcast(F32), vb[:, h, :], True, True, [r_P[cp][h // 4], r_vb], [K.r_ps[bk]])
                bks = stage(mm_u)

                def mm_w(h, o, bk):
                    mm(K, o, kbd[:, h, :], Pp[cp][:, h, :].bitcast(F32), True, True, [r_P[cp][h // 4], r_kbd], [K.r_ps[bk]])
                bks2 = stage(mm_w)
                for hf in range(2):
                    fw.op("act", lambda e: e.activation(out=usb[:, hs(hf), :], in_=pv(bks[hf]), func=AF.Copy), reads=[K.r_ps[bks[hf]]], writes=[r_usb[hf]])
                for hf in range(2):
                    fw.op("act", lambda e: e.activation(out=wT[:, hs(hf), :], in_=pv(bks2[hf]), func=AF.Copy), reads=[K.r_ps[bks2[hf]]], writes=[r_wT[hf]])

                yield
                def mm_kq(h, o, bk):
                    mm(K, o, k_[:, h, :], q_[:, h, :], True, True, [rin], [K.r_ps[bk]])
                bks = stage(mm_kq)
                for hf in range(2):
                    fw.op("dve", lambda e: e.tensor_tensor(out=inT[:, hs(hf), :], in0=pv(bks[hf]), in1=ET[:, hs(hf), :], op=ALU.mult),
                          reads=[K.r_ps[bks[hf]], r_ET[hf]], writes=[r_inT[hf]])

                yield
                def mm_ws(h, o, bk):
                    mm(K, o, wT[:, h, :], Sb[:, h, :], True, True, [r_wT[h // 4], r_Sb], [K.r_ps[bk]])
                bks = stage(mm_ws)
                for hf in range(2):
                    fw.op("dve", lambda e: e.tensor_tensor(out=vnw[:, hs(hf), :], in0=usb[:, hs(hf), :], in1=pv(bks[hf]), op=ALU.subtract),
                          reads=[K.r_ps[bks[hf]], r_usb[hf]], writes=[r_vnw[hf]])

                def mm_qs(h, o, bk):
                    mm(K, o, q_[:, h, :], Sb[:, h, :], True, True, [rin, r_Sb], [K.r_ps[bk]])
                bks = stage(mm_qs)
                for hf in range(2):
                    fw.op("dve", lambda e: e.tensor_tensor(out=ot[:, hs(hf), :], in0=pv(bks[hf]), in1=bc_h(sc["eg"], hf * 4), op=ALU.mult),
                          reads=[K.r_ps[bks[hf]], r_sc], writes=[r_ot[hf]])

                def mm_iv(h, o, bk):
                    mm(K, o, inT[:, h, :], vnw[:, h, :], True, True, [r_inT[h // 4], r_vnw[h // 4]], [K.r_ps[bk]])
                bks = stage(mm_iv)
                for hf in range(2):
                    fw.op("dve", lambda e: e.tensor_tensor(out=ot[:, hs(hf), :], in0=pv(bks[hf]), in1=ot[:, hs(hf), :], op=ALU.add),
                          reads=[K.r_ps[bks[hf]], r_ot[hf]], writes=[r_ot[hf]])

                def mm_kv(h, o, bk):
                    mm(K, o, ktl[:, h, :], vnw[:, h, :], True, True, [r_ktl, r_vnw[h // 4]], [K.r_ps[bk]])
                bks = stage(mm_kv)
                fw.op("pool", lambda e: e.tensor_tensor(out=St[:], in0=St[:], in1=bc_h(sc["gl"], 0, 8), op=ALU.mult), reads=[r_S, r_sc], writes=[r_S])
                for hf in range(2):
                    fw.op("dve", lambda e: e.tensor_tensor(out=St[:, hs(hf), :], in0=pv(bks[hf]), in1=St[:, hs(hf), :], op=ALU.add),
                          reads=[K.r_ps[bks[hf]], r_S], writes=[r_S])
                fw.op("pool", lambda e: e.tensor_copy(out=Sb[:], in_=St[:]), reads=[r_S], writes=[r_Sb])
                yield
                fw.op("pool", lambda e: e.tensor_tensor(out=sqo[:], in0=ot[:], in1=ot[:], op=ALU.mult), reads=r_ot, writes=[r_sq])
                fw.op("dve", lambda e: e.tensor_reduce(out=sc["ss"][:], in_=sqo[:], axis=mybir.AxisListType.X, op=ALU.add), reads=[r_sq], writes=[r_sc])
                fw.op("act", lambda e: e.activation(out=sc["rstd"][:], in_=sc["ss"][:], func=AF.Sqrt, bias=K.epsb[:, 0:1], scale=1.0 / 128),
                      reads=[r_sc, K.r_const], writes=[r_sc])
                fw.op("dve", lambda e: e.reciprocal(out=sc["rstd"][:], in_=sc["rstd"][:]), reads=[r_sc], writes=[r_sc])
                fw.op("pool", lambda e: e.tensor_tensor(out=og[:], in0=ot[:], in1=bc_h(sc["rstd"], 0, 8), op=ALU.mult), reads=r_ot + [r_sc], writes=[r_og])
                fw.op("pool", lambda e: e.tensor_tensor(out=og[:], in0=og[:], in1=bc_m(onorm_bc, 8), op=ALU.mult), reads=[r_og, K.r_const], writes=[r_og])
                fw.op("pool", lambda e: e.tensor_tensor(out=ogb[:], in0=og[:], in1=gt_[:], op=ALU.mult), reads=[r_og, rin], writes=[r_ogb])
                tb = K.bank()
                ptb = K.ps[:, tb, :].bitcast(BF16)
                for h in range(8):
                    fw.op("pe", lambda e: e.transpose(out=ptb[:, h * 128:(h + 1) * 128], in_=ogb[:, h, :], identity=K.identb[:]),
                          reads=[r_ogb, K.r_const], writes=[K.r_ps[tb]])
                ob_ = ib_
                fw.op("act", lambda e: e.activation(out=att[ob_][:], in_=ptb[:, :].rearrange("p (h c) -> p h c", h=8), func=AF.Copy),
                      reads=[K.r_ps[tb]], writes=[r_att[ob_]])
                fw.dma("pool", c_att[ob_], ATv[:, :, tok0:tok0 + 128], att[ob_][:], reads=[r_att[ob_]], writes=[K.r_AT[s]])
        gens = [chain(0), chain(1)]
        while gens:
            for g_ in list(gens):
                try:
                    next(g_)
                except StopIteration:
                    gens.remove(g_)
        fw.barrier(allc + [c_c])
```
